# Optimizing a Trainium2 kernel written in Bass

```python
import jax, jax.numpy as jnp
from jax import lax
import numpy as np

D_MODEL = 1024
BATCH = 8
SEQ = 4096
DEPTH = 2

GRID_W = 64
CTX_LEN = 256
HEAD_DIM = 64
N_Q_HEADS = 8
N_KV_HEADS = 2
Q_GROUP = N_Q_HEADS // N_KV_HEADS
ATTN_WIDTH = N_Q_HEADS * HEAD_DIM
KV_WIDTH = N_KV_HEADS * HEAD_DIM
CHUNK = 128
N_SG_GROUPS = 4
SG_WIDTH = D_MODEL - ATTN_WIDTH
SG_GROUP_DIM = SG_WIDTH // N_SG_GROUPS
IN_WIDTH = ATTN_WIDTH + 2 * KV_WIDTH + 2 * SG_WIDTH
Q_BLOCK = 128
ROPE_THETA = 10000.0
AXIS_ROPE_DIM = HEAD_DIM // 2
CONV_WIDTH = 31
D_FF = ((8 * D_MODEL // 3 + 255) // 256) * 256
N_EVEN = (DEPTH + 1) // 2
N_ODD = DEPTH // 2
EPS = 1e-6

kernel_name = "hybrid_attn_sgmlp_conformer_dit"


def rms_norm(x, g):
    xf = x.astype(jnp.float32)
    y = xf * lax.rsqrt(jnp.mean(xf * xf, axis=-1, keepdims=True) + EPS)
    return (y * g.astype(jnp.float32)).astype(x.dtype)


def layer_norm(x, g=None, b=None):
    xf = x.astype(jnp.float32)
    mu = jnp.mean(xf, axis=-1, keepdims=True)
    var = jnp.mean(jnp.square(xf - mu), axis=-1, keepdims=True)
    y = (xf - mu) * lax.rsqrt(var + EPS)
    if g is not None:
        y = y * g.astype(jnp.float32) + b.astype(jnp.float32)
    return y.astype(x.dtype)


def axial_rope_tables(n):
    rows = n // GRID_W
    row = jnp.broadcast_to(jnp.arange(rows)[:, None], (rows, GRID_W)).reshape(-1).astype(jnp.float32)
    col = jnp.broadcast_to(jnp.arange(GRID_W)[None, :], (rows, GRID_W)).reshape(-1).astype(jnp.float32)
    inv = ROPE_THETA ** (-jnp.arange(0, AXIS_ROPE_DIM, 2, dtype=jnp.float32) / AXIS_ROPE_DIM)
    ang_r = row[:, None] * inv[None, :]
    ang_c = col[:, None] * inv[None, :]
    return (jnp.cos(ang_r), jnp.sin(ang_r), jnp.cos(ang_c), jnp.sin(ang_c))


def rotate_half(x, cos, sin):
    x1, x2 = jnp.split(x, 2, axis=-1)
    cos = cos[None, :, None, :]
    sin = sin[None, :, None, :]
    return jnp.concatenate([x1 * cos - x2 * sin, x2 * cos + x1 * sin], axis=-1)


def apply_axial_rope(x, tables):
    cos_r, sin_r, cos_c, sin_c = tables
    xf = x.astype(jnp.float32)
    xr, xc = jnp.split(xf, 2, axis=-1)
    out = jnp.concatenate([rotate_half(xr, cos_r, sin_r), rotate_half(xc, cos_c, sin_c)], axis=-1)
    return out.astype(x.dtype)


def block_attention(q, k, v):
    b, n = q.shape[0], q.shape[1]
    nb = n // Q_BLOCK
    qb = (q * HEAD_DIM ** -0.5).reshape(b, nb, Q_BLOCK, N_KV_HEADS, Q_GROUP, HEAD_DIM)
    qb = qb.transpose(1, 0, 2, 3, 4, 5)

    def one_block(q_blk):
        s = jnp.einsum('bqkgd,bskd->bkgqs', q_blk, k).astype(jnp.float32)
        p = jax.nn.softmax(s, axis=-1).astype(v.dtype)
        return jnp.einsum('bkgqs,bskd->bqkgd', p, v)

    o = lax.map(one_block, qb)
    return o.transpose(1, 0, 2, 3, 4, 5).reshape(b, n, ATTN_WIDTH)


def spatial_gating(u, v, w_sp, b_sp):
    b, n = u.shape[0], u.shape[1]
    shape = (b, n // CHUNK, CHUNK, N_SG_GROUPS, SG_GROUP_DIM)
    vn = layer_norm(v.reshape(shape))
    mixed = jnp.einsum('gpq,bmqgc->bmpgc', w_sp, vn) + b_sp.T[:, :, None]
    return (u.reshape(shape) * mixed).reshape(b, n, SG_WIDTH)


def split_in_proj(p):
    o1 = ATTN_WIDTH
    o2 = o1 + KV_WIDTH
    o3 = o2 + KV_WIDTH
    o4 = o3 + SG_WIDTH
    return jnp.split(p, [o1, o2, o3, o4], axis=-1)


def even_mixer(xm, xc, w_in, q_gain, k_gain, w_sp, b_sp, w_out, rope, ctx_out):
    b, n, _ = xm.shape
    bc, m, _ = xc.shape
    q, k, v, su, sv = split_in_proj(xm @ w_in)
    q = apply_axial_rope(rms_norm(q.reshape(b, n, N_Q_HEADS, HEAD_DIM), q_gain), rope)
    k = apply_axial_rope(rms_norm(k.reshape(b, n, N_KV_HEADS, HEAD_DIM), k_gain), rope)
    v = v.reshape(b, n, N_KV_HEADS, HEAD_DIM)
    if ctx_out:
        qc, kc, vc, suc, svc = split_in_proj(xc @ w_in)
    else:
        kc, vc = jnp.split(xc @ w_in[:, ATTN_WIDTH:ATTN_WIDTH + 2 * KV_WIDTH], 2, axis=-1)
    kc = rms_norm(kc.reshape(bc, m, N_KV_HEADS, HEAD_DIM), k_gain)
    vc = vc.reshape(bc, m, N_KV_HEADS, HEAD_DIM)
    attn = block_attention(q, jnp.concatenate([kc, k], axis=1), jnp.concatenate([vc, v], axis=1))
    sg = spatial_gating(jax.nn.gelu(su), jax.nn.gelu(sv), w_sp, b_sp)
    y = jnp.concatenate([attn, sg], axis=-1) @ w_out
    yc = None
    if ctx_out:
        qc = rms_norm(qc.reshape(bc, m, N_Q_HEADS, HEAD_DIM), q_gain)
        attn_c = block_attention(qc, kc, vc)
        sg_c = spatial_gating(jax.nn.gelu(suc), jax.nn.gelu(svc), w_sp, b_sp)
        yc = jnp.concatenate([attn_c, sg_c], axis=-1) @ w_out
    return y, yc


def conformer_conv(x, w_pw1, b_pw1, w_dw, b_dw, ln_g, ln_b, w_pw2, b_pw2):
    a, gate = jnp.split(x @ w_pw1 + b_pw1, 2, axis=-1)
    h = a * jax.nn.sigmoid(gate)
    h = lax.conv_general_dilated(h, w_dw[:, None, :].astype(h.dtype), window_strides=(1,),
                                 padding=[(CONV_WIDTH // 2, CONV_WIDTH // 2)],
                                 dimension_numbers=('NWC', 'WIO', 'NWC'),
                                 feature_group_count=D_MODEL) + b_dw
    h = jax.nn.silu(layer_norm(h, ln_g, ln_b))
    return h @ w_pw2 + b_pw2


def swiglu_ffn(x, w_in, w_out):
    g, u = jnp.split(x @ w_in, 2, axis=-1)
    return (jax.nn.silu(g) * u) @ w_out


def adaln(cvec, w_mod, b_mod):
    return jnp.split(jax.nn.silu(cvec) @ w_mod + b_mod, 6, axis=-1)


def setup_inputs(seed: int = 0) -> dict:
    key = jax.random.key(seed)
    ks = iter(jax.random.split(key, 32))
    f32 = jnp.float32

    def nrm(shape, scale):
        return jax.random.normal(next(ks), shape, f32) * scale

    D = D_MODEL
    return {
        'x': nrm((BATCH, SEQ, D), 1.0),
        'c': nrm((BATCH, D), 1.0),
        'ctx': nrm((BATCH, CTX_LEN, D), 1.0),
        'c_ctx': nrm((D,), 1.0),
        'w_mod': nrm((DEPTH, D, 6 * D), 0.5 * D ** -0.5),
        'b_mod': nrm((DEPTH, 6 * D), 0.01),
        'g_mix': 1.0 + nrm((DEPTH, D), 0.01),
        'g_ffn': 1.0 + nrm((DEPTH, D), 0.01),
        'w_ffn_in': nrm((DEPTH, D, 2 * D_FF), D ** -0.5),
        'w_ffn_out': nrm((DEPTH, D_FF, D), D_FF ** -0.5),
        'w_in': nrm((N_EVEN, D, IN_WIDTH), D ** -0.5),
        'q_gain': 1.0 + nrm((N_EVEN, HEAD_DIM), 0.01),
        'k_gain': 1.0 + nrm((N_EVEN, HEAD_DIM), 0.01),
        'w_sp': nrm((N_EVEN, N_SG_GROUPS, CHUNK, CHUNK), CHUNK ** -0.5),
        'b_sp': 1.0 + nrm((N_EVEN, N_SG_GROUPS, CHUNK), 0.01),
        'w_out': nrm((N_EVEN, ATTN_WIDTH + SG_WIDTH, D), (ATTN_WIDTH + SG_WIDTH) ** -0.5),
        'w_pw1': nrm((N_ODD, D, 2 * D), D ** -0.5),
        'b_pw1': nrm((N_ODD, 2 * D), 0.01),
        'w_dw': nrm((N_ODD, CONV_WIDTH, D), CONV_WIDTH ** -0.5),
        'b_dw': nrm((N_ODD, D), 0.01),
        'ln_g': 1.0 + nrm((N_ODD, D), 0.01),
        'ln_b': nrm((N_ODD, D), 0.01),
        'w_pw2': nrm((N_ODD, D, D), D ** -0.5),
        'b_pw2': nrm((N_ODD, D), 0.01),
        'g_final': 1.0 + nrm((D,), 0.01),
    }


def reference(x, c, ctx, c_ctx, w_mod, b_mod, g_mix, g_ffn, w_ffn_in, w_ffn_out,
              w_in, q_gain, k_gain, w_sp, b_sp, w_out,
              w_pw1, b_pw1, w_dw, b_dw, ln_g, ln_b, w_pw2, b_pw2, g_final):
    n = x.shape[1]
    rope = axial_rope_tables(n)
    h, hc = x, ctx
    for l in range(DEPTH):
        even = (l % 2 == 0)
        i = l // 2
        ctx_after = any(j % 2 == 0 for j in range(l + 1, DEPTH))
        sh1, sc1, gt1, sh2, sc2, gt2 = [t[:, None, :] for t in adaln(c, w_mod[l], b_mod[l])]
        xm = rms_norm(h, g_mix[l]) * (1.0 + sc1) + sh1
        if even or ctx_after:
            csh1, csc1, cgt1, csh2, csc2, cgt2 = adaln(c_ctx, w_mod[l], b_mod[l])
            xc = rms_norm(hc, g_mix[l]) * (1.0 + csc1) + csh1
        if even:
            y, yc = even_mixer(xm, xc, w_in[i], q_gain[i], k_gain[i], w_sp[i], b_sp[i], w_out[i],
                               rope, ctx_after)
        else:
            conv_p = (w_pw1[i], b_pw1[i], w_dw[i], b_dw[i], ln_g[i], ln_b[i], w_pw2[i], b_pw2[i])
            y = conformer_conv(xm, *conv_p)
            yc = conformer_conv(xc, *conv_p) if ctx_after else None
        h = h + gt1 * y
        h = h + gt2 * swiglu_ffn(rms_norm(h, g_ffn[l]) * (1.0 + sc2) + sh2, w_ffn_in[l], w_ffn_out[l])
        if ctx_after:
            hc = hc + cgt1 * yc
            hc = hc + cgt2 * swiglu_ffn(rms_norm(hc, g_ffn[l]) * (1.0 + csc2) + csh2,
                                        w_ffn_in[l], w_ffn_out[l])
    return rms_norm(h, g_final)
```

```python
import numpy as np
import concourse.bass as bass
import concourse.mybir as mybir
from concourse.bass_utils import run_bass_kernel_spmd

F32 = mybir.dt.float32
BF16 = mybir.dt.bfloat16
AF = mybir.ActivationFunctionType
ALU = mybir.AluOpType
AX = mybir.AxisListType

PE, ACT, DVE, POOL, SP = "pe", "act", "dve", "pool", "sp"

D = 1024
CTX = 256
DFF = 2816
NH = 22
T = 512
EPS = 1e-6
NSLOT = 6
DRIP = 2
SHIFT = -8.0
SLOT_ELEMS = 4096


class Buf:
    __slots__ = ("name", "lw", "rd", "dsem", "dcount")

    def __init__(self, name):
        self.name = name
        self.lw = None
        self.rd = {}
        self.dsem = None
        self.dcount = 0


class Op:
    __slots__ = ("eng", "fn", "deps", "is_dma", "buf", "sig", "sigval", "dval")

    def __init__(self, eng, fn, is_dma=False):
        self.eng = eng
        self.fn = fn
        self.deps = []
        self.is_dma = is_dma
        self.buf = None
        self.sig = False
        self.sigval = 0
        self.dval = 0


class K:
    def __init__(self, nc):
        self.nc = nc
        self.ops = []
        self.h = {PE: nc.tensor, ACT: nc.scalar, DVE: nc.vector, POOL: nc.gpsimd, SP: nc.sync}
        self.dsems = {}

    def op(self, eng, fn, reads=(), writes=()):
        o = Op(eng, fn)
        self._deps(o, reads, writes)
        self.ops.append(o)
        return o

    def dma(self, eng, fn, reads=(), writes=(), track=None, disjoint=False):
        o = Op(eng, fn, is_dma=True)
        if track is None:
            track = writes[0] if writes else reads[0]
        key = (track.name, eng)
        if key not in self.dsems:
            self.dsems[key] = [None, 0]
        o.buf = key
        self.dsems[key][1] += 16
        o.dval = self.dsems[key][1]
        self._deps(o, reads, writes, disjoint)
        self.ops.append(o)
        return o

    def _deps(self, o, reads, writes, disjoint=False):
        for b in reads:
            if b.lw is not None:
                o.deps.append(b.lw)
            b.rd[id(o) if o.is_dma else o.eng] = o
        for b in writes:
            if b.lw is not None and not (disjoint and b.lw.is_dma):
                o.deps.append(b.lw)
            for r in b.rd.values():
                if r is not o:
                    o.deps.append(r)
            b.lw = o
            b.rd = {}

    def emit(self):
        nc = self.nc
        for o in self.ops:
            nd = []
            for d in o.deps:
                if d.is_dma:
                    nd.append(d)
                    continue
                if d.eng == PE and o.eng == PE and not o.is_dma:
                    continue
                d.sig = True
                nd.append(d)
            o.deps = nd
        sems = {e: nc.alloc_semaphore(name=f"s_{e}") for e in self.h}
        for key, v in self.dsems.items():
            v[0] = nc.alloc_semaphore(name=f"d_{key[0]}_{key[1]}")
        cnt = {e: 0 for e in self.h}
        for o in self.ops:
            if o.sig and not o.is_dma:
                cnt[o.eng] += 1
                o.sigval = cnt[o.eng]
        waited = {e: {} for e in self.h}
        self.nwait = 0
        for o in self.ops:
            need = {}
            for d in o.deps:
                if d.is_dma:
                    key, val = self.dsems[d.buf][0], d.dval
                else:
                    key, val = sems[d.eng], d.sigval
                kid = id(key)
                if kid not in need or need[kid][1] < val:
                    need[kid] = (key, val)
            w = waited[o.eng]
            for kid, (key, val) in need.items():
                if w.get(kid, 0) >= val:
                    continue
                self.h[o.eng].wait_ge(key, val)
                w[kid] = val
                self.nwait += 1
            ins = o.fn(self.h[o.eng])
            if o.is_dma:
                ins.then_inc(self.dsems[o.buf][0], 16)
            elif o.sig:
                ins.then_inc(sems[o.eng], 1)

    def finish(self, eng, bufs):
        names = {b.name for b in bufs}
        for key, v in self.dsems.items():
            if key[0] in names:
                self.h[eng].wait_ge(v[0], v[1])


VEC_ROWS = {}
_r = 0
for _n, _c in [("c", 8), ("cctx", 8), ("bmod", 96), ("gmix", 16), ("gffn", 16), ("bpw1", 16), ("bdw", 8),
               ("lng", 8), ("lnb", 8), ("bpw2", 8), ("gfin", 8), ("wdw", 248), ("qg", 1), ("kg", 1)]:
    VEC_ROWS[_n] = _r
    _r += _c
NVEC = 512


def rope_tables(seq):
    t = np.arange(seq)
    row = (t // 64).astype(np.float32)
    col = (t % 64).astype(np.float32)
    inv = (10000.0 ** (-np.arange(0, 32, 2, dtype=np.float32) / 32.0)).astype(np.float32)
    ang_r = (row[:, None] * inv[None, :]).astype(np.float32)
    ang_c = (col[:, None] * inv[None, :]).astype(np.float32)
    cr, sr, cc, sc = np.cos(ang_r), np.sin(ang_r), np.cos(ang_c), np.sin(ang_c)
    cos64 = np.concatenate([cr, cr, cc, cc], axis=1).T
    sin64 = np.concatenate([-sr, sr, -sc, sc], axis=1).T
    cos = np.concatenate([cos64, cos64], 0).astype(np.float32)
    sin = np.concatenate([sin64, sin64], 0).astype(np.float32)
    return np.ascontiguousarray(cos), np.ascontiguousarray(sin)


def rope_perm():
    pm = np.zeros((128, 128), np.float32)
    for hh in range(2):
        for d in range(64):
            blk = d // 16
            src = d + 16 if blk in (0, 2) else d - 16
            pm[hh * 64 + src, hh * 64 + d] = 1.0
    return pm


def piece_sequence(nt):
    seq = [("mod", 0, pc) for pc in range(12)]
    seq += [("mod", 1, pc) for pc in range(12)]
    l0 = [("wq",), ("wsu",), ("wsv",), ("wo", 0), ("wo", 1)] + [("fin", 0, p) for p in range(11)] + \
         [("fout", 0, c) for c in range(8)]
    l1a = [("pw1", p) for p in range(4)]
    l1b = [("pw2", 0), ("pw2", 1)] + [("fin", 1, p) for p in range(11)] + [("fout", 1, c) for c in range(8)]
    for i in range(nt):
        seq += l0 + l1a
        if i >= 1:
            seq += l1b
    seq += l1b
    return seq


def build(SEQ, dbg=None):
    NT = SEQ // T
    NKT = (CTX + SEQ) // 128
    nc = bass.Bass("TRN2", target_bir_lowering=False)
    k = K(nc)

    def din(name, shape, dt=F32):
        return nc.dram_tensor(name, list(shape), dt, kind="ExternalInput").ap()

    x_d = din("x", [SEQ, D])
    ctx_d = din("ctx", [CTX, D])
    vecs_d = din("vecs", [NVEC, 128])
    w_mod_d = din("w_mod", [2, D, 6 * D])
    w_ffn_in_d = din("w_ffn_in", [2, D, 2 * DFF])
    w_ffn_out_d = din("w_ffn_out", [2, DFF, D])
    w_in_d = din("w_in", [D, 1792])
    w_sp_d = din("w_sp", [4, 128, 128])
    b_sp_d = din("b_sp", [1, 512])
    w_out_d = din("w_out", [D, D])
    w_pw1_d = din("w_pw1", [D, 2 * D])
    w_pw2_d = din("w_pw2", [D, D])
    ident_d = din("ident", [128, 128])
    pm_d = din("pm", [128, 128])
    cos_d = din("cos", [128, SEQ])
    sin_d = din("sin", [128, SEQ])
    out_d = nc.dram_tensor("out", [SEQ, D], F32, kind="ExternalOutput").ap()
    dbg_d = {}
    if dbg:
        for n, shp in dbg.items():
            dbg_d[n] = nc.dram_tensor("dbg_" + n, list(shp), F32, kind="ExternalOutput").ap()

    def sb(name, shape, dt=F32):
        return nc.alloc_sbuf_tensor("s_" + name, list(shape), dt)

    ring = sb("ring", [128, NSLOT, SLOT_ELEMS], BF16)
    ringB = [Buf(f"ring{i}") for i in range(NSLOT)]
    KT = sb("KT", [128, CTX + SEQ], BF16)
    KTB = Buf("KT")
    VA = sb("VA", [128, NKT, 2, 128], BF16)
    VAB = Buf("VA")
    hT = [sb(f"hT{i}", [128, 8, T]) for i in range(2)]
    hB = [[Buf(f"h{i}_{c}") for c in range(8)] for i in range(2)]
    io = [sb(f"io{i}", [128, D]) for i in range(2)]
    ioB = [Buf(f"io{i}") for i in range(2)]
    xmT = sb("xmT", [128, 8, T], BF16)
    xmB = [Buf(f"xm{c}") for c in range(8)]
    sq = [sb(f"sq{i}", [128, T], BF16) for i in range(2)]
    sqB = [Buf(f"sq{i}") for i in range(2)]
    tmp = [sb(f"tmp{i}", [128, T]) for i in range(3)]
    tmpB = [Buf(f"tmp{i}") for i in range(3)]
    rstd = sb("rstd", [128, T])
    rstdB = Buf("rstd")
    sd = sb("sd", [128, T])
    sdB = Buf("sd")
    AR = sb("AR", [128, NH, T], BF16)
    ARB = [Buf(f"ar{j}") for j in range(NH)]
    AR32 = AR.bitcast(F32) if hasattr(AR, "bitcast") else None
    GW = [sb(f"GW{i}", [128, 8, T + 30], BF16) for i in range(2)]
    NDG = 8
    dg = [sb(f"dg{i}", [128, 128], BF16) for i in range(NDG)]
    dgB = [Buf(f"dg{i}") for i in range(NDG)]
    identb = sb("identb", [128, 128], BF16)
    dgstate = {"n": 0}
    GWB = [[Buf(f"gw{i}_{c}") for c in range(8)] for i in range(2)]
    sgl = [sb(f"sgl{i}", [128, T]) for i in range(2)]
    sglB = [Buf(f"sgl{i}") for i in range(2)]
    cs = sb("cs", [128, 2, T])
    csB = Buf("cs")
    knT = sb("knT", [128, T], BF16)
    knB = Buf("knT")
    gv = sb("gv", [128, T])
    gvB = Buf("gv")
    gv2 = sb("gv2", [128, T])
    gv2B = Buf("gv2")
    vn = sb("vn", [128, 4, 128], BF16)
    vnB = Buf("vn")
    st = sb("st", [128, 32])
    stB = Buf("st")
    st2 = sb("st2", [128, 32])
    st2B = Buf("st2")
    vstage = sb("vstage", [128, 4, 128])
    vstageB = Buf("vstage")
    vecT = sb("vecT", [128, NVEC])
    vecB = Buf("vecT")
    coef = sb("coef", [128, 96])
    coefB = Buf("coef")
    modsb = sb("modsb", [128, 96, 2])
    modB = Buf("modsb")
    scb = sb("scb", [128, 8, 2], BF16)
    scbB = Buf("scb")
    ident = sb("ident", [128, 128])
    identB = Buf("ident")
    pmf = sb("pmf", [128, 128])
    pmb = sb("pmb", [128, 128], BF16)
    pmB = Buf("pm")
    ones = sb("ones", [128, 128], BF16)
    bones = sb("bones", [128, 128], BF16)
    onesB = Buf("ones")
    onesrow = sb("onesrow", [1, 128])

    bsprow = sb("bsprow", [1, 512])
    rowB = Buf("rows")
    wspT = sb("wspT", [128, 4, 128], BF16)
    wspB = Buf("wspT")
    nshift = sb("nshift", [128, 1])
    nshiftB = Buf("nshift")
    gtmp = sb("gtmp", [1, 4])

    PS = [nc.alloc_psum_tensor(f"ps{i}", [128, T], F32) for i in range(8)]
    PSB = [Buf(f"ps{i}") for i in range(8)]

    def ar(j):
        return AR[:, j, :]

    def ar32(j2):
        return AR[:, 2 * j2:2 * j2 + 2, :].bitcast(F32).rearrange("p a t -> p (a t)")

    QT_, AT_, SG_, GU_, PT_ = 0, 4, 8, 12, 16

    def V(name, i=0):
        r = VEC_ROWS[name] + i
        return vecT[:, r:r + 1]

    CO = {}
    _cc = [0]

    def cocol(name, n=8):
        CO[name] = _cc[0]
        _cc[0] += n

    for nm in ["A1_0", "A1c", "A2_0", "A1_1", "A2_1", "gb"]:
        cocol(nm)

    def C(name, i):
        c = CO[name] + i
        return coef[:, c:c + 1]

    def M(l, which, i, col=0):
        idx = l * 48 + which * 8 + i
        return modsb[:, idx, col:col + 1]

    SH1, SC1, GT1, SH2, SC2, GT2 = range(6)

    scr = {}
    scrB = {}

    def mkscr(key, group):
        scr[key] = nc.dram_tensor("scr_" + "_".join(str(s) for s in key), [128, SLOT_ELEMS], BF16).ap()
        if group not in scrB:
            scrB[group] = Buf("scr_" + group)
        return scr[key], scrB[group]

    def cast(dst, src, gb):
        k.dma(POOL, lambda e: e.dma_start(out=dst, in_=src), writes=[gb], disjoint=True)

    def r3(ap2, a, b):
        return ap2.rearrange("p (a b) -> p a b", a=a, b=b)

    def cast_wkv():
        s, gb = mkscr(("wkv",), "wkv")
        cast(r3(s[:, 0:8 * 256], 8, 256), w_in_d[:, 512:768].rearrange("(k p) c -> p k c", p=128), gb)

    def cast_all_early():
        s, gb = mkscr(("wq",), "wq")
        sv = r3(s, 8, 512)
        for g in range(4):
            for kv in range(2):
                h = kv * 4 + g
                cast(sv[:, :, g * 128 + kv * 64: g * 128 + kv * 64 + 64],
                     w_in_d[:, h * 64:(h + 1) * 64].rearrange("(k p) c -> p k c", p=128), gb)
        s, gb = mkscr(("wsu",), "wsu")
        cast(r3(s, 8, 512), w_in_d[:, 768:1280].rearrange("(k p) c -> p k c", p=128), gb)
        s, gb = mkscr(("wsv",), "wsv")
        cast(r3(s, 8, 512), w_in_d[:, 1280:1792].rearrange("(k p) c -> p k c", p=128), gb)
        for pc in range(2):
            s, gb = mkscr(("wo", pc), "wo")
            sv = r3(s, 8, 512)
            cols = slice(pc * 512, (pc + 1) * 512)
            cast(sv[0:64, 0:4, :], w_out_d[0:256, cols].rearrange("(c p) f -> p c f", p=64), gb)
            cast(sv[64:128, 0:4, :], w_out_d[256:512, cols].rearrange("(c p) f -> p c f", p=64), gb)
            cast(sv[:, 4:8, :], w_out_d[512:1024, cols].rearrange("(k p) f -> p k f", p=128), gb)

    def cast_fin(l, pps):
        for pp in pps:
            s, gb = mkscr(("fin", l, pp), f"fin{l}")
            sv = s.rearrange("p (k s c) -> p k s c", k=8, s=2, c=256)
            wv = w_ffn_in_d[l].rearrange("(k p) (s c) -> p k s c", p=128, s=2)
            for s2 in range(2):
                cast(sv[:, :, s2, :], wv[:, :, s2, pp * 256:(pp + 1) * 256], gb)
    def cast_fout(l):
        for c in range(8):
            s, gb = mkscr(("fout", l, c), f"fout{l}")
            cast(r3(s[:, 0:NH * 128], NH, 128),
                 w_ffn_out_d[l][:, c * 128:(c + 1) * 128].rearrange("(k p) f -> p k f", p=128), gb)

    def cast_pw():
        for pp in range(4):
            s, gb = mkscr(("pw1", pp), "pw1")
            sv = s.rearrange("p (k s c) -> p k s c", k=8, s=2, c=256)
            wv = w_pw1_d.rearrange("(k p) (s c) -> p k s c", p=128, s=2)
            for s2 in range(2):
                cast(sv[:, :, s2, :], wv[:, :, s2, pp * 256:(pp + 1) * 256], gb)
        for pc in range(2):
            s, gb = mkscr(("pw2", pc), "pw2")
            cast(r3(s, 8, 512), w_pw2_d[:, pc * 512:(pc + 1) * 512].rearrange("(k p) c -> p k c", p=128), gb)

    seq = piece_sequence(NT)
    wstate = {"issued": 0, "next": 0}

    def group_of(key):
        if key[0] in ("fin", "fout"):
            return f"{key[0]}{key[1]}"
        return key[0]

    def issue_load(n):
        key = seq[n]
        slot = n % NSLOT
        if key[0] == "mod":
            _, l, pc = key
            dst = r3(ring[:, slot, :], 8, 512)
            src = w_mod_d[l][:, pc * 512:(pc + 1) * 512].rearrange("(k p) c -> p k c", p=128)
            k.dma(POOL, lambda e: e.dma_start(out=dst, in_=src), writes=[ringB[slot]])
        else:
            s = scr[key]
            n_el = SLOT_ELEMS
            if key[0] == "wkv":
                n_el = 8 * 256
            elif key[0] == "fout":
                n_el = NH * 128
            k.dma(SP, lambda e: e.dma_start(out=ring[:, slot, 0:n_el], in_=s[:, 0:n_el]),
                  reads=[scrB[group_of(key)]], writes=[ringB[slot]])

    def wnext(key):
        n = wstate["next"]
        assert seq[n] == key, (seq[n], key)
        while wstate["issued"] < min(len(seq), n + NSLOT):
            issue_load(wstate["issued"])
            wstate["issued"] += 1
        wstate["next"] += 1
        slot = n % NSLOT
        return ring[:, slot, :], ringB[slot]

    k.dma(SP, lambda e: e.dma_start(out=vstage[:], in_=vecs_d.rearrange("(g r) c -> r g c", r=128)), writes=[vstageB])
    k.dma(SP, lambda e: e.dma_start(out=ident[:], in_=ident_d[:]), writes=[identB])
    k.dma(SP, lambda e: e.dma_start(out=pmf[:], in_=pm_d[:]), writes=[pmB])
    k.dma(SP, lambda e: e.dma_start(out=bsprow[:], in_=b_sp_d[:]), writes=[rowB])

    cast_wkv()
    epsT = sb("epsT", [128, 1])
    epsB = Buf("eps")
    k.op(DVE, lambda e: e.memset(epsT[:], EPS), writes=[epsB])
    neghalf = sb("neghalf", [128, 4])
    k.op(DVE, lambda e: e.memset(neghalf[:], -0.5), reads=[epsB], writes=[epsB])

    k.op(POOL, lambda e: e.memset(ones[:], 1.0), writes=[onesB])
    k.op(POOL, lambda e: e.memset(bones[:], 0.0), writes=[onesB])
    k.op(POOL, lambda e: e.memset(bones[0:64, 0:64], 1.0), writes=[onesB])
    k.op(POOL, lambda e: e.memset(bones[64:128, 64:128], 1.0), writes=[onesB])
    k.op(POOL, lambda e: e.memset(onesrow[:], 1.0), writes=[rowB])

    k.op(POOL, lambda e: e.memset(VA[:, :, :, 64:128], 1.0), writes=[VAB])
    k.op(DVE, lambda e: e.tensor_copy(out=pmb[:], in_=pmf[:]), reads=[pmB], writes=[pmB])
    k.op(DVE, lambda e: e.tensor_copy(out=identb[:], in_=ident[:]), reads=[identB], writes=[identB])

    for g in range(4):
        k.op(PE, lambda e, g=g: e.transpose(out=PS[0][:, g * 128:(g + 1) * 128], in_=vstage[:, g, :], identity=ident[:]),
             reads=[vstageB, identB], writes=[PSB[0]])
    k.op(DVE, lambda e: e.tensor_copy(out=vecT[:], in_=PS[0][:]), reads=[PSB[0]], writes=[vecB])

    k.dma(SP, lambda e: e.dma_start(out=vstage[:], in_=w_sp_d.rearrange("g p q -> p g q")), reads=[vstageB], writes=[vstageB])
    for g in range(4):
        k.op(PE, lambda e, g=g: e.transpose(out=PS[1][:, g * 128:(g + 1) * 128], in_=vstage[:, g, :], identity=ident[:]),
             reads=[vstageB, identB], writes=[PSB[1]])
    k.op(DVE, lambda e: e.tensor_copy(out=wspT[:].rearrange("p g q -> p (g q)"), in_=PS[1][:]), reads=[PSB[1]], writes=[wspB])

    k.op(ACT, lambda e: e.activation(out=st[:, 0:2], in_=vecT[:, VEC_ROWS["qg"]:VEC_ROWS["qg"] + 2], func=AF.Abs),
         reads=[vecB], writes=[stB])
    k.op(PE, lambda e: e.transpose(out=PS[2][0:2, 0:128], in_=st[:, 0:2], identity=ident[:]), reads=[stB, identB], writes=[PSB[2]])
    k.op(DVE, lambda e: e.tensor_reduce(out=st[0:2, 2:3], in_=PS[2][0:2, 0:128], axis=AX.X, op=ALU.max), reads=[PSB[2], stB], writes=[stB])
    k.op(PE, lambda e: e.transpose(out=PS[2][0:1, 128:130], in_=st[0:2, 2:3], identity=ident[0:2, 0:2]), reads=[stB, identB], writes=[PSB[2]])
    k.op(DVE, lambda e: e.tensor_copy(out=gtmp[0:1, 0:2], in_=PS[2][0:1, 128:130]), reads=[PSB[2]], writes=[stB])
    k.op(DVE, lambda e: e.tensor_tensor(out=gtmp[0:1, 2:3], in0=gtmp[0:1, 0:1], in1=gtmp[0:1, 1:2], op=ALU.mult), reads=[stB], writes=[stB])
    k.op(PE, lambda e: e.matmul(PS[2][:, 256:257], lhsT=onesrow[0:1, :], rhs=gtmp[0:1, 2:3], start=True, stop=True), reads=[stB, rowB], writes=[PSB[2]])
    k.op(ACT, lambda e: e.activation(out=nshift[:], in_=PS[2][:, 256:257], func=AF.Copy, scale=-8.0), reads=[PSB[2]], writes=[nshiftB])

    k.op(ACT, lambda e: e.activation(out=scb[:, :, 0], in_=vecT[:, VEC_ROWS["c"]:VEC_ROWS["c"] + 8], func=AF.Silu), reads=[vecB], writes=[scbB])
    k.op(ACT, lambda e: e.activation(out=scb[:, :, 1], in_=vecT[:, VEC_ROWS["cctx"]:VEC_ROWS["cctx"] + 8], func=AF.Silu), reads=[vecB, scbB], writes=[scbB])

    def mod_pieces(l, pcs, mbank):
        for pc in pcs:
            w, wb = wnext(("mod", l, pc))
            wv = r3(w, 8, 512)
            for fc in range(4):
                idx = l * 48 + pc * 4 + fc
                for kk in range(8):
                    k.op(PE, lambda e, wv=wv, fc=fc, kk=kk, idx=idx: e.matmul(
                        PS[mbank][:, idx * 2:idx * 2 + 2], lhsT=wv[:, kk, fc * 128:(fc + 1) * 128], rhs=scb[:, kk, :],
                        start=(kk == 0), stop=(kk == 7)), reads=[wb, scbB], writes=[PSB[mbank]])

    def mod_finish(l, mbank, i0=0, i1=48):
        for col in range(2):
            k.op(DVE, lambda e, col=col: e.tensor_tensor(
                out=modsb[:, l * 48 + i0:l * 48 + i1, col],
                in0=PS[mbank][:, (l * 48 + i0) * 2:(l * 48 + i1) * 2].rearrange("p (i c) -> p i c", c=2)[:, :, col],
                in1=vecT[:, VEC_ROWS["bmod"] + l * 48 + i0:VEC_ROWS["bmod"] + l * 48 + i1], op=ALU.add), reads=[PSB[mbank], vecB, modB], writes=[modB])

    mod_pieces(0, range(4), 3)
    mod_finish(0, 3, 0, 16)
    cast_all_early()
    cast_batches = [lambda: cast_fin(0, range(0, 6)), lambda: cast_fin(0, range(6, 11)), lambda: cast_fout(0), cast_pw,
                    lambda: cast_fin(1, range(0, 6)), lambda: cast_fin(1, range(6, 11)), lambda: cast_fout(1)]

    def coef_A(name, l, which, gname, col=0):
        c0 = CO[name]
        k.op(DVE, lambda e: e.scalar_tensor_tensor(
            out=coef[:, c0:c0 + 8], in0=modsb[:, l * 48 + which * 8: l * 48 + which * 8 + 8, col], scalar=1.0,
            in1=vecT[:, VEC_ROWS[gname] + l * 8: VEC_ROWS[gname] + l * 8 + 8], op0=ALU.add, op1=ALU.mult),
            reads=[modB, vecB, coefB], writes=[coefB])

    coef_A("A1_0", 0, SC1, "gmix")
    coef_A("A1c", 0, SC1, "gmix", col=1)

    iostate = {"n": 0}

    def load_tile_T(src_rows, ntok, hbuf, hbufB):
        for m in range(ntok // 128):
            ib = iostate["n"] % 2
            iostate["n"] += 1
            k.dma(SP, lambda e, ib=ib, m=m: e.dma_start(out=io[ib][:], in_=src_rows[m * 128:(m + 1) * 128, :]), writes=[ioB[ib]])
            for half in range(2):
                bank = 6 + half
                for kk in range(4):
                    kc = half * 4 + kk
                    k.op(PE, lambda e, ib=ib, kc=kc, kk=kk, bank=bank: e.transpose(
                        out=PS[bank][:, kk * 128:(kk + 1) * 128], in_=io[ib][:, kc * 128:(kc + 1) * 128], identity=ident[:]),
                        reads=[ioB[ib], identB], writes=[PSB[bank]])
                eng = ACT if half == 0 else DVE
                if eng == ACT:
                    k.op(ACT, lambda e, half=half, m=m, bank=bank: e.activation(
                        out=hbuf[:, half * 4:half * 4 + 4, m * 128:(m + 1) * 128],
                        in_=PS[bank][:].rearrange("p (a t) -> p a t", a=4), func=AF.Copy),
                        reads=[PSB[bank]], writes=hbufB[half * 4:half * 4 + 4])
                else:
                    k.op(DVE, lambda e, half=half, m=m, bank=bank: e.tensor_copy(
                        out=hbuf[:, half * 4:half * 4 + 4, m * 128:(m + 1) * 128],
                        in_=PS[bank][:].rearrange("p (a t) -> p a t", a=4)),
                        reads=[PSB[bank]], writes=hbufB[half * 4:half * 4 + 4])

    def rms_stats(hbuf, hbufB, ntok, bank, sd_=None, sd_B=None, rs_=None, rs_B=None):
        if sd_ is None:
            sd_, sd_B, rs_, rs_B = sd[:], [sdB], rstd[:], [rstdB]
        for kc in range(8):
            s = kc % 2
            k.op(ACT, lambda e, kc=kc, s=s: e.activation(out=sq[s][:, 0:ntok], in_=hbuf[:, kc, 0:ntok], func=AF.Square),
                 reads=[hbufB[kc]], writes=[sqB[s]])
            k.op(PE, lambda e, kc=kc, s=s: e.matmul(PS[bank][:, 0:ntok], lhsT=ones[:], rhs=sq[s][:, 0:ntok],
                                                    start=(kc == 0), stop=(kc == 7)),
                 reads=[sqB[s], onesB], writes=[PSB[bank]])
        k.op(ACT, lambda e: e.activation(out=sd_[:, 0:ntok], in_=PS[bank][:, 0:ntok], func=AF.Ln, scale=1.0 / D, bias=epsT[:]),
             reads=[PSB[bank], epsB], writes=sd_B)
        k.op(ACT, lambda e: e.activation(out=rs_[:, 0:ntok], in_=sd_[:, 0:ntok], func=AF.Exp, scale=-0.5), reads=sd_B, writes=rs_B)

    tmpstate = {"n": 0}

    def norm_mod(hbuf, hbufB, ntok, Acol, Bcol, bank, ve=POOL, xm_=None, xm_B=None, sd_=None, sd_B=None, rs_=None, rs_B=None):
        if xm_ is None:
            xm_, xm_B = xmT, xmB
        if sd_ is None:
            sd_, sd_B, rs_, rs_B = sd[:], [sdB], rstd[:], [rstdB]
        rms_stats(hbuf, hbufB, ntok, bank, sd_, sd_B, rs_, rs_B)
        for kc in range(8):
            ti = tmpstate["n"] % 3
            tmpstate["n"] += 1
            k.op(DVE if (ve == DVE or kc % 3 != 2) else POOL, lambda e, kc=kc, ti=ti: e.tensor_tensor(out=tmp[ti][:, 0:ntok], in0=hbuf[:, kc, 0:ntok], in1=rs_[:, 0:ntok], op=ALU.mult),
                 reads=[hbufB[kc]] + rs_B, writes=[tmpB[ti]])
            k.op(ACT, lambda e, kc=kc, ti=ti: e.activation(out=xm_[:, kc, 0:ntok], in_=tmp[ti][:, 0:ntok], func=AF.Identity,
                                                           scale=Acol(kc), bias=Bcol(kc)),
                 reads=[tmpB[ti], coefB, modB], writes=[xm_B[kc]])

    def headnorm_rstd(src_ps, src_psB, ntok, bank, sqi=0, sd_=None, sd_B=None, rs_=None, rs_B=None):
        if sd_ is None:
            sd_, sd_B, rs_, rs_B = sd[:], [sdB], rstd[:], [rstdB]
        k.op(ACT, lambda e: e.activation(out=sq[sqi][:, 0:ntok], in_=src_ps[:, 0:ntok], func=AF.Square), reads=[src_psB], writes=[sqB[sqi]])
        k.op(PE, lambda e: e.matmul(PS[bank][:, 0:ntok], lhsT=bones[:], rhs=sq[sqi][:, 0:ntok], start=True, stop=True),
             reads=[sqB[sqi], onesB], writes=[PSB[bank]])
        k.op(ACT, lambda e: e.activation(out=sd_[:, 0:ntok], in_=PS[bank][:, 0:ntok], func=AF.Ln, scale=1.0 / 64, bias=epsT[:]),
             reads=[PSB[bank], epsB], writes=sd_B)
        k.op(ACT, lambda e: e.activation(out=rs_[:, 0:ntok], in_=sd_[:, 0:ntok], func=AF.Exp, scale=-0.5), reads=sd_B, writes=rs_B)

    def rope(src_bf, src_B, dst, dst_B, bank, ve=POOL, t0=None, t0B=None, t1=None, t1B=None):
        if t0 is None:
            t0, t0B, t1, t1B = tmp[0][:], [tmpB[0]], tmp[1][:], [tmpB[1]]
        k.op(PE, lambda e: e.matmul(PS[bank][:], lhsT=pmb[:], rhs=src_bf, start=True, stop=True), reads=[src_B, pmB], writes=[PSB[bank]])
        k.op(ve, lambda e: e.tensor_tensor(out=t0, in0=src_bf, in1=cs[:, 0, :], op=ALU.mult), reads=[src_B, csB], writes=t0B)
        k.op(DVE, lambda e: e.tensor_tensor(out=t1, in0=PS[bank][:], in1=cs[:, 1, :], op=ALU.mult), reads=[PSB[bank], csB], writes=t1B)
        k.op(ve, lambda e: e.tensor_tensor(out=dst, in0=t0, in1=t1, op=ALU.add), reads=t0B + t1B, writes=dst_B)


    def dump(name, ap_fn, bufs):
        if dbg and name in dbg_d:
            k.dma(SP, lambda e: e.dma_start(out=dbg_d[name], in_=ap_fn()), reads=bufs, track=dumpB)

    dumpB = Buf("dump")

    wkv_t = sb("wkv", [128, 8 * 256], BF16)
    wkvB = Buf("wkv")
    k.dma(SP, lambda e: e.dma_start(out=wkv_t[:], in_=scr[("wkv",)][:, 0:8 * 256]), reads=[scrB["wkv"]], writes=[wkvB])
    wkvv = r3(wkv_t[:], 8, 256)

    def phaseA(src_rows, ntok, key_off, latent, tile_i):
        par = (tile_i + (1 if latent else 0)) % 2
        hb, hbB = hT[par], hB[par]
        if par == 0:
            xm_, xm_B = xmT, xmB
            nsd, nsdB, nrs, nrsB = sd[:], [sdB], rstd[:], [rstdB]
            kn_, kn_B = knT[:], knB
        else:
            xm_, xm_B = AR[:, 0:8, :], ARB[0:8]
            nsd, nsdB = ar32(4), [ARB[8], ARB[9]]
            nrs, nrsB = ar32(5), [ARB[10], ARB[11]]
            kn_, kn_B = ar(16), ARB[16]
        hsd, hsdB = ar32(6), [ARB[12], ARB[13]]
        hrs, hrsB = ar32(7), [ARB[14], ARB[15]]
        load_tile_T(src_rows, ntok, hb, hbB)
        if latent:
            norm_mod(hb, hbB, ntok, lambda kc: C("A1_0", kc), lambda kc: M(0, SH1, kc), 0, ve=DVE, xm_=xm_, xm_B=xm_B, sd_=nsd, sd_B=nsdB, rs_=nrs, rs_B=nrsB)
        else:
            norm_mod(hb, hbB, ntok, lambda kc: C("A1c", kc), lambda kc: M(0, SH1, kc, 1), 0, ve=DVE, xm_=xm_, xm_B=xm_B, sd_=nsd, sd_B=nsdB, rs_=nrs, rs_B=nrsB)
        for kc in range(8):
            k.op(PE, lambda e, kc=kc: e.matmul(PS[1][:, 0:ntok], lhsT=wkvv[:, kc, 0:128], rhs=xm_[:, kc, 0:ntok], start=(kc == 0), stop=(kc == 7)),
                 reads=[wkvB, xm_B[kc]], writes=[PSB[1]])
        headnorm_rstd(PS[1], PSB[1], ntok, 2, sqi=par, sd_=hsd, sd_B=hsdB, rs_=hrs, rs_B=hrsB)
        dstK = KT[:, key_off:key_off + ntok] if not latent else kn_[:, 0:ntok]
        dstKB = [KTB] if not latent else [kn_B]
        k.op(DVE, lambda e: e.scalar_tensor_tensor(out=dstK, in0=PS[1][:, 0:ntok], scalar=V("kg"), in1=hrs[:, 0:ntok], op0=ALU.mult, op1=ALU.mult),
             reads=[PSB[1], vecB] + hrsB, writes=dstKB)
        if latent:
            k.dma(SP, lambda e: e.dma_start(out=cs[:, 0, :], in_=cos_d[:, tile_i * T:(tile_i + 1) * T]), writes=[csB])
            k.dma(SP, lambda e: e.dma_start(out=cs[:, 1, :], in_=sin_d[:, tile_i * T:(tile_i + 1) * T]), reads=[], writes=[csB])
            rope(kn_, kn_B, KT[:, key_off:key_off + ntok], [KTB], 3, ve=DVE)
        nsub = ntok // 128
        for m in range(nsub):
            for kc in range(8):
                k.op(PE, lambda e, kc=kc, m=m: e.matmul(PS[4][:, m * 128:(m + 1) * 128], lhsT=xm_[:, kc, m * 128:(m + 1) * 128], rhs=wkvv[:, kc, 128:256],
                                                        start=(kc == 0), stop=(kc == 7)),
                     reads=[wkvB, xm_B[kc]], writes=[PSB[4]])
        j0 = key_off // 128
        k.op(ACT, lambda e: e.activation(out=VA[:, j0:j0 + nsub, :, 0:64],
                                         in_=PS[4][:, 0:nsub * 128].rearrange("p (j h d) -> p j h d", j=nsub, h=2, d=64), func=AF.Copy),
             reads=[PSB[4]], writes=[VAB])

    phaseA(ctx_d, CTX, 0, False, 0)
    mod_todo = [(0, pc) for pc in range(4, 12)] + [(1, pc) for pc in range(12)]
    for i in range(NT):
        for _ in range(3):
            if mod_todo:
                l_, pc_ = mod_todo.pop(0)
                mod_pieces(l_, [pc_], 5)
        if NT < 8 and cast_batches:
            cast_batches.pop(0)()
        phaseA(x_d[i * T:(i + 1) * T, :], T, CTX + i * T, True, i)
    while mod_todo:
        l_, pc_ = mod_todo.pop(0)
        mod_pieces(l_, [pc_], 5)
    mod_finish(0, 5, 16, 48)
    coef_A("A2_0", 0, SC2, "gffn")
    late_casts = []
    head_casts = []
    while cast_batches:
        b = cast_batches.pop(0)
        if NT >= 8 and len(cast_batches) < 3:
            late_casts.append(b)
        elif NT >= 8:
            head_casts.append(b)
        else:
            b()
    mod_finish(1, 5)


    def ffn(l, hb, hbB):
        A2 = "A2_0" if l == 0 else "A2_1"
        norm_mod(hb, hbB, T, lambda kc: C(A2, kc), lambda kc: M(l, SH2, kc), 0)
        for pp in range(11):
            w, wb = wnext(("fin", l, pp))
            wv = w.rearrange("p (k s c) -> p k s c", k=8, s=2, c=256)
            if pp == 0:
                for kc in range(8):
                    for jj in range(2):
                        for s2 in range(2):
                            bank = jj * 2 + s2
                            k.op(PE, lambda e, kc=kc, s2=s2, bank=bank, jj=jj, wv=wv: e.matmul(
                                PS[bank][:], lhsT=wv[:, kc, s2, jj * 128:(jj + 1) * 128], rhs=xmT[:, kc, :], start=(kc == 0), stop=(kc == 7)),
                                reads=[wb, xmB[kc]], writes=[PSB[bank]])
            for jj in range(2):
                j = pp * 2 + jj
                gb_, ub_ = (0, 1) if j % 2 == 0 else (2, 3)
                for s2, bank in ((0, gb_), (1, ub_)):
                    if pp == 0:
                        break
                    for kc in range(8):
                        k.op(PE, lambda e, kc=kc, s2=s2, bank=bank, jj=jj, wv=wv: e.matmul(
                            PS[bank][:], lhsT=wv[:, kc, s2, jj * 128:(jj + 1) * 128], rhs=xmT[:, kc, :], start=(kc == 0), stop=(kc == 7)),
                            reads=[wb, xmB[kc]], writes=[PSB[bank]])
                si = j % 2
                k.op(ACT, lambda e, si=si, gb_=gb_: e.activation(out=sgl[si][:], in_=PS[gb_][:], func=AF.Silu), reads=[PSB[gb_]], writes=[sglB[si]])
                k.op(DVE, lambda e, si=si, ub_=ub_, j=j: e.tensor_tensor(out=ar(j), in0=PS[ub_][:], in1=sgl[si][:], op=ALU.mult),
                     reads=[PSB[ub_], sglB[si]], writes=[ARB[j]])
        for f in range(8):
            w, wb = wnext(("fout", l, f))
            wv = r3(w[:, 0:NH * 128], NH, 128)
            bank = 4 + f % 4
            for j in range(NH):
                k.op(PE, lambda e, j=j, wv=wv, bank=bank: e.matmul(PS[bank][:], lhsT=wv[:, j, :], rhs=ar(j), start=(j == 0), stop=(j == NH - 1)),
                     reads=[wb, ARB[j]], writes=[PSB[bank]])
            k.op(DVE, lambda e, f=f, bank=bank: e.scalar_tensor_tensor(out=hb[:, f, :], in0=PS[bank][:], scalar=M(l, GT2, f), in1=hb[:, f, :],
                                                                      op0=ALU.mult, op1=ALU.add),
                 reads=[PSB[bank], modB, hbB[f]], writes=[hbB[f]])

    def layer0(i):
        hb, hbB = hT[i % 2], hB[i % 2]
        if i == 0:
            for b_ in head_casts:
                b_()
        load_tile_T(x_d[i * T:(i + 1) * T, :], T, hb, hbB)
        norm_mod(hb, hbB, T, lambda kc: C("A1_0", kc), lambda kc: M(0, SH1, kc), 0)
        k.dma(SP, lambda e: e.dma_start(out=cs[:, 0, :], in_=cos_d[:, i * T:(i + 1) * T]), writes=[csB])
        k.dma(SP, lambda e: e.dma_start(out=cs[:, 1, :], in_=sin_d[:, i * T:(i + 1) * T]), writes=[csB])
        w, wq_b = wnext(("wq",))
        wq_v = r3(w, 8, 512)
        SA, SBK = 6, 7

        def qchunk_gen(c):
            for kc in range(8):
                k.op(PE, lambda e, kc=kc: e.matmul(PS[SA][:], lhsT=wq_v[:, kc, c * 128:(c + 1) * 128], rhs=xmT[:, kc, :], start=(kc == 0), stop=(kc == 7)),
                     reads=[wq_b, xmB[kc]], writes=[PSB[SA]])
                if kc == 3:
                    yield
            yield
            k.op(DVE, lambda e: e.tensor_copy(out=tmp[2][:], in_=PS[SA][:]), reads=[PSB[SA]], writes=[tmpB[2]])
            yield
            k.op(POOL, lambda e: e.tensor_tensor(out=sq[0][:], in0=tmp[2][:], in1=tmp[2][:], op=ALU.mult), reads=[tmpB[2]], writes=[sqB[0]])
            yield
            k.op(PE, lambda e: e.matmul(PS[SBK][:], lhsT=bones[:], rhs=sq[0][:], start=True, stop=True), reads=[sqB[0], onesB], writes=[PSB[SBK]])
            yield
            k.op(ACT, lambda e: e.activation(out=sd[:], in_=PS[SBK][:], func=AF.Ln, scale=1.0 / 64, bias=epsT[:]), reads=[PSB[SBK], epsB], writes=[sdB])
            yield
            k.op(ACT, lambda e: e.activation(out=rstd[:], in_=sd[:], func=AF.Exp, scale=-0.5), reads=[sdB], writes=[rstdB])
            yield
            k.op(DVE, lambda e: e.scalar_tensor_tensor(out=knT[:], in0=PS[SA][:], scalar=V("qg"), in1=rstd[:], op0=ALU.mult, op1=ALU.mult),
                 reads=[PSB[SA], vecB, rstdB], writes=[knB])
            yield
            k.op(PE, lambda e: e.matmul(PS[SBK][:], lhsT=pmb[:], rhs=knT[:], start=True, stop=True), reads=[knB, pmB], writes=[PSB[SBK]])
            k.op(POOL, lambda e: e.tensor_tensor(out=tmp[0][:], in0=knT[:], in1=cs[:, 0, :], op=ALU.mult), reads=[knB, csB], writes=[tmpB[0]])
            yield
            k.op(DVE, lambda e: e.tensor_tensor(out=tmp[1][:], in0=PS[SBK][:], in1=cs[:, 1, :], op=ALU.mult), reads=[PSB[SBK], csB], writes=[tmpB[1]])
            yield
            k.op(POOL, lambda e: e.tensor_tensor(out=ar(QT_ + c), in0=tmp[0][:], in1=tmp[1][:], op=ALU.add), reads=[tmpB[0], tmpB[1]], writes=[ARB[QT_ + c]])
            yield

        def su_gen():
            w2, wb2 = wnext(("wsu",))
            wvs = r3(w2, 8, 512)
            for c in range(4):
                bank = SA + c % 2
                for kc in range(8):
                    k.op(PE, lambda e, kc=kc, c=c, bank=bank: e.matmul(PS[bank][:], lhsT=wvs[:, kc, c * 128:(c + 1) * 128], rhs=xmT[:, kc, :], start=(kc == 0), stop=(kc == 7)),
                         reads=[wb2, xmB[kc]], writes=[PSB[bank]])
                    if kc == 3:
                        yield
                yield
                k.op(ACT, lambda e, c=c, bank=bank: e.activation(out=ar(GU_ + c), in_=PS[bank][:], func=AF.Gelu_apprx_tanh), reads=[PSB[bank]], writes=[ARB[GU_ + c]])
                yield

        def sv_gen():
            w3, wb3 = wnext(("wsv",))
            wv2 = r3(w3, 8, 512)
            for m in range(4):
                bank = SA + m % 2
                mb = SBK - m % 2
                for kc in range(8):
                    k.op(PE, lambda e, kc=kc, m=m, bank=bank: e.matmul(PS[bank][:], lhsT=xmT[:, kc, m * 128:(m + 1) * 128], rhs=wv2[:, kc, :], start=(kc == 0), stop=(kc == 7)),
                         reads=[wb3, xmB[kc]], writes=[PSB[bank]])
                    if kc == 3:
                        yield
                yield
                k.op(ACT, lambda e, bank=bank: e.activation(out=gv[:], in_=PS[bank][:], func=AF.Gelu_apprx_tanh), reads=[PSB[bank]], writes=[gvB])
                yield
                k.op(POOL, lambda e: e.tensor_tensor(out=gv2[:], in0=gv[:], in1=gv[:], op=ALU.mult), reads=[gvB], writes=[gv2B])
                k.op(DVE, lambda e: e.tensor_reduce(out=st[:, 4:8], in_=gv[:].rearrange("p (g c) -> p g c", g=4), axis=AX.X, op=ALU.add), reads=[gvB, stB], writes=[stB])
                yield
                k.op(DVE, lambda e: e.tensor_reduce(out=st[:, 8:12], in_=gv2[:].rearrange("p (g c) -> p g c", g=4), axis=AX.X, op=ALU.add), reads=[gv2B, stB], writes=[stB])
                yield
                k.op(DVE, lambda e: e.tensor_scalar(out=st[:, 12:16], in0=st[:, 4:8], scalar1=1.0 / 128, scalar2=None, op0=ALU.mult), reads=[stB], writes=[stB])
                yield
                k.op(DVE, lambda e: e.tensor_tensor(out=st[:, 16:20], in0=st[:, 12:16], in1=st[:, 12:16], op=ALU.mult), reads=[stB], writes=[stB])
                yield
                k.op(DVE, lambda e: e.scalar_tensor_tensor(out=st[:, 20:24], in0=st[:, 8:12], scalar=1.0 / 128, in1=st[:, 16:20], op0=ALU.mult, op1=ALU.subtract),
                     reads=[stB], writes=[stB])
                yield
                k.op(ACT, lambda e: e.activation(out=st[:, 24:28], in_=st[:, 20:24], func=AF.Ln, bias=epsT[:]), reads=[stB, epsB], writes=[stB])
                yield
                k.op(ACT, lambda e: e.activation(out=st[:, 28:32], in_=st[:, 24:28], func=AF.Exp, scale=-0.5), reads=[stB], writes=[stB])
                yield
                for g in range(4):
                    k.op(DVE, lambda e, g=g: e.tensor_scalar(out=vn[:, g, :], in0=gv[:, g * 128:(g + 1) * 128], scalar1=st[:, 12 + g:13 + g], scalar2=st[:, 28 + g:29 + g],
                                                             op0=ALU.subtract, op1=ALU.mult), reads=[gvB, stB, vnB], writes=[vnB])
                yield
                k.op(PE, lambda e, mb=mb: e.matmul(PS[mb][:], lhsT=onesrow[0:1, :], rhs=bsprow[0:1, :], start=True, stop=False), reads=[rowB], writes=[PSB[mb]])
                for g in range(4):
                    k.op(PE, lambda e, g=g, mb=mb: e.matmul(PS[mb][:, g * 128:(g + 1) * 128], lhsT=vn[:, g, :], rhs=wspT[:, g, :], start=False, stop=(g == 3)),
                         reads=[vnB, wspB], writes=[PSB[mb]])
                yield
                k.op(DVE, lambda e, m=m, mb=mb: e.tensor_tensor(out=AR[:, SG_:SG_ + 4, m * 128:(m + 1) * 128], in0=PS[mb][:].rearrange("p (g t) -> p g t", g=4),
                                                                in1=AR[:, GU_:GU_ + 4, m * 128:(m + 1) * 128], op=ALU.mult),
                     reads=[PSB[mb]] + ARB[GU_:GU_ + 4], writes=ARB[SG_:SG_ + 4])
                yield

        qdone = {0: False, 1: False, 2: False, 3: False}

        def side_gen():
            for c in (1, 2, 3):
                yield from qchunk_gen(c)
                qdone[c] = True

        for _ in qchunk_gen(0):
            pass
        qdone[0] = True
        side = side_gen()
        side_alive = [True]

        def pull():
            if side_alive[0]:
                try:
                    next(side)
                except StopIteration:
                    side_alive[0] = False

        it = 0
        for c in range(4):
            while not qdone[c]:
                pull()
            Ob = [4, 5]

            def qk(j, c=c):
                for hh in range(2):
                    bank = hh * 2 + j % 2
                    ps_ = slice(hh * 64, (hh + 1) * 64)
                    k.op(PE, lambda e, bank=bank, ps_=ps_, j=j, c=c: e.matmul(PS[bank][:], lhsT=KT[ps_, j * 128:(j + 1) * 128], rhs=AR[ps_, QT_ + c, :],
                                                                        start=True, stop=True),
                         reads=[KTB, ARB[QT_ + c]], writes=[PSB[bank]])

            qk(0)
            for j in range(NKT):
                if j + 1 < NKT:
                    qk(j + 1)
                for hh in range(2):
                    bank = hh * 2 + j % 2
                    pt = PT_ + hh * 2 + j % 2
                    k.op(ACT, lambda e, bank=bank, pt=pt: e.activation(out=ar(pt), in_=PS[bank][:], func=AF.Exp, scale=0.125, bias=SHIFT),
                         reads=[PSB[bank]], writes=[ARB[pt]])
                for hh in range(2):
                    pt = PT_ + hh * 2 + j % 2
                    k.op(PE, lambda e, hh=hh, pt=pt, j=j: e.matmul(PS[4 + hh][:], lhsT=VA[:, j, hh, :], rhs=ar(pt), start=(j == 0), stop=(j == NKT - 1)),
                         reads=[VAB, ARB[pt]], writes=[PSB[4 + hh]])
                it += 1
                if it % DRIP == 0:
                    pull()
            for hh in range(2):
                k.op(DVE, lambda e, hh=hh: e.tensor_copy(out=sgl[hh][:], in_=PS[4 + hh][:]), reads=[PSB[4 + hh]], writes=[sglB[hh]])
            for hh in range(2):
                k.op(DVE, lambda e, hh=hh: e.reciprocal(out=ar32(10)[0:64, :], in_=sgl[hh][64:128, :]), reads=[sglB[hh]], writes=[ARB[20], ARB[21]])
                k.op(DVE, lambda e, hh=hh, c=c: e.tensor_tensor(out=AR[hh * 64:(hh + 1) * 64, AT_ + c, :], in0=sgl[hh][0:64, :], in1=ar32(10)[0:64, :], op=ALU.mult),
                     reads=[sglB[hh], ARB[20], ARB[21]], writes=[ARB[AT_ + c]])
        while side_alive[0]:
            pull()
        if i == 0:
            for b in late_casts:
                b()
        w, wb = wnext(("wsu",))
        wv = r3(w, 8, 512)
        for c in range(4):
            bank = c % 2
            for kc in range(8):
                k.op(PE, lambda e, kc=kc, c=c, bank=bank, wv=wv: e.matmul(PS[bank][:], lhsT=wv[:, kc, c * 128:(c + 1) * 128], rhs=xmT[:, kc, :], start=(kc == 0), stop=(kc == 7)),
                     reads=[wb, xmB[kc]], writes=[PSB[bank]])
            k.op(ACT, lambda e, c=c, bank=bank: e.activation(out=ar(GU_ + c), in_=PS[bank][:], func=AF.Gelu_apprx_tanh), reads=[PSB[bank]], writes=[ARB[GU_ + c]])
        w, wb = wnext(("wsv",))
        wv2 = r3(w, 8, 512)
        for m in range(4):
            bank = 2 + m % 2
            for kc in range(8):
                k.op(PE, lambda e, kc=kc, m=m, bank=bank, wv2=wv2: e.matmul(PS[bank][:], lhsT=xmT[:, kc, m * 128:(m + 1) * 128], rhs=wv2[:, kc, :], start=(kc == 0), stop=(kc == 7)),
                     reads=[wb, xmB[kc]], writes=[PSB[bank]])
            if m % 2 == 0:
                gv_, gv_B, gv2_, gv2_B, vn_, vn_B, st_, st_B = gv[:], [gvB], gv2[:], [gv2B], vn[:], [vnB], st, stB
            else:
                gv_, gv_B = ar32(0), [ARB[0], ARB[1]]
                gv2_, gv2_B = ar32(1), [ARB[2], ARB[3]]
                vn_, vn_B = AR[:, 16, :].rearrange("p (g c) -> p g c", g=4), [ARB[16]]
                st_, st_B = st2, st2B
            k.op(ACT, lambda e, bank=bank, gv_=gv_: e.activation(out=gv_, in_=PS[bank][:], func=AF.Gelu_apprx_tanh), reads=[PSB[bank]], writes=gv_B)
            k.op(POOL, lambda e, gv_=gv_, gv2_=gv2_: e.tensor_tensor(out=gv2_, in0=gv_, in1=gv_, op=ALU.mult), reads=gv_B, writes=gv2_B)
            k.op(DVE, lambda e, gv_=gv_, st_=st_: e.tensor_reduce(out=st_[:, 4:8], in_=gv_.rearrange("p (g c) -> p g c", g=4), axis=AX.X, op=ALU.add), reads=gv_B + [st_B], writes=[st_B])
            k.op(DVE, lambda e, gv2_=gv2_, st_=st_: e.tensor_reduce(out=st_[:, 8:12], in_=gv2_.rearrange("p (g c) -> p g c", g=4), axis=AX.X, op=ALU.add), reads=gv2_B + [st_B], writes=[st_B])
            k.op(DVE, lambda e, st_=st_: e.tensor_scalar(out=st_[:, 12:16], in0=st_[:, 4:8], scalar1=1.0 / 128, scalar2=None, op0=ALU.mult), reads=[st_B], writes=[st_B])
            k.op(DVE, lambda e, st_=st_: e.tensor_tensor(out=st_[:, 16:20], in0=st_[:, 12:16], in1=st_[:, 12:16], op=ALU.mult), reads=[st_B], writes=[st_B])
            k.op(DVE, lambda e, st_=st_: e.scalar_tensor_tensor(out=st_[:, 20:24], in0=st_[:, 8:12], scalar=1.0 / 128, in1=st_[:, 16:20], op0=ALU.mult, op1=ALU.subtract),
                 reads=[st_B], writes=[st_B])
            k.op(DVE, lambda e, st_=st_: e.tensor_scalar(out=st_[:, 24:28], in0=st_[:, 20:24], scalar1=EPS, scalar2=None, op0=ALU.add), reads=[st_B], writes=[st_B])
            k.op(POOL, lambda e, st_=st_: e.tensor_tensor(out=st_[:, 28:32], in0=st_[:, 24:28], in1=neghalf[:, 0:4], op=ALU.pow), reads=[st_B, epsB], writes=[st_B])
            for g in range(4):
                k.op(DVE, lambda e, g=g, gv_=gv_, vn_=vn_, st_=st_: e.tensor_scalar(out=vn_[:, g, :], in0=gv_[:, g * 128:(g + 1) * 128], scalar1=st_[:, 12 + g:13 + g], scalar2=st_[:, 28 + g:29 + g],
                                                         op0=ALU.subtract, op1=ALU.mult), reads=gv_B + [st_B] + vn_B, writes=vn_B)
            mb = 4 + m % 2
            k.op(PE, lambda e, mb=mb: e.matmul(PS[mb][:], lhsT=onesrow[0:1, :], rhs=bsprow[0:1, :], start=True, stop=False), reads=[rowB], writes=[PSB[mb]])
            for g in range(4):
                k.op(PE, lambda e, g=g, mb=mb, vn_=vn_: e.matmul(PS[mb][:, g * 128:(g + 1) * 128], lhsT=vn_[:, g, :], rhs=wspT[:, g, :], start=False, stop=(g == 3)),
                     reads=vn_B + [wspB], writes=[PSB[mb]])
            k.op(DVE, lambda e, m=m, mb=mb: e.tensor_tensor(out=AR[:, SG_:SG_ + 4, m * 128:(m + 1) * 128], in0=PS[mb][:].rearrange("p (g t) -> p g t", g=4),
                                                            in1=AR[:, GU_:GU_ + 4, m * 128:(m + 1) * 128], op=ALU.mult),
                 reads=[PSB[mb]] + ARB[GU_:GU_ + 4], writes=ARB[SG_:SG_ + 4])
        for pc in range(2):
            w, wb = wnext(("wo", pc))
            wv3 = r3(w, 8, 512)
            for ff in range(4):
                f = pc * 4 + ff
                bank = 6 + f % 2
                for kk in range(8):
                    src = AT_ + kk if kk < 4 else SG_ + kk - 4
                    k.op(PE, lambda e, kk=kk, ff=ff, src=src, bank=bank, wv3=wv3: e.matmul(PS[bank][:], lhsT=wv3[:, kk, ff * 128:(ff + 1) * 128], rhs=ar(src),
                                                                                         start=(kk == 0), stop=(kk == 7)),
                         reads=[wb, ARB[src]], writes=[PSB[bank]])
                k.op(DVE, lambda e, f=f, bank=bank: e.scalar_tensor_tensor(out=hb[:, f, :], in0=PS[bank][:], scalar=M(0, GT1, f), in1=hb[:, f, :],
                                                                          op0=ALU.mult, op1=ALU.add),
                     reads=[PSB[bank], modB, hbB[f]], writes=[hbB[f]])
        ffn(0, hb, hbB)

    def layer1a(i):
        hb, hbB = hT[i % 2], hB[i % 2]
        gw, gwB = GW[i % 2], GWB[i % 2]
        norm_mod(hb, hbB, T, lambda kc: C("A1_1", kc), lambda kc: M(1, SH1, kc), 0)
        for pp in range(4):
            w, wb = wnext(("pw1", pp))
            wv = w.rearrange("p (k s c) -> p k s c", k=8, s=2, c=256)
            if pp == 0:
                for kc in range(8):
                    for jj in range(2):
                        for s2 in range(2):
                            bank = jj * 2 + s2
                            k.op(PE, lambda e, kc=kc, s2=s2, bank=bank, jj=jj, wv=wv: e.matmul(
                                PS[bank][:], lhsT=wv[:, kc, s2, jj * 128:(jj + 1) * 128], rhs=xmT[:, kc, :], start=(kc == 0), stop=(kc == 7)),
                                reads=[wb, xmB[kc]], writes=[PSB[bank]])
            for jj in range(2):
                c = pp * 2 + jj
                ab_, gb_ = (0, 1) if c % 2 == 0 else (2, 3)
                for s2, bank in ((0, ab_), (1, gb_)):
                    if pp == 0:
                        break
                    for kc in range(8):
                        k.op(PE, lambda e, kc=kc, s2=s2, bank=bank, jj=jj, wv=wv: e.matmul(
                            PS[bank][:], lhsT=wv[:, kc, s2, jj * 128:(jj + 1) * 128], rhs=xmT[:, kc, :], start=(kc == 0), stop=(kc == 7)),
                            reads=[wb, xmB[kc]], writes=[PSB[bank]])
                si = c % 2
                k.op(ACT, lambda e, si=si, gb_=gb_, c=c: e.activation(out=sgl[si][:], in_=PS[gb_][:], func=AF.Sigmoid, bias=V("bpw1", 8 + c)),
                     reads=[PSB[gb_], vecB], writes=[sglB[si]])
                k.op(DVE, lambda e, si=si, ab_=ab_, c=c: e.scalar_tensor_tensor(out=gw[:, c, 15:15 + T], in0=PS[ab_][:], scalar=V("bpw1", c), in1=sgl[si][:],
                                                                               op0=ALU.add, op1=ALU.mult),
                     reads=[PSB[ab_], vecB, sglB[si]], writes=[gwB[c]])
        if i == 0:
            k.op(POOL, lambda e: e.memset(gw[:, :, 0:15], 0.0), reads=gwB, writes=gwB)
        else:
            pg, pgB = GW[(i - 1) % 2], GWB[(i - 1) % 2]
            k.op(POOL, lambda e: e.tensor_copy(out=gw[:, :, 0:15], in_=pg[:, :, T:T + 15]), reads=pgB + gwB, writes=gwB)
            k.op(POOL, lambda e: e.tensor_copy(out=pg[:, :, T + 15:T + 30], in_=gw[:, :, 15:30]), reads=gwB + pgB, writes=pgB)
        if i == NT - 1:
            k.op(POOL, lambda e: e.memset(gw[:, :, T + 15:T + 30], 0.0), reads=gwB, writes=gwB)

    def layer1b(i):
        hb, hbB = hT[i % 2], hB[i % 2]
        gw, gwB = GW[i % 2], GWB[i % 2]
        wd0 = VEC_ROWS["wdw"]
        for f in range(8):
            k.op(DVE, lambda e, f=f: e.tensor_scalar(out=hb[:, f, :], in0=hb[:, f, :], scalar1=C("gb", f), scalar2=None, op0=ALU.add),
                 reads=[hbB[f], coefB], writes=[hbB[f]])
        for kc in range(8):
            acc = ar32(kc)
            accB = [ARB[2 * kc], ARB[2 * kc + 1]]
            bank = 2 + kc % 4
            for j in range(31):
                r = dgstate["n"] % NDG
                dgstate["n"] += 1
                k.op(POOL if j % 3 == 2 else DVE, lambda e, r=r, j=j, kc=kc: e.tensor_scalar(out=dg[r][:], in0=identb[:], scalar1=vecT[:, wd0 + j * 8 + kc:wd0 + j * 8 + kc + 1], scalar2=1.0,
                                                                     op0=ALU.mult, op1=ALU.mult),
                     reads=[identB, vecB], writes=[dgB[r]])
                k.op(PE, lambda e, r=r, j=j, kc=kc, bank=bank: e.matmul(PS[bank][:], lhsT=dg[r][:], rhs=gw[:, kc, j:j + T], start=(j == 0), stop=(j == 30)),
                     reads=[dgB[r], gwB[kc]], writes=[PSB[bank]])
            k.op(DVE, lambda e, kc=kc, acc=acc, bank=bank: e.tensor_scalar(out=acc, in0=PS[bank][:], scalar1=V("bdw", kc), scalar2=None, op0=ALU.add),
                 reads=[PSB[bank], vecB], writes=accB)
            k.op(ACT, lambda e, acc=acc: e.activation(out=sq[0][:], in_=acc, func=AF.Copy), reads=accB, writes=[sqB[0]])
            k.op(PE, lambda e, kc=kc: e.matmul(PS[0][:], lhsT=ones[:], rhs=sq[0][:], start=(kc == 0), stop=(kc == 7)), reads=[sqB[0], onesB], writes=[PSB[0]])
            k.op(ACT, lambda e, acc=acc: e.activation(out=sq[1][:], in_=acc, func=AF.Square), reads=accB, writes=[sqB[1]])
            k.op(PE, lambda e, kc=kc: e.matmul(PS[1][:], lhsT=ones[:], rhs=sq[1][:], start=(kc == 0), stop=(kc == 7)), reads=[sqB[1], onesB], writes=[PSB[1]])
        mean, meanB = sgl[0], sglB[0]
        k.op(ACT, lambda e: e.activation(out=mean[:], in_=PS[0][:], func=AF.Copy, scale=1.0 / D), reads=[PSB[0]], writes=[meanB])
        k.op(POOL, lambda e: e.tensor_tensor(out=sgl[1][:], in0=mean[:], in1=mean[:], op=ALU.mult), reads=[meanB], writes=[sglB[1]])
        k.op(DVE, lambda e: e.scalar_tensor_tensor(out=sd[:], in0=PS[1][:], scalar=1.0 / D, in1=sgl[1][:], op0=ALU.mult, op1=ALU.subtract),
             reads=[PSB[1], sglB[1]], writes=[sdB])
        k.op(ACT, lambda e: e.activation(out=sd[:], in_=sd[:], func=AF.Ln, bias=epsT[:]), reads=[sdB, epsB], writes=[sdB])
        k.op(ACT, lambda e: e.activation(out=rstd[:], in_=sd[:], func=AF.Exp, scale=-0.5), reads=[sdB], writes=[rstdB])
        for kc in range(8):
            acc = ar32(kc)
            accB = [ARB[2 * kc], ARB[2 * kc + 1]]
            if kc % 3 == 2:
                ve_, ta, taB, tb, tbB = POOL, sgl[1], sglB[1], tmp[2], tmpB[2]
            else:
                ve_, ta, taB, tb, tbB = DVE, tmp[0], tmpB[0], tmp[1], tmpB[1]
            k.op(ve_, lambda e, acc=acc, ta=ta: e.tensor_tensor(out=ta[:], in0=acc, in1=mean[:], op=ALU.subtract), reads=accB + [meanB], writes=[taB])
            k.op(ve_, lambda e, ta=ta, tb=tb: e.tensor_tensor(out=tb[:], in0=ta[:], in1=rstd[:], op=ALU.mult), reads=[taB, rstdB], writes=[tbB])
            k.op(ACT, lambda e, kc=kc, tb=tb: e.activation(out=xmT[:, kc, :], in_=tb[:], func=AF.Silu, scale=V("lng", kc), bias=V("lnb", kc)),
                 reads=[tbB, vecB], writes=[xmB[kc]])
        for pc in range(2):
            w, wb = wnext(("pw2", pc))
            wv = r3(w, 8, 512)
            for ff in range(4):
                f = pc * 4 + ff
                bank = 6 + f % 2
                for kc in range(8):
                    k.op(PE, lambda e, kc=kc, ff=ff, bank=bank, wv=wv: e.matmul(PS[bank][:], lhsT=wv[:, kc, ff * 128:(ff + 1) * 128], rhs=xmT[:, kc, :],
                                                                              start=(kc == 0), stop=(kc == 7)),
                         reads=[wb, xmB[kc]], writes=[PSB[bank]])
                k.op(DVE, lambda e, f=f, bank=bank: e.scalar_tensor_tensor(out=hb[:, f, :], in0=PS[bank][:], scalar=M(1, GT1, f), in1=hb[:, f, :],
                                                                          op0=ALU.mult, op1=ALU.add),
                     reads=[PSB[bank], modB, hbB[f]], writes=[hbB[f]])
        ffn(1, hb, hbB)
        rms_stats(hb, hbB, T, 0)
        for kc in range(8):
            k.op(DVE, lambda e, kc=kc: e.scalar_tensor_tensor(out=hb[:, kc, :], in0=hb[:, kc, :], scalar=V("gfin", kc), in1=rstd[:], op0=ALU.mult, op1=ALU.mult),
                 reads=[hbB[kc], vecB, rstdB], writes=[hbB[kc]])
        for m in range(4):
            ib = iostate["n"] % 2
            iostate["n"] += 1
            for half in range(2):
                bank = 6 + half
                for kk in range(4):
                    kc = half * 4 + kk
                    k.op(PE, lambda e, kc=kc, kk=kk, bank=bank, m=m: e.transpose(out=PS[bank][:, kk * 128:(kk + 1) * 128], in_=hb[:, kc, m * 128:(m + 1) * 128],
                                                                               identity=ident[:]),
                         reads=[hbB[kc], identB], writes=[PSB[bank]])
                if half == 0:
                    k.op(ACT, lambda e, ib=ib, bank=bank: e.activation(out=io[ib][:, 0:512], in_=PS[bank][:], func=AF.Copy), reads=[PSB[bank]], writes=[ioB[ib]])
                else:
                    k.op(DVE, lambda e, ib=ib, bank=bank: e.tensor_copy(out=io[ib][:, 512:1024], in_=PS[bank][:]), reads=[PSB[bank], ioB[ib]], writes=[ioB[ib]])
            k.dma(SP, lambda e, ib=ib, m=m: e.dma_start(out=out_d[i * T + m * 128: i * T + (m + 1) * 128, :], in_=io[ib][:]), reads=[ioB[ib]])

    coef_A("A1_1", 1, SC1, "gmix")
    coef_A("A2_1", 1, SC2, "gffn")
    c0 = CO["gb"]
    k.op(DVE, lambda e: e.tensor_tensor(out=coef[:, c0:c0 + 8], in0=modsb[:, 48 + GT1 * 8:48 + GT1 * 8 + 8, 0],
                                        in1=vecT[:, VEC_ROWS["bpw2"]:VEC_ROWS["bpw2"] + 8], op=ALU.mult),
         reads=[modB, vecB, coefB], writes=[coefB])
    outB = Buf("out")
    for i in range(NT):
        layer0(i)
        layer1a(i)
        if i >= 1:
            layer1b(i - 1)
    layer1b(NT - 1)
    assert wstate["next"] == len(seq)

    print("sbuf bytes remaining/partition:", nc.sbuf_bytes_remaining)
    k.emit()
    k.finish(SP, ioB + [dumpB])
    return nc, k


def make_vecs(c_b, c_ctx, b_mod, g_mix, g_ffn, b_pw1, b_dw, ln_g, ln_b, b_pw2, g_final, w_dw, q_gain, k_gain):
    rows = [c_b.reshape(8, 128), c_ctx.reshape(8, 128), b_mod.reshape(96, 128), g_mix.reshape(16, 128), g_ffn.reshape(16, 128),
            b_pw1.reshape(16, 128), b_dw.reshape(8, 128), ln_g.reshape(8, 128), ln_b.reshape(8, 128), b_pw2.reshape(8, 128),
            g_final.reshape(8, 128), w_dw.reshape(248, 128),
            np.concatenate([q_gain.reshape(64), q_gain.reshape(64)])[None, :],
            np.concatenate([k_gain.reshape(64), k_gain.reshape(64)])[None, :]]
    v = np.concatenate(rows, 0).astype(np.float32)
    out = np.zeros((NVEC, 128), np.float32)
    out[:v.shape[0]] = v
    return out


def make_in_maps(x, c, ctx, c_ctx, w_mod, b_mod, g_mix, g_ffn, w_ffn_in, w_ffn_out, w_in, q_gain, k_gain, w_sp, b_sp, w_out,
                 w_pw1, b_pw1, w_dw, b_dw, ln_g, ln_b, w_pw2, b_pw2, g_final):
    f = lambda a: np.ascontiguousarray(np.asarray(a, dtype=np.float32))
    B, S, _ = x.shape
    cos, sin = rope_tables(S)
    shared = {
        "w_mod": f(w_mod), "w_ffn_in": f(w_ffn_in), "w_ffn_out": f(w_ffn_out), "w_in": f(w_in[0]),
        "w_sp": f(w_sp[0]), "b_sp": f(b_sp[0]).reshape(1, 512), "w_out": f(w_out[0]), "w_pw1": f(w_pw1[0]), "w_pw2": f(w_pw2[0]),
        "ident": np.eye(128, dtype=np.float32), "pm": rope_perm(), "cos": cos, "sin": sin,
    }
    maps = []
    for b in range(B):
        m = dict(shared)
        m["x"] = f(x[b])
        m["ctx"] = f(ctx[b])
        m["vecs"] = make_vecs(f(c[b]), f(c_ctx), f(b_mod), f(g_mix), f(g_ffn), f(b_pw1[0]), f(b_dw[0]), f(ln_g[0]), f(ln_b[0]),
                              f(b_pw2[0]), f(g_final), f(w_dw[0]), f(q_gain[0]), f(k_gain[0]))
        maps.append(m)
    return maps


_NC_CACHE = {}


def kernel(**inputs):
    x = np.asarray(inputs["x"])
    B, S, _ = x.shape
    maps = make_in_maps(**inputs)
    if S not in _NC_CACHE:
        _NC_CACHE[S] = build(S)[0]
    nc = _NC_CACHE[S]
    res = run_bass_kernel_spmd(nc, maps, core_ids=list(range(B)))
    return np.stack([np.asarray(r["out"], dtype=np.float32) for r in res.results], 0)
```

```python
import numpy as np
import concourse.bass as bass
import concourse.mybir as mybir
from concourse.bass_utils import run_bass_kernel_spmd

F32 = mybir.dt.float32
BF16 = mybir.dt.bfloat16
AF = mybir.ActivationFunctionType
ALU = mybir.AluOpType
AX = mybir.AxisListType

PE, ACT, DVE, POOL, SP = "pe", "act", "dve", "pool", "sp"

D = 1024
CTX = 256
DFF = 2816
NH = 22
T = 512
EPS = 1e-6
NSLOT = 6
DRIP = 2
SHIFT = -8.0
SLOT_ELEMS = 4096


class Buf:
    __slots__ = ("name", "lw", "rd", "dsem", "dcount")

    def __init__(self, name):
        self.name = name
        self.lw = None
        self.rd = {}
        self.dsem = None
        self.dcount = 0


class Op:
    __slots__ = ("eng", "fn", "deps", "is_dma", "buf", "sig", "sigval", "dval")

    def __init__(self, eng, fn, is_dma=False):
        self.eng = eng
        self.fn = fn
        self.deps = []
        self.is_dma = is_dma
        self.buf = None
        self.sig = False
        self.sigval = 0
        self.dval = 0


class K:
    def __init__(self, nc):
        self.nc = nc
        self.ops = []
        self.h = {PE: nc.tensor, ACT: nc.scalar, DVE: nc.vector, POOL: nc.gpsimd, SP: nc.sync}
        self.dsems = {}

    def op(self, eng, fn, reads=(), writes=()):
        o = Op(eng, fn)
        self._deps(o, reads, writes)
        self.ops.append(o)
        return o

    def dma(self, eng, fn, reads=(), writes=(), track=None, disjoint=False):
        o = Op(eng, fn, is_dma=True)
        if track is None:
            track = writes[0] if writes else reads[0]
        key = (track.name, eng)
        if key not in self.dsems:
            self.dsems[key] = [None, 0]
        o.buf = key
        self.dsems[key][1] += 16
        o.dval = self.dsems[key][1]
        self._deps(o, reads, writes, disjoint)
        self.ops.append(o)
        return o

    def _deps(self, o, reads, writes, disjoint=False):
        for b in reads:
            if b.lw is not None:
                o.deps.append(b.lw)
            b.rd[id(o) if o.is_dma else o.eng] = o
        for b in writes:
            if b.lw is not None and not (disjoint and b.lw.is_dma):
                o.deps.append(b.lw)
            for r in b.rd.values():
                if r is not o:
                    o.deps.append(r)
            b.lw = o
            b.rd = {}

    def emit(self):
        nc = self.nc
        for o in self.ops:
            nd = []
            for d in o.deps:
                if d.is_dma:
                    nd.append(d)
                    continue
                if d.eng == PE and o.eng == PE and not o.is_dma:
                    continue
                d.sig = True
                nd.append(d)
            o.deps = nd
        sems = {e: nc.alloc_semaphore(name=f"s_{e}") for e in self.h}
        for key, v in self.dsems.items():
            v[0] = nc.alloc_semaphore(name=f"d_{key[0]}_{key[1]}")
        cnt = {e: 0 for e in self.h}
        for o in self.ops:
            if o.sig and not o.is_dma:
                cnt[o.eng] += 1
                o.sigval = cnt[o.eng]
        waited = {e: {} for e in self.h}
        self.nwait = 0
        for o in self.ops:
            need = {}
            for d in o.deps:
                if d.is_dma:
                    key, val = self.dsems[d.buf][0], d.dval
                else:
                    key, val = sems[d.eng], d.sigval
                kid = id(key)
                if kid not in need or need[kid][1] < val:
                    need[kid] = (key, val)
            w = waited[o.eng]
            for kid, (key, val) in need.items():
                if w.get(kid, 0) >= val:
                    continue
                self.h[o.eng].wait_ge(key, val)
                w[kid] = val
                self.nwait += 1
            ins = o.fn(self.h[o.eng])
            if o.is_dma:
                ins.then_inc(self.dsems[o.buf][0], 16)
            elif o.sig:
                ins.then_inc(sems[o.eng], 1)

    def finish(self, eng, bufs):
        names = {b.name for b in bufs}
        for key, v in self.dsems.items():
            if key[0] in names:
                self.h[eng].wait_ge(v[0], v[1])


VEC_ROWS = {}
_r = 0
for _n, _c in [("c", 8), ("cctx", 8), ("bmod", 96), ("gmix", 16), ("gffn", 16), ("bpw1", 16), ("bdw", 8),
               ("lng", 8), ("lnb", 8), ("bpw2", 8), ("gfin", 8), ("wdw", 248), ("qg", 1), ("kg", 1)]:
    VEC_ROWS[_n] = _r
    _r += _c
NVEC = 512


def rope_tables(seq):
    t = np.arange(seq)
    row = (t // 64).astype(np.float32)
    col = (t % 64).astype(np.float32)
    inv = (10000.0 ** (-np.arange(0, 32, 2, dtype=np.float32) / 32.0)).astype(np.float32)
    ang_r = (row[:, None] * inv[None, :]).astype(np.float32)
    ang_c = (col[:, None] * inv[None, :]).astype(np.float32)
    cr, sr, cc, sc = np.cos(ang_r), np.sin(ang_r), np.cos(ang_c), np.sin(ang_c)
    cos64 = np.concatenate([cr, cr, cc, cc], axis=1).T
    sin64 = np.concatenate([-sr, sr, -sc, sc], axis=1).T
    cos = np.concatenate([cos64, cos64], 0).astype(np.float32)
    sin = np.concatenate([sin64, sin64], 0).astype(np.float32)
    return np.ascontiguousarray(cos), np.ascontiguousarray(sin)


def rope_perm():
    pm = np.zeros((128, 128), np.float32)
    for hh in range(2):
        for d in range(64):
            blk = d // 16
            src = d + 16 if blk in (0, 2) else d - 16
            pm[hh * 64 + src, hh * 64 + d] = 1.0
    return pm


def piece_sequence(nt):
    seq = [("mod", 0, pc) for pc in range(12)]
    seq += [("mod", 1, pc) for pc in range(12)]
    l0 = [("wq",), ("wsu",), ("wsv",), ("wo", 0), ("wo", 1)] + [("fin", 0, p) for p in range(11)] + \
         [("fout", 0, c) for c in range(8)]
    l1a = [("pw1", p) for p in range(4)]
    l1b = [("pw2", 0), ("pw2", 1)] + [("fin", 1, p) for p in range(11)] + [("fout", 1, c) for c in range(8)]
    for i in range(nt):
        seq += l0 + l1a
        if i >= 1:
            seq += l1b
    seq += l1b
    return seq


def build(SEQ, dbg=None):
    NT = SEQ // T
    NKT = (CTX + SEQ) // 128
    nc = bass.Bass("TRN2", target_bir_lowering=False)
    k = K(nc)

    def din(name, shape, dt=F32):
        return nc.dram_tensor(name, list(shape), dt, kind="ExternalInput").ap()

    x_d = din("x", [SEQ, D])
    ctx_d = din("ctx", [CTX, D])
    vecs_d = din("vecs", [NVEC, 128])
    w_mod_d = din("w_mod", [2, D, 6 * D])
    w_ffn_in_d = din("w_ffn_in", [2, D, 2 * DFF])
    w_ffn_out_d = din("w_ffn_out", [2, DFF, D])
    w_in_d = din("w_in", [D, 1792])
    w_sp_d = din("w_sp", [4, 128, 128])
    b_sp_d = din("b_sp", [1, 512])
    w_out_d = din("w_out", [D, D])
    w_pw1_d = din("w_pw1", [D, 2 * D])
    w_pw2_d = din("w_pw2", [D, D])
    ident_d = din("ident", [128, 128])
    pm_d = din("pm", [128, 128])
    cos_d = din("cos", [128, SEQ])
    sin_d = din("sin", [128, SEQ])
    out_d = nc.dram_tensor("out", [SEQ, D], F32, kind="ExternalOutput").ap()
    dbg_d = {}
    if dbg:
        for n, shp in dbg.items():
            dbg_d[n] = nc.dram_tensor("dbg_" + n, list(shp), F32, kind="ExternalOutput").ap()

    def sb(name, shape, dt=F32):
        return nc.alloc_sbuf_tensor("s_" + name, list(shape), dt)

    ring = sb("ring", [128, NSLOT, SLOT_ELEMS], BF16)
    ringB = [Buf(f"ring{i}") for i in range(NSLOT)]
    KT = sb("KT", [128, CTX + SEQ], BF16)
    KTB = Buf("KT")
    VA = sb("VA", [128, NKT, 2, 128], BF16)
    VAB = Buf("VA")
    hT = [sb(f"hT{i}", [128, 8, T]) for i in range(2)]
    hB = [[Buf(f"h{i}_{c}") for c in range(8)] for i in range(2)]
    io = [sb(f"io{i}", [128, D]) for i in range(2)]
    ioB = [Buf(f"io{i}") for i in range(2)]
    xmT = sb("xmT", [128, 8, T], BF16)
    xmB = [Buf(f"xm{c}") for c in range(8)]
    sq = [sb(f"sq{i}", [128, T], BF16) for i in range(2)]
    sqB = [Buf(f"sq{i}") for i in range(2)]
    tmp = [sb(f"tmp{i}", [128, T]) for i in range(3)]
    tmpB = [Buf(f"tmp{i}") for i in range(3)]
    rstd = sb("rstd", [128, T])
    rstdB = Buf("rstd")
    sd = sb("sd", [128, T])
    sdB = Buf("sd")
    AR = sb("AR", [128, NH, T], BF16)
    ARB = [Buf(f"ar{j}") for j in range(NH)]
    AR32 = AR.bitcast(F32) if hasattr(AR, "bitcast") else None
    GW = [sb(f"GW{i}", [128, 8, T + 30], BF16) for i in range(2)]
    NDG = 8
    dg = [sb(f"dg{i}", [128, 128], BF16) for i in range(NDG)]
    dgB = [Buf(f"dg{i}") for i in range(NDG)]
    identb = sb("identb", [128, 128], BF16)
    dgstate = {"n": 0}
    GWB = [[Buf(f"gw{i}_{c}") for c in range(8)] for i in range(2)]
    sgl = [sb(f"sgl{i}", [128, T]) for i in range(2)]
    sglB = [Buf(f"sgl{i}") for i in range(2)]
    cs = sb("cs", [128, 2, T])
    csB = Buf("cs")
    knT = sb("knT", [128, T], BF16)
    knB = Buf("knT")
    gv = sb("gv", [128, T])
    gvB = Buf("gv")
    gv2 = sb("gv2", [128, T])
    gv2B = Buf("gv2")
    vn = sb("vn", [128, 4, 128], BF16)
    vnB = Buf("vn")
    st = sb("st", [128, 32])
    stB = Buf("st")
    st2 = sb("st2", [128, 32])
    st2B = Buf("st2")
    vstage = sb("vstage", [128, 4, 128])
    vstageB = Buf("vstage")
    vecT = sb("vecT", [128, NVEC])
    vecB = Buf("vecT")
    coef = sb("coef", [128, 96])
    coefB = Buf("coef")
    modsb = sb("modsb", [128, 96, 2])
    modB = Buf("modsb")
    scb = sb("scb", [128, 8, 2], BF16)
    scbB = Buf("scb")
    ident = sb("ident", [128, 128])
    identB = Buf("ident")
    pmf = sb("pmf", [128, 128])
    pmb = sb("pmb", [128, 128], BF16)
    pmB = Buf("pm")
    ones = sb("ones", [128, 128], BF16)
    bones = sb("bones", [128, 128], BF16)
    onesB = Buf("ones")
    onesrow = sb("onesrow", [1, 128])

    bsprow = sb("bsprow", [1, 512])
    rowB = Buf("rows")
    wspT = sb("wspT", [128, 4, 128], BF16)
    wspB = Buf("wspT")
    nshift = sb("nshift", [128, 1])
    nshiftB = Buf("nshift")
    gtmp = sb("gtmp", [1, 4])

    PS = [nc.alloc_psum_tensor(f"ps{i}", [128, T], F32) for i in range(8)]
    PSB = [Buf(f"ps{i}") for i in range(8)]

    def ar(j):
        return AR[:, j, :]

    def ar32(j2):
        return AR[:, 2 * j2:2 * j2 + 2, :].bitcast(F32).rearrange("p a t -> p (a t)")

    QT_, AT_, SG_, GU_, PT_ = 0, 4, 8, 12, 16

    def V(name, i=0):
        r = VEC_ROWS[name] + i
        return vecT[:, r:r + 1]

    CO = {}
    _cc = [0]

    def cocol(name, n=8):
        CO[name] = _cc[0]
        _cc[0] += n

    for nm in ["A1_0", "A1c", "A2_0", "A1_1", "A2_1", "gb"]:
        cocol(nm)

    def C(name, i):
        c = CO[name] + i
        return coef[:, c:c + 1]

    def M(l, which, i, col=0):
        idx = l * 48 + which * 8 + i
        return modsb[:, idx, col:col + 1]

    SH1, SC1, GT1, SH2, SC2, GT2 = range(6)

    scr = {}
    scrB = {}

    def mkscr(key, group):
        scr[key] = nc.dram_tensor("scr_" + "_".join(str(s) for s in key), [128, SLOT_ELEMS], BF16).ap()
        if group not in scrB:
            scrB[group] = Buf("scr_" + group)
        return scr[key], scrB[group]

    def cast(dst, src, gb):
        k.dma(POOL, lambda e: e.dma_start(out=dst, in_=src), writes=[gb], disjoint=True)

    def r3(ap2, a, b):
        return ap2.rearrange("p (a b) -> p a b", a=a, b=b)

    def cast_wkv():
        s, gb = mkscr(("wkv",), "wkv")
        cast(r3(s[:, 0:8 * 256], 8, 256), w_in_d[:, 512:768].rearrange("(k p) c -> p k c", p=128), gb)

    def cast_all_early():
        s, gb = mkscr(("wq",), "wq")
        sv = r3(s, 8, 512)
        for g in range(4):
            for kv in range(2):
                h = kv * 4 + g
                cast(sv[:, :, g * 128 + kv * 64: g * 128 + kv * 64 + 64],
                     w_in_d[:, h * 64:(h + 1) * 64].rearrange("(k p) c -> p k c", p=128), gb)
        s, gb = mkscr(("wsu",), "wsu")
        cast(r3(s, 8, 512), w_in_d[:, 768:1280].rearrange("(k p) c -> p k c", p=128), gb)
        s, gb = mkscr(("wsv",), "wsv")
        cast(r3(s, 8, 512), w_in_d[:, 1280:1792].rearrange("(k p) c -> p k c", p=128), gb)
        for pc in range(2):
            s, gb = mkscr(("wo", pc), "wo")
            sv = r3(s, 8, 512)
            cols = slice(pc * 512, (pc + 1) * 512)
            cast(sv[0:64, 0:4, :], w_out_d[0:256, cols].rearrange("(c p) f -> p c f", p=64), gb)
            cast(sv[64:128, 0:4, :], w_out_d[256:512, cols].rearrange("(c p) f -> p c f", p=64), gb)
            cast(sv[:, 4:8, :], w_out_d[512:1024, cols].rearrange("(k p) f -> p k f", p=128), gb)

    def cast_fin(l, pps):
        for pp in pps:
            s, gb = mkscr(("fin", l, pp), f"fin{l}")
            sv = s.rearrange("p (k s c) -> p k s c", k=8, s=2, c=256)
            wv = w_ffn_in_d[l].rearrange("(k p) (s c) -> p k s c", p=128, s=2)
            for s2 in range(2):
                cast(sv[:, :, s2, :], wv[:, :, s2, pp * 256:(pp + 1) * 256], gb)
    def cast_fout(l):
        for c in range(8):
            s, gb = mkscr(("fout", l, c), f"fout{l}")
            cast(r3(s[:, 0:NH * 128], NH, 128),
                 w_ffn_out_d[l][:, c * 128:(c + 1) * 128].rearrange("(k p) f -> p k f", p=128), gb)

    def cast_pw():
        for pp in range(4):
            s, gb = mkscr(("pw1", pp), "pw1")
            sv = s.rearrange("p (k s c) -> p k s c", k=8, s=2, c=256)
            wv = w_pw1_d.rearrange("(k p) (s c) -> p k s c", p=128, s=2)
            for s2 in range(2):
                cast(sv[:, :, s2, :], wv[:, :, s2, pp * 256:(pp + 1) * 256], gb)
        for pc in range(2):
            s, gb = mkscr(("pw2", pc), "pw2")
            cast(r3(s, 8, 512), w_pw2_d[:, pc * 512:(pc + 1) * 512].rearrange("(k p) c -> p k c", p=128), gb)

    seq = piece_sequence(NT)
    wstate = {"issued": 0, "next": 0}

    def group_of(key):
        if key[0] in ("fin", "fout"):
            return f"{key[0]}{key[1]}"
        return key[0]

    def issue_load(n):
        key = seq[n]
        slot = n % NSLOT
        if key[0] == "mod":
            _, l, pc = key
            dst = r3(ring[:, slot, :], 8, 512)
            src = w_mod_d[l][:, pc * 512:(pc + 1) * 512].rearrange("(k p) c -> p k c", p=128)
            k.dma(POOL, lambda e: e.dma_start(out=dst, in_=src), writes=[ringB[slot]])
        else:
            s = scr[key]
            n_el = SLOT_ELEMS
            if key[0] == "wkv":
                n_el = 8 * 256
            elif key[0] == "fout":
                n_el = NH * 128
            k.dma(SP, lambda e: e.dma_start(out=ring[:, slot, 0:n_el], in_=s[:, 0:n_el]),
                  reads=[scrB[group_of(key)]], writes=[ringB[slot]])

    def wnext(key):
        n = wstate["next"]
        assert seq[n] == key, (seq[n], key)
        while wstate["issued"] < min(len(seq), n + NSLOT):
            issue_load(wstate["issued"])
            wstate["issued"] += 1
        wstate["next"] += 1
        slot = n % NSLOT
        return ring[:, slot, :], ringB[slot]

    k.dma(SP, lambda e: e.dma_start(out=vstage[:], in_=vecs_d.rearrange("(g r) c -> r g c", r=128)), writes=[vstageB])
    k.dma(SP, lambda e: e.dma_start(out=ident[:], in_=ident_d[:]), writes=[identB])
    k.dma(SP, lambda e: e.dma_start(out=pmf[:], in_=pm_d[:]), writes=[pmB])
    k.dma(SP, lambda e: e.dma_start(out=bsprow[:], in_=b_sp_d[:]), writes=[rowB])

    cast_wkv()
    epsT = sb("epsT", [128, 1])
    epsB = Buf("eps")
    k.op(DVE, lambda e: e.memset(epsT[:], EPS), writes=[epsB])
    neghalf = sb("neghalf", [128, 4])
    k.op(DVE, lambda e: e.memset(neghalf[:], -0.5), reads=[epsB], writes=[epsB])

    k.op(POOL, lambda e: e.memset(ones[:], 1.0), writes=[onesB])
    k.op(POOL, lambda e: e.memset(bones[:], 0.0), writes=[onesB])
    k.op(POOL, lambda e: e.memset(bones[0:64, 0:64], 1.0), writes=[onesB])
    k.op(POOL, lambda e: e.memset(bones[64:128, 64:128], 1.0), writes=[onesB])
    k.op(POOL, lambda e: e.memset(onesrow[:], 1.0), writes=[rowB])

    k.op(POOL, lambda e: e.memset(VA[:, :, :, 64:128], 1.0), writes=[VAB])
    k.op(DVE, lambda e: e.tensor_copy(out=pmb[:], in_=pmf[:]), reads=[pmB], writes=[pmB])
    k.op(DVE, lambda e: e.tensor_copy(out=identb[:], in_=ident[:]), reads=[identB], writes=[identB])

    for g in range(4):
        k.op(PE, lambda e, g=g: e.transpose(out=PS[0][:, g * 128:(g + 1) * 128], in_=vstage[:, g, :], identity=ident[:]),
             reads=[vstageB, identB], writes=[PSB[0]])
    k.op(DVE, lambda e: e.tensor_copy(out=vecT[:], in_=PS[0][:]), reads=[PSB[0]], writes=[vecB])

    k.dma(SP, lambda e: e.dma_start(out=vstage[:], in_=w_sp_d.rearrange("g p q -> p g q")), reads=[vstageB], writes=[vstageB])
    for g in range(4):
        k.op(PE, lambda e, g=g: e.transpose(out=PS[1][:, g * 128:(g + 1) * 128], in_=vstage[:, g, :], identity=ident[:]),
             reads=[vstageB, identB], writes=[PSB[1]])
    k.op(DVE, lambda e: e.tensor_copy(out=wspT[:].rearrange("p g q -> p (g q)"), in_=PS[1][:]), reads=[PSB[1]], writes=[wspB])

    k.op(ACT, lambda e: e.activation(out=st[:, 0:2], in_=vecT[:, VEC_ROWS["qg"]:VEC_ROWS["qg"] + 2], func=AF.Abs),
         reads=[vecB], writes=[stB])
    k.op(PE, lambda e: e.transpose(out=PS[2][0:2, 0:128], in_=st[:, 0:2], identity=ident[:]), reads=[stB, identB], writes=[PSB[2]])
    k.op(DVE, lambda e: e.tensor_reduce(out=st[0:2, 2:3], in_=PS[2][0:2, 0:128], axis=AX.X, op=ALU.max), reads=[PSB[2], stB], writes=[stB])
    k.op(PE, lambda e: e.transpose(out=PS[2][0:1, 128:130], in_=st[0:2, 2:3], identity=ident[0:2, 0:2]), reads=[stB, identB], writes=[PSB[2]])
    k.op(DVE, lambda e: e.tensor_copy(out=gtmp[0:1, 0:2], in_=PS[2][0:1, 128:130]), reads=[PSB[2]], writes=[stB])
    k.op(DVE, lambda e: e.tensor_tensor(out=gtmp[0:1, 2:3], in0=gtmp[0:1, 0:1], in1=gtmp[0:1, 1:2], op=ALU.mult), reads=[stB], writes=[stB])
    k.op(PE, lambda e: e.matmul(PS[2][:, 256:257], lhsT=onesrow[0:1, :], rhs=gtmp[0:1, 2:3], start=True, stop=True), reads=[stB, rowB], writes=[PSB[2]])
    k.op(ACT, lambda e: e.activation(out=nshift[:], in_=PS[2][:, 256:257], func=AF.Copy, scale=-8.0), reads=[PSB[2]], writes=[nshiftB])

    k.op(ACT, lambda e: e.activation(out=scb[:, :, 0], in_=vecT[:, VEC_ROWS["c"]:VEC_ROWS["c"] + 8], func=AF.Silu), reads=[vecB], writes=[scbB])
    k.op(ACT, lambda e: e.activation(out=scb[:, :, 1], in_=vecT[:, VEC_ROWS["cctx"]:VEC_ROWS["cctx"] + 8], func=AF.Silu), reads=[vecB, scbB], writes=[scbB])

    def mod_pieces(l, pcs, mbank):
        for pc in pcs:
            w, wb = wnext(("mod", l, pc))
            wv = r3(w, 8, 512)
            for fc in range(4):
                idx = l * 48 + pc * 4 + fc
                for kk in range(8):
                    k.op(PE, lambda e, wv=wv, fc=fc, kk=kk, idx=idx: e.matmul(
                        PS[mbank][:, idx * 2:idx * 2 + 2], lhsT=wv[:, kk, fc * 128:(fc + 1) * 128], rhs=scb[:, kk, :],
                        start=(kk == 0), stop=(kk == 7)), reads=[wb, scbB], writes=[PSB[mbank]])

    def mod_finish(l, mbank, i0=0, i1=48):
        for col in range(2):
            k.op(DVE, lambda e, col=col: e.tensor_tensor(
                out=modsb[:, l * 48 + i0:l * 48 + i1, col],
                in0=PS[mbank][:, (l * 48 + i0) * 2:(l * 48 + i1) * 2].rearrange("p (i c) -> p i c", c=2)[:, :, col],
                in1=vecT[:, VEC_ROWS["bmod"] + l * 48 + i0:VEC_ROWS["bmod"] + l * 48 + i1], op=ALU.add), reads=[PSB[mbank], vecB, modB], writes=[modB])

    mod_pieces(0, range(4), 3)
    mod_finish(0, 3, 0, 16)
    cast_all_early()
    cast_batches = [lambda: cast_fin(0, range(0, 6)), lambda: cast_fin(0, range(6, 11)), lambda: cast_fout(0), cast_pw,
                    lambda: cast_fin(1, range(0, 6)), lambda: cast_fin(1, range(6, 11)), lambda: cast_fout(1)]

    def coef_A(name, l, which, gname, col=0):
        c0 = CO[name]
        k.op(DVE, lambda e: e.scalar_tensor_tensor(
            out=coef[:, c0:c0 + 8], in0=modsb[:, l * 48 + which * 8: l * 48 + which * 8 + 8, col], scalar=1.0,
            in1=vecT[:, VEC_ROWS[gname] + l * 8: VEC_ROWS[gname] + l * 8 + 8], op0=ALU.add, op1=ALU.mult),
            reads=[modB, vecB, coefB], writes=[coefB])

    coef_A("A1_0", 0, SC1, "gmix")
    coef_A("A1c", 0, SC1, "gmix", col=1)

    iostate = {"n": 0}

    def load_tile_T(src_rows, ntok, hbuf, hbufB):
        for m in range(ntok // 128):
            ib = iostate["n"] % 2
            iostate["n"] += 1
            k.dma(SP, lambda e, ib=ib, m=m: e.dma_start(out=io[ib][:], in_=src_rows[m * 128:(m + 1) * 128, :]), writes=[ioB[ib]])
            for half in range(2):
                bank = 6 + half
                for kk in range(4):
                    kc = half * 4 + kk
                    k.op(PE, lambda e, ib=ib, kc=kc, kk=kk, bank=bank: e.transpose(
                        out=PS[bank][:, kk * 128:(kk + 1) * 128], in_=io[ib][:, kc * 128:(kc + 1) * 128], identity=ident[:]),
                        reads=[ioB[ib], identB], writes=[PSB[bank]])
                eng = ACT if half == 0 else DVE
                if eng == ACT:
                    k.op(ACT, lambda e, half=half, m=m, bank=bank: e.activation(
                        out=hbuf[:, half * 4:half * 4 + 4, m * 128:(m + 1) * 128],
                        in_=PS[bank][:].rearrange("p (a t) -> p a t", a=4), func=AF.Copy),
                        reads=[PSB[bank]], writes=hbufB[half * 4:half * 4 + 4])
                else:
                    k.op(DVE, lambda e, half=half, m=m, bank=bank: e.tensor_copy(
                        out=hbuf[:, half * 4:half * 4 + 4, m * 128:(m + 1) * 128],
                        in_=PS[bank][:].rearrange("p (a t) -> p a t", a=4)),
                        reads=[PSB[bank]], writes=hbufB[half * 4:half * 4 + 4])

    def rms_stats(hbuf, hbufB, ntok, bank, sd_=None, sd_B=None, rs_=None, rs_B=None):
        if sd_ is None:
            sd_, sd_B, rs_, rs_B = sd[:], [sdB], rstd[:], [rstdB]
        for kc in range(8):
            s = kc % 2
            k.op(ACT, lambda e, kc=kc, s=s: e.activation(out=sq[s][:, 0:ntok], in_=hbuf[:, kc, 0:ntok], func=AF.Square),
                 reads=[hbufB[kc]], writes=[sqB[s]])
            k.op(PE, lambda e, kc=kc, s=s: e.matmul(PS[bank][:, 0:ntok], lhsT=ones[:], rhs=sq[s][:, 0:ntok],
                                                    start=(kc == 0), stop=(kc == 7)),
                 reads=[sqB[s], onesB], writes=[PSB[bank]])
        k.op(ACT, lambda e: e.activation(out=sd_[:, 0:ntok], in_=PS[bank][:, 0:ntok], func=AF.Ln, scale=1.0 / D, bias=epsT[:]),
             reads=[PSB[bank], epsB], writes=sd_B)
        k.op(ACT, lambda e: e.activation(out=rs_[:, 0:ntok], in_=sd_[:, 0:ntok], func=AF.Exp, scale=-0.5), reads=sd_B, writes=rs_B)

    tmpstate = {"n": 0}

    def norm_mod(hbuf, hbufB, ntok, Acol, Bcol, bank, ve=POOL, xm_=None, xm_B=None, sd_=None, sd_B=None, rs_=None, rs_B=None):
        if xm_ is None:
            xm_, xm_B = xmT, xmB
        if sd_ is None:
            sd_, sd_B, rs_, rs_B = sd[:], [sdB], rstd[:], [rstdB]
        rms_stats(hbuf, hbufB, ntok, bank, sd_, sd_B, rs_, rs_B)
        for kc in range(8):
            ti = tmpstate["n"] % 3
            tmpstate["n"] += 1
            k.op(DVE if (ve == DVE or kc % 3 != 2) else POOL, lambda e, kc=kc, ti=ti: e.tensor_tensor(out=tmp[ti][:, 0:ntok], in0=hbuf[:, kc, 0:ntok], in1=rs_[:, 0:ntok], op=ALU.mult),
                 reads=[hbufB[kc]] + rs_B, writes=[tmpB[ti]])
            k.op(ACT, lambda e, kc=kc, ti=ti: e.activation(out=xm_[:, kc, 0:ntok], in_=tmp[ti][:, 0:ntok], func=AF.Identity,
                                                           scale=Acol(kc), bias=Bcol(kc)),
                 reads=[tmpB[ti], coefB, modB], writes=[xm_B[kc]])

    def headnorm_rstd(src_ps, src_psB, ntok, bank, sqi=0, sd_=None, sd_B=None, rs_=None, rs_B=None):
        if sd_ is None:
            sd_, sd_B, rs_, rs_B = sd[:], [sdB], rstd[:], [rstdB]
        k.op(ACT, lambda e: e.activation(out=sq[sqi][:, 0:ntok], in_=src_ps[:, 0:ntok], func=AF.Square), reads=[src_psB], writes=[sqB[sqi]])
        k.op(PE, lambda e: e.matmul(PS[bank][:, 0:ntok], lhsT=bones[:], rhs=sq[sqi][:, 0:ntok], start=True, stop=True),
             reads=[sqB[sqi], onesB], writes=[PSB[bank]])
        k.op(ACT, lambda e: e.activation(out=sd_[:, 0:ntok], in_=PS[bank][:, 0:ntok], func=AF.Ln, scale=1.0 / 64, bias=epsT[:]),
             reads=[PSB[bank], epsB], writes=sd_B)
        k.op(ACT, lambda e: e.activation(out=rs_[:, 0:ntok], in_=sd_[:, 0:ntok], func=AF.Exp, scale=-0.5), reads=sd_B, writes=rs_B)

    def rope(src_bf, src_B, dst, dst_B, bank, ve=POOL, t0=None, t0B=None, t1=None, t1B=None):
        if t0 is None:
            t0, t0B, t1, t1B = tmp[0][:], [tmpB[0]], tmp[1][:], [tmpB[1]]
        k.op(PE, lambda e: e.matmul(PS[bank][:], lhsT=pmb[:], rhs=src_bf, start=True, stop=True), reads=[src_B, pmB], writes=[PSB[bank]])
        k.op(ve, lambda e: e.tensor_tensor(out=t0, in0=src_bf, in1=cs[:, 0, :], op=ALU.mult), reads=[src_B, csB], writes=t0B)
        k.op(DVE, lambda e: e.tensor_tensor(out=t1, in0=PS[bank][:], in1=cs[:, 1, :], op=ALU.mult), reads=[PSB[bank], csB], writes=t1B)
        k.op(ve, lambda e: e.tensor_tensor(out=dst, in0=t0, in1=t1, op=ALU.add), reads=t0B + t1B, writes=dst_B)


    def dump(name, ap_fn, bufs):
        if dbg and name in dbg_d:
            k.dma(SP, lambda e: e.dma_start(out=dbg_d[name], in_=ap_fn()), reads=bufs, track=dumpB)

    dumpB = Buf("dump")

    wkv_t = sb("wkv", [128, 8 * 256], BF16)
    wkvB = Buf("wkv")
    k.dma(SP, lambda e: e.dma_start(out=wkv_t[:], in_=scr[("wkv",)][:, 0:8 * 256]), reads=[scrB["wkv"]], writes=[wkvB])
    wkvv = r3(wkv_t[:], 8, 256)

    def phaseA(src_rows, ntok, key_off, latent, tile_i):
        par = (tile_i + (1 if latent else 0)) % 2
        hb, hbB = hT[par], hB[par]
        if par == 0:
            xm_, xm_B = xmT, xmB
            nsd, nsdB, nrs, nrsB = sd[:], [sdB], rstd[:], [rstdB]
            kn_, kn_B = knT[:], knB
        else:
            xm_, xm_B = AR[:, 0:8, :], ARB[0:8]
            nsd, nsdB = ar32(4), [ARB[8], ARB[9]]
            nrs, nrsB = ar32(5), [ARB[10], ARB[11]]
            kn_, kn_B = ar(16), ARB[16]
        hsd, hsdB = ar32(6), [ARB[12], ARB[13]]
        hrs, hrsB = ar32(7), [ARB[14], ARB[15]]
        load_tile_T(src_rows, ntok, hb, hbB)
        if latent:
            norm_mod(hb, hbB, ntok, lambda kc: C("A1_0", kc), lambda kc: M(0, SH1, kc), 0, ve=DVE, xm_=xm_, xm_B=xm_B, sd_=nsd, sd_B=nsdB, rs_=nrs, rs_B=nrsB)
        else:
            norm_mod(hb, hbB, ntok, lambda kc: C("A1c", kc), lambda kc: M(0, SH1, kc, 1), 0, ve=DVE, xm_=xm_, xm_B=xm_B, sd_=nsd, sd_B=nsdB, rs_=nrs, rs_B=nrsB)
        for kc in range(8):
            k.op(PE, lambda e, kc=kc: e.matmul(PS[1][:, 0:ntok], lhsT=wkvv[:, kc, 0:128], rhs=xm_[:, kc, 0:ntok], start=(kc == 0), stop=(kc == 7)),
                 reads=[wkvB, xm_B[kc]], writes=[PSB[1]])
        headnorm_rstd(PS[1], PSB[1], ntok, 2, sqi=par, sd_=hsd, sd_B=hsdB, rs_=hrs, rs_B=hrsB)
        dstK = KT[:, key_off:key_off + ntok] if not latent else kn_[:, 0:ntok]
        dstKB = [KTB] if not latent else [kn_B]
        k.op(DVE, lambda e: e.scalar_tensor_tensor(out=dstK, in0=PS[1][:, 0:ntok], scalar=V("kg"), in1=hrs[:, 0:ntok], op0=ALU.mult, op1=ALU.mult),
             reads=[PSB[1], vecB] + hrsB, writes=dstKB)
        if latent:
            k.dma(SP, lambda e: e.dma_start(out=cs[:, 0, :], in_=cos_d[:, tile_i * T:(tile_i + 1) * T]), writes=[csB])
            k.dma(SP, lambda e: e.dma_start(out=cs[:, 1, :], in_=sin_d[:, tile_i * T:(tile_i + 1) * T]), reads=[], writes=[csB])
            rope(kn_, kn_B, KT[:, key_off:key_off + ntok], [KTB], 3, ve=DVE)
        nsub = ntok // 128
        for m in range(nsub):
            for kc in range(8):
                k.op(PE, lambda e, kc=kc, m=m: e.matmul(PS[4][:, m * 128:(m + 1) * 128], lhsT=xm_[:, kc, m * 128:(m + 1) * 128], rhs=wkvv[:, kc, 128:256],
                                                        start=(kc == 0), stop=(kc == 7)),
                     reads=[wkvB, xm_B[kc]], writes=[PSB[4]])
        j0 = key_off // 128
        k.op(ACT, lambda e: e.activation(out=VA[:, j0:j0 + nsub, :, 0:64],
                                         in_=PS[4][:, 0:nsub * 128].rearrange("p (j h d) -> p j h d", j=nsub, h=2, d=64), func=AF.Copy),
             reads=[PSB[4]], writes=[VAB])

    phaseA(ctx_d, CTX, 0, False, 0)
    mod_todo = [(0, pc) for pc in range(4, 12)] + [(1, pc) for pc in range(12)]
    for i in range(NT):
        for _ in range(3):
            if mod_todo:
                l_, pc_ = mod_todo.pop(0)
                mod_pieces(l_, [pc_], 5)
        if NT < 8 and cast_batches:
            cast_batches.pop(0)()
        phaseA(x_d[i * T:(i + 1) * T, :], T, CTX + i * T, True, i)
    while mod_todo:
        l_, pc_ = mod_todo.pop(0)
        mod_pieces(l_, [pc_], 5)
    mod_finish(0, 5, 16, 48)
    coef_A("A2_0", 0, SC2, "gffn")
    late_casts = []
    head_casts = []
    while cast_batches:
        b = cast_batches.pop(0)
        if NT >= 8 and len(cast_batches) < 3:
            late_casts.append(b)
        elif NT >= 8:
            head_casts.append(b)
        else:
            b()
    mod_finish(1, 5)


    def ffn(l, hb, hbB):
        A2 = "A2_0" if l == 0 else "A2_1"
        norm_mod(hb, hbB, T, lambda kc: C(A2, kc), lambda kc: M(l, SH2, kc), 0)
        for pp in range(11):
            w, wb = wnext(("fin", l, pp))
            wv = w.rearrange("p (k s c) -> p k s c", k=8, s=2, c=256)
            if pp == 0:
                for kc in range(8):
                    for jj in range(2):
                        for s2 in range(2):
                            bank = jj * 2 + s2
                            k.op(PE, lambda e, kc=kc, s2=s2, bank=bank, jj=jj, wv=wv: e.matmul(
                                PS[bank][:], lhsT=wv[:, kc, s2, jj * 128:(jj + 1) * 128], rhs=xmT[:, kc, :], start=(kc == 0), stop=(kc == 7)),
                                reads=[wb, xmB[kc]], writes=[PSB[bank]])
            for jj in range(2):
                j = pp * 2 + jj
                gb_, ub_ = (0, 1) if j % 2 == 0 else (2, 3)
                for s2, bank in ((0, gb_), (1, ub_)):
                    if pp == 0:
                        break
                    for kc in range(8):
                        k.op(PE, lambda e, kc=kc, s2=s2, bank=bank, jj=jj, wv=wv: e.matmul(
                            PS[bank][:], lhsT=wv[:, kc, s2, jj * 128:(jj + 1) * 128], rhs=xmT[:, kc, :], start=(kc == 0), stop=(kc == 7)),
                            reads=[wb, xmB[kc]], writes=[PSB[bank]])
                si = j % 2
                k.op(ACT, lambda e, si=si, gb_=gb_: e.activation(out=sgl[si][:], in_=PS[gb_][:], func=AF.Silu), reads=[PSB[gb_]], writes=[sglB[si]])
                k.op(DVE, lambda e, si=si, ub_=ub_, j=j: e.tensor_tensor(out=ar(j), in0=PS[ub_][:], in1=sgl[si][:], op=ALU.mult),
                     reads=[PSB[ub_], sglB[si]], writes=[ARB[j]])
        for f in range(8):
            w, wb = wnext(("fout", l, f))
            wv = r3(w[:, 0:NH * 128], NH, 128)
            bank = 4 + f % 4
            for j in range(NH):
                k.op(PE, lambda e, j=j, wv=wv, bank=bank: e.matmul(PS[bank][:], lhsT=wv[:, j, :], rhs=ar(j), start=(j == 0), stop=(j == NH - 1)),
                     reads=[wb, ARB[j]], writes=[PSB[bank]])
            k.op(DVE, lambda e, f=f, bank=bank: e.scalar_tensor_tensor(out=hb[:, f, :], in0=PS[bank][:], scalar=M(l, GT2, f), in1=hb[:, f, :],
                                                                      op0=ALU.mult, op1=ALU.add),
                 reads=[PSB[bank], modB, hbB[f]], writes=[hbB[f]])

    def layer0(i):
        hb, hbB = hT[i % 2], hB[i % 2]
        if i == 0:
            for b_ in head_casts:
                b_()
        load_tile_T(x_d[i * T:(i + 1) * T, :], T, hb, hbB)
        norm_mod(hb, hbB, T, lambda kc: C("A1_0", kc), lambda kc: M(0, SH1, kc), 0)
        k.dma(SP, lambda e: e.dma_start(out=cs[:, 0, :], in_=cos_d[:, i * T:(i + 1) * T]), writes=[csB])
        k.dma(SP, lambda e: e.dma_start(out=cs[:, 1, :], in_=sin_d[:, i * T:(i + 1) * T]), writes=[csB])
        w, wq_b = wnext(("wq",))
        wq_v = r3(w, 8, 512)
        SA, SBK = 6, 7

        def qchunk_gen(c):
            for kc in range(8):
                k.op(PE, lambda e, kc=kc: e.matmul(PS[SA][:], lhsT=wq_v[:, kc, c * 128:(c + 1) * 128], rhs=xmT[:, kc, :], start=(kc == 0), stop=(kc == 7)),
                     reads=[wq_b, xmB[kc]], writes=[PSB[SA]])
                if kc == 3:
                    yield
            yield
            k.op(DVE, lambda e: e.tensor_copy(out=tmp[2][:], in_=PS[SA][:]), reads=[PSB[SA]], writes=[tmpB[2]])
            yield
            k.op(POOL, lambda e: e.tensor_tensor(out=sq[0][:], in0=tmp[2][:], in1=tmp[2][:], op=ALU.mult), reads=[tmpB[2]], writes=[sqB[0]])
            yield
            k.op(PE, lambda e: e.matmul(PS[SBK][:], lhsT=bones[:], rhs=sq[0][:], start=True, stop=True), reads=[sqB[0], onesB], writes=[PSB[SBK]])
            yield
            k.op(ACT, lambda e: e.activation(out=sd[:], in_=PS[SBK][:], func=AF.Ln, scale=1.0 / 64, bias=epsT[:]), reads=[PSB[SBK], epsB], writes=[sdB])
            yield
            k.op(ACT, lambda e: e.activation(out=rstd[:], in_=sd[:], func=AF.Exp, scale=-0.5), reads=[sdB], writes=[rstdB])
            yield
            k.op(DVE, lambda e: e.scalar_tensor_tensor(out=knT[:], in0=PS[SA][:], scalar=V("qg"), in1=rstd[:], op0=ALU.mult, op1=ALU.mult),
                 reads=[PSB[SA], vecB, rstdB], writes=[knB])
            yield
            k.op(PE, lambda e: e.matmul(PS[SBK][:], lhsT=pmb[:], rhs=knT[:], start=True, stop=True), reads=[knB, pmB], writes=[PSB[SBK]])
            k.op(POOL, lambda e: e.tensor_tensor(out=tmp[0][:], in0=knT[:], in1=cs[:, 0, :], op=ALU.mult), reads=[knB, csB], writes=[tmpB[0]])
            yield
            k.op(DVE, lambda e: e.tensor_tensor(out=tmp[1][:], in0=PS[SBK][:], in1=cs[:, 1, :], op=ALU.mult), reads=[PSB[SBK], csB], writes=[tmpB[1]])
            yield
            k.op(POOL, lambda e: e.tensor_tensor(out=ar(QT_ + c), in0=tmp[0][:], in1=tmp[1][:], op=ALU.add), reads=[tmpB[0], tmpB[1]], writes=[ARB[QT_ + c]])
            yield

        def su_gen():
            w2, wb2 = wnext(("wsu",))
            wvs = r3(w2, 8, 512)
            for c in range(4):
                bank = SA + c % 2
                for kc in range(8):
                    k.op(PE, lambda e, kc=kc, c=c, bank=bank: e.matmul(PS[bank][:], lhsT=wvs[:, kc, c * 128:(c + 1) * 128], rhs=xmT[:, kc, :], start=(kc == 0), stop=(kc == 7)),
                         reads=[wb2, xmB[kc]], writes=[PSB[bank]])
                    if kc == 3:
                        yield
                yield
                k.op(ACT, lambda e, c=c, bank=bank: e.activation(out=ar(GU_ + c), in_=PS[bank][:], func=AF.Gelu_apprx_tanh), reads=[PSB[bank]], writes=[ARB[GU_ + c]])
                yield

        def sv_gen():
            w3, wb3 = wnext(("wsv",))
            wv2 = r3(w3, 8, 512)
            for m in range(4):
                bank = SA + m % 2
                mb = SBK - m % 2
                for kc in range(8):
                    k.op(PE, lambda e, kc=kc, m=m, bank=bank: e.matmul(PS[bank][:], lhsT=xmT[:, kc, m * 128:(m + 1) * 128], rhs=wv2[:, kc, :], start=(kc == 0), stop=(kc == 7)),
                         reads=[wb3, xmB[kc]], writes=[PSB[bank]])
                    if kc == 3:
                        yield
                yield
                k.op(ACT, lambda e, bank=bank: e.activation(out=gv[:], in_=PS[bank][:], func=AF.Gelu_apprx_tanh), reads=[PSB[bank]], writes=[gvB])
                yield
                k.op(POOL, lambda e: e.tensor_tensor(out=gv2[:], in0=gv[:], in1=gv[:], op=ALU.mult), reads=[gvB], writes=[gv2B])
                k.op(DVE, lambda e: e.tensor_reduce(out=st[:, 4:8], in_=gv[:].rearrange("p (g c) -> p g c", g=4), axis=AX.X, op=ALU.add), reads=[gvB, stB], writes=[stB])
                yield
                k.op(DVE, lambda e: e.tensor_reduce(out=st[:, 8:12], in_=gv2[:].rearrange("p (g c) -> p g c", g=4), axis=AX.X, op=ALU.add), reads=[gv2B, stB], writes=[stB])
                yield
                k.op(DVE, lambda e: e.tensor_scalar(out=st[:, 12:16], in0=st[:, 4:8], scalar1=1.0 / 128, scalar2=None, op0=ALU.mult), reads=[stB], writes=[stB])
                yield
                k.op(DVE, lambda e: e.tensor_tensor(out=st[:, 16:20], in0=st[:, 12:16], in1=st[:, 12:16], op=ALU.mult), reads=[stB], writes=[stB])
                yield
                k.op(DVE, lambda e: e.scalar_tensor_tensor(out=st[:, 20:24], in0=st[:, 8:12], scalar=1.0 / 128, in1=st[:, 16:20], op0=ALU.mult, op1=ALU.subtract),
                     reads=[stB], writes=[stB])
                yield
                k.op(ACT, lambda e: e.activation(out=st[:, 24:28], in_=st[:, 20:24], func=AF.Ln, bias=epsT[:]), reads=[stB, epsB], writes=[stB])
                yield
                k.op(ACT, lambda e: e.activation(out=st[:, 28:32], in_=st[:, 24:28], func=AF.Exp, scale=-0.5), reads=[stB], writes=[stB])
                yield
                for g in range(4):
                    k.op(DVE, lambda e, g=g: e.tensor_scalar(out=vn[:, g, :], in0=gv[:, g * 128:(g + 1) * 128], scalar1=st[:, 12 + g:13 + g], scalar2=st[:, 28 + g:29 + g],
                                                             op0=ALU.subtract, op1=ALU.mult), reads=[gvB, stB, vnB], writes=[vnB])
                yield
                k.op(PE, lambda e, mb=mb: e.matmul(PS[mb][:], lhsT=onesrow[0:1, :], rhs=bsprow[0:1, :], start=True, stop=False), reads=[rowB], writes=[PSB[mb]])
                for g in range(4):
                    k.op(PE, lambda e, g=g, mb=mb: e.matmul(PS[mb][:, g * 128:(g + 1) * 128], lhsT=vn[:, g, :], rhs=wspT[:, g, :], start=False, stop=(g == 3)),
                         reads=[vnB, wspB], writes=[PSB[mb]])
                yield
                k.op(DVE, lambda e, m=m, mb=mb: e.tensor_tensor(out=AR[:, SG_:SG_ + 4, m * 128:(m + 1) * 128], in0=PS[mb][:].rearrange("p (g t) -> p g t", g=4),
                                                                in1=AR[:, GU_:GU_ + 4, m * 128:(m + 1) * 128], op=ALU.mult),
                     reads=[PSB[mb]] + ARB[GU_:GU_ + 4], writes=ARB[SG_:SG_ + 4])
                yield

        qdone = {0: False, 1: False, 2: False, 3: False}

        def side_gen():
            for c in (1, 2, 3):
                yield from qchunk_gen(c)
                qdone[c] = True

        for _ in qchunk_gen(0):
            pass
        qdone[0] = True
        side = side_gen()
        side_alive = [True]

        def pull():
            if side_alive[0]:
                try:
                    next(side)
                except StopIteration:
                    side_alive[0] = False

        def qk(c, j):
            for hh in range(2):
                bank = hh * 2 + j % 2
                ps_ = slice(hh * 64, (hh + 1) * 64)
                k.op(PE, lambda e, bank=bank, ps_=ps_, j=j, c=c: e.matmul(PS[bank][:], lhsT=KT[ps_, j * 128:(j + 1) * 128], rhs=AR[ps_, QT_ + c, :],
                                                                    start=True, stop=True),
                     reads=[KTB, ARB[QT_ + c]], writes=[PSB[bank]])

        its = [(c, j) for c in range(4) for j in range(NKT)]
        qk(0, 0)
        for n, (c, j) in enumerate(its):
            if n + 1 < len(its):
                c2, j2 = its[n + 1]
                while not qdone[c2]:
                    pull()
                qk(c2, j2)
            for hh in range(2):
                bank = hh * 2 + j % 2
                pt = PT_ + hh * 2 + j % 2
                k.op(ACT, lambda e, bank=bank, pt=pt: e.activation(out=ar(pt), in_=PS[bank][:], func=AF.Exp, scale=0.125, bias=SHIFT),
                     reads=[PSB[bank]], writes=[ARB[pt]])
            for hh in range(2):
                pt = PT_ + hh * 2 + j % 2
                k.op(PE, lambda e, hh=hh, pt=pt, j=j: e.matmul(PS[4 + hh][:], lhsT=VA[:, j, hh, :], rhs=ar(pt), start=(j == 0), stop=(j == NKT - 1)),
                     reads=[VAB, ARB[pt]], writes=[PSB[4 + hh]])
            if j == NKT - 1:
                for hh in range(2):
                    k.op(DVE, lambda e, hh=hh: e.tensor_copy(out=sgl[hh][:], in_=PS[4 + hh][:]), reads=[PSB[4 + hh]], writes=[sglB[hh]])
                for hh in range(2):
                    k.op(DVE, lambda e, hh=hh: e.reciprocal(out=ar32(10)[0:64, :], in_=sgl[hh][64:128, :]), reads=[sglB[hh]], writes=[ARB[20], ARB[21]])
                    k.op(DVE, lambda e, hh=hh, c=c: e.tensor_tensor(out=AR[hh * 64:(hh + 1) * 64, AT_ + c, :], in0=sgl[hh][0:64, :], in1=ar32(10)[0:64, :], op=ALU.mult),
                         reads=[sglB[hh], ARB[20], ARB[21]], writes=[ARB[AT_ + c]])
            if (n + 1) % DRIP == 0:
                pull()
        while side_alive[0]:
            pull()
        if i == 0:
            for b in late_casts:
                b()
        w, wb = wnext(("wsu",))
        wv = r3(w, 8, 512)
        for c in range(4):
            bank = c % 2
            for kc in range(8):
                k.op(PE, lambda e, kc=kc, c=c, bank=bank, wv=wv: e.matmul(PS[bank][:], lhsT=wv[:, kc, c * 128:(c + 1) * 128], rhs=xmT[:, kc, :], start=(kc == 0), stop=(kc == 7)),
                     reads=[wb, xmB[kc]], writes=[PSB[bank]])
            k.op(ACT, lambda e, c=c, bank=bank: e.activation(out=ar(GU_ + c), in_=PS[bank][:], func=AF.Gelu_apprx_tanh), reads=[PSB[bank]], writes=[ARB[GU_ + c]])
        w, wb = wnext(("wsv",))
        wv2 = r3(w, 8, 512)
        for m in range(4):
            bank = 2 + m % 2
            for kc in range(8):
                k.op(PE, lambda e, kc=kc, m=m, bank=bank, wv2=wv2: e.matmul(PS[bank][:], lhsT=xmT[:, kc, m * 128:(m + 1) * 128], rhs=wv2[:, kc, :], start=(kc == 0), stop=(kc == 7)),
                     reads=[wb, xmB[kc]], writes=[PSB[bank]])
            if m % 2 == 0:
                gv_, gv_B, gv2_, gv2_B, vn_, vn_B, st_, st_B = gv[:], [gvB], gv2[:], [gv2B], vn[:], [vnB], st, stB
            else:
                gv_, gv_B = ar32(0), [ARB[0], ARB[1]]
                gv2_, gv2_B = ar32(1), [ARB[2], ARB[3]]
                vn_, vn_B = AR[:, 16, :].rearrange("p (g c) -> p g c", g=4), [ARB[16]]
                st_, st_B = st2, st2B
            k.op(ACT, lambda e, bank=bank, gv_=gv_: e.activation(out=gv_, in_=PS[bank][:], func=AF.Gelu_apprx_tanh), reads=[PSB[bank]], writes=gv_B)
            k.op(POOL, lambda e, gv_=gv_, gv2_=gv2_: e.tensor_tensor(out=gv2_, in0=gv_, in1=gv_, op=ALU.mult), reads=gv_B, writes=gv2_B)
            k.op(DVE, lambda e, gv_=gv_, st_=st_: e.tensor_reduce(out=st_[:, 4:8], in_=gv_.rearrange("p (g c) -> p g c", g=4), axis=AX.X, op=ALU.add), reads=gv_B + [st_B], writes=[st_B])
            k.op(DVE, lambda e, gv2_=gv2_, st_=st_: e.tensor_reduce(out=st_[:, 8:12], in_=gv2_.rearrange("p (g c) -> p g c", g=4), axis=AX.X, op=ALU.add), reads=gv2_B + [st_B], writes=[st_B])
            k.op(DVE, lambda e, st_=st_: e.tensor_scalar(out=st_[:, 12:16], in0=st_[:, 4:8], scalar1=1.0 / 128, scalar2=None, op0=ALU.mult), reads=[st_B], writes=[st_B])
            k.op(DVE, lambda e, st_=st_: e.tensor_tensor(out=st_[:, 16:20], in0=st_[:, 12:16], in1=st_[:, 12:16], op=ALU.mult), reads=[st_B], writes=[st_B])
            k.op(DVE, lambda e, st_=st_: e.scalar_tensor_tensor(out=st_[:, 20:24], in0=st_[:, 8:12], scalar=1.0 / 128, in1=st_[:, 16:20], op0=ALU.mult, op1=ALU.subtract),
                 reads=[st_B], writes=[st_B])
            k.op(DVE, lambda e, st_=st_: e.tensor_scalar(out=st_[:, 24:28], in0=st_[:, 20:24], scalar1=EPS, scalar2=None, op0=ALU.add), reads=[st_B], writes=[st_B])
            k.op(POOL, lambda e, st_=st_: e.tensor_tensor(out=st_[:, 28:32], in0=st_[:, 24:28], in1=neghalf[:, 0:4], op=ALU.pow), reads=[st_B, epsB], writes=[st_B])
            for g in range(4):
                k.op(DVE, lambda e, g=g, gv_=gv_, vn_=vn_, st_=st_: e.tensor_scalar(out=vn_[:, g, :], in0=gv_[:, g * 128:(g + 1) * 128], scalar1=st_[:, 12 + g:13 + g], scalar2=st_[:, 28 + g:29 + g],
                                                         op0=ALU.subtract, op1=ALU.mult), reads=gv_B + [st_B] + vn_B, writes=vn_B)
            mb = 4 + m % 2
            k.op(PE, lambda e, mb=mb: e.matmul(PS[mb][:], lhsT=onesrow[0:1, :], rhs=bsprow[0:1, :], start=True, stop=False), reads=[rowB], writes=[PSB[mb]])
            for g in range(4):
                k.op(PE, lambda e, g=g, mb=mb, vn_=vn_: e.matmul(PS[mb][:, g * 128:(g + 1) * 128], lhsT=vn_[:, g, :], rhs=wspT[:, g, :], start=False, stop=(g == 3)),
                     reads=vn_B + [wspB], writes=[PSB[mb]])
            k.op(DVE, lambda e, m=m, mb=mb: e.tensor_tensor(out=AR[:, SG_:SG_ + 4, m * 128:(m + 1) * 128], in0=PS[mb][:].rearrange("p (g t) -> p g t", g=4),
                                                            in1=AR[:, GU_:GU_ + 4, m * 128:(m + 1) * 128], op=ALU.mult),
                 reads=[PSB[mb]] + ARB[GU_:GU_ + 4], writes=ARB[SG_:SG_ + 4])
        for pc in range(2):
            w, wb = wnext(("wo", pc))
            wv3 = r3(w, 8, 512)
            for ff in range(4):
                f = pc * 4 + ff
                bank = 6 + f % 2
                for kk in range(8):
                    src = AT_ + kk if kk < 4 else SG_ + kk - 4
                    k.op(PE, lambda e, kk=kk, ff=ff, src=src, bank=bank, wv3=wv3: e.matmul(PS[bank][:], lhsT=wv3[:, kk, ff * 128:(ff + 1) * 128], rhs=ar(src),
                                                                                         start=(kk == 0), stop=(kk == 7)),
                         reads=[wb, ARB[src]], writes=[PSB[bank]])
                k.op(DVE, lambda e, f=f, bank=bank: e.scalar_tensor_tensor(out=hb[:, f, :], in0=PS[bank][:], scalar=M(0, GT1, f), in1=hb[:, f, :],
                                                                          op0=ALU.mult, op1=ALU.add),
                     reads=[PSB[bank], modB, hbB[f]], writes=[hbB[f]])
        ffn(0, hb, hbB)

    def layer1a(i):
        hb, hbB = hT[i % 2], hB[i % 2]
        gw, gwB = GW[i % 2], GWB[i % 2]
        norm_mod(hb, hbB, T, lambda kc: C("A1_1", kc), lambda kc: M(1, SH1, kc), 0)
        for pp in range(4):
            w, wb = wnext(("pw1", pp))
            wv = w.rearrange("p (k s c) -> p k s c", k=8, s=2, c=256)
            if pp == 0:
                for kc in range(8):
                    for jj in range(2):
                        for s2 in range(2):
                            bank = jj * 2 + s2
                            k.op(PE, lambda e, kc=kc, s2=s2, bank=bank, jj=jj, wv=wv: e.matmul(
                                PS[bank][:], lhsT=wv[:, kc, s2, jj * 128:(jj + 1) * 128], rhs=xmT[:, kc, :], start=(kc == 0), stop=(kc == 7)),
                                reads=[wb, xmB[kc]], writes=[PSB[bank]])
            for jj in range(2):
                c = pp * 2 + jj
                ab_, gb_ = (0, 1) if c % 2 == 0 else (2, 3)
                for s2, bank in ((0, ab_), (1, gb_)):
                    if pp == 0:
                        break
                    for kc in range(8):
                        k.op(PE, lambda e, kc=kc, s2=s2, bank=bank, jj=jj, wv=wv: e.matmul(
                            PS[bank][:], lhsT=wv[:, kc, s2, jj * 128:(jj + 1) * 128], rhs=xmT[:, kc, :], start=(kc == 0), stop=(kc == 7)),
                            reads=[wb, xmB[kc]], writes=[PSB[bank]])
                si = c % 2
                k.op(ACT, lambda e, si=si, gb_=gb_, c=c: e.activation(out=sgl[si][:], in_=PS[gb_][:], func=AF.Sigmoid, bias=V("bpw1", 8 + c)),
                     reads=[PSB[gb_], vecB], writes=[sglB[si]])
                k.op(DVE, lambda e, si=si, ab_=ab_, c=c: e.scalar_tensor_tensor(out=gw[:, c, 15:15 + T], in0=PS[ab_][:], scalar=V("bpw1", c), in1=sgl[si][:],
                                                                               op0=ALU.add, op1=ALU.mult),
                     reads=[PSB[ab_], vecB, sglB[si]], writes=[gwB[c]])
        if i == 0:
            k.op(POOL, lambda e: e.memset(gw[:, :, 0:15], 0.0), reads=gwB, writes=gwB)
        else:
            pg, pgB = GW[(i - 1) % 2], GWB[(i - 1) % 2]
            k.op(POOL, lambda e: e.tensor_copy(out=gw[:, :, 0:15], in_=pg[:, :, T:T + 15]), reads=pgB + gwB, writes=gwB)
            k.op(POOL, lambda e: e.tensor_copy(out=pg[:, :, T + 15:T + 30], in_=gw[:, :, 15:30]), reads=gwB + pgB, writes=pgB)
        if i == NT - 1:
            k.op(POOL, lambda e: e.memset(gw[:, :, T + 15:T + 30], 0.0), reads=gwB, writes=gwB)

    def layer1b(i):
        hb, hbB = hT[i % 2], hB[i % 2]
        gw, gwB = GW[i % 2], GWB[i % 2]
        wd0 = VEC_ROWS["wdw"]
        for f in range(8):
            k.op(DVE, lambda e, f=f: e.tensor_scalar(out=hb[:, f, :], in0=hb[:, f, :], scalar1=C("gb", f), scalar2=None, op0=ALU.add),
                 reads=[hbB[f], coefB], writes=[hbB[f]])
        for kc in range(8):
            acc = ar32(kc)
            accB = [ARB[2 * kc], ARB[2 * kc + 1]]
            bank = 2 + kc % 4
            for j in range(31):
                r = dgstate["n"] % NDG
                dgstate["n"] += 1
                k.op(POOL if j % 3 == 2 else DVE, lambda e, r=r, j=j, kc=kc: e.tensor_scalar(out=dg[r][:], in0=identb[:], scalar1=vecT[:, wd0 + j * 8 + kc:wd0 + j * 8 + kc + 1], scalar2=1.0,
                                                                     op0=ALU.mult, op1=ALU.mult),
                     reads=[identB, vecB], writes=[dgB[r]])
                k.op(PE, lambda e, r=r, j=j, kc=kc, bank=bank: e.matmul(PS[bank][:], lhsT=dg[r][:], rhs=gw[:, kc, j:j + T], start=(j == 0), stop=(j == 30)),
                     reads=[dgB[r], gwB[kc]], writes=[PSB[bank]])
            k.op(DVE, lambda e, kc=kc, acc=acc, bank=bank: e.tensor_scalar(out=acc, in0=PS[bank][:], scalar1=V("bdw", kc), scalar2=None, op0=ALU.add),
                 reads=[PSB[bank], vecB], writes=accB)
            k.op(ACT, lambda e, acc=acc: e.activation(out=sq[0][:], in_=acc, func=AF.Copy), reads=accB, writes=[sqB[0]])
            k.op(PE, lambda e, kc=kc: e.matmul(PS[0][:], lhsT=ones[:], rhs=sq[0][:], start=(kc == 0), stop=(kc == 7)), reads=[sqB[0], onesB], writes=[PSB[0]])
            k.op(ACT, lambda e, acc=acc: e.activation(out=sq[1][:], in_=acc, func=AF.Square), reads=accB, writes=[sqB[1]])
            k.op(PE, lambda e, kc=kc: e.matmul(PS[1][:], lhsT=ones[:], rhs=sq[1][:], start=(kc == 0), stop=(kc == 7)), reads=[sqB[1], onesB], writes=[PSB[1]])
        mean, meanB = sgl[0], sglB[0]
        k.op(ACT, lambda e: e.activation(out=mean[:], in_=PS[0][:], func=AF.Copy, scale=1.0 / D), reads=[PSB[0]], writes=[meanB])
        k.op(POOL, lambda e: e.tensor_tensor(out=sgl[1][:], in0=mean[:], in1=mean[:], op=ALU.mult), reads=[meanB], writes=[sglB[1]])
        k.op(DVE, lambda e: e.scalar_tensor_tensor(out=sd[:], in0=PS[1][:], scalar=1.0 / D, in1=sgl[1][:], op0=ALU.mult, op1=ALU.subtract),
             reads=[PSB[1], sglB[1]], writes=[sdB])
        k.op(ACT, lambda e: e.activation(out=sd[:], in_=sd[:], func=AF.Ln, bias=epsT[:]), reads=[sdB, epsB], writes=[sdB])
        k.op(ACT, lambda e: e.activation(out=rstd[:], in_=sd[:], func=AF.Exp, scale=-0.5), reads=[sdB], writes=[rstdB])
        for kc in range(8):
            acc = ar32(kc)
            accB = [ARB[2 * kc], ARB[2 * kc + 1]]
            if kc % 3 == 2:
                ve_, ta, taB, tb, tbB = POOL, sgl[1], sglB[1], tmp[2], tmpB[2]
            else:
                ve_, ta, taB, tb, tbB = DVE, tmp[0], tmpB[0], tmp[1], tmpB[1]
            k.op(ve_, lambda e, acc=acc, ta=ta: e.tensor_tensor(out=ta[:], in0=acc, in1=mean[:], op=ALU.subtract), reads=accB + [meanB], writes=[taB])
            k.op(ve_, lambda e, ta=ta, tb=tb: e.tensor_tensor(out=tb[:], in0=ta[:], in1=rstd[:], op=ALU.mult), reads=[taB, rstdB], writes=[tbB])
            k.op(ACT, lambda e, kc=kc, tb=tb: e.activation(out=xmT[:, kc, :], in_=tb[:], func=AF.Silu, scale=V("lng", kc), bias=V("lnb", kc)),
                 reads=[tbB, vecB], writes=[xmB[kc]])
        for pc in range(2):
            w, wb = wnext(("pw2", pc))
            wv = r3(w, 8, 512)
            for ff in range(4):
                f = pc * 4 + ff
                bank = 6 + f % 2
                for kc in range(8):
                    k.op(PE, lambda e, kc=kc, ff=ff, bank=bank, wv=wv: e.matmul(PS[bank][:], lhsT=wv[:, kc, ff * 128:(ff + 1) * 128], rhs=xmT[:, kc, :],
                                                                              start=(kc == 0), stop=(kc == 7)),
                         reads=[wb, xmB[kc]], writes=[PSB[bank]])
                k.op(DVE, lambda e, f=f, bank=bank: e.scalar_tensor_tensor(out=hb[:, f, :], in0=PS[bank][:], scalar=M(1, GT1, f), in1=hb[:, f, :],
                                                                          op0=ALU.mult, op1=ALU.add),
                     reads=[PSB[bank], modB, hbB[f]], writes=[hbB[f]])
        ffn(1, hb, hbB)
        rms_stats(hb, hbB, T, 0)
        for kc in range(8):
            k.op(DVE, lambda e, kc=kc: e.scalar_tensor_tensor(out=hb[:, kc, :], in0=hb[:, kc, :], scalar=V("gfin", kc), in1=rstd[:], op0=ALU.mult, op1=ALU.mult),
                 reads=[hbB[kc], vecB, rstdB], writes=[hbB[kc]])
        for m in range(4):
            ib = iostate["n"] % 2
            iostate["n"] += 1
            for half in range(2):
                bank = 6 + half
                for kk in range(4):
                    kc = half * 4 + kk
                    k.op(PE, lambda e, kc=kc, kk=kk, bank=bank, m=m: e.transpose(out=PS[bank][:, kk * 128:(kk + 1) * 128], in_=hb[:, kc, m * 128:(m + 1) * 128],
                                                                               identity=ident[:]),
                         reads=[hbB[kc], identB], writes=[PSB[bank]])
                if half == 0:
                    k.op(ACT, lambda e, ib=ib, bank=bank: e.activation(out=io[ib][:, 0:512], in_=PS[bank][:], func=AF.Copy), reads=[PSB[bank]], writes=[ioB[ib]])
                else:
                    k.op(DVE, lambda e, ib=ib, bank=bank: e.tensor_copy(out=io[ib][:, 512:1024], in_=PS[bank][:]), reads=[PSB[bank], ioB[ib]], writes=[ioB[ib]])
            k.dma(SP, lambda e, ib=ib, m=m: e.dma_start(out=out_d[i * T + m * 128: i * T + (m + 1) * 128, :], in_=io[ib][:]), reads=[ioB[ib]])

    coef_A("A1_1", 1, SC1, "gmix")
    coef_A("A2_1", 1, SC2, "gffn")
    c0 = CO["gb"]
    k.op(DVE, lambda e: e.tensor_tensor(out=coef[:, c0:c0 + 8], in0=modsb[:, 48 + GT1 * 8:48 + GT1 * 8 + 8, 0],
                                        in1=vecT[:, VEC_ROWS["bpw2"]:VEC_ROWS["bpw2"] + 8], op=ALU.mult),
         reads=[modB, vecB, coefB], writes=[coefB])
    outB = Buf("out")
    for i in range(NT):
        layer0(i)
        layer1a(i)
        if i >= 1:
            layer1b(i - 1)
    layer1b(NT - 1)
    assert wstate["next"] == len(seq)

    print("sbuf bytes remaining/partition:", nc.sbuf_bytes_remaining)
    k.emit()
    k.finish(SP, ioB + [dumpB])
    return nc, k


def make_vecs(c_b, c_ctx, b_mod, g_mix, g_ffn, b_pw1, b_dw, ln_g, ln_b, b_pw2, g_final, w_dw, q_gain, k_gain):
    rows = [c_b.reshape(8, 128), c_ctx.reshape(8, 128), b_mod.reshape(96, 128), g_mix.reshape(16, 128), g_ffn.reshape(16, 128),
            b_pw1.reshape(16, 128), b_dw.reshape(8, 128), ln_g.reshape(8, 128), ln_b.reshape(8, 128), b_pw2.reshape(8, 128),
            g_final.reshape(8, 128), w_dw.reshape(248, 128),
            np.concatenate([q_gain.reshape(64), q_gain.reshape(64)])[None, :],
            np.concatenate([k_gain.reshape(64), k_gain.reshape(64)])[None, :]]
    v = np.concatenate(rows, 0).astype(np.float32)
    out = np.zeros((NVEC, 128), np.float32)
    out[:v.shape[0]] = v
    return out


def make_in_maps(x, c, ctx, c_ctx, w_mod, b_mod, g_mix, g_ffn, w_ffn_in, w_ffn_out, w_in, q_gain, k_gain, w_sp, b_sp, w_out,
                 w_pw1, b_pw1, w_dw, b_dw, ln_g, ln_b, w_pw2, b_pw2, g_final):
    f = lambda a: np.ascontiguousarray(np.asarray(a, dtype=np.float32))
    B, S, _ = x.shape
    cos, sin = rope_tables(S)
    shared = {
        "w_mod": f(w_mod), "w_ffn_in": f(w_ffn_in), "w_ffn_out": f(w_ffn_out), "w_in": f(w_in[0]),
        "w_sp": f(w_sp[0]), "b_sp": f(b_sp[0]).reshape(1, 512), "w_out": f(w_out[0]), "w_pw1": f(w_pw1[0]), "w_pw2": f(w_pw2[0]),
        "ident": np.eye(128, dtype=np.float32), "pm": rope_perm(), "cos": cos, "sin": sin,
    }
    maps = []
    for b in range(B):
        m = dict(shared)
        m["x"] = f(x[b])
        m["ctx"] = f(ctx[b])
        m["vecs"] = make_vecs(f(c[b]), f(c_ctx), f(b_mod), f(g_mix), f(g_ffn), f(b_pw1[0]), f(b_dw[0]), f(ln_g[0]), f(ln_b[0]),
                              f(b_pw2[0]), f(g_final), f(w_dw[0]), f(q_gain[0]), f(k_gain[0]))
        maps.append(m)
    return maps


_NC_CACHE = {}


def kernel(**inputs):
    x = np.asarray(inputs["x"])
    B, S, _ = x.shape
    maps = make_in_maps(**inputs)
    if S not in _NC_CACHE:
        _NC_CACHE[S] = build(S)[0]
    nc = _NC_CACHE[S]
    res = run_bass_kernel_spmd(nc, maps, core_ids=list(range(B)))
    return np.stack([np.asarray(r["out"], dtype=np.float32) for r in res.results], 0)
```

```python
import numpy as np
import concourse.bass as bass
import concourse.mybir as mybir
from concourse.bass_utils import run_bass_kernel_spmd

F32 = mybir.dt.float32
BF16 = mybir.dt.bfloat16
AF = mybir.ActivationFunctionType
ALU = mybir.AluOpType
AX = mybir.AxisListType

PE, ACT, DVE, POOL, SP = "pe", "act", "dve", "pool", "sp"

D = 1024
CTX = 256
DFF = 2816
NH = 22
T = 512
EPS = 1e-6
NSLOT = 6
DRIP = 2
SHIFT = -8.0
SLOT_ELEMS = 4096


class Buf:
    __slots__ = ("name", "lw", "rd", "dsem", "dcount")

    def __init__(self, name):
        self.name = name
        self.lw = None
        self.rd = {}
        self.dsem = None
        self.dcount = 0


class Op:
    __slots__ = ("eng", "fn", "deps", "is_dma", "buf", "sig", "sigval", "dval")

    def __init__(self, eng, fn, is_dma=False):
        self.eng = eng
        self.fn = fn
        self.deps = []
        self.is_dma = is_dma
        self.buf = None
        self.sig = False
        self.sigval = 0
        self.dval = 0


class K:
    def __init__(self, nc):
        self.nc = nc
        self.ops = []
        self.h = {PE: nc.tensor, ACT: nc.scalar, DVE: nc.vector, POOL: nc.gpsimd, SP: nc.sync}
        self.dsems = {}

    def op(self, eng, fn, reads=(), writes=()):
        o = Op(eng, fn)
        self._deps(o, reads, writes)
        self.ops.append(o)
        return o

    def dma(self, eng, fn, reads=(), writes=(), track=None, disjoint=False):
        o = Op(eng, fn, is_dma=True)
        if track is None:
            track = writes[0] if writes else reads[0]
        key = (track.name, eng)
        if key not in self.dsems:
            self.dsems[key] = [None, 0]
        o.buf = key
        self.dsems[key][1] += 16
        o.dval = self.dsems[key][1]
        self._deps(o, reads, writes, disjoint)
        self.ops.append(o)
        return o

    def _deps(self, o, reads, writes, disjoint=False):
        for b in reads:
            if b.lw is not None:
                o.deps.append(b.lw)
            b.rd[id(o) if o.is_dma else o.eng] = o
        for b in writes:
            if b.lw is not None and not (disjoint and b.lw.is_dma):
                o.deps.append(b.lw)
            for r in b.rd.values():
                if r is not o:
                    o.deps.append(r)
            b.lw = o
            b.rd = {}

    def emit(self):
        nc = self.nc
        for o in self.ops:
            nd = []
            for d in o.deps:
                if d.is_dma:
                    nd.append(d)
                    continue
                if d.eng == PE and o.eng == PE and not o.is_dma:
                    continue
                d.sig = True
                nd.append(d)
            o.deps = nd
        sems = {e: nc.alloc_semaphore(name=f"s_{e}") for e in self.h}
        for key, v in self.dsems.items():
            v[0] = nc.alloc_semaphore(name=f"d_{key[0]}_{key[1]}")
        cnt = {e: 0 for e in self.h}
        for o in self.ops:
            if o.sig and not o.is_dma:
                cnt[o.eng] += 1
                o.sigval = cnt[o.eng]
        waited = {e: {} for e in self.h}
        self.nwait = 0
        for o in self.ops:
            need = {}
            for d in o.deps:
                if d.is_dma:
                    key, val = self.dsems[d.buf][0], d.dval
                else:
                    key, val = sems[d.eng], d.sigval
                kid = id(key)
                if kid not in need or need[kid][1] < val:
                    need[kid] = (key, val)
            w = waited[o.eng]
            for kid, (key, val) in need.items():
                if w.get(kid, 0) >= val:
                    continue
                self.h[o.eng].wait_ge(key, val)
                w[kid] = val
                self.nwait += 1
            ins = o.fn(self.h[o.eng])
            if o.is_dma:
                ins.then_inc(self.dsems[o.buf][0], 16)
            elif o.sig:
                ins.then_inc(sems[o.eng], 1)

    def finish(self, eng, bufs):
        names = {b.name for b in bufs}
        for key, v in self.dsems.items():
            if key[0] in names:
                self.h[eng].wait_ge(v[0], v[1])


VEC_ROWS = {}
_r = 0
for _n, _c in [("c", 8), ("cctx", 8), ("bmod", 96), ("gmix", 16), ("gffn", 16), ("bpw1", 16), ("bdw", 8),
               ("lng", 8), ("lnb", 8), ("bpw2", 8), ("gfin", 8), ("wdw", 248), ("qg", 1), ("kg", 1)]:
    VEC_ROWS[_n] = _r
    _r += _c
NVEC = 512


def rope_tables(seq):
    t = np.arange(seq)
    row = (t // 64).astype(np.float32)
    col = (t % 64).astype(np.float32)
    inv = (10000.0 ** (-np.arange(0, 32, 2, dtype=np.float32) / 32.0)).astype(np.float32)
    ang_r = (row[:, None] * inv[None, :]).astype(np.float32)
    ang_c = (col[:, None] * inv[None, :]).astype(np.float32)
    cr, sr, cc, sc = np.cos(ang_r), np.sin(ang_r), np.cos(ang_c), np.sin(ang_c)
    cos64 = np.concatenate([cr, cr, cc, cc], axis=1).T
    sin64 = np.concatenate([-sr, sr, -sc, sc], axis=1).T
    cos = np.concatenate([cos64, cos64], 0).astype(np.float32)
    sin = np.concatenate([sin64, sin64], 0).astype(np.float32)
    return np.ascontiguousarray(cos), np.ascontiguousarray(sin)


def rope_perm():
    pm = np.zeros((128, 128), np.float32)
    for hh in range(2):
        for d in range(64):
            blk = d // 16
            src = d + 16 if blk in (0, 2) else d - 16
            pm[hh * 64 + src, hh * 64 + d] = 1.0
    return pm


def piece_sequence(nt):
    seq = [("mod", 0, pc) for pc in range(12)]
    seq += [("mod", 1, pc) for pc in range(12)]
    l0 = [("wq",), ("wsu",), ("wsv",), ("wo", 0), ("wo", 1)] + [("fin", 0, p) for p in range(11)] + \
         [("fout", 0, c) for c in range(8)]
    l1a = [("pw1", p) for p in range(4)]
    l1b = [("pw2", 0), ("pw2", 1)] + [("fin", 1, p) for p in range(11)] + [("fout", 1, c) for c in range(8)]
    for i in range(nt):
        seq += l0 + l1a
        if i >= 1:
            seq += l1b
    seq += l1b
    return seq


def build(SEQ, dbg=None):
    NT = SEQ // T
    NKT = (CTX + SEQ) // 128
    nc = bass.Bass("TRN2", target_bir_lowering=False)
    k = K(nc)

    def din(name, shape, dt=F32):
        return nc.dram_tensor(name, list(shape), dt, kind="ExternalInput").ap()

    x_d = din("x", [SEQ, D])
    ctx_d = din("ctx", [CTX, D])
    vecs_d = din("vecs", [NVEC, 128])
    w_mod_d = din("w_mod", [2, D, 6 * D])
    w_ffn_in_d = din("w_ffn_in", [2, D, 2 * DFF])
    w_ffn_out_d = din("w_ffn_out", [2, DFF, D])
    w_in_d = din("w_in", [D, 1792])
    w_sp_d = din("w_sp", [4, 128, 128])
    b_sp_d = din("b_sp", [1, 512])
    w_out_d = din("w_out", [D, D])
    w_pw1_d = din("w_pw1", [D, 2 * D])
    w_pw2_d = din("w_pw2", [D, D])
    ident_d = din("ident", [128, 128])
    pm_d = din("pm", [128, 128])
    cos_d = din("cos", [128, SEQ])
    sin_d = din("sin", [128, SEQ])
    out_d = nc.dram_tensor("out", [SEQ, D], F32, kind="ExternalOutput").ap()
    dbg_d = {}
    if dbg:
        for n, shp in dbg.items():
            dbg_d[n] = nc.dram_tensor("dbg_" + n, list(shp), F32, kind="ExternalOutput").ap()

    def sb(name, shape, dt=F32):
        return nc.alloc_sbuf_tensor("s_" + name, list(shape), dt)

    ring = sb("ring", [128, NSLOT, SLOT_ELEMS], BF16)
    ringB = [Buf(f"ring{i}") for i in range(NSLOT)]
    KT = sb("KT", [128, CTX + SEQ], BF16)
    KTB = Buf("KT")
    VA = sb("VA", [128, NKT, 2, 128], BF16)
    VAB = Buf("VA")
    hT = [sb(f"hT{i}", [128, 8, T]) for i in range(2)]
    hB = [[Buf(f"h{i}_{c}") for c in range(8)] for i in range(2)]
    io = [sb(f"io{i}", [128, D]) for i in range(2)]
    ioB = [Buf(f"io{i}") for i in range(2)]
    xmT = sb("xmT", [128, 8, T], BF16)
    xmB = [Buf(f"xm{c}") for c in range(8)]
    sq = [sb(f"sq{i}", [128, T], BF16) for i in range(2)]
    sqB = [Buf(f"sq{i}") for i in range(2)]
    tmp = [sb(f"tmp{i}", [128, T]) for i in range(3)]
    tmpB = [Buf(f"tmp{i}") for i in range(3)]
    rstd = sb("rstd", [128, T])
    rstdB = Buf("rstd")
    sd = sb("sd", [128, T])
    sdB = Buf("sd")
    AR = sb("AR", [128, NH, T], BF16)
    ARB = [Buf(f"ar{j}") for j in range(NH)]
    AR32 = AR.bitcast(F32) if hasattr(AR, "bitcast") else None
    GW = [sb(f"GW{i}", [128, 8, T + 30], BF16) for i in range(2)]
    NDG = 16
    dg = [sb(f"dg{i}", [128, 128], BF16) for i in range(NDG)]
    dgB = [Buf(f"dg{i}") for i in range(NDG)]
    identb = sb("identb", [128, 128], BF16)
    dgstate = {"n": 0}
    GWB = [[Buf(f"gw{i}_{c}") for c in range(8)] for i in range(2)]
    sgl = [sb(f"sgl{i}", [128, T]) for i in range(2)]
    sglB = [Buf(f"sgl{i}") for i in range(2)]
    cs = sb("cs", [128, 2, T])
    csB = Buf("cs")
    knT = sb("knT", [128, T], BF16)
    knB = Buf("knT")
    gv = sb("gv", [128, T])
    gvB = Buf("gv")
    gv2 = sb("gv2", [128, T])
    gv2B = Buf("gv2")
    vn = sb("vn", [128, 4, 128], BF16)
    vnB = Buf("vn")
    st = sb("st", [128, 32])
    stB = Buf("st")
    st2 = sb("st2", [128, 32])
    st2B = Buf("st2")
    vstage = sb("vstage", [128, 4, 128])
    vstageB = Buf("vstage")
    vecT = sb("vecT", [128, NVEC])
    vecB = Buf("vecT")
    coef = sb("coef", [128, 96])
    coefB = Buf("coef")
    modsb = sb("modsb", [128, 96, 2])
    modB = Buf("modsb")
    scb = sb("scb", [128, 8, 2], BF16)
    scbB = Buf("scb")
    ident = sb("ident", [128, 128])
    identB = Buf("ident")
    pmf = sb("pmf", [128, 128])
    pmb = sb("pmb", [128, 128], BF16)
    pmB = Buf("pm")
    ones = sb("ones", [128, 128], BF16)
    bones = sb("bones", [128, 128], BF16)
    onesB = Buf("ones")
    onesrow = sb("onesrow", [1, 128])

    bsprow = sb("bsprow", [1, 512])
    rowB = Buf("rows")
    wspT = sb("wspT", [128, 4, 128], BF16)
    wspB = Buf("wspT")
    nshift = sb("nshift", [128, 1])
    nshiftB = Buf("nshift")
    gtmp = sb("gtmp", [1, 4])

    PS = [nc.alloc_psum_tensor(f"ps{i}", [128, T], F32) for i in range(8)]
    PSB = [Buf(f"ps{i}") for i in range(8)]

    def ar(j):
        return AR[:, j, :]

    def ar32(j2):
        return AR[:, 2 * j2:2 * j2 + 2, :].bitcast(F32).rearrange("p a t -> p (a t)")

    QT_, AT_, SG_, GU_, PT_ = 0, 4, 8, 12, 16

    def V(name, i=0):
        r = VEC_ROWS[name] + i
        return vecT[:, r:r + 1]

    CO = {}
    _cc = [0]

    def cocol(name, n=8):
        CO[name] = _cc[0]
        _cc[0] += n

    for nm in ["A1_0", "A1c", "A2_0", "A1_1", "A2_1", "gb"]:
        cocol(nm)

    def C(name, i):
        c = CO[name] + i
        return coef[:, c:c + 1]

    def M(l, which, i, col=0):
        idx = l * 48 + which * 8 + i
        return modsb[:, idx, col:col + 1]

    SH1, SC1, GT1, SH2, SC2, GT2 = range(6)

    scr = {}
    scrB = {}

    def mkscr(key, group):
        scr[key] = nc.dram_tensor("scr_" + "_".join(str(s) for s in key), [128, SLOT_ELEMS], BF16).ap()
        if group not in scrB:
            scrB[group] = Buf("scr_" + group)
        return scr[key], scrB[group]

    def cast(dst, src, gb):
        k.dma(POOL, lambda e: e.dma_start(out=dst, in_=src), writes=[gb], disjoint=True)

    def r3(ap2, a, b):
        return ap2.rearrange("p (a b) -> p a b", a=a, b=b)

    def cast_wkv():
        s, gb = mkscr(("wkv",), "wkv")
        cast(r3(s[:, 0:8 * 256], 8, 256), w_in_d[:, 512:768].rearrange("(k p) c -> p k c", p=128), gb)

    def cast_all_early():
        s, gb = mkscr(("wq",), "wq")
        sv = r3(s, 8, 512)
        for g in range(4):
            for kv in range(2):
                h = kv * 4 + g
                cast(sv[:, :, g * 128 + kv * 64: g * 128 + kv * 64 + 64],
                     w_in_d[:, h * 64:(h + 1) * 64].rearrange("(k p) c -> p k c", p=128), gb)
        s, gb = mkscr(("wsu",), "wsu")
        cast(r3(s, 8, 512), w_in_d[:, 768:1280].rearrange("(k p) c -> p k c", p=128), gb)
        s, gb = mkscr(("wsv",), "wsv")
        cast(r3(s, 8, 512), w_in_d[:, 1280:1792].rearrange("(k p) c -> p k c", p=128), gb)
        for pc in range(2):
            s, gb = mkscr(("wo", pc), "wo")
            sv = r3(s, 8, 512)
            cols = slice(pc * 512, (pc + 1) * 512)
            cast(sv[0:64, 0:4, :], w_out_d[0:256, cols].rearrange("(c p) f -> p c f", p=64), gb)
            cast(sv[64:128, 0:4, :], w_out_d[256:512, cols].rearrange("(c p) f -> p c f", p=64), gb)
            cast(sv[:, 4:8, :], w_out_d[512:1024, cols].rearrange("(k p) f -> p k f", p=128), gb)

    def cast_fin(l, pps):
        for pp in pps:
            s, gb = mkscr(("fin", l, pp), f"fin{l}")
            sv = s.rearrange("p (k s c) -> p k s c", k=8, s=2, c=256)
            wv = w_ffn_in_d[l].rearrange("(k p) (s c) -> p k s c", p=128, s=2)
            for s2 in range(2):
                cast(sv[:, :, s2, :], wv[:, :, s2, pp * 256:(pp + 1) * 256], gb)
    def cast_fout(l):
        for c in range(8):
            s, gb = mkscr(("fout", l, c), f"fout{l}")
            cast(r3(s[:, 0:NH * 128], NH, 128),
                 w_ffn_out_d[l][:, c * 128:(c + 1) * 128].rearrange("(k p) f -> p k f", p=128), gb)

    def cast_pw():
        for pp in range(4):
            s, gb = mkscr(("pw1", pp), "pw1")
            sv = s.rearrange("p (k s c) -> p k s c", k=8, s=2, c=256)
            wv = w_pw1_d.rearrange("(k p) (s c) -> p k s c", p=128, s=2)
            for s2 in range(2):
                cast(sv[:, :, s2, :], wv[:, :, s2, pp * 256:(pp + 1) * 256], gb)
        for pc in range(2):
            s, gb = mkscr(("pw2", pc), "pw2")
            cast(r3(s, 8, 512), w_pw2_d[:, pc * 512:(pc + 1) * 512].rearrange("(k p) c -> p k c", p=128), gb)

    seq = piece_sequence(NT)
    wstate = {"issued": 0, "next": 0}

    def group_of(key):
        if key[0] in ("fin", "fout"):
            return f"{key[0]}{key[1]}"
        return key[0]

    def issue_load(n):
        key = seq[n]
        slot = n % NSLOT
        if key[0] == "mod":
            _, l, pc = key
            dst = r3(ring[:, slot, :], 8, 512)
            src = w_mod_d[l][:, pc * 512:(pc + 1) * 512].rearrange("(k p) c -> p k c", p=128)
            k.dma(POOL, lambda e: e.dma_start(out=dst, in_=src), writes=[ringB[slot]])
        else:
            s = scr[key]
            n_el = SLOT_ELEMS
            if key[0] == "wkv":
                n_el = 8 * 256
            elif key[0] == "fout":
                n_el = NH * 128
            k.dma(SP, lambda e: e.dma_start(out=ring[:, slot, 0:n_el], in_=s[:, 0:n_el]),
                  reads=[scrB[group_of(key)]], writes=[ringB[slot]])

    def wnext(key):
        n = wstate["next"]
        assert seq[n] == key, (seq[n], key)
        while wstate["issued"] < min(len(seq), n + NSLOT):
            issue_load(wstate["issued"])
            wstate["issued"] += 1
        wstate["next"] += 1
        slot = n % NSLOT
        return ring[:, slot, :], ringB[slot]

    k.dma(SP, lambda e: e.dma_start(out=vstage[:], in_=vecs_d.rearrange("(g r) c -> r g c", r=128)), writes=[vstageB])
    k.dma(SP, lambda e: e.dma_start(out=ident[:], in_=ident_d[:]), writes=[identB])
    k.dma(SP, lambda e: e.dma_start(out=pmf[:], in_=pm_d[:]), writes=[pmB])
    k.dma(SP, lambda e: e.dma_start(out=bsprow[:], in_=b_sp_d[:]), writes=[rowB])

    cast_wkv()
    epsT = sb("epsT", [128, 1])
    epsB = Buf("eps")
    k.op(DVE, lambda e: e.memset(epsT[:], EPS), writes=[epsB])
    neghalf = sb("neghalf", [128, 4])
    k.op(DVE, lambda e: e.memset(neghalf[:], -0.5), reads=[epsB], writes=[epsB])

    k.op(POOL, lambda e: e.memset(ones[:], 1.0), writes=[onesB])
    k.op(POOL, lambda e: e.memset(bones[:], 0.0), writes=[onesB])
    k.op(POOL, lambda e: e.memset(bones[0:64, 0:64], 1.0), writes=[onesB])
    k.op(POOL, lambda e: e.memset(bones[64:128, 64:128], 1.0), writes=[onesB])
    k.op(POOL, lambda e: e.memset(onesrow[:], 1.0), writes=[rowB])

    k.op(POOL, lambda e: e.memset(VA[:, :, :, 64:128], 1.0), writes=[VAB])
    k.op(DVE, lambda e: e.tensor_copy(out=pmb[:], in_=pmf[:]), reads=[pmB], writes=[pmB])
    k.op(DVE, lambda e: e.tensor_copy(out=identb[:], in_=ident[:]), reads=[identB], writes=[identB])

    for g in range(4):
        k.op(PE, lambda e, g=g: e.transpose(out=PS[0][:, g * 128:(g + 1) * 128], in_=vstage[:, g, :], identity=ident[:]),
             reads=[vstageB, identB], writes=[PSB[0]])
    k.op(DVE, lambda e: e.tensor_copy(out=vecT[:], in_=PS[0][:]), reads=[PSB[0]], writes=[vecB])

    k.dma(SP, lambda e: e.dma_start(out=vstage[:], in_=w_sp_d.rearrange("g p q -> p g q")), reads=[vstageB], writes=[vstageB])
    for g in range(4):
        k.op(PE, lambda e, g=g: e.transpose(out=PS[1][:, g * 128:(g + 1) * 128], in_=vstage[:, g, :], identity=ident[:]),
             reads=[vstageB, identB], writes=[PSB[1]])
    k.op(DVE, lambda e: e.tensor_copy(out=wspT[:].rearrange("p g q -> p (g q)"), in_=PS[1][:]), reads=[PSB[1]], writes=[wspB])

    k.op(ACT, lambda e: e.activation(out=st[:, 0:2], in_=vecT[:, VEC_ROWS["qg"]:VEC_ROWS["qg"] + 2], func=AF.Abs),
         reads=[vecB], writes=[stB])
    k.op(PE, lambda e: e.transpose(out=PS[2][0:2, 0:128], in_=st[:, 0:2], identity=ident[:]), reads=[stB, identB], writes=[PSB[2]])
    k.op(DVE, lambda e: e.tensor_reduce(out=st[0:2, 2:3], in_=PS[2][0:2, 0:128], axis=AX.X, op=ALU.max), reads=[PSB[2], stB], writes=[stB])
    k.op(PE, lambda e: e.transpose(out=PS[2][0:1, 128:130], in_=st[0:2, 2:3], identity=ident[0:2, 0:2]), reads=[stB, identB], writes=[PSB[2]])
    k.op(DVE, lambda e: e.tensor_copy(out=gtmp[0:1, 0:2], in_=PS[2][0:1, 128:130]), reads=[PSB[2]], writes=[stB])
    k.op(DVE, lambda e: e.tensor_tensor(out=gtmp[0:1, 2:3], in0=gtmp[0:1, 0:1], in1=gtmp[0:1, 1:2], op=ALU.mult), reads=[stB], writes=[stB])
    k.op(PE, lambda e: e.matmul(PS[2][:, 256:257], lhsT=onesrow[0:1, :], rhs=gtmp[0:1, 2:3], start=True, stop=True), reads=[stB, rowB], writes=[PSB[2]])
    k.op(ACT, lambda e: e.activation(out=nshift[:], in_=PS[2][:, 256:257], func=AF.Copy, scale=-8.0), reads=[PSB[2]], writes=[nshiftB])

    k.op(ACT, lambda e: e.activation(out=scb[:, :, 0], in_=vecT[:, VEC_ROWS["c"]:VEC_ROWS["c"] + 8], func=AF.Silu), reads=[vecB], writes=[scbB])
    k.op(ACT, lambda e: e.activation(out=scb[:, :, 1], in_=vecT[:, VEC_ROWS["cctx"]:VEC_ROWS["cctx"] + 8], func=AF.Silu), reads=[vecB, scbB], writes=[scbB])

    def mod_pieces(l, pcs, mbank):
        for pc in pcs:
            w, wb = wnext(("mod", l, pc))
            wv = r3(w, 8, 512)
            for fc in range(4):
                idx = l * 48 + pc * 4 + fc
                for kk in range(8):
                    k.op(PE, lambda e, wv=wv, fc=fc, kk=kk, idx=idx: e.matmul(
                        PS[mbank][:, idx * 2:idx * 2 + 2], lhsT=wv[:, kk, fc * 128:(fc + 1) * 128], rhs=scb[:, kk, :],
                        start=(kk == 0), stop=(kk == 7)), reads=[wb, scbB], writes=[PSB[mbank]])

    def mod_finish(l, mbank, i0=0, i1=48):
        for col in range(2):
            k.op(DVE, lambda e, col=col: e.tensor_tensor(
                out=modsb[:, l * 48 + i0:l * 48 + i1, col],
                in0=PS[mbank][:, (l * 48 + i0) * 2:(l * 48 + i1) * 2].rearrange("p (i c) -> p i c", c=2)[:, :, col],
                in1=vecT[:, VEC_ROWS["bmod"] + l * 48 + i0:VEC_ROWS["bmod"] + l * 48 + i1], op=ALU.add), reads=[PSB[mbank], vecB, modB], writes=[modB])

    mod_pieces(0, range(4), 3)
    mod_finish(0, 3, 0, 16)
    cast_all_early()
    cast_batches = [lambda: cast_fin(0, range(0, 6)), lambda: cast_fin(0, range(6, 11)), lambda: cast_fout(0), cast_pw,
                    lambda: cast_fin(1, range(0, 6)), lambda: cast_fin(1, range(6, 11)), lambda: cast_fout(1)]

    def coef_A(name, l, which, gname, col=0):
        c0 = CO[name]
        k.op(DVE, lambda e: e.scalar_tensor_tensor(
            out=coef[:, c0:c0 + 8], in0=modsb[:, l * 48 + which * 8: l * 48 + which * 8 + 8, col], scalar=1.0,
            in1=vecT[:, VEC_ROWS[gname] + l * 8: VEC_ROWS[gname] + l * 8 + 8], op0=ALU.add, op1=ALU.mult),
            reads=[modB, vecB, coefB], writes=[coefB])

    coef_A("A1_0", 0, SC1, "gmix")
    coef_A("A1c", 0, SC1, "gmix", col=1)

    iostate = {"n": 0}

    def load_tile_T(src_rows, ntok, hbuf, hbufB):
        for m in range(ntok // 128):
            ib = iostate["n"] % 2
            iostate["n"] += 1
            k.dma(SP, lambda e, ib=ib, m=m: e.dma_start(out=io[ib][:], in_=src_rows[m * 128:(m + 1) * 128, :]), writes=[ioB[ib]])
            for half in range(2):
                bank = 6 + half
                for kk in range(4):
                    kc = half * 4 + kk
                    k.op(PE, lambda e, ib=ib, kc=kc, kk=kk, bank=bank: e.transpose(
                        out=PS[bank][:, kk * 128:(kk + 1) * 128], in_=io[ib][:, kc * 128:(kc + 1) * 128], identity=ident[:]),
                        reads=[ioB[ib], identB], writes=[PSB[bank]])
                eng = ACT if half == 0 else DVE
                if eng == ACT:
                    k.op(ACT, lambda e, half=half, m=m, bank=bank: e.activation(
                        out=hbuf[:, half * 4:half * 4 + 4, m * 128:(m + 1) * 128],
                        in_=PS[bank][:].rearrange("p (a t) -> p a t", a=4), func=AF.Copy),
                        reads=[PSB[bank]], writes=hbufB[half * 4:half * 4 + 4])
                else:
                    k.op(DVE, lambda e, half=half, m=m, bank=bank: e.tensor_copy(
                        out=hbuf[:, half * 4:half * 4 + 4, m * 128:(m + 1) * 128],
                        in_=PS[bank][:].rearrange("p (a t) -> p a t", a=4)),
                        reads=[PSB[bank]], writes=hbufB[half * 4:half * 4 + 4])

    def rms_stats(hbuf, hbufB, ntok, bank, sd_=None, sd_B=None, rs_=None, rs_B=None):
        if sd_ is None:
            sd_, sd_B, rs_, rs_B = sd[:], [sdB], rstd[:], [rstdB]
        for kc in range(8):
            s = kc % 2
            k.op(ACT, lambda e, kc=kc, s=s: e.activation(out=sq[s][:, 0:ntok], in_=hbuf[:, kc, 0:ntok], func=AF.Square),
                 reads=[hbufB[kc]], writes=[sqB[s]])
            k.op(PE, lambda e, kc=kc, s=s: e.matmul(PS[bank][:, 0:ntok], lhsT=ones[:], rhs=sq[s][:, 0:ntok],
                                                    start=(kc == 0), stop=(kc == 7)),
                 reads=[sqB[s], onesB], writes=[PSB[bank]])
        k.op(ACT, lambda e: e.activation(out=sd_[:, 0:ntok], in_=PS[bank][:, 0:ntok], func=AF.Ln, scale=1.0 / D, bias=epsT[:]),
             reads=[PSB[bank], epsB], writes=sd_B)
        k.op(ACT, lambda e: e.activation(out=rs_[:, 0:ntok], in_=sd_[:, 0:ntok], func=AF.Exp, scale=-0.5), reads=sd_B, writes=rs_B)

    tmpstate = {"n": 0}

    def norm_mod(hbuf, hbufB, ntok, Acol, Bcol, bank, ve=POOL, xm_=None, xm_B=None, sd_=None, sd_B=None, rs_=None, rs_B=None):
        if xm_ is None:
            xm_, xm_B = xmT, xmB
        if sd_ is None:
            sd_, sd_B, rs_, rs_B = sd[:], [sdB], rstd[:], [rstdB]
        rms_stats(hbuf, hbufB, ntok, bank, sd_, sd_B, rs_, rs_B)
        for kc in range(8):
            ti = tmpstate["n"] % 3
            tmpstate["n"] += 1
            k.op(DVE if (ve == DVE or kc % 3 != 2) else POOL, lambda e, kc=kc, ti=ti: e.tensor_tensor(out=tmp[ti][:, 0:ntok], in0=hbuf[:, kc, 0:ntok], in1=rs_[:, 0:ntok], op=ALU.mult),
                 reads=[hbufB[kc]] + rs_B, writes=[tmpB[ti]])
            k.op(ACT, lambda e, kc=kc, ti=ti: e.activation(out=xm_[:, kc, 0:ntok], in_=tmp[ti][:, 0:ntok], func=AF.Identity,
                                                           scale=Acol(kc), bias=Bcol(kc)),
                 reads=[tmpB[ti], coefB, modB], writes=[xm_B[kc]])

    def headnorm_rstd(src_ps, src_psB, ntok, bank, sqi=0, sd_=None, sd_B=None, rs_=None, rs_B=None):
        if sd_ is None:
            sd_, sd_B, rs_, rs_B = sd[:], [sdB], rstd[:], [rstdB]
        k.op(ACT, lambda e: e.activation(out=sq[sqi][:, 0:ntok], in_=src_ps[:, 0:ntok], func=AF.Square), reads=[src_psB], writes=[sqB[sqi]])
        k.op(PE, lambda e: e.matmul(PS[bank][:, 0:ntok], lhsT=bones[:], rhs=sq[sqi][:, 0:ntok], start=True, stop=True),
             reads=[sqB[sqi], onesB], writes=[PSB[bank]])
        k.op(ACT, lambda e: e.activation(out=sd_[:, 0:ntok], in_=PS[bank][:, 0:ntok], func=AF.Ln, scale=1.0 / 64, bias=epsT[:]),
             reads=[PSB[bank], epsB], writes=sd_B)
        k.op(ACT, lambda e: e.activation(out=rs_[:, 0:ntok], in_=sd_[:, 0:ntok], func=AF.Exp, scale=-0.5), reads=sd_B, writes=rs_B)

    def rope(src_bf, src_B, dst, dst_B, bank, ve=POOL, t0=None, t0B=None, t1=None, t1B=None):
        if t0 is None:
            t0, t0B, t1, t1B = tmp[0][:], [tmpB[0]], tmp[1][:], [tmpB[1]]
        k.op(PE, lambda e: e.matmul(PS[bank][:], lhsT=pmb[:], rhs=src_bf, start=True, stop=True), reads=[src_B, pmB], writes=[PSB[bank]])
        k.op(ve, lambda e: e.tensor_tensor(out=t0, in0=src_bf, in1=cs[:, 0, :], op=ALU.mult), reads=[src_B, csB], writes=t0B)
        k.op(DVE, lambda e: e.tensor_tensor(out=t1, in0=PS[bank][:], in1=cs[:, 1, :], op=ALU.mult), reads=[PSB[bank], csB], writes=t1B)
        k.op(ve, lambda e: e.tensor_tensor(out=dst, in0=t0, in1=t1, op=ALU.add), reads=t0B + t1B, writes=dst_B)


    def dump(name, ap_fn, bufs):
        if dbg and name in dbg_d:
            k.dma(SP, lambda e: e.dma_start(out=dbg_d[name], in_=ap_fn()), reads=bufs, track=dumpB)

    dumpB = Buf("dump")

    wkv_t = sb("wkv", [128, 8 * 256], BF16)
    wkvB = Buf("wkv")
    k.dma(SP, lambda e: e.dma_start(out=wkv_t[:], in_=scr[("wkv",)][:, 0:8 * 256]), reads=[scrB["wkv"]], writes=[wkvB])
    wkvv = r3(wkv_t[:], 8, 256)

    def phaseA(src_rows, ntok, key_off, latent, tile_i):
        par = (tile_i + (1 if latent else 0)) % 2
        hb, hbB = hT[par], hB[par]
        if par == 0:
            xm_, xm_B = xmT, xmB
            nsd, nsdB, nrs, nrsB = sd[:], [sdB], rstd[:], [rstdB]
            kn_, kn_B = knT[:], knB
        else:
            xm_, xm_B = AR[:, 0:8, :], ARB[0:8]
            nsd, nsdB = ar32(4), [ARB[8], ARB[9]]
            nrs, nrsB = ar32(5), [ARB[10], ARB[11]]
            kn_, kn_B = ar(16), ARB[16]
        hsd, hsdB = ar32(6), [ARB[12], ARB[13]]
        hrs, hrsB = ar32(7), [ARB[14], ARB[15]]
        load_tile_T(src_rows, ntok, hb, hbB)
        if latent:
            norm_mod(hb, hbB, ntok, lambda kc: C("A1_0", kc), lambda kc: M(0, SH1, kc), 0, ve=DVE, xm_=xm_, xm_B=xm_B, sd_=nsd, sd_B=nsdB, rs_=nrs, rs_B=nrsB)
        else:
            norm_mod(hb, hbB, ntok, lambda kc: C("A1c", kc), lambda kc: M(0, SH1, kc, 1), 0, ve=DVE, xm_=xm_, xm_B=xm_B, sd_=nsd, sd_B=nsdB, rs_=nrs, rs_B=nrsB)
        for kc in range(8):
            k.op(PE, lambda e, kc=kc: e.matmul(PS[1][:, 0:ntok], lhsT=wkvv[:, kc, 0:128], rhs=xm_[:, kc, 0:ntok], start=(kc == 0), stop=(kc == 7)),
                 reads=[wkvB, xm_B[kc]], writes=[PSB[1]])
        headnorm_rstd(PS[1], PSB[1], ntok, 2, sqi=par, sd_=hsd, sd_B=hsdB, rs_=hrs, rs_B=hrsB)
        dstK = KT[:, key_off:key_off + ntok] if not latent else kn_[:, 0:ntok]
        dstKB = [KTB] if not latent else [kn_B]
        k.op(DVE, lambda e: e.scalar_tensor_tensor(out=dstK, in0=PS[1][:, 0:ntok], scalar=V("kg"), in1=hrs[:, 0:ntok], op0=ALU.mult, op1=ALU.mult),
             reads=[PSB[1], vecB] + hrsB, writes=dstKB)
        if latent:
            k.dma(SP, lambda e: e.dma_start(out=cs[:, 0, :], in_=cos_d[:, tile_i * T:(tile_i + 1) * T]), writes=[csB])
            k.dma(SP, lambda e: e.dma_start(out=cs[:, 1, :], in_=sin_d[:, tile_i * T:(tile_i + 1) * T]), reads=[], writes=[csB])
            rope(kn_, kn_B, KT[:, key_off:key_off + ntok], [KTB], 3, ve=DVE)
        nsub = ntok // 128
        for m in range(nsub):
            for kc in range(8):
                k.op(PE, lambda e, kc=kc, m=m: e.matmul(PS[4][:, m * 128:(m + 1) * 128], lhsT=xm_[:, kc, m * 128:(m + 1) * 128], rhs=wkvv[:, kc, 128:256],
                                                        start=(kc == 0), stop=(kc == 7)),
                     reads=[wkvB, xm_B[kc]], writes=[PSB[4]])
        j0 = key_off // 128
        k.op(ACT, lambda e: e.activation(out=VA[:, j0:j0 + nsub, :, 0:64],
                                         in_=PS[4][:, 0:nsub * 128].rearrange("p (j h d) -> p j h d", j=nsub, h=2, d=64), func=AF.Copy),
             reads=[PSB[4]], writes=[VAB])

    phaseA(ctx_d, CTX, 0, False, 0)
    mod_todo = [(0, pc) for pc in range(4, 12)] + [(1, pc) for pc in range(12)]
    for i in range(NT):
        for _ in range(3):
            if mod_todo:
                l_, pc_ = mod_todo.pop(0)
                mod_pieces(l_, [pc_], 5)
        if NT < 8 and cast_batches:
            cast_batches.pop(0)()
        phaseA(x_d[i * T:(i + 1) * T, :], T, CTX + i * T, True, i)
    while mod_todo:
        l_, pc_ = mod_todo.pop(0)
        mod_pieces(l_, [pc_], 5)
    mod_finish(0, 5, 16, 48)
    coef_A("A2_0", 0, SC2, "gffn")
    late_casts = []
    head_casts = []
    while cast_batches:
        b = cast_batches.pop(0)
        if NT >= 8 and len(cast_batches) < 3:
            late_casts.append(b)
        elif NT >= 8:
            head_casts.append(b)
        else:
            b()
    mod_finish(1, 5)


    def ffn(l, hb, hbB):
        A2 = "A2_0" if l == 0 else "A2_1"
        norm_mod(hb, hbB, T, lambda kc: C(A2, kc), lambda kc: M(l, SH2, kc), 0)
        for pp in range(11):
            w, wb = wnext(("fin", l, pp))
            wv = w.rearrange("p (k s c) -> p k s c", k=8, s=2, c=256)
            if pp == 0:
                for kc in range(8):
                    for jj in range(2):
                        for s2 in range(2):
                            bank = jj * 2 + s2
                            k.op(PE, lambda e, kc=kc, s2=s2, bank=bank, jj=jj, wv=wv: e.matmul(
                                PS[bank][:], lhsT=wv[:, kc, s2, jj * 128:(jj + 1) * 128], rhs=xmT[:, kc, :], start=(kc == 0), stop=(kc == 7)),
                                reads=[wb, xmB[kc]], writes=[PSB[bank]])
            for jj in range(2):
                j = pp * 2 + jj
                gb_, ub_ = (0, 1) if j % 2 == 0 else (2, 3)
                for s2, bank in ((0, gb_), (1, ub_)):
                    if pp == 0:
                        break
                    for kc in range(8):
                        k.op(PE, lambda e, kc=kc, s2=s2, bank=bank, jj=jj, wv=wv: e.matmul(
                            PS[bank][:], lhsT=wv[:, kc, s2, jj * 128:(jj + 1) * 128], rhs=xmT[:, kc, :], start=(kc == 0), stop=(kc == 7)),
                            reads=[wb, xmB[kc]], writes=[PSB[bank]])
                si = j % 2
                k.op(ACT, lambda e, si=si, gb_=gb_: e.activation(out=sgl[si][:], in_=PS[gb_][:], func=AF.Silu), reads=[PSB[gb_]], writes=[sglB[si]])
                k.op(DVE, lambda e, si=si, ub_=ub_, j=j: e.tensor_tensor(out=ar(j), in0=PS[ub_][:], in1=sgl[si][:], op=ALU.mult),
                     reads=[PSB[ub_], sglB[si]], writes=[ARB[j]])
        for f in range(8):
            w, wb = wnext(("fout", l, f))
            wv = r3(w[:, 0:NH * 128], NH, 128)
            bank = 4 + f % 4
            for j in range(NH):
                k.op(PE, lambda e, j=j, wv=wv, bank=bank: e.matmul(PS[bank][:], lhsT=wv[:, j, :], rhs=ar(j), start=(j == 0), stop=(j == NH - 1)),
                     reads=[wb, ARB[j]], writes=[PSB[bank]])
            k.op(DVE, lambda e, f=f, bank=bank: e.scalar_tensor_tensor(out=hb[:, f, :], in0=PS[bank][:], scalar=M(l, GT2, f), in1=hb[:, f, :],
                                                                      op0=ALU.mult, op1=ALU.add),
                 reads=[PSB[bank], modB, hbB[f]], writes=[hbB[f]])

    def layer0(i):
        hb, hbB = hT[i % 2], hB[i % 2]
        if i == 0:
            for b_ in head_casts:
                b_()
        load_tile_T(x_d[i * T:(i + 1) * T, :], T, hb, hbB)
        norm_mod(hb, hbB, T, lambda kc: C("A1_0", kc), lambda kc: M(0, SH1, kc), 0)
        k.dma(SP, lambda e: e.dma_start(out=cs[:, 0, :], in_=cos_d[:, i * T:(i + 1) * T]), writes=[csB])
        k.dma(SP, lambda e: e.dma_start(out=cs[:, 1, :], in_=sin_d[:, i * T:(i + 1) * T]), writes=[csB])
        w, wq_b = wnext(("wq",))
        wq_v = r3(w, 8, 512)
        SA, SBK = 6, 7

        def qchunk_gen(c):
            for kc in range(8):
                k.op(PE, lambda e, kc=kc: e.matmul(PS[SA][:], lhsT=wq_v[:, kc, c * 128:(c + 1) * 128], rhs=xmT[:, kc, :], start=(kc == 0), stop=(kc == 7)),
                     reads=[wq_b, xmB[kc]], writes=[PSB[SA]])
                if kc == 3:
                    yield
            yield
            k.op(DVE, lambda e: e.tensor_copy(out=tmp[2][:], in_=PS[SA][:]), reads=[PSB[SA]], writes=[tmpB[2]])
            yield
            k.op(POOL, lambda e: e.tensor_tensor(out=sq[0][:], in0=tmp[2][:], in1=tmp[2][:], op=ALU.mult), reads=[tmpB[2]], writes=[sqB[0]])
            yield
            k.op(PE, lambda e: e.matmul(PS[SBK][:], lhsT=bones[:], rhs=sq[0][:], start=True, stop=True), reads=[sqB[0], onesB], writes=[PSB[SBK]])
            yield
            k.op(ACT, lambda e: e.activation(out=sd[:], in_=PS[SBK][:], func=AF.Ln, scale=1.0 / 64, bias=epsT[:]), reads=[PSB[SBK], epsB], writes=[sdB])
            yield
            k.op(ACT, lambda e: e.activation(out=rstd[:], in_=sd[:], func=AF.Exp, scale=-0.5), reads=[sdB], writes=[rstdB])
            yield
            k.op(DVE, lambda e: e.scalar_tensor_tensor(out=knT[:], in0=PS[SA][:], scalar=V("qg"), in1=rstd[:], op0=ALU.mult, op1=ALU.mult),
                 reads=[PSB[SA], vecB, rstdB], writes=[knB])
            yield
            k.op(PE, lambda e: e.matmul(PS[SBK][:], lhsT=pmb[:], rhs=knT[:], start=True, stop=True), reads=[knB, pmB], writes=[PSB[SBK]])
            k.op(POOL, lambda e: e.tensor_tensor(out=tmp[0][:], in0=knT[:], in1=cs[:, 0, :], op=ALU.mult), reads=[knB, csB], writes=[tmpB[0]])
            yield
            k.op(DVE, lambda e: e.tensor_tensor(out=tmp[1][:], in0=PS[SBK][:], in1=cs[:, 1, :], op=ALU.mult), reads=[PSB[SBK], csB], writes=[tmpB[1]])
            yield
            k.op(POOL, lambda e: e.tensor_tensor(out=ar(QT_ + c), in0=tmp[0][:], in1=tmp[1][:], op=ALU.add), reads=[tmpB[0], tmpB[1]], writes=[ARB[QT_ + c]])
            yield

        def su_gen():
            w2, wb2 = wnext(("wsu",))
            wvs = r3(w2, 8, 512)
            for c in range(4):
                bank = SA + c % 2
                for kc in range(8):
                    k.op(PE, lambda e, kc=kc, c=c, bank=bank: e.matmul(PS[bank][:], lhsT=wvs[:, kc, c * 128:(c + 1) * 128], rhs=xmT[:, kc, :], start=(kc == 0), stop=(kc == 7)),
                         reads=[wb2, xmB[kc]], writes=[PSB[bank]])
                    if kc == 3:
                        yield
                yield
                k.op(ACT, lambda e, c=c, bank=bank: e.activation(out=ar(GU_ + c), in_=PS[bank][:], func=AF.Gelu_apprx_tanh), reads=[PSB[bank]], writes=[ARB[GU_ + c]])
                yield

        def sv_gen():
            w3, wb3 = wnext(("wsv",))
            wv2 = r3(w3, 8, 512)
            for m in range(4):
                bank = SA + m % 2
                mb = SBK - m % 2
                for kc in range(8):
                    k.op(PE, lambda e, kc=kc, m=m, bank=bank: e.matmul(PS[bank][:], lhsT=xmT[:, kc, m * 128:(m + 1) * 128], rhs=wv2[:, kc, :], start=(kc == 0), stop=(kc == 7)),
                         reads=[wb3, xmB[kc]], writes=[PSB[bank]])
                    if kc == 3:
                        yield
                yield
                k.op(ACT, lambda e, bank=bank: e.activation(out=gv[:], in_=PS[bank][:], func=AF.Gelu_apprx_tanh), reads=[PSB[bank]], writes=[gvB])
                yield
                k.op(POOL, lambda e: e.tensor_tensor(out=gv2[:], in0=gv[:], in1=gv[:], op=ALU.mult), reads=[gvB], writes=[gv2B])
                k.op(DVE, lambda e: e.tensor_reduce(out=st[:, 4:8], in_=gv[:].rearrange("p (g c) -> p g c", g=4), axis=AX.X, op=ALU.add), reads=[gvB, stB], writes=[stB])
                yield
                k.op(DVE, lambda e: e.tensor_reduce(out=st[:, 8:12], in_=gv2[:].rearrange("p (g c) -> p g c", g=4), axis=AX.X, op=ALU.add), reads=[gv2B, stB], writes=[stB])
                yield
                k.op(DVE, lambda e: e.tensor_scalar(out=st[:, 12:16], in0=st[:, 4:8], scalar1=1.0 / 128, scalar2=None, op0=ALU.mult), reads=[stB], writes=[stB])
                yield
                k.op(DVE, lambda e: e.tensor_tensor(out=st[:, 16:20], in0=st[:, 12:16], in1=st[:, 12:16], op=ALU.mult), reads=[stB], writes=[stB])
                yield
                k.op(DVE, lambda e: e.scalar_tensor_tensor(out=st[:, 20:24], in0=st[:, 8:12], scalar=1.0 / 128, in1=st[:, 16:20], op0=ALU.mult, op1=ALU.subtract),
                     reads=[stB], writes=[stB])
                yield
                k.op(ACT, lambda e: e.activation(out=st[:, 24:28], in_=st[:, 20:24], func=AF.Ln, bias=epsT[:]), reads=[stB, epsB], writes=[stB])
                yield
                k.op(ACT, lambda e: e.activation(out=st[:, 28:32], in_=st[:, 24:28], func=AF.Exp, scale=-0.5), reads=[stB], writes=[stB])
                yield
                for g in range(4):
                    k.op(DVE, lambda e, g=g: e.tensor_scalar(out=vn[:, g, :], in0=gv[:, g * 128:(g + 1) * 128], scalar1=st[:, 12 + g:13 + g], scalar2=st[:, 28 + g:29 + g],
                                                             op0=ALU.subtract, op1=ALU.mult), reads=[gvB, stB, vnB], writes=[vnB])
                yield
                k.op(PE, lambda e, mb=mb: e.matmul(PS[mb][:], lhsT=onesrow[0:1, :], rhs=bsprow[0:1, :], start=True, stop=False), reads=[rowB], writes=[PSB[mb]])
                for g in range(4):
                    k.op(PE, lambda e, g=g, mb=mb: e.matmul(PS[mb][:, g * 128:(g + 1) * 128], lhsT=vn[:, g, :], rhs=wspT[:, g, :], start=False, stop=(g == 3)),
                         reads=[vnB, wspB], writes=[PSB[mb]])
                yield
                k.op(DVE, lambda e, m=m, mb=mb: e.tensor_tensor(out=AR[:, SG_:SG_ + 4, m * 128:(m + 1) * 128], in0=PS[mb][:].rearrange("p (g t) -> p g t", g=4),
                                                                in1=AR[:, GU_:GU_ + 4, m * 128:(m + 1) * 128], op=ALU.mult),
                     reads=[PSB[mb]] + ARB[GU_:GU_ + 4], writes=ARB[SG_:SG_ + 4])
                yield

        qdone = {0: False, 1: False, 2: False, 3: False}

        def side_gen():
            for c in (1, 2, 3):
                yield from qchunk_gen(c)
                qdone[c] = True

        for _ in qchunk_gen(0):
            pass
        qdone[0] = True
        side = side_gen()
        side_alive = [True]

        def pull():
            if side_alive[0]:
                try:
                    next(side)
                except StopIteration:
                    side_alive[0] = False

        def qk(c, j):
            for hh in range(2):
                bank = hh * 2 + j % 2
                ps_ = slice(hh * 64, (hh + 1) * 64)
                k.op(PE, lambda e, bank=bank, ps_=ps_, j=j, c=c: e.matmul(PS[bank][:], lhsT=KT[ps_, j * 128:(j + 1) * 128], rhs=AR[ps_, QT_ + c, :],
                                                                    start=True, stop=True),
                     reads=[KTB, ARB[QT_ + c]], writes=[PSB[bank]])

        its = [(c, j) for c in range(4) for j in range(NKT)]
        qk(0, 0)
        for n, (c, j) in enumerate(its):
            if n + 1 < len(its):
                c2, j2 = its[n + 1]
                while not qdone[c2]:
                    pull()
                qk(c2, j2)
            for hh in range(2):
                bank = hh * 2 + j % 2
                pt = PT_ + hh * 2 + j % 2
                k.op(ACT, lambda e, bank=bank, pt=pt: e.activation(out=ar(pt), in_=PS[bank][:], func=AF.Exp, scale=0.125, bias=SHIFT),
                     reads=[PSB[bank]], writes=[ARB[pt]])
            for hh in range(2):
                pt = PT_ + hh * 2 + j % 2
                k.op(PE, lambda e, hh=hh, pt=pt, j=j: e.matmul(PS[4 + hh][:], lhsT=VA[:, j, hh, :], rhs=ar(pt), start=(j == 0), stop=(j == NKT - 1)),
                     reads=[VAB, ARB[pt]], writes=[PSB[4 + hh]])
            if j == NKT - 1:
                for hh in range(2):
                    k.op(DVE, lambda e, hh=hh: e.tensor_copy(out=sgl[hh][:], in_=PS[4 + hh][:]), reads=[PSB[4 + hh]], writes=[sglB[hh]])
                for hh in range(2):
                    k.op(DVE, lambda e, hh=hh: e.reciprocal(out=ar32(10)[0:64, :], in_=sgl[hh][64:128, :]), reads=[sglB[hh]], writes=[ARB[20], ARB[21]])
                    k.op(DVE, lambda e, hh=hh, c=c: e.tensor_tensor(out=AR[hh * 64:(hh + 1) * 64, AT_ + c, :], in0=sgl[hh][0:64, :], in1=ar32(10)[0:64, :], op=ALU.mult),
                         reads=[sglB[hh], ARB[20], ARB[21]], writes=[ARB[AT_ + c]])
            if (n + 1) % DRIP == 0:
                pull()
        while side_alive[0]:
            pull()
        if i == 0:
            for b in late_casts:
                b()
        w, wb = wnext(("wsu",))
        wv = r3(w, 8, 512)
        for c in range(4):
            bank = c % 2
            for kc in range(8):
                k.op(PE, lambda e, kc=kc, c=c, bank=bank, wv=wv: e.matmul(PS[bank][:], lhsT=wv[:, kc, c * 128:(c + 1) * 128], rhs=xmT[:, kc, :], start=(kc == 0), stop=(kc == 7)),
                     reads=[wb, xmB[kc]], writes=[PSB[bank]])
            k.op(ACT, lambda e, c=c, bank=bank: e.activation(out=ar(GU_ + c), in_=PS[bank][:], func=AF.Gelu_apprx_tanh), reads=[PSB[bank]], writes=[ARB[GU_ + c]])
        w, wb = wnext(("wsv",))
        wv2 = r3(w, 8, 512)
        for m in range(4):
            bank = 2 + m % 2
            for kc in range(8):
                k.op(PE, lambda e, kc=kc, m=m, bank=bank, wv2=wv2: e.matmul(PS[bank][:], lhsT=xmT[:, kc, m * 128:(m + 1) * 128], rhs=wv2[:, kc, :], start=(kc == 0), stop=(kc == 7)),
                     reads=[wb, xmB[kc]], writes=[PSB[bank]])
            if m % 2 == 0:
                gv_, gv_B, gv2_, gv2_B, vn_, vn_B, st_, st_B = gv[:], [gvB], gv2[:], [gv2B], vn[:], [vnB], st, stB
            else:
                gv_, gv_B = ar32(0), [ARB[0], ARB[1]]
                gv2_, gv2_B = ar32(1), [ARB[2], ARB[3]]
                vn_, vn_B = AR[:, 16, :].rearrange("p (g c) -> p g c", g=4), [ARB[16]]
                st_, st_B = st2, st2B
            k.op(ACT, lambda e, bank=bank, gv_=gv_: e.activation(out=gv_, in_=PS[bank][:], func=AF.Gelu_apprx_tanh), reads=[PSB[bank]], writes=gv_B)
            k.op(POOL, lambda e, gv_=gv_, gv2_=gv2_: e.tensor_tensor(out=gv2_, in0=gv_, in1=gv_, op=ALU.mult), reads=gv_B, writes=gv2_B)
            k.op(DVE, lambda e, gv_=gv_, st_=st_: e.tensor_reduce(out=st_[:, 4:8], in_=gv_.rearrange("p (g c) -> p g c", g=4), axis=AX.X, op=ALU.add), reads=gv_B + [st_B], writes=[st_B])
            k.op(DVE, lambda e, gv2_=gv2_, st_=st_: e.tensor_reduce(out=st_[:, 8:12], in_=gv2_.rearrange("p (g c) -> p g c", g=4), axis=AX.X, op=ALU.add), reads=gv2_B + [st_B], writes=[st_B])
            k.op(DVE, lambda e, st_=st_: e.tensor_scalar(out=st_[:, 12:16], in0=st_[:, 4:8], scalar1=1.0 / 128, scalar2=None, op0=ALU.mult), reads=[st_B], writes=[st_B])
            k.op(DVE, lambda e, st_=st_: e.tensor_tensor(out=st_[:, 16:20], in0=st_[:, 12:16], in1=st_[:, 12:16], op=ALU.mult), reads=[st_B], writes=[st_B])
            k.op(DVE, lambda e, st_=st_: e.scalar_tensor_tensor(out=st_[:, 20:24], in0=st_[:, 8:12], scalar=1.0 / 128, in1=st_[:, 16:20], op0=ALU.mult, op1=ALU.subtract),
                 reads=[st_B], writes=[st_B])
            k.op(DVE, lambda e, st_=st_: e.tensor_scalar(out=st_[:, 24:28], in0=st_[:, 20:24], scalar1=EPS, scalar2=None, op0=ALU.add), reads=[st_B], writes=[st_B])
            k.op(POOL, lambda e, st_=st_: e.tensor_tensor(out=st_[:, 28:32], in0=st_[:, 24:28], in1=neghalf[:, 0:4], op=ALU.pow), reads=[st_B, epsB], writes=[st_B])
            for g in range(4):
                k.op(DVE, lambda e, g=g, gv_=gv_, vn_=vn_, st_=st_: e.tensor_scalar(out=vn_[:, g, :], in0=gv_[:, g * 128:(g + 1) * 128], scalar1=st_[:, 12 + g:13 + g], scalar2=st_[:, 28 + g:29 + g],
                                                         op0=ALU.subtract, op1=ALU.mult), reads=gv_B + [st_B] + vn_B, writes=vn_B)
            mb = 4 + m % 2
            k.op(PE, lambda e, mb=mb: e.matmul(PS[mb][:], lhsT=onesrow[0:1, :], rhs=bsprow[0:1, :], start=True, stop=False), reads=[rowB], writes=[PSB[mb]])
            for g in range(4):
                k.op(PE, lambda e, g=g, mb=mb, vn_=vn_: e.matmul(PS[mb][:, g * 128:(g + 1) * 128], lhsT=vn_[:, g, :], rhs=wspT[:, g, :], start=False, stop=(g == 3)),
                     reads=vn_B + [wspB], writes=[PSB[mb]])
            k.op(DVE, lambda e, m=m, mb=mb: e.tensor_tensor(out=AR[:, SG_:SG_ + 4, m * 128:(m + 1) * 128], in0=PS[mb][:].rearrange("p (g t) -> p g t", g=4),
                                                            in1=AR[:, GU_:GU_ + 4, m * 128:(m + 1) * 128], op=ALU.mult),
                 reads=[PSB[mb]] + ARB[GU_:GU_ + 4], writes=ARB[SG_:SG_ + 4])
        for pc in range(2):
            w, wb = wnext(("wo", pc))
            wv3 = r3(w, 8, 512)
            for ff in range(4):
                f = pc * 4 + ff
                bank = 6 + f % 2
                for kk in range(8):
                    src = AT_ + kk if kk < 4 else SG_ + kk - 4
                    k.op(PE, lambda e, kk=kk, ff=ff, src=src, bank=bank, wv3=wv3: e.matmul(PS[bank][:], lhsT=wv3[:, kk, ff * 128:(ff + 1) * 128], rhs=ar(src),
                                                                                         start=(kk == 0), stop=(kk == 7)),
                         reads=[wb, ARB[src]], writes=[PSB[bank]])
                k.op(DVE, lambda e, f=f, bank=bank: e.scalar_tensor_tensor(out=hb[:, f, :], in0=PS[bank][:], scalar=M(0, GT1, f), in1=hb[:, f, :],
                                                                          op0=ALU.mult, op1=ALU.add),
                     reads=[PSB[bank], modB, hbB[f]], writes=[hbB[f]])
        ffn(0, hb, hbB)

    def layer1a(i):
        hb, hbB = hT[i % 2], hB[i % 2]
        gw, gwB = GW[i % 2], GWB[i % 2]
        norm_mod(hb, hbB, T, lambda kc: C("A1_1", kc), lambda kc: M(1, SH1, kc), 0)
        for pp in range(4):
            w, wb = wnext(("pw1", pp))
            wv = w.rearrange("p (k s c) -> p k s c", k=8, s=2, c=256)
            if pp == 0:
                for kc in range(8):
                    for jj in range(2):
                        for s2 in range(2):
                            bank = jj * 2 + s2
                            k.op(PE, lambda e, kc=kc, s2=s2, bank=bank, jj=jj, wv=wv: e.matmul(
                                PS[bank][:], lhsT=wv[:, kc, s2, jj * 128:(jj + 1) * 128], rhs=xmT[:, kc, :], start=(kc == 0), stop=(kc == 7)),
                                reads=[wb, xmB[kc]], writes=[PSB[bank]])
            for jj in range(2):
                c = pp * 2 + jj
                ab_, gb_ = (0, 1) if c % 2 == 0 else (2, 3)
                for s2, bank in ((0, ab_), (1, gb_)):
                    if pp == 0:
                        break
                    for kc in range(8):
                        k.op(PE, lambda e, kc=kc, s2=s2, bank=bank, jj=jj, wv=wv: e.matmul(
                            PS[bank][:], lhsT=wv[:, kc, s2, jj * 128:(jj + 1) * 128], rhs=xmT[:, kc, :], start=(kc == 0), stop=(kc == 7)),
                            reads=[wb, xmB[kc]], writes=[PSB[bank]])
                si = c % 2
                k.op(ACT, lambda e, si=si, gb_=gb_, c=c: e.activation(out=sgl[si][:], in_=PS[gb_][:], func=AF.Sigmoid, bias=V("bpw1", 8 + c)),
                     reads=[PSB[gb_], vecB], writes=[sglB[si]])
                k.op(DVE, lambda e, si=si, ab_=ab_, c=c: e.scalar_tensor_tensor(out=gw[:, c, 15:15 + T], in0=PS[ab_][:], scalar=V("bpw1", c), in1=sgl[si][:],
                                                                               op0=ALU.add, op1=ALU.mult),
                     reads=[PSB[ab_], vecB, sglB[si]], writes=[gwB[c]])
        if i == 0:
            k.op(POOL, lambda e: e.memset(gw[:, :, 0:15], 0.0), reads=gwB, writes=gwB)
        else:
            pg, pgB = GW[(i - 1) % 2], GWB[(i - 1) % 2]
            k.op(POOL, lambda e: e.tensor_copy(out=gw[:, :, 0:15], in_=pg[:, :, T:T + 15]), reads=pgB + gwB, writes=gwB)
            k.op(POOL, lambda e: e.tensor_copy(out=pg[:, :, T + 15:T + 30], in_=gw[:, :, 15:30]), reads=gwB + pgB, writes=pgB)
        if i == NT - 1:
            k.op(POOL, lambda e: e.memset(gw[:, :, T + 15:T + 30], 0.0), reads=gwB, writes=gwB)

    def layer1b(i):
        hb, hbB = hT[i % 2], hB[i % 2]
        gw, gwB = GW[i % 2], GWB[i % 2]
        wd0 = VEC_ROWS["wdw"]
        for f in range(8):
            k.op(DVE, lambda e, f=f: e.tensor_scalar(out=hb[:, f, :], in0=hb[:, f, :], scalar1=C("gb", f), scalar2=None, op0=ALU.add),
                 reads=[hbB[f], coefB], writes=[hbB[f]])
        for kc in range(8):
            acc = ar32(kc)
            accB = [ARB[2 * kc], ARB[2 * kc + 1]]
            bank = 2 + kc % 4
            for j in range(31):
                r = dgstate["n"] % NDG
                dgstate["n"] += 1
                k.op(POOL if j % 3 == 2 else DVE, lambda e, r=r, j=j, kc=kc: e.tensor_scalar(out=dg[r][:], in0=identb[:], scalar1=vecT[:, wd0 + j * 8 + kc:wd0 + j * 8 + kc + 1], scalar2=1.0,
                                                                     op0=ALU.mult, op1=ALU.mult),
                     reads=[identB, vecB], writes=[dgB[r]])
                k.op(PE, lambda e, r=r, j=j, kc=kc, bank=bank: e.matmul(PS[bank][:], lhsT=dg[r][:], rhs=gw[:, kc, j:j + T], start=(j == 0), stop=(j == 30)),
                     reads=[dgB[r], gwB[kc]], writes=[PSB[bank]])
            k.op(ACT, lambda e, kc=kc, acc=acc, bank=bank: e.activation(out=acc, in_=PS[bank][:], func=AF.Identity, bias=V("bdw", kc)),
                 reads=[PSB[bank], vecB], writes=accB)
            k.op(ACT, lambda e, acc=acc: e.activation(out=sq[0][:], in_=acc, func=AF.Copy), reads=accB, writes=[sqB[0]])
            k.op(PE, lambda e, kc=kc: e.matmul(PS[0][:], lhsT=ones[:], rhs=sq[0][:], start=(kc == 0), stop=(kc == 7)), reads=[sqB[0], onesB], writes=[PSB[0]])
            k.op(ACT, lambda e, acc=acc: e.activation(out=sq[1][:], in_=acc, func=AF.Square), reads=accB, writes=[sqB[1]])
            k.op(PE, lambda e, kc=kc: e.matmul(PS[1][:], lhsT=ones[:], rhs=sq[1][:], start=(kc == 0), stop=(kc == 7)), reads=[sqB[1], onesB], writes=[PSB[1]])
        mean, meanB = sgl[0], sglB[0]
        k.op(ACT, lambda e: e.activation(out=mean[:], in_=PS[0][:], func=AF.Copy, scale=1.0 / D), reads=[PSB[0]], writes=[meanB])
        k.op(POOL, lambda e: e.tensor_tensor(out=sgl[1][:], in0=mean[:], in1=mean[:], op=ALU.mult), reads=[meanB], writes=[sglB[1]])
        k.op(DVE, lambda e: e.scalar_tensor_tensor(out=sd[:], in0=PS[1][:], scalar=1.0 / D, in1=sgl[1][:], op0=ALU.mult, op1=ALU.subtract),
             reads=[PSB[1], sglB[1]], writes=[sdB])
        k.op(ACT, lambda e: e.activation(out=sd[:], in_=sd[:], func=AF.Ln, bias=epsT[:]), reads=[sdB, epsB], writes=[sdB])
        k.op(ACT, lambda e: e.activation(out=rstd[:], in_=sd[:], func=AF.Exp, scale=-0.5), reads=[sdB], writes=[rstdB])
        for kc in range(8):
            acc = ar32(kc)
            accB = [ARB[2 * kc], ARB[2 * kc + 1]]
            if kc % 3 == 2:
                ve_, ta, taB, tb, tbB = POOL, sgl[1], sglB[1], tmp[2], tmpB[2]
            else:
                ve_, ta, taB, tb, tbB = DVE, tmp[0], tmpB[0], tmp[1], tmpB[1]
            k.op(ve_, lambda e, acc=acc, ta=ta: e.tensor_tensor(out=ta[:], in0=acc, in1=mean[:], op=ALU.subtract), reads=accB + [meanB], writes=[taB])
            k.op(ve_, lambda e, ta=ta, tb=tb: e.tensor_tensor(out=tb[:], in0=ta[:], in1=rstd[:], op=ALU.mult), reads=[taB, rstdB], writes=[tbB])
            k.op(ACT, lambda e, kc=kc, tb=tb: e.activation(out=xmT[:, kc, :], in_=tb[:], func=AF.Silu, scale=V("lng", kc), bias=V("lnb", kc)),
                 reads=[tbB, vecB], writes=[xmB[kc]])
        for pc in range(2):
            w, wb = wnext(("pw2", pc))
            wv = r3(w, 8, 512)
            for ff in range(4):
                f = pc * 4 + ff
                bank = 6 + f % 2
                for kc in range(8):
                    k.op(PE, lambda e, kc=kc, ff=ff, bank=bank, wv=wv: e.matmul(PS[bank][:], lhsT=wv[:, kc, ff * 128:(ff + 1) * 128], rhs=xmT[:, kc, :],
                                                                              start=(kc == 0), stop=(kc == 7)),
                         reads=[wb, xmB[kc]], writes=[PSB[bank]])
                k.op(DVE, lambda e, f=f, bank=bank: e.scalar_tensor_tensor(out=hb[:, f, :], in0=PS[bank][:], scalar=M(1, GT1, f), in1=hb[:, f, :],
                                                                          op0=ALU.mult, op1=ALU.add),
                     reads=[PSB[bank], modB, hbB[f]], writes=[hbB[f]])
        ffn(1, hb, hbB)
        rms_stats(hb, hbB, T, 0)
        for kc in range(8):
            k.op(DVE, lambda e, kc=kc: e.scalar_tensor_tensor(out=hb[:, kc, :], in0=hb[:, kc, :], scalar=V("gfin", kc), in1=rstd[:], op0=ALU.mult, op1=ALU.mult),
                 reads=[hbB[kc], vecB, rstdB], writes=[hbB[kc]])
        for m in range(4):
            ib = iostate["n"] % 2
            iostate["n"] += 1
            for half in range(2):
                bank = 6 + half
                for kk in range(4):
                    kc = half * 4 + kk
                    k.op(PE, lambda e, kc=kc, kk=kk, bank=bank, m=m: e.transpose(out=PS[bank][:, kk * 128:(kk + 1) * 128], in_=hb[:, kc, m * 128:(m + 1) * 128],
                                                                               identity=ident[:]),
                         reads=[hbB[kc], identB], writes=[PSB[bank]])
                if half == 0:
                    k.op(ACT, lambda e, ib=ib, bank=bank: e.activation(out=io[ib][:, 0:512], in_=PS[bank][:], func=AF.Copy), reads=[PSB[bank]], writes=[ioB[ib]])
                else:
                    k.op(DVE, lambda e, ib=ib, bank=bank: e.tensor_copy(out=io[ib][:, 512:1024], in_=PS[bank][:]), reads=[PSB[bank], ioB[ib]], writes=[ioB[ib]])
            k.dma(SP, lambda e, ib=ib, m=m: e.dma_start(out=out_d[i * T + m * 128: i * T + (m + 1) * 128, :], in_=io[ib][:]), reads=[ioB[ib]])

    coef_A("A1_1", 1, SC1, "gmix")
    coef_A("A2_1", 1, SC2, "gffn")
    c0 = CO["gb"]
    k.op(DVE, lambda e: e.tensor_tensor(out=coef[:, c0:c0 + 8], in0=modsb[:, 48 + GT1 * 8:48 + GT1 * 8 + 8, 0],
                                        in1=vecT[:, VEC_ROWS["bpw2"]:VEC_ROWS["bpw2"] + 8], op=ALU.mult),
         reads=[modB, vecB, coefB], writes=[coefB])
    outB = Buf("out")
    for i in range(NT):
        layer0(i)
        layer1a(i)
        if i >= 1:
            layer1b(i - 1)
    layer1b(NT - 1)
    assert wstate["next"] == len(seq)

    print("sbuf bytes remaining/partition:", nc.sbuf_bytes_remaining)
    k.emit()
    k.finish(SP, ioB + [dumpB])
    return nc, k


def make_vecs(c_b, c_ctx, b_mod, g_mix, g_ffn, b_pw1, b_dw, ln_g, ln_b, b_pw2, g_final, w_dw, q_gain, k_gain):
    rows = [c_b.reshape(8, 128), c_ctx.reshape(8, 128), b_mod.reshape(96, 128), g_mix.reshape(16, 128), g_ffn.reshape(16, 128),
            b_pw1.reshape(16, 128), b_dw.reshape(8, 128), ln_g.reshape(8, 128), ln_b.reshape(8, 128), b_pw2.reshape(8, 128),
            g_final.reshape(8, 128), w_dw.reshape(248, 128),
            np.concatenate([q_gain.reshape(64), q_gain.reshape(64)])[None, :],
            np.concatenate([k_gain.reshape(64), k_gain.reshape(64)])[None, :]]
    v = np.concatenate(rows, 0).astype(np.float32)
    out = np.zeros((NVEC, 128), np.float32)
    out[:v.shape[0]] = v
    return out


def make_in_maps(x, c, ctx, c_ctx, w_mod, b_mod, g_mix, g_ffn, w_ffn_in, w_ffn_out, w_in, q_gain, k_gain, w_sp, b_sp, w_out,
                 w_pw1, b_pw1, w_dw, b_dw, ln_g, ln_b, w_pw2, b_pw2, g_final):
    f = lambda a: np.ascontiguousarray(np.asarray(a, dtype=np.float32))
    B, S, _ = x.shape
    cos, sin = rope_tables(S)
    shared = {
        "w_mod": f(w_mod), "w_ffn_in": f(w_ffn_in), "w_ffn_out": f(w_ffn_out), "w_in": f(w_in[0]),
        "w_sp": f(w_sp[0]), "b_sp": f(b_sp[0]).reshape(1, 512), "w_out": f(w_out[0]), "w_pw1": f(w_pw1[0]), "w_pw2": f(w_pw2[0]),
        "ident": np.eye(128, dtype=np.float32), "pm": rope_perm(), "cos": cos, "sin": sin,
    }
    maps = []
    for b in range(B):
        m = dict(shared)
        m["x"] = f(x[b])
        m["ctx"] = f(ctx[b])
        m["vecs"] = make_vecs(f(c[b]), f(c_ctx), f(b_mod), f(g_mix), f(g_ffn), f(b_pw1[0]), f(b_dw[0]), f(ln_g[0]), f(ln_b[0]),
                              f(b_pw2[0]), f(g_final), f(w_dw[0]), f(q_gain[0]), f(k_gain[0]))
        maps.append(m)
    return maps


_NC_CACHE = {}


def kernel(**inputs):
    x = np.asarray(inputs["x"])
    B, S, _ = x.shape
    maps = make_in_maps(**inputs)
    if S not in _NC_CACHE:
        _NC_CACHE[S] = build(S)[0]
    nc = _NC_CACHE[S]
    res = run_bass_kernel_spmd(nc, maps, core_ids=list(range(B)))
    return np.stack([np.asarray(r["out"], dtype=np.float32) for r in res.results], 0)
```

```python
import numpy as np
import concourse.bass as bass
import concourse.mybir as mybir
from concourse.bass_utils import run_bass_kernel_spmd

F32 = mybir.dt.float32
BF16 = mybir.dt.bfloat16
AF = mybir.ActivationFunctionType
ALU = mybir.AluOpType
AX = mybir.AxisListType

PE, ACT, DVE, POOL, SP = "pe", "act", "dve", "pool", "sp"

D = 1024
CTX = 256
DFF = 2816
NH = 22
T = 512
EPS = 1e-6
NSLOT = 6
DRIP = 2
SHIFT = -8.0
SLOT_ELEMS = 4096


class Buf:
    __slots__ = ("name", "lw", "rd", "dsem", "dcount")

    def __init__(self, name):
        self.name = name
        self.lw = None
        self.rd = {}
        self.dsem = None
        self.dcount = 0


class Op:
    __slots__ = ("eng", "fn", "deps", "is_dma", "buf", "sig", "sigval", "dval")

    def __init__(self, eng, fn, is_dma=False):
        self.eng = eng
        self.fn = fn
        self.deps = []
        self.is_dma = is_dma
        self.buf = None
        self.sig = False
        self.sigval = 0
        self.dval = 0


class K:
    def __init__(self, nc):
        self.nc = nc
        self.ops = []
        self.h = {PE: nc.tensor, ACT: nc.scalar, DVE: nc.vector, POOL: nc.gpsimd, SP: nc.sync}
        self.dsems = {}

    def op(self, eng, fn, reads=(), writes=()):
        o = Op(eng, fn)
        self._deps(o, reads, writes)
        self.ops.append(o)
        return o

    def dma(self, eng, fn, reads=(), writes=(), track=None, disjoint=False):
        o = Op(eng, fn, is_dma=True)
        if track is None:
            track = writes[0] if writes else reads[0]
        key = (track.name, eng)
        if key not in self.dsems:
            self.dsems[key] = [None, 0]
        o.buf = key
        self.dsems[key][1] += 16
        o.dval = self.dsems[key][1]
        self._deps(o, reads, writes, disjoint)
        self.ops.append(o)
        return o

    def _deps(self, o, reads, writes, disjoint=False):
        for b in reads:
            if b.lw is not None:
                o.deps.append(b.lw)
            b.rd[id(o) if o.is_dma else o.eng] = o
        for b in writes:
            if b.lw is not None and not (disjoint and b.lw.is_dma):
                o.deps.append(b.lw)
            for r in b.rd.values():
                if r is not o:
                    o.deps.append(r)
            b.lw = o
            b.rd = {}

    def emit(self):
        nc = self.nc
        for o in self.ops:
            nd = []
            for d in o.deps:
                if d.is_dma:
                    nd.append(d)
                    continue
                if d.eng == PE and o.eng == PE and not o.is_dma:
                    continue
                d.sig = True
                nd.append(d)
            o.deps = nd
        sems = {e: nc.alloc_semaphore(name=f"s_{e}") for e in self.h}
        for key, v in self.dsems.items():
            v[0] = nc.alloc_semaphore(name=f"d_{key[0]}_{key[1]}")
        cnt = {e: 0 for e in self.h}
        for o in self.ops:
            if o.sig and not o.is_dma:
                cnt[o.eng] += 1
                o.sigval = cnt[o.eng]
        waited = {e: {} for e in self.h}
        self.nwait = 0
        for o in self.ops:
            need = {}
            for d in o.deps:
                if d.is_dma:
                    key, val = self.dsems[d.buf][0], d.dval
                else:
                    key, val = sems[d.eng], d.sigval
                kid = id(key)
                if kid not in need or need[kid][1] < val:
                    need[kid] = (key, val)
            w = waited[o.eng]
            for kid, (key, val) in need.items():
                if w.get(kid, 0) >= val:
                    continue
                self.h[o.eng].wait_ge(key, val)
                w[kid] = val
                self.nwait += 1
            ins = o.fn(self.h[o.eng])
            if o.is_dma:
                ins.then_inc(self.dsems[o.buf][0], 16)
            elif o.sig:
                ins.then_inc(sems[o.eng], 1)

    def finish(self, eng, bufs):
        names = {b.name for b in bufs}
        for key, v in self.dsems.items():
            if key[0] in names:
                self.h[eng].wait_ge(v[0], v[1])


VEC_ROWS = {}
_r = 0
for _n, _c in [("c", 8), ("cctx", 8), ("bmod", 96), ("gmix", 16), ("gffn", 16), ("bpw1", 16), ("bdw", 8),
               ("lng", 8), ("lnb", 8), ("bpw2", 8), ("gfin", 8), ("wdw", 248), ("qg", 1), ("kg", 1)]:
    VEC_ROWS[_n] = _r
    _r += _c
NVEC = 512


def rope_tables(seq):
    t = np.arange(seq)
    row = (t // 64).astype(np.float32)
    col = (t % 64).astype(np.float32)
    inv = (10000.0 ** (-np.arange(0, 32, 2, dtype=np.float32) / 32.0)).astype(np.float32)
    ang_r = (row[:, None] * inv[None, :]).astype(np.float32)
    ang_c = (col[:, None] * inv[None, :]).astype(np.float32)
    cr, sr, cc, sc = np.cos(ang_r), np.sin(ang_r), np.cos(ang_c), np.sin(ang_c)
    cos64 = np.concatenate([cr, cr, cc, cc], axis=1).T
    sin64 = np.concatenate([-sr, sr, -sc, sc], axis=1).T
    cos = np.concatenate([cos64, cos64], 0).astype(np.float32)
    sin = np.concatenate([sin64, sin64], 0).astype(np.float32)
    return np.ascontiguousarray(cos), np.ascontiguousarray(sin)


def rope_perm():
    pm = np.zeros((128, 128), np.float32)
    for hh in range(2):
        for d in range(64):
            blk = d // 16
            src = d + 16 if blk in (0, 2) else d - 16
            pm[hh * 64 + src, hh * 64 + d] = 1.0
    return pm


def piece_sequence(nt):
    seq = [("mod", 0, pc) for pc in range(12)]
    seq += [("mod", 1, pc) for pc in range(12)]
    l0 = [("wq",), ("wsu",), ("wsv",), ("wo", 0), ("wo", 1)] + [("fin", 0, p) for p in range(11)] + \
         [("fout", 0, c) for c in range(8)]
    l1a = [("pw1", p) for p in range(4)]
    l1b = [("pw2", 0), ("pw2", 1)] + [("fin", 1, p) for p in range(11)] + [("fout", 1, c) for c in range(8)]
    for i in range(nt):
        seq += l0 + l1a
        if i >= 1:
            seq += l1b
    seq += l1b
    return seq


def build(SEQ, dbg=None):
    NT = SEQ // T
    NKT = (CTX + SEQ) // 128
    nc = bass.Bass("TRN2", target_bir_lowering=False)
    k = K(nc)

    def din(name, shape, dt=F32):
        return nc.dram_tensor(name, list(shape), dt, kind="ExternalInput").ap()

    x_d = din("x", [SEQ, D])
    ctx_d = din("ctx", [CTX, D])
    vecs_d = din("vecs", [NVEC, 128])
    w_mod_d = din("w_mod", [2, D, 6 * D])
    w_ffn_in_d = din("w_ffn_in", [2, D, 2 * DFF])
    w_ffn_out_d = din("w_ffn_out", [2, DFF, D])
    w_in_d = din("w_in", [D, 1792])
    w_sp_d = din("w_sp", [4, 128, 128])
    b_sp_d = din("b_sp", [1, 512])
    w_out_d = din("w_out", [D, D])
    w_pw1_d = din("w_pw1", [D, 2 * D])
    w_pw2_d = din("w_pw2", [D, D])
    ident_d = din("ident", [128, 128])
    pm_d = din("pm", [128, 128])
    cos_d = din("cos", [128, SEQ])
    sin_d = din("sin", [128, SEQ])
    out_d = nc.dram_tensor("out", [SEQ, D], F32, kind="ExternalOutput").ap()
    dbg_d = {}
    if dbg:
        for n, shp in dbg.items():
            dbg_d[n] = nc.dram_tensor("dbg_" + n, list(shp), F32, kind="ExternalOutput").ap()

    def sb(name, shape, dt=F32):
        return nc.alloc_sbuf_tensor("s_" + name, list(shape), dt)

    ring = sb("ring", [128, NSLOT, SLOT_ELEMS], BF16)
    ringB = [Buf(f"ring{i}") for i in range(NSLOT)]
    KT = sb("KT", [128, CTX + SEQ], BF16)
    KTB = Buf("KT")
    VA = sb("VA", [128, NKT, 2, 128], BF16)
    VAB = Buf("VA")
    hT = [sb(f"hT{i}", [128, 8, T]) for i in range(2)]
    hB = [[Buf(f"h{i}_{c}") for c in range(8)] for i in range(2)]
    io = [sb(f"io{i}", [128, D]) for i in range(2)]
    ioB = [Buf(f"io{i}") for i in range(2)]
    xmT = sb("xmT", [128, 8, T], BF16)
    xmB = [Buf(f"xm{c}") for c in range(8)]
    sq = [sb(f"sq{i}", [128, T], BF16) for i in range(2)]
    sqB = [Buf(f"sq{i}") for i in range(2)]
    tmp = [sb(f"tmp{i}", [128, T]) for i in range(3)]
    tmpB = [Buf(f"tmp{i}") for i in range(3)]
    rstd = sb("rstd", [128, T])
    rstdB = Buf("rstd")
    sd = sb("sd", [128, T])
    sdB = Buf("sd")
    AR = sb("AR", [128, NH, T], BF16)
    ARB = [Buf(f"ar{j}") for j in range(NH)]
    AR32 = AR.bitcast(F32) if hasattr(AR, "bitcast") else None
    GW = [sb(f"GW{i}", [128, 8, T + 30], BF16) for i in range(2)]
    NDG = 8
    dg = [sb(f"dg{i}", [128, 128], BF16) for i in range(NDG)]
    dgB = [Buf(f"dg{i}") for i in range(NDG)]
    identb = sb("identb", [128, 128], BF16)
    dgstate = {"n": 0}
    GWB = [[Buf(f"gw{i}_{c}") for c in range(8)] for i in range(2)]
    sgl = [sb(f"sgl{i}", [128, T]) for i in range(2)]
    sglB = [Buf(f"sgl{i}") for i in range(2)]
    cs = sb("cs", [128, 2, T])
    csB = Buf("cs")
    knT = sb("knT", [128, T], BF16)
    knB = Buf("knT")
    gv = sb("gv", [128, T])
    gvB = Buf("gv")
    gv2 = sb("gv2", [128, T])
    gv2B = Buf("gv2")
    vn = sb("vn", [128, 4, 128], BF16)
    vnB = Buf("vn")
    st = sb("st", [128, 32])
    stB = Buf("st")
    st2 = sb("st2", [128, 32])
    st2B = Buf("st2")
    vstage = sb("vstage", [128, 4, 128])
    vstageB = Buf("vstage")
    vecT = sb("vecT", [128, NVEC])
    vecB = Buf("vecT")
    coef = sb("coef", [128, 96])
    coefB = Buf("coef")
    modsb = sb("modsb", [128, 96, 2])
    modB = Buf("modsb")
    scb = sb("scb", [128, 8, 2], BF16)
    scbB = Buf("scb")
    ident = sb("ident", [128, 128])
    identB = Buf("ident")
    pmf = sb("pmf", [128, 128])
    pmb = sb("pmb", [128, 128], BF16)
    pmB = Buf("pm")
    ones = sb("ones", [128, 128], BF16)
    bones = sb("bones", [128, 128], BF16)
    onesB = Buf("ones")
    onesrow = sb("onesrow", [1, 128])

    bsprow = sb("bsprow", [1, 512])
    rowB = Buf("rows")
    wspT = sb("wspT", [128, 4, 128], BF16)
    wspB = Buf("wspT")
    nshift = sb("nshift", [128, 1])
    nshiftB = Buf("nshift")
    gtmp = sb("gtmp", [1, 4])

    PS = [nc.alloc_psum_tensor(f"ps{i}", [128, T], F32) for i in range(8)]
    PSB = [Buf(f"ps{i}") for i in range(8)]

    def ar(j):
        return AR[:, j, :]

    def ar32(j2):
        return AR[:, 2 * j2:2 * j2 + 2, :].bitcast(F32).rearrange("p a t -> p (a t)")

    QT_, AT_, SG_, GU_, PT_ = 0, 4, 8, 12, 16

    def V(name, i=0):
        r = VEC_ROWS[name] + i
        return vecT[:, r:r + 1]

    CO = {}
    _cc = [0]

    def cocol(name, n=8):
        CO[name] = _cc[0]
        _cc[0] += n

    for nm in ["A1_0", "A1c", "A2_0", "A1_1", "A2_1", "gb"]:
        cocol(nm)

    def C(name, i):
        c = CO[name] + i
        return coef[:, c:c + 1]

    def M(l, which, i, col=0):
        idx = l * 48 + which * 8 + i
        return modsb[:, idx, col:col + 1]

    SH1, SC1, GT1, SH2, SC2, GT2 = range(6)

    scr = {}
    scrB = {}

    def mkscr(key, group):
        scr[key] = nc.dram_tensor("scr_" + "_".join(str(s) for s in key), [128, SLOT_ELEMS], BF16).ap()
        if group not in scrB:
            scrB[group] = Buf("scr_" + group)
        return scr[key], scrB[group]

    def cast(dst, src, gb):
        k.dma(POOL, lambda e: e.dma_start(out=dst, in_=src), writes=[gb], disjoint=True)

    def r3(ap2, a, b):
        return ap2.rearrange("p (a b) -> p a b", a=a, b=b)

    def cast_wkv():
        s, gb = mkscr(("wkv",), "wkv")
        cast(r3(s[:, 0:8 * 256], 8, 256), w_in_d[:, 512:768].rearrange("(k p) c -> p k c", p=128), gb)

    def cast_all_early():
        s, gb = mkscr(("wq",), "wq")
        sv = r3(s, 8, 512)
        for g in range(4):
            for kv in range(2):
                h = kv * 4 + g
                cast(sv[:, :, g * 128 + kv * 64: g * 128 + kv * 64 + 64],
                     w_in_d[:, h * 64:(h + 1) * 64].rearrange("(k p) c -> p k c", p=128), gb)
        s, gb = mkscr(("wsu",), "wsu")
        cast(r3(s, 8, 512), w_in_d[:, 768:1280].rearrange("(k p) c -> p k c", p=128), gb)
        s, gb = mkscr(("wsv",), "wsv")
        cast(r3(s, 8, 512), w_in_d[:, 1280:1792].rearrange("(k p) c -> p k c", p=128), gb)
        for pc in range(2):
            s, gb = mkscr(("wo", pc), "wo")
            sv = r3(s, 8, 512)
            cols = slice(pc * 512, (pc + 1) * 512)
            cast(sv[0:64, 0:4, :], w_out_d[0:256, cols].rearrange("(c p) f -> p c f", p=64), gb)
            cast(sv[64:128, 0:4, :], w_out_d[256:512, cols].rearrange("(c p) f -> p c f", p=64), gb)
            cast(sv[:, 4:8, :], w_out_d[512:1024, cols].rearrange("(k p) f -> p k f", p=128), gb)

    def cast_fin(l, pps):
        for pp in pps:
            s, gb = mkscr(("fin", l, pp), f"fin{l}")
            sv = s.rearrange("p (k s c) -> p k s c", k=8, s=2, c=256)
            wv = w_ffn_in_d[l].rearrange("(k p) (s c) -> p k s c", p=128, s=2)
            for s2 in range(2):
                cast(sv[:, :, s2, :], wv[:, :, s2, pp * 256:(pp + 1) * 256], gb)
    def cast_fout(l):
        for c in range(8):
            s, gb = mkscr(("fout", l, c), f"fout{l}")
            cast(r3(s[:, 0:NH * 128], NH, 128),
                 w_ffn_out_d[l][:, c * 128:(c + 1) * 128].rearrange("(k p) f -> p k f", p=128), gb)

    def cast_pw():
        for pp in range(4):
            s, gb = mkscr(("pw1", pp), "pw1")
            sv = s.rearrange("p (k s c) -> p k s c", k=8, s=2, c=256)
            wv = w_pw1_d.rearrange("(k p) (s c) -> p k s c", p=128, s=2)
            for s2 in range(2):
                cast(sv[:, :, s2, :], wv[:, :, s2, pp * 256:(pp + 1) * 256], gb)
        for pc in range(2):
            s, gb = mkscr(("pw2", pc), "pw2")
            cast(r3(s, 8, 512), w_pw2_d[:, pc * 512:(pc + 1) * 512].rearrange("(k p) c -> p k c", p=128), gb)

    seq = piece_sequence(NT)
    wstate = {"issued": 0, "next": 0}

    def group_of(key):
        if key[0] in ("fin", "fout"):
            return f"{key[0]}{key[1]}"
        return key[0]

    def issue_load(n):
        key = seq[n]
        slot = n % NSLOT
        if key[0] == "mod":
            _, l, pc = key
            dst = r3(ring[:, slot, :], 8, 512)
            src = w_mod_d[l][:, pc * 512:(pc + 1) * 512].rearrange("(k p) c -> p k c", p=128)
            k.dma(POOL, lambda e: e.dma_start(out=dst, in_=src), writes=[ringB[slot]])
        else:
            s = scr[key]
            n_el = SLOT_ELEMS
            if key[0] == "wkv":
                n_el = 8 * 256
            elif key[0] == "fout":
                n_el = NH * 128
            k.dma(SP, lambda e: e.dma_start(out=ring[:, slot, 0:n_el], in_=s[:, 0:n_el]),
                  reads=[scrB[group_of(key)]], writes=[ringB[slot]])

    def wnext(key):
        n = wstate["next"]
        assert seq[n] == key, (seq[n], key)
        while wstate["issued"] < min(len(seq), n + NSLOT):
            issue_load(wstate["issued"])
            wstate["issued"] += 1
        wstate["next"] += 1
        slot = n % NSLOT
        return ring[:, slot, :], ringB[slot]

    k.dma(SP, lambda e: e.dma_start(out=vstage[:], in_=vecs_d.rearrange("(g r) c -> r g c", r=128)), writes=[vstageB])
    k.dma(SP, lambda e: e.dma_start(out=ident[:], in_=ident_d[:]), writes=[identB])
    k.dma(SP, lambda e: e.dma_start(out=pmf[:], in_=pm_d[:]), writes=[pmB])
    k.dma(SP, lambda e: e.dma_start(out=bsprow[:], in_=b_sp_d[:]), writes=[rowB])

    cast_wkv()
    epsT = sb("epsT", [128, 1])
    epsB = Buf("eps")
    k.op(DVE, lambda e: e.memset(epsT[:], EPS), writes=[epsB])
    neghalf = sb("neghalf", [128, 4])
    k.op(DVE, lambda e: e.memset(neghalf[:], -0.5), reads=[epsB], writes=[epsB])

    k.op(POOL, lambda e: e.memset(ones[:], 1.0), writes=[onesB])
    k.op(POOL, lambda e: e.memset(bones[:], 0.0), writes=[onesB])
    k.op(POOL, lambda e: e.memset(bones[0:64, 0:64], 1.0), writes=[onesB])
    k.op(POOL, lambda e: e.memset(bones[64:128, 64:128], 1.0), writes=[onesB])
    k.op(POOL, lambda e: e.memset(onesrow[:], 1.0), writes=[rowB])

    k.op(POOL, lambda e: e.memset(VA[:, :, :, 64:128], 1.0), writes=[VAB])
    k.op(DVE, lambda e: e.tensor_copy(out=pmb[:], in_=pmf[:]), reads=[pmB], writes=[pmB])
    k.op(DVE, lambda e: e.tensor_copy(out=identb[:], in_=ident[:]), reads=[identB], writes=[identB])

    for g in range(4):
        k.op(PE, lambda e, g=g: e.transpose(out=PS[0][:, g * 128:(g + 1) * 128], in_=vstage[:, g, :], identity=ident[:]),
             reads=[vstageB, identB], writes=[PSB[0]])
    k.op(DVE, lambda e: e.tensor_copy(out=vecT[:], in_=PS[0][:]), reads=[PSB[0]], writes=[vecB])

    k.dma(SP, lambda e: e.dma_start(out=vstage[:], in_=w_sp_d.rearrange("g p q -> p g q")), reads=[vstageB], writes=[vstageB])
    for g in range(4):
        k.op(PE, lambda e, g=g: e.transpose(out=PS[1][:, g * 128:(g + 1) * 128], in_=vstage[:, g, :], identity=ident[:]),
             reads=[vstageB, identB], writes=[PSB[1]])
    k.op(DVE, lambda e: e.tensor_copy(out=wspT[:].rearrange("p g q -> p (g q)"), in_=PS[1][:]), reads=[PSB[1]], writes=[wspB])

    k.op(ACT, lambda e: e.activation(out=st[:, 0:2], in_=vecT[:, VEC_ROWS["qg"]:VEC_ROWS["qg"] + 2], func=AF.Abs),
         reads=[vecB], writes=[stB])
    k.op(PE, lambda e: e.transpose(out=PS[2][0:2, 0:128], in_=st[:, 0:2], identity=ident[:]), reads=[stB, identB], writes=[PSB[2]])
    k.op(DVE, lambda e: e.tensor_reduce(out=st[0:2, 2:3], in_=PS[2][0:2, 0:128], axis=AX.X, op=ALU.max), reads=[PSB[2], stB], writes=[stB])
    k.op(PE, lambda e: e.transpose(out=PS[2][0:1, 128:130], in_=st[0:2, 2:3], identity=ident[0:2, 0:2]), reads=[stB, identB], writes=[PSB[2]])
    k.op(DVE, lambda e: e.tensor_copy(out=gtmp[0:1, 0:2], in_=PS[2][0:1, 128:130]), reads=[PSB[2]], writes=[stB])
    k.op(DVE, lambda e: e.tensor_tensor(out=gtmp[0:1, 2:3], in0=gtmp[0:1, 0:1], in1=gtmp[0:1, 1:2], op=ALU.mult), reads=[stB], writes=[stB])
    k.op(PE, lambda e: e.matmul(PS[2][:, 256:257], lhsT=onesrow[0:1, :], rhs=gtmp[0:1, 2:3], start=True, stop=True), reads=[stB, rowB], writes=[PSB[2]])
    k.op(ACT, lambda e: e.activation(out=nshift[:], in_=PS[2][:, 256:257], func=AF.Copy, scale=-8.0), reads=[PSB[2]], writes=[nshiftB])

    k.op(ACT, lambda e: e.activation(out=scb[:, :, 0], in_=vecT[:, VEC_ROWS["c"]:VEC_ROWS["c"] + 8], func=AF.Silu), reads=[vecB], writes=[scbB])
    k.op(ACT, lambda e: e.activation(out=scb[:, :, 1], in_=vecT[:, VEC_ROWS["cctx"]:VEC_ROWS["cctx"] + 8], func=AF.Silu), reads=[vecB, scbB], writes=[scbB])

    def mod_pieces(l, pcs, mbank):
        for pc in pcs:
            w, wb = wnext(("mod", l, pc))
            wv = r3(w, 8, 512)
            for fc in range(4):
                idx = l * 48 + pc * 4 + fc
                for kk in range(8):
                    k.op(PE, lambda e, wv=wv, fc=fc, kk=kk, idx=idx: e.matmul(
                        PS[mbank][:, idx * 2:idx * 2 + 2], lhsT=wv[:, kk, fc * 128:(fc + 1) * 128], rhs=scb[:, kk, :],
                        start=(kk == 0), stop=(kk == 7)), reads=[wb, scbB], writes=[PSB[mbank]])

    def mod_finish(l, mbank, i0=0, i1=48):
        for col in range(2):
            k.op(DVE, lambda e, col=col: e.tensor_tensor(
                out=modsb[:, l * 48 + i0:l * 48 + i1, col],
                in0=PS[mbank][:, (l * 48 + i0) * 2:(l * 48 + i1) * 2].rearrange("p (i c) -> p i c", c=2)[:, :, col],
                in1=vecT[:, VEC_ROWS["bmod"] + l * 48 + i0:VEC_ROWS["bmod"] + l * 48 + i1], op=ALU.add), reads=[PSB[mbank], vecB, modB], writes=[modB])

    mod_pieces(0, range(4), 3)
    mod_finish(0, 3, 0, 16)
    cast_all_early()
    cast_batches = [lambda: cast_fin(0, range(0, 6)), lambda: cast_fin(0, range(6, 11)), lambda: cast_fout(0), cast_pw,
                    lambda: cast_fin(1, range(0, 6)), lambda: cast_fin(1, range(6, 11)), lambda: cast_fout(1)]

    def coef_A(name, l, which, gname, col=0):
        c0 = CO[name]
        k.op(DVE, lambda e: e.scalar_tensor_tensor(
            out=coef[:, c0:c0 + 8], in0=modsb[:, l * 48 + which * 8: l * 48 + which * 8 + 8, col], scalar=1.0,
            in1=vecT[:, VEC_ROWS[gname] + l * 8: VEC_ROWS[gname] + l * 8 + 8], op0=ALU.add, op1=ALU.mult),
            reads=[modB, vecB, coefB], writes=[coefB])

    coef_A("A1_0", 0, SC1, "gmix")
    coef_A("A1c", 0, SC1, "gmix", col=1)

    iostate = {"n": 0}

    def load_tile_T(src_rows, ntok, hbuf, hbufB):
        for m in range(ntok // 128):
            ib = iostate["n"] % 2
            iostate["n"] += 1
            k.dma(SP, lambda e, ib=ib, m=m: e.dma_start(out=io[ib][:], in_=src_rows[m * 128:(m + 1) * 128, :]), writes=[ioB[ib]])
            for half in range(2):
                bank = 6 + half
                for kk in range(4):
                    kc = half * 4 + kk
                    k.op(PE, lambda e, ib=ib, kc=kc, kk=kk, bank=bank: e.transpose(
                        out=PS[bank][:, kk * 128:(kk + 1) * 128], in_=io[ib][:, kc * 128:(kc + 1) * 128], identity=ident[:]),
                        reads=[ioB[ib], identB], writes=[PSB[bank]])
                eng = ACT if half == 0 else DVE
                if eng == ACT:
                    k.op(ACT, lambda e, half=half, m=m, bank=bank: e.activation(
                        out=hbuf[:, half * 4:half * 4 + 4, m * 128:(m + 1) * 128],
                        in_=PS[bank][:].rearrange("p (a t) -> p a t", a=4), func=AF.Copy),
                        reads=[PSB[bank]], writes=hbufB[half * 4:half * 4 + 4])
                else:
                    k.op(DVE, lambda e, half=half, m=m, bank=bank: e.tensor_copy(
                        out=hbuf[:, half * 4:half * 4 + 4, m * 128:(m + 1) * 128],
                        in_=PS[bank][:].rearrange("p (a t) -> p a t", a=4)),
                        reads=[PSB[bank]], writes=hbufB[half * 4:half * 4 + 4])

    def rms_stats(hbuf, hbufB, ntok, bank, sd_=None, sd_B=None, rs_=None, rs_B=None):
        if sd_ is None:
            sd_, sd_B, rs_, rs_B = sd[:], [sdB], rstd[:], [rstdB]
        for kc in range(8):
            s = kc % 2
            k.op(ACT, lambda e, kc=kc, s=s: e.activation(out=sq[s][:, 0:ntok], in_=hbuf[:, kc, 0:ntok], func=AF.Square),
                 reads=[hbufB[kc]], writes=[sqB[s]])
            k.op(PE, lambda e, kc=kc, s=s: e.matmul(PS[bank][:, 0:ntok], lhsT=ones[:], rhs=sq[s][:, 0:ntok],
                                                    start=(kc == 0), stop=(kc == 7)),
                 reads=[sqB[s], onesB], writes=[PSB[bank]])
        k.op(ACT, lambda e: e.activation(out=sd_[:, 0:ntok], in_=PS[bank][:, 0:ntok], func=AF.Ln, scale=1.0 / D, bias=epsT[:]),
             reads=[PSB[bank], epsB], writes=sd_B)
        k.op(ACT, lambda e: e.activation(out=rs_[:, 0:ntok], in_=sd_[:, 0:ntok], func=AF.Exp, scale=-0.5), reads=sd_B, writes=rs_B)

    tmpstate = {"n": 0}

    def norm_mod(hbuf, hbufB, ntok, Acol, Bcol, bank, ve=POOL, xm_=None, xm_B=None, sd_=None, sd_B=None, rs_=None, rs_B=None):
        if xm_ is None:
            xm_, xm_B = xmT, xmB
        if sd_ is None:
            sd_, sd_B, rs_, rs_B = sd[:], [sdB], rstd[:], [rstdB]
        rms_stats(hbuf, hbufB, ntok, bank, sd_, sd_B, rs_, rs_B)
        for kc in range(8):
            ti = tmpstate["n"] % 3
            tmpstate["n"] += 1
            k.op(DVE if (ve == DVE or kc % 3 != 2) else POOL, lambda e, kc=kc, ti=ti: e.tensor_tensor(out=tmp[ti][:, 0:ntok], in0=hbuf[:, kc, 0:ntok], in1=rs_[:, 0:ntok], op=ALU.mult),
                 reads=[hbufB[kc]] + rs_B, writes=[tmpB[ti]])
            k.op(ACT, lambda e, kc=kc, ti=ti: e.activation(out=xm_[:, kc, 0:ntok], in_=tmp[ti][:, 0:ntok], func=AF.Identity,
                                                           scale=Acol(kc), bias=Bcol(kc)),
                 reads=[tmpB[ti], coefB, modB], writes=[xm_B[kc]])

    def headnorm_rstd(src_ps, src_psB, ntok, bank, sqi=0, sd_=None, sd_B=None, rs_=None, rs_B=None):
        if sd_ is None:
            sd_, sd_B, rs_, rs_B = sd[:], [sdB], rstd[:], [rstdB]
        k.op(ACT, lambda e: e.activation(out=sq[sqi][:, 0:ntok], in_=src_ps[:, 0:ntok], func=AF.Square), reads=[src_psB], writes=[sqB[sqi]])
        k.op(PE, lambda e: e.matmul(PS[bank][:, 0:ntok], lhsT=bones[:], rhs=sq[sqi][:, 0:ntok], start=True, stop=True),
             reads=[sqB[sqi], onesB], writes=[PSB[bank]])
        k.op(ACT, lambda e: e.activation(out=sd_[:, 0:ntok], in_=PS[bank][:, 0:ntok], func=AF.Ln, scale=1.0 / 64, bias=epsT[:]),
             reads=[PSB[bank], epsB], writes=sd_B)
        k.op(ACT, lambda e: e.activation(out=rs_[:, 0:ntok], in_=sd_[:, 0:ntok], func=AF.Exp, scale=-0.5), reads=sd_B, writes=rs_B)

    def rope(src_bf, src_B, dst, dst_B, bank, ve=POOL, t0=None, t0B=None, t1=None, t1B=None):
        if t0 is None:
            t0, t0B, t1, t1B = tmp[0][:], [tmpB[0]], tmp[1][:], [tmpB[1]]
        k.op(PE, lambda e: e.matmul(PS[bank][:], lhsT=pmb[:], rhs=src_bf, start=True, stop=True), reads=[src_B, pmB], writes=[PSB[bank]])
        k.op(ve, lambda e: e.tensor_tensor(out=t0, in0=src_bf, in1=cs[:, 0, :], op=ALU.mult), reads=[src_B, csB], writes=t0B)
        k.op(DVE, lambda e: e.tensor_tensor(out=t1, in0=PS[bank][:], in1=cs[:, 1, :], op=ALU.mult), reads=[PSB[bank], csB], writes=t1B)
        k.op(ve, lambda e: e.tensor_tensor(out=dst, in0=t0, in1=t1, op=ALU.add), reads=t0B + t1B, writes=dst_B)


    def dump(name, ap_fn, bufs):
        if dbg and name in dbg_d:
            k.dma(SP, lambda e: e.dma_start(out=dbg_d[name], in_=ap_fn()), reads=bufs, track=dumpB)

    dumpB = Buf("dump")

    wkv_t = sb("wkv", [128, 8 * 256], BF16)
    wkvB = Buf("wkv")
    k.dma(SP, lambda e: e.dma_start(out=wkv_t[:], in_=scr[("wkv",)][:, 0:8 * 256]), reads=[scrB["wkv"]], writes=[wkvB])
    wkvv = r3(wkv_t[:], 8, 256)

    def phaseA(src_rows, ntok, key_off, latent, tile_i):
        par = (tile_i + (1 if latent else 0)) % 2
        hb, hbB = hT[par], hB[par]
        if par == 0:
            xm_, xm_B = xmT, xmB
            nsd, nsdB, nrs, nrsB = sd[:], [sdB], rstd[:], [rstdB]
            kn_, kn_B = knT[:], knB
        else:
            xm_, xm_B = AR[:, 0:8, :], ARB[0:8]
            nsd, nsdB = ar32(4), [ARB[8], ARB[9]]
            nrs, nrsB = ar32(5), [ARB[10], ARB[11]]
            kn_, kn_B = ar(16), ARB[16]
        hsd, hsdB = ar32(6), [ARB[12], ARB[13]]
        hrs, hrsB = ar32(7), [ARB[14], ARB[15]]
        load_tile_T(src_rows, ntok, hb, hbB)
        if latent:
            norm_mod(hb, hbB, ntok, lambda kc: C("A1_0", kc), lambda kc: M(0, SH1, kc), 0, ve=DVE, xm_=xm_, xm_B=xm_B, sd_=nsd, sd_B=nsdB, rs_=nrs, rs_B=nrsB)
        else:
            norm_mod(hb, hbB, ntok, lambda kc: C("A1c", kc), lambda kc: M(0, SH1, kc, 1), 0, ve=DVE, xm_=xm_, xm_B=xm_B, sd_=nsd, sd_B=nsdB, rs_=nrs, rs_B=nrsB)
        for kc in range(8):
            k.op(PE, lambda e, kc=kc: e.matmul(PS[1][:, 0:ntok], lhsT=wkvv[:, kc, 0:128], rhs=xm_[:, kc, 0:ntok], start=(kc == 0), stop=(kc == 7)),
                 reads=[wkvB, xm_B[kc]], writes=[PSB[1]])
        headnorm_rstd(PS[1], PSB[1], ntok, 2, sqi=par, sd_=hsd, sd_B=hsdB, rs_=hrs, rs_B=hrsB)
        dstK = KT[:, key_off:key_off + ntok] if not latent else kn_[:, 0:ntok]
        dstKB = [KTB] if not latent else [kn_B]
        k.op(DVE, lambda e: e.scalar_tensor_tensor(out=dstK, in0=PS[1][:, 0:ntok], scalar=V("kg"), in1=hrs[:, 0:ntok], op0=ALU.mult, op1=ALU.mult),
             reads=[PSB[1], vecB] + hrsB, writes=dstKB)
        if latent:
            k.dma(SP, lambda e: e.dma_start(out=cs[:, 0, :], in_=cos_d[:, tile_i * T:(tile_i + 1) * T]), writes=[csB])
            k.dma(SP, lambda e: e.dma_start(out=cs[:, 1, :], in_=sin_d[:, tile_i * T:(tile_i + 1) * T]), reads=[], writes=[csB])
            rope(kn_, kn_B, KT[:, key_off:key_off + ntok], [KTB], 3, ve=DVE)
        nsub = ntok // 128
        for m in range(nsub):
            for kc in range(8):
                k.op(PE, lambda e, kc=kc, m=m: e.matmul(PS[4][:, m * 128:(m + 1) * 128], lhsT=xm_[:, kc, m * 128:(m + 1) * 128], rhs=wkvv[:, kc, 128:256],
                                                        start=(kc == 0), stop=(kc == 7)),
                     reads=[wkvB, xm_B[kc]], writes=[PSB[4]])
        j0 = key_off // 128
        k.op(ACT, lambda e: e.activation(out=VA[:, j0:j0 + nsub, :, 0:64],
                                         in_=PS[4][:, 0:nsub * 128].rearrange("p (j h d) -> p j h d", j=nsub, h=2, d=64), func=AF.Copy),
             reads=[PSB[4]], writes=[VAB])

    phaseA(ctx_d, CTX, 0, False, 0)
    mod_todo = [(0, pc) for pc in range(4, 12)] + [(1, pc) for pc in range(12)]
    for i in range(NT):
        for _ in range(3):
            if mod_todo:
                l_, pc_ = mod_todo.pop(0)
                mod_pieces(l_, [pc_], 5)
        if NT < 8 and cast_batches:
            cast_batches.pop(0)()
        phaseA(x_d[i * T:(i + 1) * T, :], T, CTX + i * T, True, i)
    while mod_todo:
        l_, pc_ = mod_todo.pop(0)
        mod_pieces(l_, [pc_], 5)
    mod_finish(0, 5, 16, 48)
    coef_A("A2_0", 0, SC2, "gffn")
    late_casts = []
    head_casts = []
    while cast_batches:
        b = cast_batches.pop(0)
        if NT >= 8 and len(cast_batches) < 3:
            late_casts.append(b)
        elif NT >= 8:
            head_casts.append(b)
        else:
            b()
    mod_finish(1, 5)


    def ffn(l, hb, hbB):
        A2 = "A2_0" if l == 0 else "A2_1"
        norm_mod(hb, hbB, T, lambda kc: C(A2, kc), lambda kc: M(l, SH2, kc), 0)
        for pp in range(11):
            w, wb = wnext(("fin", l, pp))
            wv = w.rearrange("p (k s c) -> p k s c", k=8, s=2, c=256)
            if pp == 0:
                for kc in range(8):
                    for jj in range(2):
                        for s2 in range(2):
                            bank = jj * 2 + s2
                            k.op(PE, lambda e, kc=kc, s2=s2, bank=bank, jj=jj, wv=wv: e.matmul(
                                PS[bank][:], lhsT=wv[:, kc, s2, jj * 128:(jj + 1) * 128], rhs=xmT[:, kc, :], start=(kc == 0), stop=(kc == 7)),
                                reads=[wb, xmB[kc]], writes=[PSB[bank]])
            for jj in range(2):
                j = pp * 2 + jj
                gb_, ub_ = (0, 1) if j % 2 == 0 else (2, 3)
                for s2, bank in ((0, gb_), (1, ub_)):
                    if pp == 0:
                        break
                    for kc in range(8):
                        k.op(PE, lambda e, kc=kc, s2=s2, bank=bank, jj=jj, wv=wv: e.matmul(
                            PS[bank][:], lhsT=wv[:, kc, s2, jj * 128:(jj + 1) * 128], rhs=xmT[:, kc, :], start=(kc == 0), stop=(kc == 7)),
                            reads=[wb, xmB[kc]], writes=[PSB[bank]])
                si = j % 2
                k.op(ACT, lambda e, si=si, gb_=gb_: e.activation(out=sgl[si][:], in_=PS[gb_][:], func=AF.Silu), reads=[PSB[gb_]], writes=[sglB[si]])
                k.op(DVE, lambda e, si=si, ub_=ub_, j=j: e.tensor_tensor(out=ar(j), in0=PS[ub_][:], in1=sgl[si][:], op=ALU.mult),
                     reads=[PSB[ub_], sglB[si]], writes=[ARB[j]])
        for f in range(8):
            w, wb = wnext(("fout", l, f))
            wv = r3(w[:, 0:NH * 128], NH, 128)
            bank = 4 + f % 4
            for j in range(NH):
                k.op(PE, lambda e, j=j, wv=wv, bank=bank: e.matmul(PS[bank][:], lhsT=wv[:, j, :], rhs=ar(j), start=(j == 0), stop=(j == NH - 1)),
                     reads=[wb, ARB[j]], writes=[PSB[bank]])
            k.op(DVE, lambda e, f=f, bank=bank: e.scalar_tensor_tensor(out=hb[:, f, :], in0=PS[bank][:], scalar=M(l, GT2, f), in1=hb[:, f, :],
                                                                      op0=ALU.mult, op1=ALU.add),
                 reads=[PSB[bank], modB, hbB[f]], writes=[hbB[f]])

    def layer0(i):
        hb, hbB = hT[i % 2], hB[i % 2]
        if i == 0:
            for b_ in head_casts:
                b_()
        load_tile_T(x_d[i * T:(i + 1) * T, :], T, hb, hbB)
        norm_mod(hb, hbB, T, lambda kc: C("A1_0", kc), lambda kc: M(0, SH1, kc), 0)
        k.dma(SP, lambda e: e.dma_start(out=cs[:, 0, :], in_=cos_d[:, i * T:(i + 1) * T]), writes=[csB])
        k.dma(SP, lambda e: e.dma_start(out=cs[:, 1, :], in_=sin_d[:, i * T:(i + 1) * T]), writes=[csB])
        w, wq_b = wnext(("wq",))
        wq_v = r3(w, 8, 512)
        SA, SBK = 6, 7

        def qchunk_gen(c):
            for kc in range(8):
                k.op(PE, lambda e, kc=kc: e.matmul(PS[SA][:], lhsT=wq_v[:, kc, c * 128:(c + 1) * 128], rhs=xmT[:, kc, :], start=(kc == 0), stop=(kc == 7)),
                     reads=[wq_b, xmB[kc]], writes=[PSB[SA]])
                if kc == 3:
                    yield
            yield
            k.op(DVE, lambda e: e.tensor_copy(out=tmp[2][:], in_=PS[SA][:]), reads=[PSB[SA]], writes=[tmpB[2]])
            yield
            k.op(POOL, lambda e: e.tensor_tensor(out=sq[0][:], in0=tmp[2][:], in1=tmp[2][:], op=ALU.mult), reads=[tmpB[2]], writes=[sqB[0]])
            yield
            k.op(PE, lambda e: e.matmul(PS[SBK][:], lhsT=bones[:], rhs=sq[0][:], start=True, stop=True), reads=[sqB[0], onesB], writes=[PSB[SBK]])
            yield
            k.op(ACT, lambda e: e.activation(out=sd[:], in_=PS[SBK][:], func=AF.Ln, scale=1.0 / 64, bias=epsT[:]), reads=[PSB[SBK], epsB], writes=[sdB])
            yield
            k.op(ACT, lambda e: e.activation(out=rstd[:], in_=sd[:], func=AF.Exp, scale=-0.5), reads=[sdB], writes=[rstdB])
            yield
            k.op(DVE, lambda e: e.scalar_tensor_tensor(out=knT[:], in0=PS[SA][:], scalar=V("qg"), in1=rstd[:], op0=ALU.mult, op1=ALU.mult),
                 reads=[PSB[SA], vecB, rstdB], writes=[knB])
            yield
            k.op(PE, lambda e: e.matmul(PS[SBK][:], lhsT=pmb[:], rhs=knT[:], start=True, stop=True), reads=[knB, pmB], writes=[PSB[SBK]])
            k.op(POOL, lambda e: e.tensor_tensor(out=tmp[0][:], in0=knT[:], in1=cs[:, 0, :], op=ALU.mult), reads=[knB, csB], writes=[tmpB[0]])
            yield
            k.op(DVE, lambda e: e.tensor_tensor(out=tmp[1][:], in0=PS[SBK][:], in1=cs[:, 1, :], op=ALU.mult), reads=[PSB[SBK], csB], writes=[tmpB[1]])
            yield
            k.op(POOL, lambda e: e.tensor_tensor(out=ar(QT_ + c), in0=tmp[0][:], in1=tmp[1][:], op=ALU.add), reads=[tmpB[0], tmpB[1]], writes=[ARB[QT_ + c]])
            yield

        def su_gen():
            w2, wb2 = wnext(("wsu",))
            wvs = r3(w2, 8, 512)
            for c in range(4):
                bank = SA + c % 2
                for kc in range(8):
                    k.op(PE, lambda e, kc=kc, c=c, bank=bank: e.matmul(PS[bank][:], lhsT=wvs[:, kc, c * 128:(c + 1) * 128], rhs=xmT[:, kc, :], start=(kc == 0), stop=(kc == 7)),
                         reads=[wb2, xmB[kc]], writes=[PSB[bank]])
                    if kc == 3:
                        yield
                yield
                k.op(ACT, lambda e, c=c, bank=bank: e.activation(out=ar(GU_ + c), in_=PS[bank][:], func=AF.Gelu_apprx_tanh), reads=[PSB[bank]], writes=[ARB[GU_ + c]])
                yield

        def sv_gen():
            w3, wb3 = wnext(("wsv",))
            wv2 = r3(w3, 8, 512)
            for m in range(4):
                bank = SA + m % 2
                mb = SBK - m % 2
                for kc in range(8):
                    k.op(PE, lambda e, kc=kc, m=m, bank=bank: e.matmul(PS[bank][:], lhsT=xmT[:, kc, m * 128:(m + 1) * 128], rhs=wv2[:, kc, :], start=(kc == 0), stop=(kc == 7)),
                         reads=[wb3, xmB[kc]], writes=[PSB[bank]])
                    if kc == 3:
                        yield
                yield
                k.op(ACT, lambda e, bank=bank: e.activation(out=gv[:], in_=PS[bank][:], func=AF.Gelu_apprx_tanh), reads=[PSB[bank]], writes=[gvB])
                yield
                k.op(POOL, lambda e: e.tensor_tensor(out=gv2[:], in0=gv[:], in1=gv[:], op=ALU.mult), reads=[gvB], writes=[gv2B])
                k.op(DVE, lambda e: e.tensor_reduce(out=st[:, 4:8], in_=gv[:].rearrange("p (g c) -> p g c", g=4), axis=AX.X, op=ALU.add), reads=[gvB, stB], writes=[stB])
                yield
                k.op(DVE, lambda e: e.tensor_reduce(out=st[:, 8:12], in_=gv2[:].rearrange("p (g c) -> p g c", g=4), axis=AX.X, op=ALU.add), reads=[gv2B, stB], writes=[stB])
                yield
                k.op(DVE, lambda e: e.tensor_scalar(out=st[:, 12:16], in0=st[:, 4:8], scalar1=1.0 / 128, scalar2=None, op0=ALU.mult), reads=[stB], writes=[stB])
                yield
                k.op(DVE, lambda e: e.tensor_tensor(out=st[:, 16:20], in0=st[:, 12:16], in1=st[:, 12:16], op=ALU.mult), reads=[stB], writes=[stB])
                yield
                k.op(DVE, lambda e: e.scalar_tensor_tensor(out=st[:, 20:24], in0=st[:, 8:12], scalar=1.0 / 128, in1=st[:, 16:20], op0=ALU.mult, op1=ALU.subtract),
                     reads=[stB], writes=[stB])
                yield
                k.op(ACT, lambda e: e.activation(out=st[:, 24:28], in_=st[:, 20:24], func=AF.Ln, bias=epsT[:]), reads=[stB, epsB], writes=[stB])
                yield
                k.op(ACT, lambda e: e.activation(out=st[:, 28:32], in_=st[:, 24:28], func=AF.Exp, scale=-0.5), reads=[stB], writes=[stB])
                yield
                for g in range(4):
                    k.op(DVE, lambda e, g=g: e.tensor_scalar(out=vn[:, g, :], in0=gv[:, g * 128:(g + 1) * 128], scalar1=st[:, 12 + g:13 + g], scalar2=st[:, 28 + g:29 + g],
                                                             op0=ALU.subtract, op1=ALU.mult), reads=[gvB, stB, vnB], writes=[vnB])
                yield
                k.op(PE, lambda e, mb=mb: e.matmul(PS[mb][:], lhsT=onesrow[0:1, :], rhs=bsprow[0:1, :], start=True, stop=False), reads=[rowB], writes=[PSB[mb]])
                for g in range(4):
                    k.op(PE, lambda e, g=g, mb=mb: e.matmul(PS[mb][:, g * 128:(g + 1) * 128], lhsT=vn[:, g, :], rhs=wspT[:, g, :], start=False, stop=(g == 3)),
                         reads=[vnB, wspB], writes=[PSB[mb]])
                yield
                k.op(DVE, lambda e, m=m, mb=mb: e.tensor_tensor(out=AR[:, SG_:SG_ + 4, m * 128:(m + 1) * 128], in0=PS[mb][:].rearrange("p (g t) -> p g t", g=4),
                                                                in1=AR[:, GU_:GU_ + 4, m * 128:(m + 1) * 128], op=ALU.mult),
                     reads=[PSB[mb]] + ARB[GU_:GU_ + 4], writes=ARB[SG_:SG_ + 4])
                yield

        qdone = {0: False, 1: False, 2: False, 3: False}

        def side_gen():
            for c in (1, 2, 3):
                yield from qchunk_gen(c)
                qdone[c] = True

        for _ in qchunk_gen(0):
            pass
        qdone[0] = True
        side = side_gen()
        side_alive = [True]

        def pull():
            if side_alive[0]:
                try:
                    next(side)
                except StopIteration:
                    side_alive[0] = False

        def qk(c, j):
            for hh in range(2):
                bank = hh * 2 + j % 2
                ps_ = slice(hh * 64, (hh + 1) * 64)
                k.op(PE, lambda e, bank=bank, ps_=ps_, j=j, c=c: e.matmul(PS[bank][:], lhsT=KT[ps_, j * 128:(j + 1) * 128], rhs=AR[ps_, QT_ + c, :],
                                                                    start=True, stop=True),
                     reads=[KTB, ARB[QT_ + c]], writes=[PSB[bank]])

        its = [(c, j) for c in range(4) for j in range(NKT)]
        qk(0, 0)
        for n, (c, j) in enumerate(its):
            if n + 1 < len(its):
                c2, j2 = its[n + 1]
                while not qdone[c2]:
                    pull()
                qk(c2, j2)
            for hh in range(2):
                bank = hh * 2 + j % 2
                pt = PT_ + hh * 2 + j % 2
                k.op(ACT, lambda e, bank=bank, pt=pt: e.activation(out=ar(pt), in_=PS[bank][:], func=AF.Exp, scale=0.125, bias=SHIFT),
                     reads=[PSB[bank]], writes=[ARB[pt]])
            for hh in range(2):
                pt = PT_ + hh * 2 + j % 2
                k.op(PE, lambda e, hh=hh, pt=pt, j=j: e.matmul(PS[4 + hh][:], lhsT=VA[:, j, hh, :], rhs=ar(pt), start=(j == 0), stop=(j == NKT - 1)),
                     reads=[VAB, ARB[pt]], writes=[PSB[4 + hh]])
            if j == NKT - 1:
                for hh in range(2):
                    k.op(DVE, lambda e, hh=hh: e.tensor_copy(out=sgl[hh][:], in_=PS[4 + hh][:]), reads=[PSB[4 + hh]], writes=[sglB[hh]])
                for hh in range(2):
                    k.op(DVE, lambda e, hh=hh: e.reciprocal(out=ar32(10)[0:64, :], in_=sgl[hh][64:128, :]), reads=[sglB[hh]], writes=[ARB[20], ARB[21]])
                    k.op(DVE, lambda e, hh=hh, c=c: e.tensor_tensor(out=AR[hh * 64:(hh + 1) * 64, AT_ + c, :], in0=sgl[hh][0:64, :], in1=ar32(10)[0:64, :], op=ALU.mult),
                         reads=[sglB[hh], ARB[20], ARB[21]], writes=[ARB[AT_ + c]])
            if (n + 1) % DRIP == 0:
                pull()
        while side_alive[0]:
            pull()
        if i == 0:
            for b in late_casts:
                b()
        w, wb = wnext(("wsu",))
        wv = r3(w, 8, 512)
        for c in range(4):
            bank = c % 2
            for kc in range(8):
                k.op(PE, lambda e, kc=kc, c=c, bank=bank, wv=wv: e.matmul(PS[bank][:], lhsT=wv[:, kc, c * 128:(c + 1) * 128], rhs=xmT[:, kc, :], start=(kc == 0), stop=(kc == 7)),
                     reads=[wb, xmB[kc]], writes=[PSB[bank]])
            k.op(ACT, lambda e, c=c, bank=bank: e.activation(out=ar(GU_ + c), in_=PS[bank][:], func=AF.Gelu_apprx_tanh), reads=[PSB[bank]], writes=[ARB[GU_ + c]])
        w, wb = wnext(("wsv",))
        wv2 = r3(w, 8, 512)
        for m in range(4):
            bank = 2 + m % 2
            for kc in range(8):
                k.op(PE, lambda e, kc=kc, m=m, bank=bank, wv2=wv2: e.matmul(PS[bank][:], lhsT=xmT[:, kc, m * 128:(m + 1) * 128], rhs=wv2[:, kc, :], start=(kc == 0), stop=(kc == 7)),
                     reads=[wb, xmB[kc]], writes=[PSB[bank]])
            if m % 2 == 0:
                gv_, gv_B, gv2_, gv2_B, vn_, vn_B, st_, st_B = gv[:], [gvB], gv2[:], [gv2B], vn[:], [vnB], st, stB
            else:
                gv_, gv_B = ar32(0), [ARB[0], ARB[1]]
                gv2_, gv2_B = ar32(1), [ARB[2], ARB[3]]
                vn_, vn_B = AR[:, 16, :].rearrange("p (g c) -> p g c", g=4), [ARB[16]]
                st_, st_B = st2, st2B
            k.op(ACT, lambda e, bank=bank, gv_=gv_: e.activation(out=gv_, in_=PS[bank][:], func=AF.Gelu_apprx_tanh), reads=[PSB[bank]], writes=gv_B)
            k.op(POOL, lambda e, gv_=gv_, gv2_=gv2_: e.tensor_tensor(out=gv2_, in0=gv_, in1=gv_, op=ALU.mult), reads=gv_B, writes=gv2_B)
            k.op(DVE, lambda e, gv_=gv_, st_=st_: e.tensor_reduce(out=st_[:, 4:8], in_=gv_.rearrange("p (g c) -> p g c", g=4), axis=AX.X, op=ALU.add), reads=gv_B + [st_B], writes=[st_B])
            k.op(DVE, lambda e, gv2_=gv2_, st_=st_: e.tensor_reduce(out=st_[:, 8:12], in_=gv2_.rearrange("p (g c) -> p g c", g=4), axis=AX.X, op=ALU.add), reads=gv2_B + [st_B], writes=[st_B])
            k.op(DVE, lambda e, st_=st_: e.tensor_scalar(out=st_[:, 12:16], in0=st_[:, 4:8], scalar1=1.0 / 128, scalar2=None, op0=ALU.mult), reads=[st_B], writes=[st_B])
            k.op(DVE, lambda e, st_=st_: e.tensor_tensor(out=st_[:, 16:20], in0=st_[:, 12:16], in1=st_[:, 12:16], op=ALU.mult), reads=[st_B], writes=[st_B])
            k.op(DVE, lambda e, st_=st_: e.scalar_tensor_tensor(out=st_[:, 20:24], in0=st_[:, 8:12], scalar=1.0 / 128, in1=st_[:, 16:20], op0=ALU.mult, op1=ALU.subtract),
                 reads=[st_B], writes=[st_B])
            k.op(DVE, lambda e, st_=st_: e.tensor_scalar(out=st_[:, 24:28], in0=st_[:, 20:24], scalar1=EPS, scalar2=None, op0=ALU.add), reads=[st_B], writes=[st_B])
            k.op(POOL, lambda e, st_=st_: e.tensor_tensor(out=st_[:, 28:32], in0=st_[:, 24:28], in1=neghalf[:, 0:4], op=ALU.pow), reads=[st_B, epsB], writes=[st_B])
            for g in range(4):
                k.op(DVE, lambda e, g=g, gv_=gv_, vn_=vn_, st_=st_: e.tensor_scalar(out=vn_[:, g, :], in0=gv_[:, g * 128:(g + 1) * 128], scalar1=st_[:, 12 + g:13 + g], scalar2=st_[:, 28 + g:29 + g],
                                                         op0=ALU.subtract, op1=ALU.mult), reads=gv_B + [st_B] + vn_B, writes=vn_B)
            mb = 4 + m % 2
            k.op(PE, lambda e, mb=mb: e.matmul(PS[mb][:], lhsT=onesrow[0:1, :], rhs=bsprow[0:1, :], start=True, stop=False), reads=[rowB], writes=[PSB[mb]])
            for g in range(4):
                k.op(PE, lambda e, g=g, mb=mb, vn_=vn_: e.matmul(PS[mb][:, g * 128:(g + 1) * 128], lhsT=vn_[:, g, :], rhs=wspT[:, g, :], start=False, stop=(g == 3)),
                     reads=vn_B + [wspB], writes=[PSB[mb]])
            k.op(DVE, lambda e, m=m, mb=mb: e.tensor_tensor(out=AR[:, SG_:SG_ + 4, m * 128:(m + 1) * 128], in0=PS[mb][:].rearrange("p (g t) -> p g t", g=4),
                                                            in1=AR[:, GU_:GU_ + 4, m * 128:(m + 1) * 128], op=ALU.mult),
                 reads=[PSB[mb]] + ARB[GU_:GU_ + 4], writes=ARB[SG_:SG_ + 4])
        for pc in range(2):
            w, wb = wnext(("wo", pc))
            wv3 = r3(w, 8, 512)
            for ff in range(4):
                f = pc * 4 + ff
                bank = 6 + f % 2
                for kk in range(8):
                    src = AT_ + kk if kk < 4 else SG_ + kk - 4
                    k.op(PE, lambda e, kk=kk, ff=ff, src=src, bank=bank, wv3=wv3: e.matmul(PS[bank][:], lhsT=wv3[:, kk, ff * 128:(ff + 1) * 128], rhs=ar(src),
                                                                                         start=(kk == 0), stop=(kk == 7)),
                         reads=[wb, ARB[src]], writes=[PSB[bank]])
                k.op(DVE, lambda e, f=f, bank=bank: e.scalar_tensor_tensor(out=hb[:, f, :], in0=PS[bank][:], scalar=M(0, GT1, f), in1=hb[:, f, :],
                                                                          op0=ALU.mult, op1=ALU.add),
                     reads=[PSB[bank], modB, hbB[f]], writes=[hbB[f]])
        ffn(0, hb, hbB)

    def layer1a(i):
        hb, hbB = hT[i % 2], hB[i % 2]
        gw, gwB = GW[i % 2], GWB[i % 2]
        norm_mod(hb, hbB, T, lambda kc: C("A1_1", kc), lambda kc: M(1, SH1, kc), 0)
        for pp in range(4):
            w, wb = wnext(("pw1", pp))
            wv = w.rearrange("p (k s c) -> p k s c", k=8, s=2, c=256)
            if pp == 0:
                for kc in range(8):
                    for jj in range(2):
                        for s2 in range(2):
                            bank = jj * 2 + s2
                            k.op(PE, lambda e, kc=kc, s2=s2, bank=bank, jj=jj, wv=wv: e.matmul(
                                PS[bank][:], lhsT=wv[:, kc, s2, jj * 128:(jj + 1) * 128], rhs=xmT[:, kc, :], start=(kc == 0), stop=(kc == 7)),
                                reads=[wb, xmB[kc]], writes=[PSB[bank]])
            for jj in range(2):
                c = pp * 2 + jj
                ab_, gb_ = (0, 1) if c % 2 == 0 else (2, 3)
                for s2, bank in ((0, ab_), (1, gb_)):
                    if pp == 0:
                        break
                    for kc in range(8):
                        k.op(PE, lambda e, kc=kc, s2=s2, bank=bank, jj=jj, wv=wv: e.matmul(
                            PS[bank][:], lhsT=wv[:, kc, s2, jj * 128:(jj + 1) * 128], rhs=xmT[:, kc, :], start=(kc == 0), stop=(kc == 7)),
                            reads=[wb, xmB[kc]], writes=[PSB[bank]])
                si = c % 2
                k.op(ACT, lambda e, si=si, gb_=gb_, c=c: e.activation(out=sgl[si][:], in_=PS[gb_][:], func=AF.Sigmoid, bias=V("bpw1", 8 + c)),
                     reads=[PSB[gb_], vecB], writes=[sglB[si]])
                k.op(DVE, lambda e, si=si, ab_=ab_, c=c: e.scalar_tensor_tensor(out=gw[:, c, 15:15 + T], in0=PS[ab_][:], scalar=V("bpw1", c), in1=sgl[si][:],
                                                                               op0=ALU.add, op1=ALU.mult),
                     reads=[PSB[ab_], vecB, sglB[si]], writes=[gwB[c]])
        if i == 0:
            k.op(POOL, lambda e: e.memset(gw[:, :, 0:15], 0.0), reads=gwB, writes=gwB)
        else:
            pg, pgB = GW[(i - 1) % 2], GWB[(i - 1) % 2]
            k.op(POOL, lambda e: e.tensor_copy(out=gw[:, :, 0:15], in_=pg[:, :, T:T + 15]), reads=pgB + gwB, writes=gwB)
            k.op(POOL, lambda e: e.tensor_copy(out=pg[:, :, T + 15:T + 30], in_=gw[:, :, 15:30]), reads=gwB + pgB, writes=pgB)
        if i == NT - 1:
            k.op(POOL, lambda e: e.memset(gw[:, :, T + 15:T + 30], 0.0), reads=gwB, writes=gwB)

    def layer1b(i):
        hb, hbB = hT[i % 2], hB[i % 2]
        gw, gwB = GW[i % 2], GWB[i % 2]
        wd0 = VEC_ROWS["wdw"]
        for f in range(8):
            k.op(DVE, lambda e, f=f: e.tensor_scalar(out=hb[:, f, :], in0=hb[:, f, :], scalar1=C("gb", f), scalar2=None, op0=ALU.add),
                 reads=[hbB[f], coefB], writes=[hbB[f]])
        def stat_mm(kc):
            k.op(PE, lambda e, kc=kc: e.matmul(PS[0][:], lhsT=ones[:], rhs=sq[0][:], start=(kc == 0), stop=(kc == 7)), reads=[sqB[0], onesB], writes=[PSB[0]])
            k.op(PE, lambda e, kc=kc: e.matmul(PS[1][:], lhsT=ones[:], rhs=sq[1][:], start=(kc == 0), stop=(kc == 7)), reads=[sqB[1], onesB], writes=[PSB[1]])

        for kc in range(8):
            acc = ar32(kc)
            accB = [ARB[2 * kc], ARB[2 * kc + 1]]
            bank = 2 + kc % 4
            for j in range(31):
                r = dgstate["n"] % NDG
                dgstate["n"] += 1
                k.op(POOL if j % 3 == 2 else DVE, lambda e, r=r, j=j, kc=kc: e.tensor_scalar(out=dg[r][:], in0=identb[:], scalar1=vecT[:, wd0 + j * 8 + kc:wd0 + j * 8 + kc + 1], scalar2=1.0,
                                                                     op0=ALU.mult, op1=ALU.mult),
                     reads=[identB, vecB], writes=[dgB[r]])
                k.op(PE, lambda e, r=r, j=j, kc=kc, bank=bank: e.matmul(PS[bank][:], lhsT=dg[r][:], rhs=gw[:, kc, j:j + T], start=(j == 0), stop=(j == 30)),
                     reads=[dgB[r], gwB[kc]], writes=[PSB[bank]])
            if kc > 0:
                stat_mm(kc - 1)
            k.op(DVE, lambda e, kc=kc, acc=acc, bank=bank: e.tensor_scalar(out=acc, in0=PS[bank][:], scalar1=V("bdw", kc), scalar2=None, op0=ALU.add),
                 reads=[PSB[bank], vecB], writes=accB)
            k.op(ACT, lambda e, acc=acc: e.activation(out=sq[0][:], in_=acc, func=AF.Copy), reads=accB, writes=[sqB[0]])
            k.op(ACT, lambda e, acc=acc: e.activation(out=sq[1][:], in_=acc, func=AF.Square), reads=accB, writes=[sqB[1]])
        stat_mm(7)
        mean, meanB = sgl[0], sglB[0]
        k.op(ACT, lambda e: e.activation(out=mean[:], in_=PS[0][:], func=AF.Copy, scale=1.0 / D), reads=[PSB[0]], writes=[meanB])
        k.op(POOL, lambda e: e.tensor_tensor(out=sgl[1][:], in0=mean[:], in1=mean[:], op=ALU.mult), reads=[meanB], writes=[sglB[1]])
        k.op(DVE, lambda e: e.scalar_tensor_tensor(out=sd[:], in0=PS[1][:], scalar=1.0 / D, in1=sgl[1][:], op0=ALU.mult, op1=ALU.subtract),
             reads=[PSB[1], sglB[1]], writes=[sdB])
        k.op(ACT, lambda e: e.activation(out=sd[:], in_=sd[:], func=AF.Ln, bias=epsT[:]), reads=[sdB, epsB], writes=[sdB])
        k.op(ACT, lambda e: e.activation(out=rstd[:], in_=sd[:], func=AF.Exp, scale=-0.5), reads=[sdB], writes=[rstdB])
        for kc in range(8):
            acc = ar32(kc)
            accB = [ARB[2 * kc], ARB[2 * kc + 1]]
            if kc % 3 == 2:
                ve_, ta, taB, tb, tbB = POOL, sgl[1], sglB[1], tmp[2], tmpB[2]
            else:
                ve_, ta, taB, tb, tbB = DVE, tmp[0], tmpB[0], tmp[1], tmpB[1]
            k.op(ve_, lambda e, acc=acc, ta=ta: e.tensor_tensor(out=ta[:], in0=acc, in1=mean[:], op=ALU.subtract), reads=accB + [meanB], writes=[taB])
            k.op(ve_, lambda e, ta=ta, tb=tb: e.tensor_tensor(out=tb[:], in0=ta[:], in1=rstd[:], op=ALU.mult), reads=[taB, rstdB], writes=[tbB])
            k.op(ACT, lambda e, kc=kc, tb=tb: e.activation(out=xmT[:, kc, :], in_=tb[:], func=AF.Silu, scale=V("lng", kc), bias=V("lnb", kc)),
                 reads=[tbB, vecB], writes=[xmB[kc]])
        for pc in range(2):
            w, wb = wnext(("pw2", pc))
            wv = r3(w, 8, 512)
            for ff in range(4):
                f = pc * 4 + ff
                bank = 6 + f % 2
                for kc in range(8):
                    k.op(PE, lambda e, kc=kc, ff=ff, bank=bank, wv=wv: e.matmul(PS[bank][:], lhsT=wv[:, kc, ff * 128:(ff + 1) * 128], rhs=xmT[:, kc, :],
                                                                              start=(kc == 0), stop=(kc == 7)),
                         reads=[wb, xmB[kc]], writes=[PSB[bank]])
                k.op(DVE, lambda e, f=f, bank=bank: e.scalar_tensor_tensor(out=hb[:, f, :], in0=PS[bank][:], scalar=M(1, GT1, f), in1=hb[:, f, :],
                                                                          op0=ALU.mult, op1=ALU.add),
                     reads=[PSB[bank], modB, hbB[f]], writes=[hbB[f]])
        ffn(1, hb, hbB)
        rms_stats(hb, hbB, T, 0)
        for kc in range(8):
            k.op(DVE, lambda e, kc=kc: e.scalar_tensor_tensor(out=hb[:, kc, :], in0=hb[:, kc, :], scalar=V("gfin", kc), in1=rstd[:], op0=ALU.mult, op1=ALU.mult),
                 reads=[hbB[kc], vecB, rstdB], writes=[hbB[kc]])
        for m in range(4):
            ib = iostate["n"] % 2
            iostate["n"] += 1
            for half in range(2):
                bank = 6 + half
                for kk in range(4):
                    kc = half * 4 + kk
                    k.op(PE, lambda e, kc=kc, kk=kk, bank=bank, m=m: e.transpose(out=PS[bank][:, kk * 128:(kk + 1) * 128], in_=hb[:, kc, m * 128:(m + 1) * 128],
                                                                               identity=ident[:]),
                         reads=[hbB[kc], identB], writes=[PSB[bank]])
                if half == 0:
                    k.op(ACT, lambda e, ib=ib, bank=bank: e.activation(out=io[ib][:, 0:512], in_=PS[bank][:], func=AF.Copy), reads=[PSB[bank]], writes=[ioB[ib]])
                else:
                    k.op(DVE, lambda e, ib=ib, bank=bank: e.tensor_copy(out=io[ib][:, 512:1024], in_=PS[bank][:]), reads=[PSB[bank], ioB[ib]], writes=[ioB[ib]])
            k.dma(SP, lambda e, ib=ib, m=m: e.dma_start(out=out_d[i * T + m * 128: i * T + (m + 1) * 128, :], in_=io[ib][:]), reads=[ioB[ib]])

    coef_A("A1_1", 1, SC1, "gmix")
    coef_A("A2_1", 1, SC2, "gffn")
    c0 = CO["gb"]
    k.op(DVE, lambda e: e.tensor_tensor(out=coef[:, c0:c0 + 8], in0=modsb[:, 48 + GT1 * 8:48 + GT1 * 8 + 8, 0],
                                        in1=vecT[:, VEC_ROWS["bpw2"]:VEC_ROWS["bpw2"] + 8], op=ALU.mult),
         reads=[modB, vecB, coefB], writes=[coefB])
    outB = Buf("out")
    for i in range(NT):
        layer0(i)
        layer1a(i)
        if i >= 1:
            layer1b(i - 1)
    layer1b(NT - 1)
    assert wstate["next"] == len(seq)

    print("sbuf bytes remaining/partition:", nc.sbuf_bytes_remaining)
    k.emit()
    k.finish(SP, ioB + [dumpB])
    return nc, k


def make_vecs(c_b, c_ctx, b_mod, g_mix, g_ffn, b_pw1, b_dw, ln_g, ln_b, b_pw2, g_final, w_dw, q_gain, k_gain):
    rows = [c_b.reshape(8, 128), c_ctx.reshape(8, 128), b_mod.reshape(96, 128), g_mix.reshape(16, 128), g_ffn.reshape(16, 128),
            b_pw1.reshape(16, 128), b_dw.reshape(8, 128), ln_g.reshape(8, 128), ln_b.reshape(8, 128), b_pw2.reshape(8, 128),
            g_final.reshape(8, 128), w_dw.reshape(248, 128),
            np.concatenate([q_gain.reshape(64), q_gain.reshape(64)])[None, :],
            np.concatenate([k_gain.reshape(64), k_gain.reshape(64)])[None, :]]
    v = np.concatenate(rows, 0).astype(np.float32)
    out = np.zeros((NVEC, 128), np.float32)
    out[:v.shape[0]] = v
    return out


def make_in_maps(x, c, ctx, c_ctx, w_mod, b_mod, g_mix, g_ffn, w_ffn_in, w_ffn_out, w_in, q_gain, k_gain, w_sp, b_sp, w_out,
                 w_pw1, b_pw1, w_dw, b_dw, ln_g, ln_b, w_pw2, b_pw2, g_final):
    f = lambda a: np.ascontiguousarray(np.asarray(a, dtype=np.float32))
    B, S, _ = x.shape
    cos, sin = rope_tables(S)
    shared = {
        "w_mod": f(w_mod), "w_ffn_in": f(w_ffn_in), "w_ffn_out": f(w_ffn_out), "w_in": f(w_in[0]),
        "w_sp": f(w_sp[0]), "b_sp": f(b_sp[0]).reshape(1, 512), "w_out": f(w_out[0]), "w_pw1": f(w_pw1[0]), "w_pw2": f(w_pw2[0]),
        "ident": np.eye(128, dtype=np.float32), "pm": rope_perm(), "cos": cos, "sin": sin,
    }
    maps = []
    for b in range(B):
        m = dict(shared)
        m["x"] = f(x[b])
        m["ctx"] = f(ctx[b])
        m["vecs"] = make_vecs(f(c[b]), f(c_ctx), f(b_mod), f(g_mix), f(g_ffn), f(b_pw1[0]), f(b_dw[0]), f(ln_g[0]), f(ln_b[0]),
                              f(b_pw2[0]), f(g_final), f(w_dw[0]), f(q_gain[0]), f(k_gain[0]))
        maps.append(m)
    return maps


_NC_CACHE = {}


def kernel(**inputs):
    x = np.asarray(inputs["x"])
    B, S, _ = x.shape
    maps = make_in_maps(**inputs)
    if S not in _NC_CACHE:
        _NC_CACHE[S] = build(S)[0]
    nc = _NC_CACHE[S]
    res = run_bass_kernel_spmd(nc, maps, core_ids=list(range(B)))
    return np.stack([np.asarray(r["out"], dtype=np.float32) for r in res.results], 0)
```

```python
import numpy as np
import concourse.bass as bass
import concourse.mybir as mybir
from concourse.bass_utils import run_bass_kernel_spmd

F32 = mybir.dt.float32
BF16 = mybir.dt.bfloat16
AF = mybir.ActivationFunctionType
ALU = mybir.AluOpType
AX = mybir.AxisListType

PE, ACT, DVE, POOL, SP = "pe", "act", "dve", "pool", "sp"

D = 1024
CTX = 256
DFF = 2816
NH = 22
T = 512
EPS = 1e-6
NSLOT = 6
DRIP = 2
SHIFT = -8.0
SLOT_ELEMS = 4096


class Buf:
    __slots__ = ("name", "lw", "rd", "dsem", "dcount")

    def __init__(self, name):
        self.name = name
        self.lw = None
        self.rd = {}
        self.dsem = None
        self.dcount = 0


class Op:
    __slots__ = ("eng", "fn", "deps", "is_dma", "buf", "sig", "sigval", "dval")

    def __init__(self, eng, fn, is_dma=False):
        self.eng = eng
        self.fn = fn
        self.deps = []
        self.is_dma = is_dma
        self.buf = None
        self.sig = False
        self.sigval = 0
        self.dval = 0


class K:
    def __init__(self, nc):
        self.nc = nc
        self.ops = []
        self.h = {PE: nc.tensor, ACT: nc.scalar, DVE: nc.vector, POOL: nc.gpsimd, SP: nc.sync}
        self.dsems = {}

    def op(self, eng, fn, reads=(), writes=()):
        o = Op(eng, fn)
        self._deps(o, reads, writes)
        self.ops.append(o)
        return o

    def dma(self, eng, fn, reads=(), writes=(), track=None, disjoint=False):
        o = Op(eng, fn, is_dma=True)
        if track is None:
            track = writes[0] if writes else reads[0]
        key = (track.name, eng)
        if key not in self.dsems:
            self.dsems[key] = [None, 0]
        o.buf = key
        self.dsems[key][1] += 16
        o.dval = self.dsems[key][1]
        self._deps(o, reads, writes, disjoint)
        self.ops.append(o)
        return o

    def _deps(self, o, reads, writes, disjoint=False):
        for b in reads:
            if b.lw is not None:
                o.deps.append(b.lw)
            b.rd[id(o) if o.is_dma else o.eng] = o
        for b in writes:
            if b.lw is not None and not (disjoint and b.lw.is_dma):
                o.deps.append(b.lw)
            for r in b.rd.values():
                if r is not o:
                    o.deps.append(r)
            b.lw = o
            b.rd = {}

    def emit(self):
        nc = self.nc
        for o in self.ops:
            nd = []
            for d in o.deps:
                if d.is_dma:
                    nd.append(d)
                    continue
                if d.eng == PE and o.eng == PE and not o.is_dma:
                    continue
                d.sig = True
                nd.append(d)
            o.deps = nd
        sems = {e: nc.alloc_semaphore(name=f"s_{e}") for e in self.h}
        for key, v in self.dsems.items():
            v[0] = nc.alloc_semaphore(name=f"d_{key[0]}_{key[1]}")
        cnt = {e: 0 for e in self.h}
        for o in self.ops:
            if o.sig and not o.is_dma:
                cnt[o.eng] += 1
                o.sigval = cnt[o.eng]
        waited = {e: {} for e in self.h}
        self.nwait = 0
        for o in self.ops:
            need = {}
            for d in o.deps:
                if d.is_dma:
                    key, val = self.dsems[d.buf][0], d.dval
                else:
                    key, val = sems[d.eng], d.sigval
                kid = id(key)
                if kid not in need or need[kid][1] < val:
                    need[kid] = (key, val)
            w = waited[o.eng]
            for kid, (key, val) in need.items():
                if w.get(kid, 0) >= val:
                    continue
                self.h[o.eng].wait_ge(key, val)
                w[kid] = val
                self.nwait += 1
            ins = o.fn(self.h[o.eng])
            if o.is_dma:
                ins.then_inc(self.dsems[o.buf][0], 16)
            elif o.sig:
                ins.then_inc(sems[o.eng], 1)

    def finish(self, eng, bufs):
        names = {b.name for b in bufs}
        for key, v in self.dsems.items():
            if key[0] in names:
                self.h[eng].wait_ge(v[0], v[1])


VEC_ROWS = {}
_r = 0
for _n, _c in [("c", 8), ("cctx", 8), ("bmod", 96), ("gmix", 16), ("gffn", 16), ("bpw1", 16), ("bdw", 8),
               ("lng", 8), ("lnb", 8), ("bpw2", 8), ("gfin", 8), ("wdw", 248), ("qg", 1), ("kg", 1)]:
    VEC_ROWS[_n] = _r
    _r += _c
NVEC = 512


def rope_tables(seq):
    t = np.arange(seq)
    row = (t // 64).astype(np.float32)
    col = (t % 64).astype(np.float32)
    inv = (10000.0 ** (-np.arange(0, 32, 2, dtype=np.float32) / 32.0)).astype(np.float32)
    ang_r = (row[:, None] * inv[None, :]).astype(np.float32)
    ang_c = (col[:, None] * inv[None, :]).astype(np.float32)
    cr, sr, cc, sc = np.cos(ang_r), np.sin(ang_r), np.cos(ang_c), np.sin(ang_c)
    cos64 = np.concatenate([cr, cr, cc, cc], axis=1).T
    sin64 = np.concatenate([-sr, sr, -sc, sc], axis=1).T
    cos = np.concatenate([cos64, cos64], 0).astype(np.float32)
    sin = np.concatenate([sin64, sin64], 0).astype(np.float32)
    return np.ascontiguousarray(cos), np.ascontiguousarray(sin)


def rope_perm():
    pm = np.zeros((128, 128), np.float32)
    for hh in range(2):
        for d in range(64):
            blk = d // 16
            src = d + 16 if blk in (0, 2) else d - 16
            pm[hh * 64 + src, hh * 64 + d] = 1.0
    return pm


def piece_sequence(nt):
    seq = [("mod", 0, pc) for pc in range(12)]
    seq += [("mod", 1, pc) for pc in range(12)]
    l0 = [("wq",), ("wsu",), ("wsv",), ("wo", 0), ("wo", 1)] + [("fin", 0, p) for p in range(11)] + \
         [("fout", 0, c) for c in range(8)]
    l1a = [("pw1", p) for p in range(4)]
    l1b = [("pw2", 0), ("pw2", 1)] + [("fin", 1, p) for p in range(11)] + [("fout", 1, c) for c in range(8)]
    for i in range(nt):
        seq += l0 + l1a
        if i >= 1:
            seq += l1b
    seq += l1b
    return seq


def build(SEQ, dbg=None):
    NT = SEQ // T
    NKT = (CTX + SEQ) // 128
    nc = bass.Bass("TRN2", target_bir_lowering=False)
    k = K(nc)

    def din(name, shape, dt=F32):
        return nc.dram_tensor(name, list(shape), dt, kind="ExternalInput").ap()

    x_d = din("x", [SEQ, D])
    ctx_d = din("ctx", [CTX, D])
    vecs_d = din("vecs", [NVEC, 128])
    w_mod_d = din("w_mod", [2, D, 6 * D])
    w_ffn_in_d = din("w_ffn_in", [2, D, 2 * DFF])
    w_ffn_out_d = din("w_ffn_out", [2, DFF, D])
    w_in_d = din("w_in", [D, 1792])
    w_sp_d = din("w_sp", [4, 128, 128])
    b_sp_d = din("b_sp", [1, 512])
    w_out_d = din("w_out", [D, D])
    w_pw1_d = din("w_pw1", [D, 2 * D])
    w_pw2_d = din("w_pw2", [D, D])
    ident_d = din("ident", [128, 128])
    pm_d = din("pm", [128, 128])
    cos_d = din("cos", [128, SEQ])
    sin_d = din("sin", [128, SEQ])
    out_d = nc.dram_tensor("out", [SEQ, D], F32, kind="ExternalOutput").ap()
    dbg_d = {}
    if dbg:
        for n, shp in dbg.items():
            dbg_d[n] = nc.dram_tensor("dbg_" + n, list(shp), F32, kind="ExternalOutput").ap()

    def sb(name, shape, dt=F32):
        return nc.alloc_sbuf_tensor("s_" + name, list(shape), dt)

    ring = sb("ring", [128, NSLOT, SLOT_ELEMS], BF16)
    ringB = [Buf(f"ring{i}") for i in range(NSLOT)]
    KT = sb("KT", [128, CTX + SEQ], BF16)
    KTB = Buf("KT")
    VA = sb("VA", [128, NKT, 2, 128], BF16)
    VAB = Buf("VA")
    hT = [sb(f"hT{i}", [128, 8, T]) for i in range(2)]
    hB = [[Buf(f"h{i}_{c}") for c in range(8)] for i in range(2)]
    io = [sb(f"io{i}", [128, D]) for i in range(2)]
    ioB = [Buf(f"io{i}") for i in range(2)]
    xmT = sb("xmT", [128, 8, T], BF16)
    xmB = [Buf(f"xm{c}") for c in range(8)]
    sq = [sb(f"sq{i}", [128, T], BF16) for i in range(2)]
    sqB = [Buf(f"sq{i}") for i in range(2)]
    tmp = [sb(f"tmp{i}", [128, T]) for i in range(3)]
    tmpB = [Buf(f"tmp{i}") for i in range(3)]
    rstd = sb("rstd", [128, T])
    rstdB = Buf("rstd")
    sd = sb("sd", [128, T])
    sdB = Buf("sd")
    AR = sb("AR", [128, NH, T], BF16)
    ARB = [Buf(f"ar{j}") for j in range(NH)]
    AR32 = AR.bitcast(F32) if hasattr(AR, "bitcast") else None
    GW = [sb(f"GW{i}", [128, 8, T + 30], BF16) for i in range(2)]
    NDG = 8
    dg = [sb(f"dg{i}", [128, 128], BF16) for i in range(NDG)]
    dgB = [Buf(f"dg{i}") for i in range(NDG)]
    identb = sb("identb", [128, 128], BF16)
    dgstate = {"n": 0}
    GWB = [[Buf(f"gw{i}_{c}") for c in range(8)] for i in range(2)]
    sgl = [sb(f"sgl{i}", [128, T]) for i in range(2)]
    sglB = [Buf(f"sgl{i}") for i in range(2)]
    cs = sb("cs", [128, 2, T])
    csB = Buf("cs")
    knT = sb("knT", [128, T], BF16)
    knB = Buf("knT")
    gv = sb("gv", [128, T])
    gvB = Buf("gv")
    gv2 = sb("gv2", [128, T])
    gv2B = Buf("gv2")
    vn = sb("vn", [128, 4, 128], BF16)
    vnB = Buf("vn")
    st = sb("st", [128, 32])
    stB = Buf("st")
    st2 = sb("st2", [128, 32])
    st2B = Buf("st2")
    vstage = sb("vstage", [128, 4, 128])
    vstageB = Buf("vstage")
    vecT = sb("vecT", [128, NVEC])
    vecB = Buf("vecT")
    coef = sb("coef", [128, 96])
    coefB = Buf("coef")
    modsb = sb("modsb", [128, 96, 2])
    modB = Buf("modsb")
    scb = sb("scb", [128, 8, 2], BF16)
    scbB = Buf("scb")
    ident = sb("ident", [128, 128])
    identB = Buf("ident")
    pmf = sb("pmf", [128, 128])
    pmb = sb("pmb", [128, 128], BF16)
    pmB = Buf("pm")
    ones = sb("ones", [128, 128], BF16)
    bones = sb("bones", [128, 128], BF16)
    onesB = Buf("ones")
    onesrow = sb("onesrow", [1, 128])

    bsprow = sb("bsprow", [1, 512])
    rowB = Buf("rows")
    wspT = sb("wspT", [128, 4, 128], BF16)
    wspB = Buf("wspT")
    nshift = sb("nshift", [128, 1])
    nshiftB = Buf("nshift")
    gtmp = sb("gtmp", [1, 4])

    PS = [nc.alloc_psum_tensor(f"ps{i}", [128, T], F32) for i in range(8)]
    PSB = [Buf(f"ps{i}") for i in range(8)]

    def ar(j):
        return AR[:, j, :]

    def ar32(j2):
        return AR[:, 2 * j2:2 * j2 + 2, :].bitcast(F32).rearrange("p a t -> p (a t)")

    QT_, AT_, SG_, GU_, PT_ = 0, 4, 8, 12, 16

    def V(name, i=0):
        r = VEC_ROWS[name] + i
        return vecT[:, r:r + 1]

    CO = {}
    _cc = [0]

    def cocol(name, n=8):
        CO[name] = _cc[0]
        _cc[0] += n

    for nm in ["A1_0", "A1c", "A2_0", "A1_1", "A2_1", "gb"]:
        cocol(nm)

    def C(name, i):
        c = CO[name] + i
        return coef[:, c:c + 1]

    def M(l, which, i, col=0):
        idx = l * 48 + which * 8 + i
        return modsb[:, idx, col:col + 1]

    SH1, SC1, GT1, SH2, SC2, GT2 = range(6)

    scr = {}
    scrB = {}

    def mkscr(key, group):
        scr[key] = nc.dram_tensor("scr_" + "_".join(str(s) for s in key), [128, SLOT_ELEMS], BF16).ap()
        if group not in scrB:
            scrB[group] = Buf("scr_" + group)
        return scr[key], scrB[group]

    def cast(dst, src, gb):
        k.dma(POOL, lambda e: e.dma_start(out=dst, in_=src), writes=[gb], disjoint=True)

    def r3(ap2, a, b):
        return ap2.rearrange("p (a b) -> p a b", a=a, b=b)

    def cast_wkv():
        s, gb = mkscr(("wkv",), "wkv")
        cast(r3(s[:, 0:8 * 256], 8, 256), w_in_d[:, 512:768].rearrange("(k p) c -> p k c", p=128), gb)

    def cast_all_early():
        s, gb = mkscr(("wq",), "wq")
        sv = r3(s, 8, 512)
        for g in range(4):
            for kv in range(2):
                h = kv * 4 + g
                cast(sv[:, :, g * 128 + kv * 64: g * 128 + kv * 64 + 64],
                     w_in_d[:, h * 64:(h + 1) * 64].rearrange("(k p) c -> p k c", p=128), gb)
        s, gb = mkscr(("wsu",), "wsu")
        cast(r3(s, 8, 512), w_in_d[:, 768:1280].rearrange("(k p) c -> p k c", p=128), gb)
        s, gb = mkscr(("wsv",), "wsv")
        cast(r3(s, 8, 512), w_in_d[:, 1280:1792].rearrange("(k p) c -> p k c", p=128), gb)
        for pc in range(2):
            s, gb = mkscr(("wo", pc), "wo")
            sv = r3(s, 8, 512)
            cols = slice(pc * 512, (pc + 1) * 512)
            cast(sv[0:64, 0:4, :], w_out_d[0:256, cols].rearrange("(c p) f -> p c f", p=64), gb)
            cast(sv[64:128, 0:4, :], w_out_d[256:512, cols].rearrange("(c p) f -> p c f", p=64), gb)
            cast(sv[:, 4:8, :], w_out_d[512:1024, cols].rearrange("(k p) f -> p k f", p=128), gb)

    def cast_fin(l, pps):
        for pp in pps:
            s, gb = mkscr(("fin", l, pp), f"fin{l}")
            sv = s.rearrange("p (k s c) -> p k s c", k=8, s=2, c=256)
            wv = w_ffn_in_d[l].rearrange("(k p) (s c) -> p k s c", p=128, s=2)
            for s2 in range(2):
                cast(sv[:, :, s2, :], wv[:, :, s2, pp * 256:(pp + 1) * 256], gb)
    def cast_fout(l):
        for c in range(8):
            s, gb = mkscr(("fout", l, c), f"fout{l}")
            cast(r3(s[:, 0:NH * 128], NH, 128),
                 w_ffn_out_d[l][:, c * 128:(c + 1) * 128].rearrange("(k p) f -> p k f", p=128), gb)

    def cast_pw():
        for pp in range(4):
            s, gb = mkscr(("pw1", pp), "pw1")
            sv = s.rearrange("p (k s c) -> p k s c", k=8, s=2, c=256)
            wv = w_pw1_d.rearrange("(k p) (s c) -> p k s c", p=128, s=2)
            for s2 in range(2):
                cast(sv[:, :, s2, :], wv[:, :, s2, pp * 256:(pp + 1) * 256], gb)
        for pc in range(2):
            s, gb = mkscr(("pw2", pc), "pw2")
            cast(r3(s, 8, 512), w_pw2_d[:, pc * 512:(pc + 1) * 512].rearrange("(k p) c -> p k c", p=128), gb)

    seq = piece_sequence(NT)
    wstate = {"issued": 0, "next": 0}

    def group_of(key):
        if key[0] in ("fin", "fout"):
            return f"{key[0]}{key[1]}"
        return key[0]

    def issue_load(n):
        key = seq[n]
        slot = n % NSLOT
        if key[0] == "mod":
            _, l, pc = key
            dst = r3(ring[:, slot, :], 8, 512)
            src = w_mod_d[l][:, pc * 512:(pc + 1) * 512].rearrange("(k p) c -> p k c", p=128)
            k.dma(POOL, lambda e: e.dma_start(out=dst, in_=src), writes=[ringB[slot]])
        else:
            s = scr[key]
            n_el = SLOT_ELEMS
            if key[0] == "wkv":
                n_el = 8 * 256
            elif key[0] == "fout":
                n_el = NH * 128
            k.dma(SP, lambda e: e.dma_start(out=ring[:, slot, 0:n_el], in_=s[:, 0:n_el]),
                  reads=[scrB[group_of(key)]], writes=[ringB[slot]])

    def wnext(key):
        n = wstate["next"]
        assert seq[n] == key, (seq[n], key)
        while wstate["issued"] < min(len(seq), n + NSLOT):
            issue_load(wstate["issued"])
            wstate["issued"] += 1
        wstate["next"] += 1
        slot = n % NSLOT
        return ring[:, slot, :], ringB[slot]

    k.dma(SP, lambda e: e.dma_start(out=vstage[:], in_=vecs_d.rearrange("(g r) c -> r g c", r=128)), writes=[vstageB])
    k.dma(SP, lambda e: e.dma_start(out=ident[:], in_=ident_d[:]), writes=[identB])
    k.dma(SP, lambda e: e.dma_start(out=pmf[:], in_=pm_d[:]), writes=[pmB])
    k.dma(SP, lambda e: e.dma_start(out=bsprow[:], in_=b_sp_d[:]), writes=[rowB])

    cast_wkv()
    epsT = sb("epsT", [128, 1])
    epsB = Buf("eps")
    k.op(DVE, lambda e: e.memset(epsT[:], EPS), writes=[epsB])
    neghalf = sb("neghalf", [128, 4])
    k.op(DVE, lambda e: e.memset(neghalf[:], -0.5), reads=[epsB], writes=[epsB])

    k.op(POOL, lambda e: e.memset(ones[:], 1.0), writes=[onesB])
    k.op(POOL, lambda e: e.memset(bones[:], 0.0), writes=[onesB])
    k.op(POOL, lambda e: e.memset(bones[0:64, 0:64], 1.0), writes=[onesB])
    k.op(POOL, lambda e: e.memset(bones[64:128, 64:128], 1.0), writes=[onesB])
    k.op(POOL, lambda e: e.memset(onesrow[:], 1.0), writes=[rowB])

    k.op(POOL, lambda e: e.memset(VA[:, :, :, 64:128], 1.0), writes=[VAB])
    k.op(DVE, lambda e: e.tensor_copy(out=pmb[:], in_=pmf[:]), reads=[pmB], writes=[pmB])
    k.op(DVE, lambda e: e.tensor_copy(out=identb[:], in_=ident[:]), reads=[identB], writes=[identB])

    for g in range(4):
        k.op(PE, lambda e, g=g: e.transpose(out=PS[0][:, g * 128:(g + 1) * 128], in_=vstage[:, g, :], identity=ident[:]),
             reads=[vstageB, identB], writes=[PSB[0]])
    k.op(DVE, lambda e: e.tensor_copy(out=vecT[:], in_=PS[0][:]), reads=[PSB[0]], writes=[vecB])

    k.dma(SP, lambda e: e.dma_start(out=vstage[:], in_=w_sp_d.rearrange("g p q -> p g q")), reads=[vstageB], writes=[vstageB])
    for g in range(4):
        k.op(PE, lambda e, g=g: e.transpose(out=PS[1][:, g * 128:(g + 1) * 128], in_=vstage[:, g, :], identity=ident[:]),
             reads=[vstageB, identB], writes=[PSB[1]])
    k.op(DVE, lambda e: e.tensor_copy(out=wspT[:].rearrange("p g q -> p (g q)"), in_=PS[1][:]), reads=[PSB[1]], writes=[wspB])

    k.op(ACT, lambda e: e.activation(out=st[:, 0:2], in_=vecT[:, VEC_ROWS["qg"]:VEC_ROWS["qg"] + 2], func=AF.Abs),
         reads=[vecB], writes=[stB])
    k.op(PE, lambda e: e.transpose(out=PS[2][0:2, 0:128], in_=st[:, 0:2], identity=ident[:]), reads=[stB, identB], writes=[PSB[2]])
    k.op(DVE, lambda e: e.tensor_reduce(out=st[0:2, 2:3], in_=PS[2][0:2, 0:128], axis=AX.X, op=ALU.max), reads=[PSB[2], stB], writes=[stB])
    k.op(PE, lambda e: e.transpose(out=PS[2][0:1, 128:130], in_=st[0:2, 2:3], identity=ident[0:2, 0:2]), reads=[stB, identB], writes=[PSB[2]])
    k.op(DVE, lambda e: e.tensor_copy(out=gtmp[0:1, 0:2], in_=PS[2][0:1, 128:130]), reads=[PSB[2]], writes=[stB])
    k.op(DVE, lambda e: e.tensor_tensor(out=gtmp[0:1, 2:3], in0=gtmp[0:1, 0:1], in1=gtmp[0:1, 1:2], op=ALU.mult), reads=[stB], writes=[stB])
    k.op(PE, lambda e: e.matmul(PS[2][:, 256:257], lhsT=onesrow[0:1, :], rhs=gtmp[0:1, 2:3], start=True, stop=True), reads=[stB, rowB], writes=[PSB[2]])
    k.op(ACT, lambda e: e.activation(out=nshift[:], in_=PS[2][:, 256:257], func=AF.Copy, scale=-8.0), reads=[PSB[2]], writes=[nshiftB])

    k.op(ACT, lambda e: e.activation(out=scb[:, :, 0], in_=vecT[:, VEC_ROWS["c"]:VEC_ROWS["c"] + 8], func=AF.Silu), reads=[vecB], writes=[scbB])
    k.op(ACT, lambda e: e.activation(out=scb[:, :, 1], in_=vecT[:, VEC_ROWS["cctx"]:VEC_ROWS["cctx"] + 8], func=AF.Silu), reads=[vecB, scbB], writes=[scbB])

    def mod_pieces(l, pcs, mbank):
        for pc in pcs:
            w, wb = wnext(("mod", l, pc))
            wv = r3(w, 8, 512)
            for fc in range(4):
                idx = l * 48 + pc * 4 + fc
                for kk in range(8):
                    k.op(PE, lambda e, wv=wv, fc=fc, kk=kk, idx=idx: e.matmul(
                        PS[mbank][:, idx * 2:idx * 2 + 2], lhsT=wv[:, kk, fc * 128:(fc + 1) * 128], rhs=scb[:, kk, :],
                        start=(kk == 0), stop=(kk == 7)), reads=[wb, scbB], writes=[PSB[mbank]])

    def mod_finish(l, mbank, i0=0, i1=48):
        for col in range(2):
            k.op(DVE, lambda e, col=col: e.tensor_tensor(
                out=modsb[:, l * 48 + i0:l * 48 + i1, col],
                in0=PS[mbank][:, (l * 48 + i0) * 2:(l * 48 + i1) * 2].rearrange("p (i c) -> p i c", c=2)[:, :, col],
                in1=vecT[:, VEC_ROWS["bmod"] + l * 48 + i0:VEC_ROWS["bmod"] + l * 48 + i1], op=ALU.add), reads=[PSB[mbank], vecB, modB], writes=[modB])

    mod_pieces(0, range(4), 3)
    mod_finish(0, 3, 0, 16)
    cast_all_early()
    cast_batches = [lambda: cast_fin(0, range(0, 6)), lambda: cast_fin(0, range(6, 11)), lambda: cast_fout(0), cast_pw,
                    lambda: cast_fin(1, range(0, 6)), lambda: cast_fin(1, range(6, 11)), lambda: cast_fout(1)]

    def coef_A(name, l, which, gname, col=0):
        c0 = CO[name]
        k.op(DVE, lambda e: e.scalar_tensor_tensor(
            out=coef[:, c0:c0 + 8], in0=modsb[:, l * 48 + which * 8: l * 48 + which * 8 + 8, col], scalar=1.0,
            in1=vecT[:, VEC_ROWS[gname] + l * 8: VEC_ROWS[gname] + l * 8 + 8], op0=ALU.add, op1=ALU.mult),
            reads=[modB, vecB, coefB], writes=[coefB])

    coef_A("A1_0", 0, SC1, "gmix")
    coef_A("A1c", 0, SC1, "gmix", col=1)

    iostate = {"n": 0}

    def load_tile_T(src_rows, ntok, hbuf, hbufB):
        for m in range(ntok // 128):
            ib = iostate["n"] % 2
            iostate["n"] += 1
            k.dma(SP, lambda e, ib=ib, m=m: e.dma_start(out=io[ib][:], in_=src_rows[m * 128:(m + 1) * 128, :]), writes=[ioB[ib]])
            for half in range(2):
                bank = 6 + half
                for kk in range(4):
                    kc = half * 4 + kk
                    k.op(PE, lambda e, ib=ib, kc=kc, kk=kk, bank=bank: e.transpose(
                        out=PS[bank][:, kk * 128:(kk + 1) * 128], in_=io[ib][:, kc * 128:(kc + 1) * 128], identity=ident[:]),
                        reads=[ioB[ib], identB], writes=[PSB[bank]])
                eng = ACT if half == 0 else DVE
                if eng == ACT:
                    k.op(ACT, lambda e, half=half, m=m, bank=bank: e.activation(
                        out=hbuf[:, half * 4:half * 4 + 4, m * 128:(m + 1) * 128],
                        in_=PS[bank][:].rearrange("p (a t) -> p a t", a=4), func=AF.Copy),
                        reads=[PSB[bank]], writes=hbufB[half * 4:half * 4 + 4])
                else:
                    k.op(DVE, lambda e, half=half, m=m, bank=bank: e.tensor_copy(
                        out=hbuf[:, half * 4:half * 4 + 4, m * 128:(m + 1) * 128],
                        in_=PS[bank][:].rearrange("p (a t) -> p a t", a=4)),
                        reads=[PSB[bank]], writes=hbufB[half * 4:half * 4 + 4])

    def rms_stats(hbuf, hbufB, ntok, bank, sd_=None, sd_B=None, rs_=None, rs_B=None):
        if sd_ is None:
            sd_, sd_B, rs_, rs_B = sd[:], [sdB], rstd[:], [rstdB]
        for kc in range(8):
            s = kc % 2
            k.op(ACT, lambda e, kc=kc, s=s: e.activation(out=sq[s][:, 0:ntok], in_=hbuf[:, kc, 0:ntok], func=AF.Square),
                 reads=[hbufB[kc]], writes=[sqB[s]])
            k.op(PE, lambda e, kc=kc, s=s: e.matmul(PS[bank][:, 0:ntok], lhsT=ones[:], rhs=sq[s][:, 0:ntok],
                                                    start=(kc == 0), stop=(kc == 7)),
                 reads=[sqB[s], onesB], writes=[PSB[bank]])
        k.op(ACT, lambda e: e.activation(out=sd_[:, 0:ntok], in_=PS[bank][:, 0:ntok], func=AF.Ln, scale=1.0 / D, bias=epsT[:]),
             reads=[PSB[bank], epsB], writes=sd_B)
        k.op(ACT, lambda e: e.activation(out=rs_[:, 0:ntok], in_=sd_[:, 0:ntok], func=AF.Exp, scale=-0.5), reads=sd_B, writes=rs_B)

    tmpstate = {"n": 0}

    def norm_mod(hbuf, hbufB, ntok, Acol, Bcol, bank, ve=POOL, xm_=None, xm_B=None, sd_=None, sd_B=None, rs_=None, rs_B=None):
        if xm_ is None:
            xm_, xm_B = xmT, xmB
        if sd_ is None:
            sd_, sd_B, rs_, rs_B = sd[:], [sdB], rstd[:], [rstdB]
        rms_stats(hbuf, hbufB, ntok, bank, sd_, sd_B, rs_, rs_B)
        for kc in range(8):
            ti = tmpstate["n"] % 3
            tmpstate["n"] += 1
            k.op(DVE if (ve == DVE or kc % 3 != 2) else POOL, lambda e, kc=kc, ti=ti: e.tensor_tensor(out=tmp[ti][:, 0:ntok], in0=hbuf[:, kc, 0:ntok], in1=rs_[:, 0:ntok], op=ALU.mult),
                 reads=[hbufB[kc]] + rs_B, writes=[tmpB[ti]])
            k.op(ACT, lambda e, kc=kc, ti=ti: e.activation(out=xm_[:, kc, 0:ntok], in_=tmp[ti][:, 0:ntok], func=AF.Identity,
                                                           scale=Acol(kc), bias=Bcol(kc)),
                 reads=[tmpB[ti], coefB, modB], writes=[xm_B[kc]])

    def headnorm_rstd(src_ps, src_psB, ntok, bank, sqi=0, sd_=None, sd_B=None, rs_=None, rs_B=None):
        if sd_ is None:
            sd_, sd_B, rs_, rs_B = sd[:], [sdB], rstd[:], [rstdB]
        k.op(ACT, lambda e: e.activation(out=sq[sqi][:, 0:ntok], in_=src_ps[:, 0:ntok], func=AF.Square), reads=[src_psB], writes=[sqB[sqi]])
        k.op(PE, lambda e: e.matmul(PS[bank][:, 0:ntok], lhsT=bones[:], rhs=sq[sqi][:, 0:ntok], start=True, stop=True),
             reads=[sqB[sqi], onesB], writes=[PSB[bank]])
        k.op(ACT, lambda e: e.activation(out=sd_[:, 0:ntok], in_=PS[bank][:, 0:ntok], func=AF.Ln, scale=1.0 / 64, bias=epsT[:]),
             reads=[PSB[bank], epsB], writes=sd_B)
        k.op(ACT, lambda e: e.activation(out=rs_[:, 0:ntok], in_=sd_[:, 0:ntok], func=AF.Exp, scale=-0.5), reads=sd_B, writes=rs_B)

    def rope(src_bf, src_B, dst, dst_B, bank, ve=POOL, t0=None, t0B=None, t1=None, t1B=None):
        if t0 is None:
            t0, t0B, t1, t1B = tmp[0][:], [tmpB[0]], tmp[1][:], [tmpB[1]]
        k.op(PE, lambda e: e.matmul(PS[bank][:], lhsT=pmb[:], rhs=src_bf, start=True, stop=True), reads=[src_B, pmB], writes=[PSB[bank]])
        k.op(ve, lambda e: e.tensor_tensor(out=t0, in0=src_bf, in1=cs[:, 0, :], op=ALU.mult), reads=[src_B, csB], writes=t0B)
        k.op(DVE, lambda e: e.tensor_tensor(out=t1, in0=PS[bank][:], in1=cs[:, 1, :], op=ALU.mult), reads=[PSB[bank], csB], writes=t1B)
        k.op(ve, lambda e: e.tensor_tensor(out=dst, in0=t0, in1=t1, op=ALU.add), reads=t0B + t1B, writes=dst_B)


    def dump(name, ap_fn, bufs):
        if dbg and name in dbg_d:
            k.dma(SP, lambda e: e.dma_start(out=dbg_d[name], in_=ap_fn()), reads=bufs, track=dumpB)

    dumpB = Buf("dump")

    wkv_t = sb("wkv", [128, 8 * 256], BF16)
    wkvB = Buf("wkv")
    k.dma(SP, lambda e: e.dma_start(out=wkv_t[:], in_=scr[("wkv",)][:, 0:8 * 256]), reads=[scrB["wkv"]], writes=[wkvB])
    wkvv = r3(wkv_t[:], 8, 256)

    def phaseA(src_rows, ntok, key_off, latent, tile_i):
        par = (tile_i + (1 if latent else 0)) % 2
        hb, hbB = hT[par], hB[par]
        if par == 0:
            xm_, xm_B = xmT, xmB
            nsd, nsdB, nrs, nrsB = sd[:], [sdB], rstd[:], [rstdB]
            kn_, kn_B = knT[:], knB
        else:
            xm_, xm_B = AR[:, 0:8, :], ARB[0:8]
            nsd, nsdB = ar32(4), [ARB[8], ARB[9]]
            nrs, nrsB = ar32(5), [ARB[10], ARB[11]]
            kn_, kn_B = ar(16), ARB[16]
        hsd, hsdB = ar32(6), [ARB[12], ARB[13]]
        hrs, hrsB = ar32(7), [ARB[14], ARB[15]]
        load_tile_T(src_rows, ntok, hb, hbB)
        if latent:
            norm_mod(hb, hbB, ntok, lambda kc: C("A1_0", kc), lambda kc: M(0, SH1, kc), 0, ve=DVE, xm_=xm_, xm_B=xm_B, sd_=nsd, sd_B=nsdB, rs_=nrs, rs_B=nrsB)
        else:
            norm_mod(hb, hbB, ntok, lambda kc: C("A1c", kc), lambda kc: M(0, SH1, kc, 1), 0, ve=DVE, xm_=xm_, xm_B=xm_B, sd_=nsd, sd_B=nsdB, rs_=nrs, rs_B=nrsB)
        for kc in range(8):
            k.op(PE, lambda e, kc=kc: e.matmul(PS[1][:, 0:ntok], lhsT=wkvv[:, kc, 0:128], rhs=xm_[:, kc, 0:ntok], start=(kc == 0), stop=(kc == 7)),
                 reads=[wkvB, xm_B[kc]], writes=[PSB[1]])
        nsub = ntok // 128
        for m in range(nsub):
            for kc in range(8):
                k.op(PE, lambda e, kc=kc, m=m: e.matmul(PS[4][:, m * 128:(m + 1) * 128], lhsT=xm_[:, kc, m * 128:(m + 1) * 128], rhs=wkvv[:, kc, 128:256],
                                                        start=(kc == 0), stop=(kc == 7)),
                     reads=[wkvB, xm_B[kc]], writes=[PSB[4]])
        headnorm_rstd(PS[1], PSB[1], ntok, 2, sqi=par, sd_=hsd, sd_B=hsdB, rs_=hrs, rs_B=hrsB)
        dstK = KT[:, key_off:key_off + ntok] if not latent else kn_[:, 0:ntok]
        dstKB = [KTB] if not latent else [kn_B]
        k.op(DVE, lambda e: e.scalar_tensor_tensor(out=dstK, in0=PS[1][:, 0:ntok], scalar=V("kg"), in1=hrs[:, 0:ntok], op0=ALU.mult, op1=ALU.mult),
             reads=[PSB[1], vecB] + hrsB, writes=dstKB)
        if latent:
            k.dma(SP, lambda e: e.dma_start(out=cs[:, 0, :], in_=cos_d[:, tile_i * T:(tile_i + 1) * T]), writes=[csB])
            k.dma(SP, lambda e: e.dma_start(out=cs[:, 1, :], in_=sin_d[:, tile_i * T:(tile_i + 1) * T]), reads=[], writes=[csB])
            rope(kn_, kn_B, KT[:, key_off:key_off + ntok], [KTB], 3, ve=DVE)
        j0 = key_off // 128
        k.op(ACT, lambda e: e.activation(out=VA[:, j0:j0 + nsub, :, 0:64],
                                         in_=PS[4][:, 0:nsub * 128].rearrange("p (j h d) -> p j h d", j=nsub, h=2, d=64), func=AF.Copy),
             reads=[PSB[4]], writes=[VAB])

    phaseA(ctx_d, CTX, 0, False, 0)
    mod_todo = [(0, pc) for pc in range(4, 12)] + [(1, pc) for pc in range(12)]
    for i in range(NT):
        for _ in range(3):
            if mod_todo:
                l_, pc_ = mod_todo.pop(0)
                mod_pieces(l_, [pc_], 5)
        if NT < 8 and cast_batches:
            cast_batches.pop(0)()
        phaseA(x_d[i * T:(i + 1) * T, :], T, CTX + i * T, True, i)
    while mod_todo:
        l_, pc_ = mod_todo.pop(0)
        mod_pieces(l_, [pc_], 5)
    mod_finish(0, 5, 16, 48)
    coef_A("A2_0", 0, SC2, "gffn")
    late_casts = []
    head_casts = []
    while cast_batches:
        b = cast_batches.pop(0)
        if NT >= 8 and len(cast_batches) < 3:
            late_casts.append(b)
        elif NT >= 8:
            head_casts.append(b)
        else:
            b()
    mod_finish(1, 5)


    def ffn(l, hb, hbB):
        A2 = "A2_0" if l == 0 else "A2_1"
        norm_mod(hb, hbB, T, lambda kc: C(A2, kc), lambda kc: M(l, SH2, kc), 0)
        for pp in range(11):
            w, wb = wnext(("fin", l, pp))
            wv = w.rearrange("p (k s c) -> p k s c", k=8, s=2, c=256)
            if pp == 0:
                for kc in range(8):
                    for jj in range(2):
                        for s2 in range(2):
                            bank = jj * 2 + s2
                            k.op(PE, lambda e, kc=kc, s2=s2, bank=bank, jj=jj, wv=wv: e.matmul(
                                PS[bank][:], lhsT=wv[:, kc, s2, jj * 128:(jj + 1) * 128], rhs=xmT[:, kc, :], start=(kc == 0), stop=(kc == 7)),
                                reads=[wb, xmB[kc]], writes=[PSB[bank]])
            for jj in range(2):
                j = pp * 2 + jj
                gb_, ub_ = (0, 1) if j % 2 == 0 else (2, 3)
                for s2, bank in ((0, gb_), (1, ub_)):
                    if pp == 0:
                        break
                    for kc in range(8):
                        k.op(PE, lambda e, kc=kc, s2=s2, bank=bank, jj=jj, wv=wv: e.matmul(
                            PS[bank][:], lhsT=wv[:, kc, s2, jj * 128:(jj + 1) * 128], rhs=xmT[:, kc, :], start=(kc == 0), stop=(kc == 7)),
                            reads=[wb, xmB[kc]], writes=[PSB[bank]])
                si = j % 2
                k.op(ACT, lambda e, si=si, gb_=gb_: e.activation(out=sgl[si][:], in_=PS[gb_][:], func=AF.Silu), reads=[PSB[gb_]], writes=[sglB[si]])
                k.op(DVE, lambda e, si=si, ub_=ub_, j=j: e.tensor_tensor(out=ar(j), in0=PS[ub_][:], in1=sgl[si][:], op=ALU.mult),
                     reads=[PSB[ub_], sglB[si]], writes=[ARB[j]])
        for f in range(8):
            w, wb = wnext(("fout", l, f))
            wv = r3(w[:, 0:NH * 128], NH, 128)
            bank = 4 + f % 4
            for j in range(NH):
                k.op(PE, lambda e, j=j, wv=wv, bank=bank: e.matmul(PS[bank][:], lhsT=wv[:, j, :], rhs=ar(j), start=(j == 0), stop=(j == NH - 1)),
                     reads=[wb, ARB[j]], writes=[PSB[bank]])
            k.op(DVE, lambda e, f=f, bank=bank: e.scalar_tensor_tensor(out=hb[:, f, :], in0=PS[bank][:], scalar=M(l, GT2, f), in1=hb[:, f, :],
                                                                      op0=ALU.mult, op1=ALU.add),
                 reads=[PSB[bank], modB, hbB[f]], writes=[hbB[f]])

    def layer0(i):
        hb, hbB = hT[i % 2], hB[i % 2]
        if i == 0:
            for b_ in head_casts:
                b_()
        load_tile_T(x_d[i * T:(i + 1) * T, :], T, hb, hbB)
        norm_mod(hb, hbB, T, lambda kc: C("A1_0", kc), lambda kc: M(0, SH1, kc), 0)
        k.dma(SP, lambda e: e.dma_start(out=cs[:, 0, :], in_=cos_d[:, i * T:(i + 1) * T]), writes=[csB])
        k.dma(SP, lambda e: e.dma_start(out=cs[:, 1, :], in_=sin_d[:, i * T:(i + 1) * T]), writes=[csB])
        w, wq_b = wnext(("wq",))
        wq_v = r3(w, 8, 512)
        SA, SBK = 6, 7

        def qchunk_gen(c):
            for kc in range(8):
                k.op(PE, lambda e, kc=kc: e.matmul(PS[SA][:], lhsT=wq_v[:, kc, c * 128:(c + 1) * 128], rhs=xmT[:, kc, :], start=(kc == 0), stop=(kc == 7)),
                     reads=[wq_b, xmB[kc]], writes=[PSB[SA]])
                if kc == 3:
                    yield
            yield
            k.op(DVE, lambda e: e.tensor_copy(out=tmp[2][:], in_=PS[SA][:]), reads=[PSB[SA]], writes=[tmpB[2]])
            yield
            k.op(POOL, lambda e: e.tensor_tensor(out=sq[0][:], in0=tmp[2][:], in1=tmp[2][:], op=ALU.mult), reads=[tmpB[2]], writes=[sqB[0]])
            yield
            k.op(PE, lambda e: e.matmul(PS[SBK][:], lhsT=bones[:], rhs=sq[0][:], start=True, stop=True), reads=[sqB[0], onesB], writes=[PSB[SBK]])
            yield
            k.op(ACT, lambda e: e.activation(out=sd[:], in_=PS[SBK][:], func=AF.Ln, scale=1.0 / 64, bias=epsT[:]), reads=[PSB[SBK], epsB], writes=[sdB])
            yield
            k.op(ACT, lambda e: e.activation(out=rstd[:], in_=sd[:], func=AF.Exp, scale=-0.5), reads=[sdB], writes=[rstdB])
            yield
            k.op(DVE, lambda e: e.scalar_tensor_tensor(out=knT[:], in0=PS[SA][:], scalar=V("qg"), in1=rstd[:], op0=ALU.mult, op1=ALU.mult),
                 reads=[PSB[SA], vecB, rstdB], writes=[knB])
            yield
            k.op(PE, lambda e: e.matmul(PS[SBK][:], lhsT=pmb[:], rhs=knT[:], start=True, stop=True), reads=[knB, pmB], writes=[PSB[SBK]])
            k.op(POOL, lambda e: e.tensor_tensor(out=tmp[0][:], in0=knT[:], in1=cs[:, 0, :], op=ALU.mult), reads=[knB, csB], writes=[tmpB[0]])
            yield
            k.op(DVE, lambda e: e.tensor_tensor(out=tmp[1][:], in0=PS[SBK][:], in1=cs[:, 1, :], op=ALU.mult), reads=[PSB[SBK], csB], writes=[tmpB[1]])
            yield
            k.op(POOL, lambda e: e.tensor_tensor(out=ar(QT_ + c), in0=tmp[0][:], in1=tmp[1][:], op=ALU.add), reads=[tmpB[0], tmpB[1]], writes=[ARB[QT_ + c]])
            yield

        def su_gen():
            w2, wb2 = wnext(("wsu",))
            wvs = r3(w2, 8, 512)
            for c in range(4):
                bank = SA + c % 2
                for kc in range(8):
                    k.op(PE, lambda e, kc=kc, c=c, bank=bank: e.matmul(PS[bank][:], lhsT=wvs[:, kc, c * 128:(c + 1) * 128], rhs=xmT[:, kc, :], start=(kc == 0), stop=(kc == 7)),
                         reads=[wb2, xmB[kc]], writes=[PSB[bank]])
                    if kc == 3:
                        yield
                yield
                k.op(ACT, lambda e, c=c, bank=bank: e.activation(out=ar(GU_ + c), in_=PS[bank][:], func=AF.Gelu_apprx_tanh), reads=[PSB[bank]], writes=[ARB[GU_ + c]])
                yield

        def sv_gen():
            w3, wb3 = wnext(("wsv",))
            wv2 = r3(w3, 8, 512)
            for m in range(4):
                bank = SA + m % 2
                mb = SBK - m % 2
                for kc in range(8):
                    k.op(PE, lambda e, kc=kc, m=m, bank=bank: e.matmul(PS[bank][:], lhsT=xmT[:, kc, m * 128:(m + 1) * 128], rhs=wv2[:, kc, :], start=(kc == 0), stop=(kc == 7)),
                         reads=[wb3, xmB[kc]], writes=[PSB[bank]])
                    if kc == 3:
                        yield
                yield
                k.op(ACT, lambda e, bank=bank: e.activation(out=gv[:], in_=PS[bank][:], func=AF.Gelu_apprx_tanh), reads=[PSB[bank]], writes=[gvB])
                yield
                k.op(POOL, lambda e: e.tensor_tensor(out=gv2[:], in0=gv[:], in1=gv[:], op=ALU.mult), reads=[gvB], writes=[gv2B])
                k.op(DVE, lambda e: e.tensor_reduce(out=st[:, 4:8], in_=gv[:].rearrange("p (g c) -> p g c", g=4), axis=AX.X, op=ALU.add), reads=[gvB, stB], writes=[stB])
                yield
                k.op(DVE, lambda e: e.tensor_reduce(out=st[:, 8:12], in_=gv2[:].rearrange("p (g c) -> p g c", g=4), axis=AX.X, op=ALU.add), reads=[gv2B, stB], writes=[stB])
                yield
                k.op(DVE, lambda e: e.tensor_scalar(out=st[:, 12:16], in0=st[:, 4:8], scalar1=1.0 / 128, scalar2=None, op0=ALU.mult), reads=[stB], writes=[stB])
                yield
                k.op(DVE, lambda e: e.tensor_tensor(out=st[:, 16:20], in0=st[:, 12:16], in1=st[:, 12:16], op=ALU.mult), reads=[stB], writes=[stB])
                yield
                k.op(DVE, lambda e: e.scalar_tensor_tensor(out=st[:, 20:24], in0=st[:, 8:12], scalar=1.0 / 128, in1=st[:, 16:20], op0=ALU.mult, op1=ALU.subtract),
                     reads=[stB], writes=[stB])
                yield
                k.op(ACT, lambda e: e.activation(out=st[:, 24:28], in_=st[:, 20:24], func=AF.Ln, bias=epsT[:]), reads=[stB, epsB], writes=[stB])
                yield
                k.op(ACT, lambda e: e.activation(out=st[:, 28:32], in_=st[:, 24:28], func=AF.Exp, scale=-0.5), reads=[stB], writes=[stB])
                yield
                for g in range(4):
                    k.op(DVE, lambda e, g=g: e.tensor_scalar(out=vn[:, g, :], in0=gv[:, g * 128:(g + 1) * 128], scalar1=st[:, 12 + g:13 + g], scalar2=st[:, 28 + g:29 + g],
                                                             op0=ALU.subtract, op1=ALU.mult), reads=[gvB, stB, vnB], writes=[vnB])
                yield
                k.op(PE, lambda e, mb=mb: e.matmul(PS[mb][:], lhsT=onesrow[0:1, :], rhs=bsprow[0:1, :], start=True, stop=False), reads=[rowB], writes=[PSB[mb]])
                for g in range(4):
                    k.op(PE, lambda e, g=g, mb=mb: e.matmul(PS[mb][:, g * 128:(g + 1) * 128], lhsT=vn[:, g, :], rhs=wspT[:, g, :], start=False, stop=(g == 3)),
                         reads=[vnB, wspB], writes=[PSB[mb]])
                yield
                k.op(DVE, lambda e, m=m, mb=mb: e.tensor_tensor(out=AR[:, SG_:SG_ + 4, m * 128:(m + 1) * 128], in0=PS[mb][:].rearrange("p (g t) -> p g t", g=4),
                                                                in1=AR[:, GU_:GU_ + 4, m * 128:(m + 1) * 128], op=ALU.mult),
                     reads=[PSB[mb]] + ARB[GU_:GU_ + 4], writes=ARB[SG_:SG_ + 4])
                yield

        qdone = {0: False, 1: False, 2: False, 3: False}

        def side_gen():
            for c in (1, 2, 3):
                yield from qchunk_gen(c)
                qdone[c] = True

        for _ in qchunk_gen(0):
            pass
        qdone[0] = True
        side = side_gen()
        side_alive = [True]

        def pull():
            if side_alive[0]:
                try:
                    next(side)
                except StopIteration:
                    side_alive[0] = False

        def qk(c, j):
            for hh in range(2):
                bank = hh * 2 + j % 2
                ps_ = slice(hh * 64, (hh + 1) * 64)
                k.op(PE, lambda e, bank=bank, ps_=ps_, j=j, c=c: e.matmul(PS[bank][:], lhsT=KT[ps_, j * 128:(j + 1) * 128], rhs=AR[ps_, QT_ + c, :],
                                                                    start=True, stop=True),
                     reads=[KTB, ARB[QT_ + c]], writes=[PSB[bank]])

        its = [(c, j) for c in range(4) for j in range(NKT)]
        qk(0, 0)
        for n, (c, j) in enumerate(its):
            if n + 1 < len(its):
                c2, j2 = its[n + 1]
                while not qdone[c2]:
                    pull()
                qk(c2, j2)
            for hh in range(2):
                bank = hh * 2 + j % 2
                pt = PT_ + hh * 2 + j % 2
                k.op(ACT, lambda e, bank=bank, pt=pt: e.activation(out=ar(pt), in_=PS[bank][:], func=AF.Exp, scale=0.125, bias=SHIFT),
                     reads=[PSB[bank]], writes=[ARB[pt]])
            for hh in range(2):
                pt = PT_ + hh * 2 + j % 2
                k.op(PE, lambda e, hh=hh, pt=pt, j=j: e.matmul(PS[4 + hh][:], lhsT=VA[:, j, hh, :], rhs=ar(pt), start=(j == 0), stop=(j == NKT - 1)),
                     reads=[VAB, ARB[pt]], writes=[PSB[4 + hh]])
            if j == NKT - 1:
                for hh in range(2):
                    k.op(DVE, lambda e, hh=hh: e.tensor_copy(out=sgl[hh][:], in_=PS[4 + hh][:]), reads=[PSB[4 + hh]], writes=[sglB[hh]])
                for hh in range(2):
                    k.op(DVE, lambda e, hh=hh: e.reciprocal(out=ar32(10)[0:64, :], in_=sgl[hh][64:128, :]), reads=[sglB[hh]], writes=[ARB[20], ARB[21]])
                    k.op(DVE, lambda e, hh=hh, c=c: e.tensor_tensor(out=AR[hh * 64:(hh + 1) * 64, AT_ + c, :], in0=sgl[hh][0:64, :], in1=ar32(10)[0:64, :], op=ALU.mult),
                         reads=[sglB[hh], ARB[20], ARB[21]], writes=[ARB[AT_ + c]])
            if (n + 1) % DRIP == 0:
                pull()
        while side_alive[0]:
            pull()
        if i == 0:
            for b in late_casts:
                b()
        w, wb = wnext(("wsu",))
        wv = r3(w, 8, 512)
        for c in range(4):
            bank = c % 2
            for kc in range(8):
                k.op(PE, lambda e, kc=kc, c=c, bank=bank, wv=wv: e.matmul(PS[bank][:], lhsT=wv[:, kc, c * 128:(c + 1) * 128], rhs=xmT[:, kc, :], start=(kc == 0), stop=(kc == 7)),
                     reads=[wb, xmB[kc]], writes=[PSB[bank]])
            k.op(ACT, lambda e, c=c, bank=bank: e.activation(out=ar(GU_ + c), in_=PS[bank][:], func=AF.Gelu_apprx_tanh), reads=[PSB[bank]], writes=[ARB[GU_ + c]])
        w, wb = wnext(("wsv",))
        wv2 = r3(w, 8, 512)
        def sv_proj(m):
            bank = 2 + m % 2
            for kc in range(8):
                k.op(PE, lambda e, kc=kc, m=m, bank=bank, wv2=wv2: e.matmul(PS[bank][:], lhsT=xmT[:, kc, m * 128:(m + 1) * 128], rhs=wv2[:, kc, :], start=(kc == 0), stop=(kc == 7)),
                     reads=[wb, xmB[kc]], writes=[PSB[bank]])

        sv_wb = wb
        sv_proj(0)
        for m in range(4):
            bank = 2 + m % 2
            if m + 1 < 4:
                sv_proj(m + 1)
            if m % 2 == 0:
                gv_, gv_B, gv2_, gv2_B, vn_, vn_B, st_, st_B = gv[:], [gvB], gv2[:], [gv2B], vn[:], [vnB], st, stB
            else:
                gv_, gv_B = ar32(0), [ARB[0], ARB[1]]
                gv2_, gv2_B = ar32(1), [ARB[2], ARB[3]]
                vn_, vn_B = AR[:, 16, :].rearrange("p (g c) -> p g c", g=4), [ARB[16]]
                st_, st_B = st2, st2B
            k.op(ACT, lambda e, bank=bank, gv_=gv_: e.activation(out=gv_, in_=PS[bank][:], func=AF.Gelu_apprx_tanh), reads=[PSB[bank]], writes=gv_B)
            k.op(POOL, lambda e, gv_=gv_, gv2_=gv2_: e.tensor_tensor(out=gv2_, in0=gv_, in1=gv_, op=ALU.mult), reads=gv_B, writes=gv2_B)
            k.op(DVE, lambda e, gv_=gv_, st_=st_: e.tensor_reduce(out=st_[:, 4:8], in_=gv_.rearrange("p (g c) -> p g c", g=4), axis=AX.X, op=ALU.add), reads=gv_B + [st_B], writes=[st_B])
            k.op(DVE, lambda e, gv2_=gv2_, st_=st_: e.tensor_reduce(out=st_[:, 8:12], in_=gv2_.rearrange("p (g c) -> p g c", g=4), axis=AX.X, op=ALU.add), reads=gv2_B + [st_B], writes=[st_B])
            k.op(DVE, lambda e, st_=st_: e.tensor_scalar(out=st_[:, 12:16], in0=st_[:, 4:8], scalar1=1.0 / 128, scalar2=None, op0=ALU.mult), reads=[st_B], writes=[st_B])
            k.op(DVE, lambda e, st_=st_: e.tensor_tensor(out=st_[:, 16:20], in0=st_[:, 12:16], in1=st_[:, 12:16], op=ALU.mult), reads=[st_B], writes=[st_B])
            k.op(DVE, lambda e, st_=st_: e.scalar_tensor_tensor(out=st_[:, 20:24], in0=st_[:, 8:12], scalar=1.0 / 128, in1=st_[:, 16:20], op0=ALU.mult, op1=ALU.subtract),
                 reads=[st_B], writes=[st_B])
            k.op(DVE, lambda e, st_=st_: e.tensor_scalar(out=st_[:, 24:28], in0=st_[:, 20:24], scalar1=EPS, scalar2=None, op0=ALU.add), reads=[st_B], writes=[st_B])
            k.op(POOL, lambda e, st_=st_: e.tensor_tensor(out=st_[:, 28:32], in0=st_[:, 24:28], in1=neghalf[:, 0:4], op=ALU.pow), reads=[st_B, epsB], writes=[st_B])
            for g in range(4):
                k.op(DVE, lambda e, g=g, gv_=gv_, vn_=vn_, st_=st_: e.tensor_scalar(out=vn_[:, g, :], in0=gv_[:, g * 128:(g + 1) * 128], scalar1=st_[:, 12 + g:13 + g], scalar2=st_[:, 28 + g:29 + g],
                                                         op0=ALU.subtract, op1=ALU.mult), reads=gv_B + [st_B] + vn_B, writes=vn_B)
            mb = 4 + m % 2
            k.op(PE, lambda e, mb=mb: e.matmul(PS[mb][:], lhsT=onesrow[0:1, :], rhs=bsprow[0:1, :], start=True, stop=False), reads=[rowB], writes=[PSB[mb]])
            for g in range(4):
                k.op(PE, lambda e, g=g, mb=mb, vn_=vn_: e.matmul(PS[mb][:, g * 128:(g + 1) * 128], lhsT=vn_[:, g, :], rhs=wspT[:, g, :], start=False, stop=(g == 3)),
                     reads=vn_B + [wspB], writes=[PSB[mb]])
            k.op(DVE, lambda e, m=m, mb=mb: e.tensor_tensor(out=AR[:, SG_:SG_ + 4, m * 128:(m + 1) * 128], in0=PS[mb][:].rearrange("p (g t) -> p g t", g=4),
                                                            in1=AR[:, GU_:GU_ + 4, m * 128:(m + 1) * 128], op=ALU.mult),
                 reads=[PSB[mb]] + ARB[GU_:GU_ + 4], writes=ARB[SG_:SG_ + 4])
        for pc in range(2):
            w, wb = wnext(("wo", pc))
            wv3 = r3(w, 8, 512)
            for ff in range(4):
                f = pc * 4 + ff
                bank = 6 + f % 2
                for kk in range(8):
                    src = AT_ + kk if kk < 4 else SG_ + kk - 4
                    k.op(PE, lambda e, kk=kk, ff=ff, src=src, bank=bank, wv3=wv3: e.matmul(PS[bank][:], lhsT=wv3[:, kk, ff * 128:(ff + 1) * 128], rhs=ar(src),
                                                                                         start=(kk == 0), stop=(kk == 7)),
                         reads=[wb, ARB[src]], writes=[PSB[bank]])
                k.op(DVE, lambda e, f=f, bank=bank: e.scalar_tensor_tensor(out=hb[:, f, :], in0=PS[bank][:], scalar=M(0, GT1, f), in1=hb[:, f, :],
                                                                          op0=ALU.mult, op1=ALU.add),
                     reads=[PSB[bank], modB, hbB[f]], writes=[hbB[f]])
        ffn(0, hb, hbB)

    def layer1a(i):
        hb, hbB = hT[i % 2], hB[i % 2]
        gw, gwB = GW[i % 2], GWB[i % 2]
        norm_mod(hb, hbB, T, lambda kc: C("A1_1", kc), lambda kc: M(1, SH1, kc), 0)
        for pp in range(4):
            w, wb = wnext(("pw1", pp))
            wv = w.rearrange("p (k s c) -> p k s c", k=8, s=2, c=256)
            if pp == 0:
                for kc in range(8):
                    for jj in range(2):
                        for s2 in range(2):
                            bank = jj * 2 + s2
                            k.op(PE, lambda e, kc=kc, s2=s2, bank=bank, jj=jj, wv=wv: e.matmul(
                                PS[bank][:], lhsT=wv[:, kc, s2, jj * 128:(jj + 1) * 128], rhs=xmT[:, kc, :], start=(kc == 0), stop=(kc == 7)),
                                reads=[wb, xmB[kc]], writes=[PSB[bank]])
            for jj in range(2):
                c = pp * 2 + jj
                ab_, gb_ = (0, 1) if c % 2 == 0 else (2, 3)
                for s2, bank in ((0, ab_), (1, gb_)):
                    if pp == 0:
                        break
                    for kc in range(8):
                        k.op(PE, lambda e, kc=kc, s2=s2, bank=bank, jj=jj, wv=wv: e.matmul(
                            PS[bank][:], lhsT=wv[:, kc, s2, jj * 128:(jj + 1) * 128], rhs=xmT[:, kc, :], start=(kc == 0), stop=(kc == 7)),
                            reads=[wb, xmB[kc]], writes=[PSB[bank]])
                si = c % 2
                k.op(ACT, lambda e, si=si, gb_=gb_, c=c: e.activation(out=sgl[si][:], in_=PS[gb_][:], func=AF.Sigmoid, bias=V("bpw1", 8 + c)),
                     reads=[PSB[gb_], vecB], writes=[sglB[si]])
                k.op(DVE, lambda e, si=si, ab_=ab_, c=c: e.scalar_tensor_tensor(out=gw[:, c, 15:15 + T], in0=PS[ab_][:], scalar=V("bpw1", c), in1=sgl[si][:],
                                                                               op0=ALU.add, op1=ALU.mult),
                     reads=[PSB[ab_], vecB, sglB[si]], writes=[gwB[c]])
        if i == 0:
            k.op(POOL, lambda e: e.memset(gw[:, :, 0:15], 0.0), reads=gwB, writes=gwB)
        else:
            pg, pgB = GW[(i - 1) % 2], GWB[(i - 1) % 2]
            k.op(POOL, lambda e: e.tensor_copy(out=gw[:, :, 0:15], in_=pg[:, :, T:T + 15]), reads=pgB + gwB, writes=gwB)
            k.op(POOL, lambda e: e.tensor_copy(out=pg[:, :, T + 15:T + 30], in_=gw[:, :, 15:30]), reads=gwB + pgB, writes=pgB)
        if i == NT - 1:
            k.op(POOL, lambda e: e.memset(gw[:, :, T + 15:T + 30], 0.0), reads=gwB, writes=gwB)

    def layer1b(i):
        hb, hbB = hT[i % 2], hB[i % 2]
        gw, gwB = GW[i % 2], GWB[i % 2]
        wd0 = VEC_ROWS["wdw"]
        for f in range(8):
            k.op(DVE, lambda e, f=f: e.tensor_scalar(out=hb[:, f, :], in0=hb[:, f, :], scalar1=C("gb", f), scalar2=None, op0=ALU.add),
                 reads=[hbB[f], coefB], writes=[hbB[f]])
        def stat_mm(kc):
            k.op(PE, lambda e, kc=kc: e.matmul(PS[0][:], lhsT=ones[:], rhs=sq[0][:], start=(kc == 0), stop=(kc == 7)), reads=[sqB[0], onesB], writes=[PSB[0]])
            k.op(PE, lambda e, kc=kc: e.matmul(PS[1][:], lhsT=ones[:], rhs=sq[1][:], start=(kc == 0), stop=(kc == 7)), reads=[sqB[1], onesB], writes=[PSB[1]])

        for kc in range(8):
            acc = ar32(kc)
            accB = [ARB[2 * kc], ARB[2 * kc + 1]]
            bank = 2 + kc % 4
            for j in range(31):
                r = dgstate["n"] % NDG
                dgstate["n"] += 1
                k.op(POOL if j % 3 == 2 else DVE, lambda e, r=r, j=j, kc=kc: e.tensor_scalar(out=dg[r][:], in0=identb[:], scalar1=vecT[:, wd0 + j * 8 + kc:wd0 + j * 8 + kc + 1], scalar2=1.0,
                                                                     op0=ALU.mult, op1=ALU.mult),
                     reads=[identB, vecB], writes=[dgB[r]])
                k.op(PE, lambda e, r=r, j=j, kc=kc, bank=bank: e.matmul(PS[bank][:], lhsT=dg[r][:], rhs=gw[:, kc, j:j + T], start=(j == 0), stop=(j == 30)),
                     reads=[dgB[r], gwB[kc]], writes=[PSB[bank]])
            if kc > 0:
                stat_mm(kc - 1)
            k.op(DVE, lambda e, kc=kc, acc=acc, bank=bank: e.tensor_scalar(out=acc, in0=PS[bank][:], scalar1=V("bdw", kc), scalar2=None, op0=ALU.add),
                 reads=[PSB[bank], vecB], writes=accB)
            k.op(ACT, lambda e, acc=acc: e.activation(out=sq[0][:], in_=acc, func=AF.Copy), reads=accB, writes=[sqB[0]])
            k.op(ACT, lambda e, acc=acc: e.activation(out=sq[1][:], in_=acc, func=AF.Square), reads=accB, writes=[sqB[1]])
        stat_mm(7)
        mean, meanB = sgl[0], sglB[0]
        k.op(ACT, lambda e: e.activation(out=mean[:], in_=PS[0][:], func=AF.Copy, scale=1.0 / D), reads=[PSB[0]], writes=[meanB])
        k.op(POOL, lambda e: e.tensor_tensor(out=sgl[1][:], in0=mean[:], in1=mean[:], op=ALU.mult), reads=[meanB], writes=[sglB[1]])
        k.op(DVE, lambda e: e.scalar_tensor_tensor(out=sd[:], in0=PS[1][:], scalar=1.0 / D, in1=sgl[1][:], op0=ALU.mult, op1=ALU.subtract),
             reads=[PSB[1], sglB[1]], writes=[sdB])
        k.op(ACT, lambda e: e.activation(out=sd[:], in_=sd[:], func=AF.Ln, bias=epsT[:]), reads=[sdB, epsB], writes=[sdB])
        k.op(ACT, lambda e: e.activation(out=rstd[:], in_=sd[:], func=AF.Exp, scale=-0.5), reads=[sdB], writes=[rstdB])
        for kc in range(8):
            acc = ar32(kc)
            accB = [ARB[2 * kc], ARB[2 * kc + 1]]
            if kc % 3 == 2:
                ve_, ta, taB, tb, tbB = POOL, sgl[1], sglB[1], tmp[2], tmpB[2]
            else:
                ve_, ta, taB, tb, tbB = DVE, tmp[0], tmpB[0], tmp[1], tmpB[1]
            k.op(ve_, lambda e, acc=acc, ta=ta: e.tensor_tensor(out=ta[:], in0=acc, in1=mean[:], op=ALU.subtract), reads=accB + [meanB], writes=[taB])
            k.op(ve_, lambda e, ta=ta, tb=tb: e.tensor_tensor(out=tb[:], in0=ta[:], in1=rstd[:], op=ALU.mult), reads=[taB, rstdB], writes=[tbB])
            k.op(ACT, lambda e, kc=kc, tb=tb: e.activation(out=xmT[:, kc, :], in_=tb[:], func=AF.Silu, scale=V("lng", kc), bias=V("lnb", kc)),
                 reads=[tbB, vecB], writes=[xmB[kc]])
        for pc in range(2):
            w, wb = wnext(("pw2", pc))
            wv = r3(w, 8, 512)
            for ff in range(4):
                f = pc * 4 + ff
                bank = 6 + f % 2
                for kc in range(8):
                    k.op(PE, lambda e, kc=kc, ff=ff, bank=bank, wv=wv: e.matmul(PS[bank][:], lhsT=wv[:, kc, ff * 128:(ff + 1) * 128], rhs=xmT[:, kc, :],
                                                                              start=(kc == 0), stop=(kc == 7)),
                         reads=[wb, xmB[kc]], writes=[PSB[bank]])
                k.op(DVE, lambda e, f=f, bank=bank: e.scalar_tensor_tensor(out=hb[:, f, :], in0=PS[bank][:], scalar=M(1, GT1, f), in1=hb[:, f, :],
                                                                          op0=ALU.mult, op1=ALU.add),
                     reads=[PSB[bank], modB, hbB[f]], writes=[hbB[f]])
        ffn(1, hb, hbB)
        rms_stats(hb, hbB, T, 0)
        for kc in range(8):
            k.op(DVE, lambda e, kc=kc: e.scalar_tensor_tensor(out=hb[:, kc, :], in0=hb[:, kc, :], scalar=V("gfin", kc), in1=rstd[:], op0=ALU.mult, op1=ALU.mult),
                 reads=[hbB[kc], vecB, rstdB], writes=[hbB[kc]])
        for m in range(4):
            ib = iostate["n"] % 2
            iostate["n"] += 1
            for half in range(2):
                bank = 6 + half
                for kk in range(4):
                    kc = half * 4 + kk
                    k.op(PE, lambda e, kc=kc, kk=kk, bank=bank, m=m: e.transpose(out=PS[bank][:, kk * 128:(kk + 1) * 128], in_=hb[:, kc, m * 128:(m + 1) * 128],
                                                                               identity=ident[:]),
                         reads=[hbB[kc], identB], writes=[PSB[bank]])
                if half == 0:
                    k.op(ACT, lambda e, ib=ib, bank=bank: e.activation(out=io[ib][:, 0:512], in_=PS[bank][:], func=AF.Copy), reads=[PSB[bank]], writes=[ioB[ib]])
                else:
                    k.op(DVE, lambda e, ib=ib, bank=bank: e.tensor_copy(out=io[ib][:, 512:1024], in_=PS[bank][:]), reads=[PSB[bank], ioB[ib]], writes=[ioB[ib]])
            k.dma(SP, lambda e, ib=ib, m=m: e.dma_start(out=out_d[i * T + m * 128: i * T + (m + 1) * 128, :], in_=io[ib][:]), reads=[ioB[ib]])

    coef_A("A1_1", 1, SC1, "gmix")
    coef_A("A2_1", 1, SC2, "gffn")
    c0 = CO["gb"]
    k.op(DVE, lambda e: e.tensor_tensor(out=coef[:, c0:c0 + 8], in0=modsb[:, 48 + GT1 * 8:48 + GT1 * 8 + 8, 0],
                                        in1=vecT[:, VEC_ROWS["bpw2"]:VEC_ROWS["bpw2"] + 8], op=ALU.mult),
         reads=[modB, vecB, coefB], writes=[coefB])
    outB = Buf("out")
    for i in range(NT):
        layer0(i)
        layer1a(i)
        if i >= 1:
            layer1b(i - 1)
    layer1b(NT - 1)
    assert wstate["next"] == len(seq)

    print("sbuf bytes remaining/partition:", nc.sbuf_bytes_remaining)
    k.emit()
    k.finish(SP, ioB + [dumpB])
    return nc, k


def make_vecs(c_b, c_ctx, b_mod, g_mix, g_ffn, b_pw1, b_dw, ln_g, ln_b, b_pw2, g_final, w_dw, q_gain, k_gain):
    rows = [c_b.reshape(8, 128), c_ctx.reshape(8, 128), b_mod.reshape(96, 128), g_mix.reshape(16, 128), g_ffn.reshape(16, 128),
            b_pw1.reshape(16, 128), b_dw.reshape(8, 128), ln_g.reshape(8, 128), ln_b.reshape(8, 128), b_pw2.reshape(8, 128),
            g_final.reshape(8, 128), w_dw.reshape(248, 128),
            np.concatenate([q_gain.reshape(64), q_gain.reshape(64)])[None, :],
            np.concatenate([k_gain.reshape(64), k_gain.reshape(64)])[None, :]]
    v = np.concatenate(rows, 0).astype(np.float32)
    out = np.zeros((NVEC, 128), np.float32)
    out[:v.shape[0]] = v
    return out


def make_in_maps(x, c, ctx, c_ctx, w_mod, b_mod, g_mix, g_ffn, w_ffn_in, w_ffn_out, w_in, q_gain, k_gain, w_sp, b_sp, w_out,
                 w_pw1, b_pw1, w_dw, b_dw, ln_g, ln_b, w_pw2, b_pw2, g_final):
    f = lambda a: np.ascontiguousarray(np.asarray(a, dtype=np.float32))
    B, S, _ = x.shape
    cos, sin = rope_tables(S)
    shared = {
        "w_mod": f(w_mod), "w_ffn_in": f(w_ffn_in), "w_ffn_out": f(w_ffn_out), "w_in": f(w_in[0]),
        "w_sp": f(w_sp[0]), "b_sp": f(b_sp[0]).reshape(1, 512), "w_out": f(w_out[0]), "w_pw1": f(w_pw1[0]), "w_pw2": f(w_pw2[0]),
        "ident": np.eye(128, dtype=np.float32), "pm": rope_perm(), "cos": cos, "sin": sin,
    }
    maps = []
    for b in range(B):
        m = dict(shared)
        m["x"] = f(x[b])
        m["ctx"] = f(ctx[b])
        m["vecs"] = make_vecs(f(c[b]), f(c_ctx), f(b_mod), f(g_mix), f(g_ffn), f(b_pw1[0]), f(b_dw[0]), f(ln_g[0]), f(ln_b[0]),
                              f(b_pw2[0]), f(g_final), f(w_dw[0]), f(q_gain[0]), f(k_gain[0]))
        maps.append(m)
    return maps


_NC_CACHE = {}


def kernel(**inputs):
    x = np.asarray(inputs["x"])
    B, S, _ = x.shape
    maps = make_in_maps(**inputs)
    if S not in _NC_CACHE:
        _NC_CACHE[S] = build(S)[0]
    nc = _NC_CACHE[S]
    res = run_bass_kernel_spmd(nc, maps, core_ids=list(range(B)))
    return np.stack([np.asarray(r["out"], dtype=np.float32) for r in res.results], 0)
```

```python
import numpy as np
import concourse.bass as bass
import concourse.mybir as mybir
from concourse.bass_utils import run_bass_kernel_spmd

F32 = mybir.dt.float32
BF16 = mybir.dt.bfloat16
AF = mybir.ActivationFunctionType
ALU = mybir.AluOpType
AX = mybir.AxisListType

PE, ACT, DVE, POOL, SP = "pe", "act", "dve", "pool", "sp"

D = 1024
CTX = 256
DFF = 2816
NH = 22
T = 512
EPS = 1e-6
NSLOT = 6
DRIP = 2
SHIFT = -8.0
SLOT_ELEMS = 4096


class Buf:
    __slots__ = ("name", "lw", "rd", "dsem", "dcount")

    def __init__(self, name):
        self.name = name
        self.lw = None
        self.rd = {}
        self.dsem = None
        self.dcount = 0


class Op:
    __slots__ = ("eng", "fn", "deps", "is_dma", "buf", "sig", "sigval", "dval")

    def __init__(self, eng, fn, is_dma=False):
        self.eng = eng
        self.fn = fn
        self.deps = []
        self.is_dma = is_dma
        self.buf = None
        self.sig = False
        self.sigval = 0
        self.dval = 0


class K:
    def __init__(self, nc):
        self.nc = nc
        self.ops = []
        self.h = {PE: nc.tensor, ACT: nc.scalar, DVE: nc.vector, POOL: nc.gpsimd, SP: nc.sync}
        self.dsems = {}

    def op(self, eng, fn, reads=(), writes=()):
        o = Op(eng, fn)
        self._deps(o, reads, writes)
        self.ops.append(o)
        return o

    def dma(self, eng, fn, reads=(), writes=(), track=None, disjoint=False):
        o = Op(eng, fn, is_dma=True)
        if track is None:
            track = writes[0] if writes else reads[0]
        key = (track.name, eng)
        if key not in self.dsems:
            self.dsems[key] = [None, 0]
        o.buf = key
        self.dsems[key][1] += 16
        o.dval = self.dsems[key][1]
        self._deps(o, reads, writes, disjoint)
        self.ops.append(o)
        return o

    def _deps(self, o, reads, writes, disjoint=False):
        for b in reads:
            if b.lw is not None:
                o.deps.append(b.lw)
            b.rd[id(o) if o.is_dma else o.eng] = o
        for b in writes:
            if b.lw is not None and not (disjoint and b.lw.is_dma):
                o.deps.append(b.lw)
            for r in b.rd.values():
                if r is not o:
                    o.deps.append(r)
            b.lw = o
            b.rd = {}

    def emit(self):
        nc = self.nc
        for o in self.ops:
            nd = []
            for d in o.deps:
                if d.is_dma:
                    nd.append(d)
                    continue
                if d.eng == PE and o.eng == PE and not o.is_dma:
                    continue
                d.sig = True
                nd.append(d)
            o.deps = nd
        sems = {e: nc.alloc_semaphore(name=f"s_{e}") for e in self.h}
        for key, v in self.dsems.items():
            v[0] = nc.alloc_semaphore(name=f"d_{key[0]}_{key[1]}")
        cnt = {e: 0 for e in self.h}
        for o in self.ops:
            if o.sig and not o.is_dma:
                cnt[o.eng] += 1
                o.sigval = cnt[o.eng]
        waited = {e: {} for e in self.h}
        self.nwait = 0
        for o in self.ops:
            need = {}
            for d in o.deps:
                if d.is_dma:
                    key, val = self.dsems[d.buf][0], d.dval
                else:
                    key, val = sems[d.eng], d.sigval
                kid = id(key)
                if kid not in need or need[kid][1] < val:
                    need[kid] = (key, val)
            w = waited[o.eng]
            for kid, (key, val) in need.items():
                if w.get(kid, 0) >= val:
                    continue
                self.h[o.eng].wait_ge(key, val)
                w[kid] = val
                self.nwait += 1
            ins = o.fn(self.h[o.eng])
            if o.is_dma:
                ins.then_inc(self.dsems[o.buf][0], 16)
            elif o.sig:
                ins.then_inc(sems[o.eng], 1)

    def finish(self, eng, bufs):
        names = {b.name for b in bufs}
        for key, v in self.dsems.items():
            if key[0] in names:
                self.h[eng].wait_ge(v[0], v[1])


VEC_ROWS = {}
_r = 0
for _n, _c in [("c", 8), ("cctx", 8), ("bmod", 96), ("gmix", 16), ("gffn", 16), ("bpw1", 16), ("bdw", 8),
               ("lng", 8), ("lnb", 8), ("bpw2", 8), ("gfin", 8), ("wdw", 248), ("qg", 1), ("kg", 1)]:
    VEC_ROWS[_n] = _r
    _r += _c
NVEC = 512


def rope_tables(seq):
    t = np.arange(seq)
    row = (t // 64).astype(np.float32)
    col = (t % 64).astype(np.float32)
    inv = (10000.0 ** (-np.arange(0, 32, 2, dtype=np.float32) / 32.0)).astype(np.float32)
    ang_r = (row[:, None] * inv[None, :]).astype(np.float32)
    ang_c = (col[:, None] * inv[None, :]).astype(np.float32)
    cr, sr, cc, sc = np.cos(ang_r), np.sin(ang_r), np.cos(ang_c), np.sin(ang_c)
    cos64 = np.concatenate([cr, cr, cc, cc], axis=1).T
    sin64 = np.concatenate([-sr, sr, -sc, sc], axis=1).T
    cos = np.concatenate([cos64, cos64], 0).astype(np.float32)
    sin = np.concatenate([sin64, sin64], 0).astype(np.float32)
    return np.ascontiguousarray(cos), np.ascontiguousarray(sin)


def rope_perm():
    pm = np.zeros((128, 128), np.float32)
    for hh in range(2):
        for d in range(64):
            blk = d // 16
            src = d + 16 if blk in (0, 2) else d - 16
            pm[hh * 64 + src, hh * 64 + d] = 1.0
    return pm


def piece_sequence(nt):
    seq = [("mod", 0, pc) for pc in range(12)]
    seq += [("mod", 1, pc) for pc in range(12)]
    l0 = [("wq",), ("wsu",), ("wsv",), ("wo", 0), ("wo", 1)] + [("fin", 0, p) for p in range(11)] + \
         [("fout", 0, c) for c in range(8)]
    l1a = [("pw1", p) for p in range(4)]
    l1b = [("pw2", 0), ("pw2", 1)] + [("fin", 1, p) for p in range(11)] + [("fout", 1, c) for c in range(8)]
    for i in range(nt):
        seq += l0 + l1a
        if i >= 1:
            seq += l1b
    seq += l1b
    return seq


def build(SEQ, dbg=None):
    NT = SEQ // T
    NKT = (CTX + SEQ) // 128
    nc = bass.Bass("TRN2", target_bir_lowering=False)
    k = K(nc)

    def din(name, shape, dt=F32):
        return nc.dram_tensor(name, list(shape), dt, kind="ExternalInput").ap()

    x_d = din("x", [SEQ, D])
    ctx_d = din("ctx", [CTX, D])
    vecs_d = din("vecs", [NVEC, 128])
    w_mod_d = din("w_mod", [2, D, 6 * D])
    w_ffn_in_d = din("w_ffn_in", [2, D, 2 * DFF])
    w_ffn_out_d = din("w_ffn_out", [2, DFF, D])
    w_in_d = din("w_in", [D, 1792])
    w_sp_d = din("w_sp", [4, 128, 128])
    b_sp_d = din("b_sp", [1, 512])
    w_out_d = din("w_out", [D, D])
    w_pw1_d = din("w_pw1", [D, 2 * D])
    w_pw2_d = din("w_pw2", [D, D])
    ident_d = din("ident", [128, 128])
    pm_d = din("pm", [128, 128])
    cos_d = din("cos", [128, SEQ])
    sin_d = din("sin", [128, SEQ])
    out_d = nc.dram_tensor("out", [SEQ, D], F32, kind="ExternalOutput").ap()
    dbg_d = {}
    if dbg:
        for n, shp in dbg.items():
            dbg_d[n] = nc.dram_tensor("dbg_" + n, list(shp), F32, kind="ExternalOutput").ap()

    def sb(name, shape, dt=F32):
        return nc.alloc_sbuf_tensor("s_" + name, list(shape), dt)

    ring = sb("ring", [128, NSLOT, SLOT_ELEMS], BF16)
    ringB = [Buf(f"ring{i}") for i in range(NSLOT)]
    KT = sb("KT", [128, CTX + SEQ], BF16)
    KTB = Buf("KT")
    VA = sb("VA", [128, NKT, 2, 128], BF16)
    VAB = Buf("VA")
    hT = [sb(f"hT{i}", [128, 8, T]) for i in range(2)]
    hB = [[Buf(f"h{i}_{c}") for c in range(8)] for i in range(2)]
    io = [sb(f"io{i}", [128, D]) for i in range(2)]
    ioB = [Buf(f"io{i}") for i in range(2)]
    xmT = sb("xmT", [128, 8, T], BF16)
    xmB = [Buf(f"xm{c}") for c in range(8)]
    sq = [sb(f"sq{i}", [128, T], BF16) for i in range(2)]
    sqB = [Buf(f"sq{i}") for i in range(2)]
    tmp = [sb(f"tmp{i}", [128, T]) for i in range(3)]
    tmpB = [Buf(f"tmp{i}") for i in range(3)]
    rstd = sb("rstd", [128, T])
    rstdB = Buf("rstd")
    sd = sb("sd", [128, T])
    sdB = Buf("sd")
    AR = sb("AR", [128, NH, T], BF16)
    ARB = [Buf(f"ar{j}") for j in range(NH)]
    AR32 = AR.bitcast(F32) if hasattr(AR, "bitcast") else None
    GW = [sb(f"GW{i}", [128, 8, T + 30], BF16) for i in range(2)]
    NDG = 8
    dg = [sb(f"dg{i}", [128, 128], BF16) for i in range(NDG)]
    dgB = [Buf(f"dg{i}") for i in range(NDG)]
    identb = sb("identb", [128, 128], BF16)
    dgstate = {"n": 0}
    GWB = [[Buf(f"gw{i}_{c}") for c in range(8)] for i in range(2)]
    sgl = [sb(f"sgl{i}", [128, T]) for i in range(2)]
    sglB = [Buf(f"sgl{i}") for i in range(2)]
    cs = sb("cs", [128, 2, T])
    csB = Buf("cs")
    knT = sb("knT", [128, T], BF16)
    knB = Buf("knT")
    gv = sb("gv", [128, T])
    gvB = Buf("gv")
    gv2 = sb("gv2", [128, T])
    gv2B = Buf("gv2")
    vn = sb("vn", [128, 4, 128], BF16)
    vnB = Buf("vn")
    st = sb("st", [128, 32])
    stB = Buf("st")
    st2 = sb("st2", [128, 32])
    st2B = Buf("st2")
    vstage = sb("vstage", [128, 4, 128])
    vstageB = Buf("vstage")
    vecT = sb("vecT", [128, NVEC])
    vecB = Buf("vecT")
    coef = sb("coef", [128, 96])
    coefB = Buf("coef")
    modsb = sb("modsb", [128, 96, 2])
    modB = Buf("modsb")
    scb = sb("scb", [128, 8, 2], BF16)
    scbB = Buf("scb")
    ident = sb("ident", [128, 128])
    identB = Buf("ident")
    pmf = sb("pmf", [128, 128])
    pmb = sb("pmb", [128, 128], BF16)
    pmB = Buf("pm")
    ones = sb("ones", [128, 128], BF16)
    bones = sb("bones", [128, 128], BF16)
    onesB = Buf("ones")
    onesrow = sb("onesrow", [1, 128])

    bsprow = sb("bsprow", [1, 512])
    rowB = Buf("rows")
    wspT = sb("wspT", [128, 4, 128], BF16)
    wspB = Buf("wspT")
    nshift = sb("nshift", [128, 1])
    nshiftB = Buf("nshift")
    gtmp = sb("gtmp", [1, 4])

    PS = [nc.alloc_psum_tensor(f"ps{i}", [128, T], F32) for i in range(8)]
    PSB = [Buf(f"ps{i}") for i in range(8)]

    def ar(j):
        return AR[:, j, :]

    def ar32(j2):
        return AR[:, 2 * j2:2 * j2 + 2, :].bitcast(F32).rearrange("p a t -> p (a t)")

    QT_, AT_, SG_, GU_, PT_ = 0, 4, 8, 12, 16

    def V(name, i=0):
        r = VEC_ROWS[name] + i
        return vecT[:, r:r + 1]

    CO = {}
    _cc = [0]

    def cocol(name, n=8):
        CO[name] = _cc[0]
        _cc[0] += n

    for nm in ["A1_0", "A1c", "A2_0", "A1_1", "A2_1", "gb"]:
        cocol(nm)

    def C(name, i):
        c = CO[name] + i
        return coef[:, c:c + 1]

    def M(l, which, i, col=0):
        idx = l * 48 + which * 8 + i
        return modsb[:, idx, col:col + 1]

    SH1, SC1, GT1, SH2, SC2, GT2 = range(6)

    scr = {}
    scrB = {}

    def mkscr(key, group):
        scr[key] = nc.dram_tensor("scr_" + "_".join(str(s) for s in key), [128, SLOT_ELEMS], BF16).ap()
        if group not in scrB:
            scrB[group] = Buf("scr_" + group)
        return scr[key], scrB[group]

    def cast(dst, src, gb):
        k.dma(POOL, lambda e: e.dma_start(out=dst, in_=src), writes=[gb], disjoint=True)

    def r3(ap2, a, b):
        return ap2.rearrange("p (a b) -> p a b", a=a, b=b)

    def cast_wkv():
        s, gb = mkscr(("wkv",), "wkv")
        cast(r3(s[:, 0:8 * 256], 8, 256), w_in_d[:, 512:768].rearrange("(k p) c -> p k c", p=128), gb)

    def cast_all_early():
        s, gb = mkscr(("wq",), "wq")
        sv = r3(s, 8, 512)
        for g in range(4):
            for kv in range(2):
                h = kv * 4 + g
                cast(sv[:, :, g * 128 + kv * 64: g * 128 + kv * 64 + 64],
                     w_in_d[:, h * 64:(h + 1) * 64].rearrange("(k p) c -> p k c", p=128), gb)
        s, gb = mkscr(("wsu",), "wsu")
        cast(r3(s, 8, 512), w_in_d[:, 768:1280].rearrange("(k p) c -> p k c", p=128), gb)
        s, gb = mkscr(("wsv",), "wsv")
        cast(r3(s, 8, 512), w_in_d[:, 1280:1792].rearrange("(k p) c -> p k c", p=128), gb)
        for pc in range(2):
            s, gb = mkscr(("wo", pc), "wo")
            sv = r3(s, 8, 512)
            cols = slice(pc * 512, (pc + 1) * 512)
            cast(sv[0:64, 0:4, :], w_out_d[0:256, cols].rearrange("(c p) f -> p c f", p=64), gb)
            cast(sv[64:128, 0:4, :], w_out_d[256:512, cols].rearrange("(c p) f -> p c f", p=64), gb)
            cast(sv[:, 4:8, :], w_out_d[512:1024, cols].rearrange("(k p) f -> p k f", p=128), gb)

    def cast_fin(l, pps):
        for pp in pps:
            s, gb = mkscr(("fin", l, pp), f"fin{l}")
            sv = s.rearrange("p (k s c) -> p k s c", k=8, s=2, c=256)
            wv = w_ffn_in_d[l].rearrange("(k p) (s c) -> p k s c", p=128, s=2)
            for s2 in range(2):
                cast(sv[:, :, s2, :], wv[:, :, s2, pp * 256:(pp + 1) * 256], gb)
    def cast_fout(l):
        for c in range(8):
            s, gb = mkscr(("fout", l, c), f"fout{l}")
            cast(r3(s[:, 0:NH * 128], NH, 128),
                 w_ffn_out_d[l][:, c * 128:(c + 1) * 128].rearrange("(k p) f -> p k f", p=128), gb)

    def cast_pw():
        for pp in range(4):
            s, gb = mkscr(("pw1", pp), "pw1")
            sv = s.rearrange("p (k s c) -> p k s c", k=8, s=2, c=256)
            wv = w_pw1_d.rearrange("(k p) (s c) -> p k s c", p=128, s=2)
            for s2 in range(2):
                cast(sv[:, :, s2, :], wv[:, :, s2, pp * 256:(pp + 1) * 256], gb)
        for pc in range(2):
            s, gb = mkscr(("pw2", pc), "pw2")
            cast(r3(s, 8, 512), w_pw2_d[:, pc * 512:(pc + 1) * 512].rearrange("(k p) c -> p k c", p=128), gb)

    seq = piece_sequence(NT)
    wstate = {"issued": 0, "next": 0}

    def group_of(key):
        if key[0] in ("fin", "fout"):
            return f"{key[0]}{key[1]}"
        return key[0]

    def issue_load(n):
        key = seq[n]
        slot = n % NSLOT
        if key[0] == "mod":
            _, l, pc = key
            dst = r3(ring[:, slot, :], 8, 512)
            src = w_mod_d[l][:, pc * 512:(pc + 1) * 512].rearrange("(k p) c -> p k c", p=128)
            k.dma(POOL, lambda e: e.dma_start(out=dst, in_=src), writes=[ringB[slot]])
        else:
            s = scr[key]
            n_el = SLOT_ELEMS
            if key[0] == "wkv":
                n_el = 8 * 256
            elif key[0] == "fout":
                n_el = NH * 128
            k.dma(SP, lambda e: e.dma_start(out=ring[:, slot, 0:n_el], in_=s[:, 0:n_el]),
                  reads=[scrB[group_of(key)]], writes=[ringB[slot]])

    def wnext(key):
        n = wstate["next"]
        assert seq[n] == key, (seq[n], key)
        while wstate["issued"] < min(len(seq), n + NSLOT):
            issue_load(wstate["issued"])
            wstate["issued"] += 1
        wstate["next"] += 1
        slot = n % NSLOT
        return ring[:, slot, :], ringB[slot]

    k.dma(SP, lambda e: e.dma_start(out=vstage[:], in_=vecs_d.rearrange("(g r) c -> r g c", r=128)), writes=[vstageB])
    k.dma(SP, lambda e: e.dma_start(out=ident[:], in_=ident_d[:]), writes=[identB])
    k.dma(SP, lambda e: e.dma_start(out=pmf[:], in_=pm_d[:]), writes=[pmB])
    k.dma(SP, lambda e: e.dma_start(out=bsprow[:], in_=b_sp_d[:]), writes=[rowB])

    cast_wkv()
    epsT = sb("epsT", [128, 1])
    epsB = Buf("eps")
    k.op(DVE, lambda e: e.memset(epsT[:], EPS), writes=[epsB])
    neghalf = sb("neghalf", [128, 4])
    k.op(DVE, lambda e: e.memset(neghalf[:], -0.5), reads=[epsB], writes=[epsB])

    k.op(POOL, lambda e: e.memset(ones[:], 1.0), writes=[onesB])
    k.op(POOL, lambda e: e.memset(bones[:], 0.0), writes=[onesB])
    k.op(POOL, lambda e: e.memset(bones[0:64, 0:64], 1.0), writes=[onesB])
    k.op(POOL, lambda e: e.memset(bones[64:128, 64:128], 1.0), writes=[onesB])
    k.op(POOL, lambda e: e.memset(onesrow[:], 1.0), writes=[rowB])

    k.op(POOL, lambda e: e.memset(VA[:, :, :, 64:128], 1.0), writes=[VAB])
    k.op(DVE, lambda e: e.tensor_copy(out=pmb[:], in_=pmf[:]), reads=[pmB], writes=[pmB])
    k.op(DVE, lambda e: e.tensor_copy(out=identb[:], in_=ident[:]), reads=[identB], writes=[identB])

    for g in range(4):
        k.op(PE, lambda e, g=g: e.transpose(out=PS[0][:, g * 128:(g + 1) * 128], in_=vstage[:, g, :], identity=ident[:]),
             reads=[vstageB, identB], writes=[PSB[0]])
    k.op(DVE, lambda e: e.tensor_copy(out=vecT[:], in_=PS[0][:]), reads=[PSB[0]], writes=[vecB])

    k.dma(SP, lambda e: e.dma_start(out=vstage[:], in_=w_sp_d.rearrange("g p q -> p g q")), reads=[vstageB], writes=[vstageB])
    for g in range(4):
        k.op(PE, lambda e, g=g: e.transpose(out=PS[1][:, g * 128:(g + 1) * 128], in_=vstage[:, g, :], identity=ident[:]),
             reads=[vstageB, identB], writes=[PSB[1]])
    k.op(DVE, lambda e: e.tensor_copy(out=wspT[:].rearrange("p g q -> p (g q)"), in_=PS[1][:]), reads=[PSB[1]], writes=[wspB])

    k.op(ACT, lambda e: e.activation(out=st[:, 0:2], in_=vecT[:, VEC_ROWS["qg"]:VEC_ROWS["qg"] + 2], func=AF.Abs),
         reads=[vecB], writes=[stB])
    k.op(PE, lambda e: e.transpose(out=PS[2][0:2, 0:128], in_=st[:, 0:2], identity=ident[:]), reads=[stB, identB], writes=[PSB[2]])
    k.op(DVE, lambda e: e.tensor_reduce(out=st[0:2, 2:3], in_=PS[2][0:2, 0:128], axis=AX.X, op=ALU.max), reads=[PSB[2], stB], writes=[stB])
    k.op(PE, lambda e: e.transpose(out=PS[2][0:1, 128:130], in_=st[0:2, 2:3], identity=ident[0:2, 0:2]), reads=[stB, identB], writes=[PSB[2]])
    k.op(DVE, lambda e: e.tensor_copy(out=gtmp[0:1, 0:2], in_=PS[2][0:1, 128:130]), reads=[PSB[2]], writes=[stB])
    k.op(DVE, lambda e: e.tensor_tensor(out=gtmp[0:1, 2:3], in0=gtmp[0:1, 0:1], in1=gtmp[0:1, 1:2], op=ALU.mult), reads=[stB], writes=[stB])
    k.op(PE, lambda e: e.matmul(PS[2][:, 256:257], lhsT=onesrow[0:1, :], rhs=gtmp[0:1, 2:3], start=True, stop=True), reads=[stB, rowB], writes=[PSB[2]])
    k.op(ACT, lambda e: e.activation(out=nshift[:], in_=PS[2][:, 256:257], func=AF.Copy, scale=-8.0), reads=[PSB[2]], writes=[nshiftB])

    k.op(ACT, lambda e: e.activation(out=scb[:, :, 0], in_=vecT[:, VEC_ROWS["c"]:VEC_ROWS["c"] + 8], func=AF.Silu), reads=[vecB], writes=[scbB])
    k.op(ACT, lambda e: e.activation(out=scb[:, :, 1], in_=vecT[:, VEC_ROWS["cctx"]:VEC_ROWS["cctx"] + 8], func=AF.Silu), reads=[vecB, scbB], writes=[scbB])

    def mod_pieces(l, pcs, mbank):
        for pc in pcs:
            w, wb = wnext(("mod", l, pc))
            wv = r3(w, 8, 512)
            for fc in range(4):
                idx = l * 48 + pc * 4 + fc
                for kk in range(8):
                    k.op(PE, lambda e, wv=wv, fc=fc, kk=kk, idx=idx: e.matmul(
                        PS[mbank][:, idx * 2:idx * 2 + 2], lhsT=wv[:, kk, fc * 128:(fc + 1) * 128], rhs=scb[:, kk, :],
                        start=(kk == 0), stop=(kk == 7)), reads=[wb, scbB], writes=[PSB[mbank]])

    def mod_finish(l, mbank, i0=0, i1=48):
        for col in range(2):
            k.op(DVE, lambda e, col=col: e.tensor_tensor(
                out=modsb[:, l * 48 + i0:l * 48 + i1, col],
                in0=PS[mbank][:, (l * 48 + i0) * 2:(l * 48 + i1) * 2].rearrange("p (i c) -> p i c", c=2)[:, :, col],
                in1=vecT[:, VEC_ROWS["bmod"] + l * 48 + i0:VEC_ROWS["bmod"] + l * 48 + i1], op=ALU.add), reads=[PSB[mbank], vecB, modB], writes=[modB])

    mod_pieces(0, range(4), 3)
    mod_finish(0, 3, 0, 16)
    cast_all_early()
    cast_batches = [lambda: cast_fin(0, range(0, 6)), lambda: cast_fin(0, range(6, 11)), lambda: cast_fout(0), cast_pw,
                    lambda: cast_fin(1, range(0, 6)), lambda: cast_fin(1, range(6, 11)), lambda: cast_fout(1)]

    def coef_A(name, l, which, gname, col=0):
        c0 = CO[name]
        k.op(DVE, lambda e: e.scalar_tensor_tensor(
            out=coef[:, c0:c0 + 8], in0=modsb[:, l * 48 + which * 8: l * 48 + which * 8 + 8, col], scalar=1.0,
            in1=vecT[:, VEC_ROWS[gname] + l * 8: VEC_ROWS[gname] + l * 8 + 8], op0=ALU.add, op1=ALU.mult),
            reads=[modB, vecB, coefB], writes=[coefB])

    coef_A("A1_0", 0, SC1, "gmix")
    coef_A("A1c", 0, SC1, "gmix", col=1)

    iostate = {"n": 0}

    def load_tile_T(src_rows, ntok, hbuf, hbufB):
        for m in range(ntok // 128):
            ib = iostate["n"] % 2
            iostate["n"] += 1
            k.dma(SP, lambda e, ib=ib, m=m: e.dma_start(out=io[ib][:], in_=src_rows[m * 128:(m + 1) * 128, :]), writes=[ioB[ib]])
            for half in range(2):
                bank = 6 + half
                for kk in range(4):
                    kc = half * 4 + kk
                    k.op(PE, lambda e, ib=ib, kc=kc, kk=kk, bank=bank: e.transpose(
                        out=PS[bank][:, kk * 128:(kk + 1) * 128], in_=io[ib][:, kc * 128:(kc + 1) * 128], identity=ident[:]),
                        reads=[ioB[ib], identB], writes=[PSB[bank]])
                eng = ACT if half == 0 else DVE
                if eng == ACT:
                    k.op(ACT, lambda e, half=half, m=m, bank=bank: e.activation(
                        out=hbuf[:, half * 4:half * 4 + 4, m * 128:(m + 1) * 128],
                        in_=PS[bank][:].rearrange("p (a t) -> p a t", a=4), func=AF.Copy),
                        reads=[PSB[bank]], writes=hbufB[half * 4:half * 4 + 4])
                else:
                    k.op(DVE, lambda e, half=half, m=m, bank=bank: e.tensor_copy(
                        out=hbuf[:, half * 4:half * 4 + 4, m * 128:(m + 1) * 128],
                        in_=PS[bank][:].rearrange("p (a t) -> p a t", a=4)),
                        reads=[PSB[bank]], writes=hbufB[half * 4:half * 4 + 4])

    def rms_stats(hbuf, hbufB, ntok, bank, sd_=None, sd_B=None, rs_=None, rs_B=None):
        if sd_ is None:
            sd_, sd_B, rs_, rs_B = sd[:], [sdB], rstd[:], [rstdB]
        for kc in range(8):
            s = kc % 2
            k.op(ACT, lambda e, kc=kc, s=s: e.activation(out=sq[s][:, 0:ntok], in_=hbuf[:, kc, 0:ntok], func=AF.Square),
                 reads=[hbufB[kc]], writes=[sqB[s]])
            k.op(PE, lambda e, kc=kc, s=s: e.matmul(PS[bank][:, 0:ntok], lhsT=ones[:], rhs=sq[s][:, 0:ntok],
                                                    start=(kc == 0), stop=(kc == 7)),
                 reads=[sqB[s], onesB], writes=[PSB[bank]])
        k.op(ACT, lambda e: e.activation(out=sd_[:, 0:ntok], in_=PS[bank][:, 0:ntok], func=AF.Ln, scale=1.0 / D, bias=epsT[:]),
             reads=[PSB[bank], epsB], writes=sd_B)
        k.op(ACT, lambda e: e.activation(out=rs_[:, 0:ntok], in_=sd_[:, 0:ntok], func=AF.Exp, scale=-0.5), reads=sd_B, writes=rs_B)

    tmpstate = {"n": 0}

    def norm_mod(hbuf, hbufB, ntok, Acol, Bcol, bank, ve=POOL, xm_=None, xm_B=None, sd_=None, sd_B=None, rs_=None, rs_B=None):
        if xm_ is None:
            xm_, xm_B = xmT, xmB
        if sd_ is None:
            sd_, sd_B, rs_, rs_B = sd[:], [sdB], rstd[:], [rstdB]
        rms_stats(hbuf, hbufB, ntok, bank, sd_, sd_B, rs_, rs_B)
        for kc in range(8):
            ti = tmpstate["n"] % 3
            tmpstate["n"] += 1
            k.op(DVE if (ve == DVE or kc % 3 != 2) else POOL, lambda e, kc=kc, ti=ti: e.tensor_tensor(out=tmp[ti][:, 0:ntok], in0=hbuf[:, kc, 0:ntok], in1=rs_[:, 0:ntok], op=ALU.mult),
                 reads=[hbufB[kc]] + rs_B, writes=[tmpB[ti]])
            k.op(ACT, lambda e, kc=kc, ti=ti: e.activation(out=xm_[:, kc, 0:ntok], in_=tmp[ti][:, 0:ntok], func=AF.Identity,
                                                           scale=Acol(kc), bias=Bcol(kc)),
                 reads=[tmpB[ti], coefB, modB], writes=[xm_B[kc]])

    def headnorm_rstd(src_ps, src_psB, ntok, bank, sqi=0, sd_=None, sd_B=None, rs_=None, rs_B=None):
        if sd_ is None:
            sd_, sd_B, rs_, rs_B = sd[:], [sdB], rstd[:], [rstdB]
        k.op(ACT, lambda e: e.activation(out=sq[sqi][:, 0:ntok], in_=src_ps[:, 0:ntok], func=AF.Square), reads=[src_psB], writes=[sqB[sqi]])
        k.op(PE, lambda e: e.matmul(PS[bank][:, 0:ntok], lhsT=bones[:], rhs=sq[sqi][:, 0:ntok], start=True, stop=True),
             reads=[sqB[sqi], onesB], writes=[PSB[bank]])
        k.op(ACT, lambda e: e.activation(out=sd_[:, 0:ntok], in_=PS[bank][:, 0:ntok], func=AF.Ln, scale=1.0 / 64, bias=epsT[:]),
             reads=[PSB[bank], epsB], writes=sd_B)
        k.op(ACT, lambda e: e.activation(out=rs_[:, 0:ntok], in_=sd_[:, 0:ntok], func=AF.Exp, scale=-0.5), reads=sd_B, writes=rs_B)

    def rope(src_bf, src_B, dst, dst_B, bank, ve=POOL, t0=None, t0B=None, t1=None, t1B=None):
        if t0 is None:
            t0, t0B, t1, t1B = tmp[0][:], [tmpB[0]], tmp[1][:], [tmpB[1]]
        k.op(PE, lambda e: e.matmul(PS[bank][:], lhsT=pmb[:], rhs=src_bf, start=True, stop=True), reads=[src_B, pmB], writes=[PSB[bank]])
        k.op(ve, lambda e: e.tensor_tensor(out=t0, in0=src_bf, in1=cs[:, 0, :], op=ALU.mult), reads=[src_B, csB], writes=t0B)
        k.op(DVE, lambda e: e.tensor_tensor(out=t1, in0=PS[bank][:], in1=cs[:, 1, :], op=ALU.mult), reads=[PSB[bank], csB], writes=t1B)
        k.op(ve, lambda e: e.tensor_tensor(out=dst, in0=t0, in1=t1, op=ALU.add), reads=t0B + t1B, writes=dst_B)


    def dump(name, ap_fn, bufs):
        if dbg and name in dbg_d:
            k.dma(SP, lambda e: e.dma_start(out=dbg_d[name], in_=ap_fn()), reads=bufs, track=dumpB)

    dumpB = Buf("dump")

    wkv_t = sb("wkv", [128, 8 * 256], BF16)
    wkvB = Buf("wkv")
    k.dma(SP, lambda e: e.dma_start(out=wkv_t[:], in_=scr[("wkv",)][:, 0:8 * 256]), reads=[scrB["wkv"]], writes=[wkvB])
    wkvv = r3(wkv_t[:], 8, 256)

    def phaseA(src_rows, ntok, key_off, latent, tile_i):
        par = (tile_i + (1 if latent else 0)) % 2
        hb, hbB = hT[par], hB[par]
        if par == 0:
            xm_, xm_B = xmT, xmB
            nsd, nsdB, nrs, nrsB = sd[:], [sdB], rstd[:], [rstdB]
            kn_, kn_B = knT[:], knB
        else:
            xm_, xm_B = AR[:, 0:8, :], ARB[0:8]
            nsd, nsdB = ar32(4), [ARB[8], ARB[9]]
            nrs, nrsB = ar32(5), [ARB[10], ARB[11]]
            kn_, kn_B = ar(16), ARB[16]
        hsd, hsdB = ar32(6), [ARB[12], ARB[13]]
        hrs, hrsB = ar32(7), [ARB[14], ARB[15]]
        load_tile_T(src_rows, ntok, hb, hbB)
        if latent:
            norm_mod(hb, hbB, ntok, lambda kc: C("A1_0", kc), lambda kc: M(0, SH1, kc), 0, ve=DVE, xm_=xm_, xm_B=xm_B, sd_=nsd, sd_B=nsdB, rs_=nrs, rs_B=nrsB)
        else:
            norm_mod(hb, hbB, ntok, lambda kc: C("A1c", kc), lambda kc: M(0, SH1, kc, 1), 0, ve=DVE, xm_=xm_, xm_B=xm_B, sd_=nsd, sd_B=nsdB, rs_=nrs, rs_B=nrsB)
        for kc in range(8):
            k.op(PE, lambda e, kc=kc: e.matmul(PS[1][:, 0:ntok], lhsT=wkvv[:, kc, 0:128], rhs=xm_[:, kc, 0:ntok], start=(kc == 0), stop=(kc == 7)),
                 reads=[wkvB, xm_B[kc]], writes=[PSB[1]])
        headnorm_rstd(PS[1], PSB[1], ntok, 2, sqi=par, sd_=hsd, sd_B=hsdB, rs_=hrs, rs_B=hrsB)
        dstK = KT[:, key_off:key_off + ntok] if not latent else kn_[:, 0:ntok]
        dstKB = [KTB] if not latent else [kn_B]
        k.op(DVE, lambda e: e.scalar_tensor_tensor(out=dstK, in0=PS[1][:, 0:ntok], scalar=V("kg"), in1=hrs[:, 0:ntok], op0=ALU.mult, op1=ALU.mult),
             reads=[PSB[1], vecB] + hrsB, writes=dstKB)
        if latent:
            k.dma(SP, lambda e: e.dma_start(out=cs[:, 0, :], in_=cos_d[:, tile_i * T:(tile_i + 1) * T]), writes=[csB])
            k.dma(SP, lambda e: e.dma_start(out=cs[:, 1, :], in_=sin_d[:, tile_i * T:(tile_i + 1) * T]), reads=[], writes=[csB])
            rope(kn_, kn_B, KT[:, key_off:key_off + ntok], [KTB], 3, ve=DVE)
        nsub = ntok // 128
        for m in range(nsub):
            for kc in range(8):
                k.op(PE, lambda e, kc=kc, m=m: e.matmul(PS[4][:, m * 128:(m + 1) * 128], lhsT=xm_[:, kc, m * 128:(m + 1) * 128], rhs=wkvv[:, kc, 128:256],
                                                        start=(kc == 0), stop=(kc == 7)),
                     reads=[wkvB, xm_B[kc]], writes=[PSB[4]])
        j0 = key_off // 128
        k.op(ACT, lambda e: e.activation(out=VA[:, j0:j0 + nsub, :, 0:64],
                                         in_=PS[4][:, 0:nsub * 128].rearrange("p (j h d) -> p j h d", j=nsub, h=2, d=64), func=AF.Copy),
             reads=[PSB[4]], writes=[VAB])

    phaseA(ctx_d, CTX, 0, False, 0)
    mod_todo = [(0, pc) for pc in range(4, 12)] + [(1, pc) for pc in range(12)]
    for i in range(NT):
        for _ in range(3):
            if mod_todo:
                l_, pc_ = mod_todo.pop(0)
                mod_pieces(l_, [pc_], 5)
        if NT < 8 and cast_batches:
            cast_batches.pop(0)()
        phaseA(x_d[i * T:(i + 1) * T, :], T, CTX + i * T, True, i)
    while mod_todo:
        l_, pc_ = mod_todo.pop(0)
        mod_pieces(l_, [pc_], 5)
    mod_finish(0, 5, 16, 48)
    coef_A("A2_0", 0, SC2, "gffn")
    late_casts = []
    head_casts = []
    while cast_batches:
        b = cast_batches.pop(0)
        if NT >= 8 and len(cast_batches) < 3:
            late_casts.append(b)
        elif NT >= 8:
            head_casts.append(b)
        else:
            b()
    mod_finish(1, 5)


    def ffn(l, hb, hbB):
        A2 = "A2_0" if l == 0 else "A2_1"
        norm_mod(hb, hbB, T, lambda kc: C(A2, kc), lambda kc: M(l, SH2, kc), 0)
        for pp in range(11):
            w, wb = wnext(("fin", l, pp))
            wv = w.rearrange("p (k s c) -> p k s c", k=8, s=2, c=256)
            if pp == 0:
                for kc in range(8):
                    for jj in range(2):
                        for s2 in range(2):
                            bank = jj * 2 + s2
                            k.op(PE, lambda e, kc=kc, s2=s2, bank=bank, jj=jj, wv=wv: e.matmul(
                                PS[bank][:], lhsT=wv[:, kc, s2, jj * 128:(jj + 1) * 128], rhs=xmT[:, kc, :], start=(kc == 0), stop=(kc == 7)),
                                reads=[wb, xmB[kc]], writes=[PSB[bank]])
            for jj in range(2):
                j = pp * 2 + jj
                gb_, ub_ = (0, 1) if j % 2 == 0 else (2, 3)
                for s2, bank in ((0, gb_), (1, ub_)):
                    if pp == 0:
                        break
                    for kc in range(8):
                        k.op(PE, lambda e, kc=kc, s2=s2, bank=bank, jj=jj, wv=wv: e.matmul(
                            PS[bank][:], lhsT=wv[:, kc, s2, jj * 128:(jj + 1) * 128], rhs=xmT[:, kc, :], start=(kc == 0), stop=(kc == 7)),
                            reads=[wb, xmB[kc]], writes=[PSB[bank]])
                si = j % 2
                k.op(ACT, lambda e, si=si, gb_=gb_: e.activation(out=sgl[si][:], in_=PS[gb_][:], func=AF.Silu), reads=[PSB[gb_]], writes=[sglB[si]])
                k.op(DVE, lambda e, si=si, ub_=ub_, j=j: e.tensor_tensor(out=ar(j), in0=PS[ub_][:], in1=sgl[si][:], op=ALU.mult),
                     reads=[PSB[ub_], sglB[si]], writes=[ARB[j]])
        for f in range(8):
            w, wb = wnext(("fout", l, f))
            wv = r3(w[:, 0:NH * 128], NH, 128)
            bank = 4 + f % 4
            for j in range(NH):
                k.op(PE, lambda e, j=j, wv=wv, bank=bank: e.matmul(PS[bank][:], lhsT=wv[:, j, :], rhs=ar(j), start=(j == 0), stop=(j == NH - 1)),
                     reads=[wb, ARB[j]], writes=[PSB[bank]])
            k.op(DVE, lambda e, f=f, bank=bank: e.scalar_tensor_tensor(out=hb[:, f, :], in0=PS[bank][:], scalar=M(l, GT2, f), in1=hb[:, f, :],
                                                                      op0=ALU.mult, op1=ALU.add),
                 reads=[PSB[bank], modB, hbB[f]], writes=[hbB[f]])

    def layer0(i):
        hb, hbB = hT[i % 2], hB[i % 2]
        if i == 0:
            for b_ in head_casts:
                b_()
        load_tile_T(x_d[i * T:(i + 1) * T, :], T, hb, hbB)
        norm_mod(hb, hbB, T, lambda kc: C("A1_0", kc), lambda kc: M(0, SH1, kc), 0)
        k.dma(SP, lambda e: e.dma_start(out=cs[:, 0, :], in_=cos_d[:, i * T:(i + 1) * T]), writes=[csB])
        k.dma(SP, lambda e: e.dma_start(out=cs[:, 1, :], in_=sin_d[:, i * T:(i + 1) * T]), writes=[csB])
        w, wq_b = wnext(("wq",))
        wq_v = r3(w, 8, 512)
        SA, SBK = 6, 7

        def qchunk_gen(c):
            for kc in range(8):
                k.op(PE, lambda e, kc=kc: e.matmul(PS[SA][:], lhsT=wq_v[:, kc, c * 128:(c + 1) * 128], rhs=xmT[:, kc, :], start=(kc == 0), stop=(kc == 7)),
                     reads=[wq_b, xmB[kc]], writes=[PSB[SA]])
                if kc == 3:
                    yield
            yield
            k.op(DVE, lambda e: e.tensor_copy(out=tmp[2][:], in_=PS[SA][:]), reads=[PSB[SA]], writes=[tmpB[2]])
            yield
            k.op(POOL, lambda e: e.tensor_tensor(out=sq[0][:], in0=tmp[2][:], in1=tmp[2][:], op=ALU.mult), reads=[tmpB[2]], writes=[sqB[0]])
            yield
            k.op(PE, lambda e: e.matmul(PS[SBK][:], lhsT=bones[:], rhs=sq[0][:], start=True, stop=True), reads=[sqB[0], onesB], writes=[PSB[SBK]])
            yield
            k.op(ACT, lambda e: e.activation(out=sd[:], in_=PS[SBK][:], func=AF.Ln, scale=1.0 / 64, bias=epsT[:]), reads=[PSB[SBK], epsB], writes=[sdB])
            yield
            k.op(ACT, lambda e: e.activation(out=rstd[:], in_=sd[:], func=AF.Exp, scale=-0.5), reads=[sdB], writes=[rstdB])
            yield
            k.op(DVE, lambda e: e.scalar_tensor_tensor(out=knT[:], in0=PS[SA][:], scalar=V("qg"), in1=rstd[:], op0=ALU.mult, op1=ALU.mult),
                 reads=[PSB[SA], vecB, rstdB], writes=[knB])
            yield
            k.op(PE, lambda e: e.matmul(PS[SBK][:], lhsT=pmb[:], rhs=knT[:], start=True, stop=True), reads=[knB, pmB], writes=[PSB[SBK]])
            k.op(POOL, lambda e: e.tensor_tensor(out=tmp[0][:], in0=knT[:], in1=cs[:, 0, :], op=ALU.mult), reads=[knB, csB], writes=[tmpB[0]])
            yield
            k.op(DVE, lambda e: e.tensor_tensor(out=tmp[1][:], in0=PS[SBK][:], in1=cs[:, 1, :], op=ALU.mult), reads=[PSB[SBK], csB], writes=[tmpB[1]])
            yield
            k.op(POOL, lambda e: e.tensor_tensor(out=ar(QT_ + c), in0=tmp[0][:], in1=tmp[1][:], op=ALU.add), reads=[tmpB[0], tmpB[1]], writes=[ARB[QT_ + c]])
            yield

        def su_gen():
            w2, wb2 = wnext(("wsu",))
            wvs = r3(w2, 8, 512)
            for c in range(4):
                bank = SA + c % 2
                for kc in range(8):
                    k.op(PE, lambda e, kc=kc, c=c, bank=bank: e.matmul(PS[bank][:], lhsT=wvs[:, kc, c * 128:(c + 1) * 128], rhs=xmT[:, kc, :], start=(kc == 0), stop=(kc == 7)),
                         reads=[wb2, xmB[kc]], writes=[PSB[bank]])
                    if kc == 3:
                        yield
                yield
                k.op(ACT, lambda e, c=c, bank=bank: e.activation(out=ar(GU_ + c), in_=PS[bank][:], func=AF.Gelu_apprx_tanh), reads=[PSB[bank]], writes=[ARB[GU_ + c]])
                yield

        def sv_gen():
            w3, wb3 = wnext(("wsv",))
            wv2 = r3(w3, 8, 512)
            for m in range(4):
                bank = SA + m % 2
                mb = SBK - m % 2
                for kc in range(8):
                    k.op(PE, lambda e, kc=kc, m=m, bank=bank: e.matmul(PS[bank][:], lhsT=xmT[:, kc, m * 128:(m + 1) * 128], rhs=wv2[:, kc, :], start=(kc == 0), stop=(kc == 7)),
                         reads=[wb3, xmB[kc]], writes=[PSB[bank]])
                    if kc == 3:
                        yield
                yield
                k.op(ACT, lambda e, bank=bank: e.activation(out=gv[:], in_=PS[bank][:], func=AF.Gelu_apprx_tanh), reads=[PSB[bank]], writes=[gvB])
                yield
                k.op(POOL, lambda e: e.tensor_tensor(out=gv2[:], in0=gv[:], in1=gv[:], op=ALU.mult), reads=[gvB], writes=[gv2B])
                k.op(DVE, lambda e: e.tensor_reduce(out=st[:, 4:8], in_=gv[:].rearrange("p (g c) -> p g c", g=4), axis=AX.X, op=ALU.add), reads=[gvB, stB], writes=[stB])
                yield
                k.op(DVE, lambda e: e.tensor_reduce(out=st[:, 8:12], in_=gv2[:].rearrange("p (g c) -> p g c", g=4), axis=AX.X, op=ALU.add), reads=[gv2B, stB], writes=[stB])
                yield
                k.op(DVE, lambda e: e.tensor_scalar(out=st[:, 12:16], in0=st[:, 4:8], scalar1=1.0 / 128, scalar2=None, op0=ALU.mult), reads=[stB], writes=[stB])
                yield
                k.op(DVE, lambda e: e.tensor_tensor(out=st[:, 16:20], in0=st[:, 12:16], in1=st[:, 12:16], op=ALU.mult), reads=[stB], writes=[stB])
                yield
                k.op(DVE, lambda e: e.scalar_tensor_tensor(out=st[:, 20:24], in0=st[:, 8:12], scalar=1.0 / 128, in1=st[:, 16:20], op0=ALU.mult, op1=ALU.subtract),
                     reads=[stB], writes=[stB])
                yield
                k.op(ACT, lambda e: e.activation(out=st[:, 24:28], in_=st[:, 20:24], func=AF.Ln, bias=epsT[:]), reads=[stB, epsB], writes=[stB])
                yield
                k.op(ACT, lambda e: e.activation(out=st[:, 28:32], in_=st[:, 24:28], func=AF.Exp, scale=-0.5), reads=[stB], writes=[stB])
                yield
                for g in range(4):
                    k.op(DVE, lambda e, g=g: e.tensor_scalar(out=vn[:, g, :], in0=gv[:, g * 128:(g + 1) * 128], scalar1=st[:, 12 + g:13 + g], scalar2=st[:, 28 + g:29 + g],
                                                             op0=ALU.subtract, op1=ALU.mult), reads=[gvB, stB, vnB], writes=[vnB])
                yield
                k.op(PE, lambda e, mb=mb: e.matmul(PS[mb][:], lhsT=onesrow[0:1, :], rhs=bsprow[0:1, :], start=True, stop=False), reads=[rowB], writes=[PSB[mb]])
                for g in range(4):
                    k.op(PE, lambda e, g=g, mb=mb: e.matmul(PS[mb][:, g * 128:(g + 1) * 128], lhsT=vn[:, g, :], rhs=wspT[:, g, :], start=False, stop=(g == 3)),
                         reads=[vnB, wspB], writes=[PSB[mb]])
                yield
                k.op(DVE, lambda e, m=m, mb=mb: e.tensor_tensor(out=AR[:, SG_:SG_ + 4, m * 128:(m + 1) * 128], in0=PS[mb][:].rearrange("p (g t) -> p g t", g=4),
                                                                in1=AR[:, GU_:GU_ + 4, m * 128:(m + 1) * 128], op=ALU.mult),
                     reads=[PSB[mb]] + ARB[GU_:GU_ + 4], writes=ARB[SG_:SG_ + 4])
                yield

        qdone = {0: False, 1: False, 2: False, 3: False}

        def side_gen():
            for c in (1, 2, 3):
                yield from qchunk_gen(c)
                qdone[c] = True

        for _ in qchunk_gen(0):
            pass
        qdone[0] = True
        side = side_gen()
        side_alive = [True]

        def pull():
            if side_alive[0]:
                try:
                    next(side)
                except StopIteration:
                    side_alive[0] = False

        def qk(c, j):
            for hh in range(2):
                bank = hh * 2 + j % 2
                ps_ = slice(hh * 64, (hh + 1) * 64)
                k.op(PE, lambda e, bank=bank, ps_=ps_, j=j, c=c: e.matmul(PS[bank][:], lhsT=KT[ps_, j * 128:(j + 1) * 128], rhs=AR[ps_, QT_ + c, :],
                                                                    start=True, stop=True),
                     reads=[KTB, ARB[QT_ + c]], writes=[PSB[bank]])

        its = [(c, j) for c in range(4) for j in range(NKT)]
        qk(0, 0)
        for n, (c, j) in enumerate(its):
            if n + 1 < len(its):
                c2, j2 = its[n + 1]
                while not qdone[c2]:
                    pull()
                qk(c2, j2)
            for hh in range(2):
                bank = hh * 2 + j % 2
                pt = PT_ + hh * 2 + j % 2
                k.op(ACT, lambda e, bank=bank, pt=pt: e.activation(out=ar(pt), in_=PS[bank][:], func=AF.Exp, scale=0.125, bias=SHIFT),
                     reads=[PSB[bank]], writes=[ARB[pt]])
            for hh in range(2):
                pt = PT_ + hh * 2 + j % 2
                k.op(PE, lambda e, hh=hh, pt=pt, j=j: e.matmul(PS[4 + hh][:], lhsT=VA[:, j, hh, :], rhs=ar(pt), start=(j == 0), stop=(j == NKT - 1)),
                     reads=[VAB, ARB[pt]], writes=[PSB[4 + hh]])
            if j == NKT - 1:
                for hh in range(2):
                    k.op(DVE, lambda e, hh=hh: e.tensor_copy(out=sgl[hh][:], in_=PS[4 + hh][:]), reads=[PSB[4 + hh]], writes=[sglB[hh]])
                for hh in range(2):
                    k.op(DVE, lambda e, hh=hh: e.reciprocal(out=ar32(10)[0:64, :], in_=sgl[hh][64:128, :]), reads=[sglB[hh]], writes=[ARB[20], ARB[21]])
                    k.op(DVE, lambda e, hh=hh, c=c: e.tensor_tensor(out=AR[hh * 64:(hh + 1) * 64, AT_ + c, :], in0=sgl[hh][0:64, :], in1=ar32(10)[0:64, :], op=ALU.mult),
                         reads=[sglB[hh], ARB[20], ARB[21]], writes=[ARB[AT_ + c]])
            if (n + 1) % DRIP == 0:
                pull()
        while side_alive[0]:
            pull()
        if i == 0:
            for b in late_casts:
                b()
        w, wb = wnext(("wsu",))
        wv = r3(w, 8, 512)
        for c in range(4):
            bank = c % 2
            for kc in range(8):
                k.op(PE, lambda e, kc=kc, c=c, bank=bank, wv=wv: e.matmul(PS[bank][:], lhsT=wv[:, kc, c * 128:(c + 1) * 128], rhs=xmT[:, kc, :], start=(kc == 0), stop=(kc == 7)),
                     reads=[wb, xmB[kc]], writes=[PSB[bank]])
            k.op(ACT, lambda e, c=c, bank=bank: e.activation(out=ar(GU_ + c), in_=PS[bank][:], func=AF.Gelu_apprx_tanh), reads=[PSB[bank]], writes=[ARB[GU_ + c]])
        w, wb = wnext(("wsv",))
        wv2 = r3(w, 8, 512)
        def sv_proj(m):
            bank = 2 + m % 2
            for kc in range(8):
                k.op(PE, lambda e, kc=kc, m=m, bank=bank, wv2=wv2: e.matmul(PS[bank][:], lhsT=xmT[:, kc, m * 128:(m + 1) * 128], rhs=wv2[:, kc, :], start=(kc == 0), stop=(kc == 7)),
                     reads=[wb, xmB[kc]], writes=[PSB[bank]])

        sv_proj(0)
        for m in range(4):
            bank = 2 + m % 2
            if m + 1 < 4:
                sv_proj(m + 1)
            if m % 2 == 0:
                gv_, gv_B, gv2_, gv2_B, vn_, vn_B, st_, st_B = gv[:], [gvB], gv2[:], [gv2B], vn[:], [vnB], st, stB
            else:
                gv_, gv_B = ar32(0), [ARB[0], ARB[1]]
                gv2_, gv2_B = ar32(1), [ARB[2], ARB[3]]
                vn_, vn_B = AR[:, 16, :].rearrange("p (g c) -> p g c", g=4), [ARB[16]]
                st_, st_B = st2, st2B
            k.op(ACT, lambda e, bank=bank, gv_=gv_: e.activation(out=gv_, in_=PS[bank][:], func=AF.Gelu_apprx_tanh), reads=[PSB[bank]], writes=gv_B)
            k.op(POOL, lambda e, gv_=gv_, gv2_=gv2_: e.tensor_tensor(out=gv2_, in0=gv_, in1=gv_, op=ALU.mult), reads=gv_B, writes=gv2_B)
            k.op(DVE, lambda e, gv_=gv_, st_=st_: e.tensor_reduce(out=st_[:, 4:8], in_=gv_.rearrange("p (g c) -> p g c", g=4), axis=AX.X, op=ALU.add), reads=gv_B + [st_B], writes=[st_B])
            k.op(DVE, lambda e, gv2_=gv2_, st_=st_: e.tensor_reduce(out=st_[:, 8:12], in_=gv2_.rearrange("p (g c) -> p g c", g=4), axis=AX.X, op=ALU.add), reads=gv2_B + [st_B], writes=[st_B])
            k.op(DVE, lambda e, st_=st_: e.tensor_scalar(out=st_[:, 12:16], in0=st_[:, 4:8], scalar1=1.0 / 128, scalar2=None, op0=ALU.mult), reads=[st_B], writes=[st_B])
            k.op(DVE, lambda e, st_=st_: e.tensor_tensor(out=st_[:, 16:20], in0=st_[:, 12:16], in1=st_[:, 12:16], op=ALU.mult), reads=[st_B], writes=[st_B])
            k.op(DVE, lambda e, st_=st_: e.scalar_tensor_tensor(out=st_[:, 20:24], in0=st_[:, 8:12], scalar=1.0 / 128, in1=st_[:, 16:20], op0=ALU.mult, op1=ALU.subtract),
                 reads=[st_B], writes=[st_B])
            k.op(DVE, lambda e, st_=st_: e.tensor_scalar(out=st_[:, 24:28], in0=st_[:, 20:24], scalar1=EPS, scalar2=None, op0=ALU.add), reads=[st_B], writes=[st_B])
            k.op(POOL, lambda e, st_=st_: e.tensor_tensor(out=st_[:, 28:32], in0=st_[:, 24:28], in1=neghalf[:, 0:4], op=ALU.pow), reads=[st_B, epsB], writes=[st_B])
            for g in range(4):
                k.op(DVE, lambda e, g=g, gv_=gv_, vn_=vn_, st_=st_: e.tensor_scalar(out=vn_[:, g, :], in0=gv_[:, g * 128:(g + 1) * 128], scalar1=st_[:, 12 + g:13 + g], scalar2=st_[:, 28 + g:29 + g],
                                                         op0=ALU.subtract, op1=ALU.mult), reads=gv_B + [st_B] + vn_B, writes=vn_B)
            mb = 4 + m % 2
            k.op(PE, lambda e, mb=mb: e.matmul(PS[mb][:], lhsT=onesrow[0:1, :], rhs=bsprow[0:1, :], start=True, stop=False), reads=[rowB], writes=[PSB[mb]])
            for g in range(4):
                k.op(PE, lambda e, g=g, mb=mb, vn_=vn_: e.matmul(PS[mb][:, g * 128:(g + 1) * 128], lhsT=vn_[:, g, :], rhs=wspT[:, g, :], start=False, stop=(g == 3)),
                     reads=vn_B + [wspB], writes=[PSB[mb]])
            k.op(DVE, lambda e, m=m, mb=mb: e.tensor_tensor(out=AR[:, SG_:SG_ + 4, m * 128:(m + 1) * 128], in0=PS[mb][:].rearrange("p (g t) -> p g t", g=4),
                                                            in1=AR[:, GU_:GU_ + 4, m * 128:(m + 1) * 128], op=ALU.mult),
                 reads=[PSB[mb]] + ARB[GU_:GU_ + 4], writes=ARB[SG_:SG_ + 4])
        for pc in range(2):
            w, wb = wnext(("wo", pc))
            wv3 = r3(w, 8, 512)
            for ff in range(4):
                f = pc * 4 + ff
                bank = 6 + f % 2
                for kk in range(8):
                    src = AT_ + kk if kk < 4 else SG_ + kk - 4
                    k.op(PE, lambda e, kk=kk, ff=ff, src=src, bank=bank, wv3=wv3: e.matmul(PS[bank][:], lhsT=wv3[:, kk, ff * 128:(ff + 1) * 128], rhs=ar(src),
                                                                                         start=(kk == 0), stop=(kk == 7)),
                         reads=[wb, ARB[src]], writes=[PSB[bank]])
                k.op(DVE, lambda e, f=f, bank=bank: e.scalar_tensor_tensor(out=hb[:, f, :], in0=PS[bank][:], scalar=M(0, GT1, f), in1=hb[:, f, :],
                                                                          op0=ALU.mult, op1=ALU.add),
                     reads=[PSB[bank], modB, hbB[f]], writes=[hbB[f]])
        ffn(0, hb, hbB)

    def layer1a(i):
        hb, hbB = hT[i % 2], hB[i % 2]
        gw, gwB = GW[i % 2], GWB[i % 2]
        norm_mod(hb, hbB, T, lambda kc: C("A1_1", kc), lambda kc: M(1, SH1, kc), 0)
        for pp in range(4):
            w, wb = wnext(("pw1", pp))
            wv = w.rearrange("p (k s c) -> p k s c", k=8, s=2, c=256)
            if pp == 0:
                for kc in range(8):
                    for jj in range(2):
                        for s2 in range(2):
                            bank = jj * 2 + s2
                            k.op(PE, lambda e, kc=kc, s2=s2, bank=bank, jj=jj, wv=wv: e.matmul(
                                PS[bank][:], lhsT=wv[:, kc, s2, jj * 128:(jj + 1) * 128], rhs=xmT[:, kc, :], start=(kc == 0), stop=(kc == 7)),
                                reads=[wb, xmB[kc]], writes=[PSB[bank]])
            for jj in range(2):
                c = pp * 2 + jj
                ab_, gb_ = (0, 1) if c % 2 == 0 else (2, 3)
                for s2, bank in ((0, ab_), (1, gb_)):
                    if pp == 0:
                        break
                    for kc in range(8):
                        k.op(PE, lambda e, kc=kc, s2=s2, bank=bank, jj=jj, wv=wv: e.matmul(
                            PS[bank][:], lhsT=wv[:, kc, s2, jj * 128:(jj + 1) * 128], rhs=xmT[:, kc, :], start=(kc == 0), stop=(kc == 7)),
                            reads=[wb, xmB[kc]], writes=[PSB[bank]])
                si = c % 2
                k.op(ACT, lambda e, si=si, gb_=gb_, c=c: e.activation(out=sgl[si][:], in_=PS[gb_][:], func=AF.Sigmoid, bias=V("bpw1", 8 + c)),
                     reads=[PSB[gb_], vecB], writes=[sglB[si]])
                k.op(DVE, lambda e, si=si, ab_=ab_, c=c: e.scalar_tensor_tensor(out=gw[:, c, 15:15 + T], in0=PS[ab_][:], scalar=V("bpw1", c), in1=sgl[si][:],
                                                                               op0=ALU.add, op1=ALU.mult),
                     reads=[PSB[ab_], vecB, sglB[si]], writes=[gwB[c]])
        if i == 0:
            k.op(POOL, lambda e: e.memset(gw[:, :, 0:15], 0.0), reads=gwB, writes=gwB)
        else:
            pg, pgB = GW[(i - 1) % 2], GWB[(i - 1) % 2]
            k.op(POOL, lambda e: e.tensor_copy(out=gw[:, :, 0:15], in_=pg[:, :, T:T + 15]), reads=pgB + gwB, writes=gwB)
            k.op(POOL, lambda e: e.tensor_copy(out=pg[:, :, T + 15:T + 30], in_=gw[:, :, 15:30]), reads=gwB + pgB, writes=pgB)
        if i == NT - 1:
            k.op(POOL, lambda e: e.memset(gw[:, :, T + 15:T + 30], 0.0), reads=gwB, writes=gwB)

    def layer1b(i):
        hb, hbB = hT[i % 2], hB[i % 2]
        gw, gwB = GW[i % 2], GWB[i % 2]
        wd0 = VEC_ROWS["wdw"]
        for f in range(8):
            k.op(DVE, lambda e, f=f: e.tensor_scalar(out=hb[:, f, :], in0=hb[:, f, :], scalar1=C("gb", f), scalar2=None, op0=ALU.add),
                 reads=[hbB[f], coefB], writes=[hbB[f]])
        def stat_mm(kc):
            k.op(PE, lambda e, kc=kc: e.matmul(PS[0][:], lhsT=ones[:], rhs=sq[0][:], start=(kc == 0), stop=(kc == 7)), reads=[sqB[0], onesB], writes=[PSB[0]])
            k.op(PE, lambda e, kc=kc: e.matmul(PS[1][:], lhsT=ones[:], rhs=sq[1][:], start=(kc == 0), stop=(kc == 7)), reads=[sqB[1], onesB], writes=[PSB[1]])

        for kc in range(8):
            acc = ar32(kc)
            accB = [ARB[2 * kc], ARB[2 * kc + 1]]
            bank = 2 + kc % 4
            for j in range(31):
                r = dgstate["n"] % NDG
                dgstate["n"] += 1
                k.op(POOL if j % 3 == 2 else DVE, lambda e, r=r, j=j, kc=kc: e.tensor_scalar(out=dg[r][:], in0=identb[:], scalar1=vecT[:, wd0 + j * 8 + kc:wd0 + j * 8 + kc + 1], scalar2=1.0,
                                                                     op0=ALU.mult, op1=ALU.mult),
                     reads=[identB, vecB], writes=[dgB[r]])
                k.op(PE, lambda e, r=r, j=j, kc=kc, bank=bank: e.matmul(PS[bank][:], lhsT=dg[r][:], rhs=gw[:, kc, j:j + T], start=(j == 0), stop=(j == 30)),
                     reads=[dgB[r], gwB[kc]], writes=[PSB[bank]])
            if kc > 0:
                stat_mm(kc - 1)
            k.op(DVE, lambda e, kc=kc, acc=acc, bank=bank: e.tensor_scalar(out=acc, in0=PS[bank][:], scalar1=V("bdw", kc), scalar2=None, op0=ALU.add),
                 reads=[PSB[bank], vecB], writes=accB)
            k.op(ACT, lambda e, acc=acc: e.activation(out=sq[0][:], in_=acc, func=AF.Copy), reads=accB, writes=[sqB[0]])
            k.op(ACT, lambda e, acc=acc: e.activation(out=sq[1][:], in_=acc, func=AF.Square), reads=accB, writes=[sqB[1]])
        stat_mm(7)
        mean, meanB = sgl[0], sglB[0]
        k.op(ACT, lambda e: e.activation(out=mean[:], in_=PS[0][:], func=AF.Copy, scale=1.0 / D), reads=[PSB[0]], writes=[meanB])
        k.op(POOL, lambda e: e.tensor_tensor(out=sgl[1][:], in0=mean[:], in1=mean[:], op=ALU.mult), reads=[meanB], writes=[sglB[1]])
        k.op(DVE, lambda e: e.scalar_tensor_tensor(out=sd[:], in0=PS[1][:], scalar=1.0 / D, in1=sgl[1][:], op0=ALU.mult, op1=ALU.subtract),
             reads=[PSB[1], sglB[1]], writes=[sdB])
        k.op(ACT, lambda e: e.activation(out=sd[:], in_=sd[:], func=AF.Ln, bias=epsT[:]), reads=[sdB, epsB], writes=[sdB])
        k.op(ACT, lambda e: e.activation(out=rstd[:], in_=sd[:], func=AF.Exp, scale=-0.5), reads=[sdB], writes=[rstdB])
        for kc in range(8):
            acc = ar32(kc)
            accB = [ARB[2 * kc], ARB[2 * kc + 1]]
            if kc % 3 == 2:
                ve_, ta, taB, tb, tbB = POOL, sgl[1], sglB[1], tmp[2], tmpB[2]
            else:
                ve_, ta, taB, tb, tbB = DVE, tmp[0], tmpB[0], tmp[1], tmpB[1]
            k.op(ve_, lambda e, acc=acc, ta=ta: e.tensor_tensor(out=ta[:], in0=acc, in1=mean[:], op=ALU.subtract), reads=accB + [meanB], writes=[taB])
            k.op(ve_, lambda e, ta=ta, tb=tb: e.tensor_tensor(out=tb[:], in0=ta[:], in1=rstd[:], op=ALU.mult), reads=[taB, rstdB], writes=[tbB])
            k.op(ACT, lambda e, kc=kc, tb=tb: e.activation(out=xmT[:, kc, :], in_=tb[:], func=AF.Silu, scale=V("lng", kc), bias=V("lnb", kc)),
                 reads=[tbB, vecB], writes=[xmB[kc]])
        for pc in range(2):
            w, wb = wnext(("pw2", pc))
            wv = r3(w, 8, 512)
            for ff in range(4):
                f = pc * 4 + ff
                bank = 6 + f % 2
                for kc in range(8):
                    k.op(PE, lambda e, kc=kc, ff=ff, bank=bank, wv=wv: e.matmul(PS[bank][:], lhsT=wv[:, kc, ff * 128:(ff + 1) * 128], rhs=xmT[:, kc, :],
                                                                              start=(kc == 0), stop=(kc == 7)),
                         reads=[wb, xmB[kc]], writes=[PSB[bank]])
                k.op(DVE, lambda e, f=f, bank=bank: e.scalar_tensor_tensor(out=hb[:, f, :], in0=PS[bank][:], scalar=M(1, GT1, f), in1=hb[:, f, :],
                                                                          op0=ALU.mult, op1=ALU.add),
                     reads=[PSB[bank], modB, hbB[f]], writes=[hbB[f]])
        ffn(1, hb, hbB)
        rms_stats(hb, hbB, T, 0)
        for kc in range(8):
            k.op(DVE, lambda e, kc=kc: e.scalar_tensor_tensor(out=hb[:, kc, :], in0=hb[:, kc, :], scalar=V("gfin", kc), in1=rstd[:], op0=ALU.mult, op1=ALU.mult),
                 reads=[hbB[kc], vecB, rstdB], writes=[hbB[kc]])
        for m in range(4):
            ib = iostate["n"] % 2
            iostate["n"] += 1
            for half in range(2):
                bank = 6 + half
                for kk in range(4):
                    kc = half * 4 + kk
                    k.op(PE, lambda e, kc=kc, kk=kk, bank=bank, m=m: e.transpose(out=PS[bank][:, kk * 128:(kk + 1) * 128], in_=hb[:, kc, m * 128:(m + 1) * 128],
                                                                               identity=ident[:]),
                         reads=[hbB[kc], identB], writes=[PSB[bank]])
                if half == 0:
                    k.op(ACT, lambda e, ib=ib, bank=bank: e.activation(out=io[ib][:, 0:512], in_=PS[bank][:], func=AF.Copy), reads=[PSB[bank]], writes=[ioB[ib]])
                else:
                    k.op(DVE, lambda e, ib=ib, bank=bank: e.tensor_copy(out=io[ib][:, 512:1024], in_=PS[bank][:]), reads=[PSB[bank], ioB[ib]], writes=[ioB[ib]])
            k.dma(SP, lambda e, ib=ib, m=m: e.dma_start(out=out_d[i * T + m * 128: i * T + (m + 1) * 128, :], in_=io[ib][:]), reads=[ioB[ib]])

    coef_A("A1_1", 1, SC1, "gmix")
    coef_A("A2_1", 1, SC2, "gffn")
    c0 = CO["gb"]
    k.op(DVE, lambda e: e.tensor_tensor(out=coef[:, c0:c0 + 8], in0=modsb[:, 48 + GT1 * 8:48 + GT1 * 8 + 8, 0],
                                        in1=vecT[:, VEC_ROWS["bpw2"]:VEC_ROWS["bpw2"] + 8], op=ALU.mult),
         reads=[modB, vecB, coefB], writes=[coefB])
    outB = Buf("out")
    for i in range(NT):
        layer0(i)
        layer1a(i)
        if i >= 1:
            layer1b(i - 1)
    layer1b(NT - 1)
    assert wstate["next"] == len(seq)

    print("sbuf bytes remaining/partition:", nc.sbuf_bytes_remaining)
    k.emit()
    k.finish(SP, ioB + [dumpB])
    return nc, k


def make_vecs(c_b, c_ctx, b_mod, g_mix, g_ffn, b_pw1, b_dw, ln_g, ln_b, b_pw2, g_final, w_dw, q_gain, k_gain):
    rows = [c_b.reshape(8, 128), c_ctx.reshape(8, 128), b_mod.reshape(96, 128), g_mix.reshape(16, 128), g_ffn.reshape(16, 128),
            b_pw1.reshape(16, 128), b_dw.reshape(8, 128), ln_g.reshape(8, 128), ln_b.reshape(8, 128), b_pw2.reshape(8, 128),
            g_final.reshape(8, 128), w_dw.reshape(248, 128),
            np.concatenate([q_gain.reshape(64), q_gain.reshape(64)])[None, :],
            np.concatenate([k_gain.reshape(64), k_gain.reshape(64)])[None, :]]
    v = np.concatenate(rows, 0).astype(np.float32)
    out = np.zeros((NVEC, 128), np.float32)
    out[:v.shape[0]] = v
    return out


def make_in_maps(x, c, ctx, c_ctx, w_mod, b_mod, g_mix, g_ffn, w_ffn_in, w_ffn_out, w_in, q_gain, k_gain, w_sp, b_sp, w_out,
                 w_pw1, b_pw1, w_dw, b_dw, ln_g, ln_b, w_pw2, b_pw2, g_final):
    f = lambda a: np.ascontiguousarray(np.asarray(a, dtype=np.float32))
    B, S, _ = x.shape
    cos, sin = rope_tables(S)
    shared = {
        "w_mod": f(w_mod), "w_ffn_in": f(w_ffn_in), "w_ffn_out": f(w_ffn_out), "w_in": f(w_in[0]),
        "w_sp": f(w_sp[0]), "b_sp": f(b_sp[0]).reshape(1, 512), "w_out": f(w_out[0]), "w_pw1": f(w_pw1[0]), "w_pw2": f(w_pw2[0]),
        "ident": np.eye(128, dtype=np.float32), "pm": rope_perm(), "cos": cos, "sin": sin,
    }
    maps = []
    for b in range(B):
        m = dict(shared)
        m["x"] = f(x[b])
        m["ctx"] = f(ctx[b])
        m["vecs"] = make_vecs(f(c[b]), f(c_ctx), f(b_mod), f(g_mix), f(g_ffn), f(b_pw1[0]), f(b_dw[0]), f(ln_g[0]), f(ln_b[0]),
                              f(b_pw2[0]), f(g_final), f(w_dw[0]), f(q_gain[0]), f(k_gain[0]))
        maps.append(m)
    return maps


_NC_CACHE = {}


def kernel(**inputs):
    x = np.asarray(inputs["x"])
    B, S, _ = x.shape
    maps = make_in_maps(**inputs)
    if S not in _NC_CACHE:
        _NC_CACHE[S] = build(S)[0]
    nc = _NC_CACHE[S]
    res = run_bass_kernel_spmd(nc, maps, core_ids=list(range(B)))
    return np.stack([np.asarray(r["out"], dtype=np.float32) for r in res.results], 0)
```
